# Optimizing a Trainium2 kernel written in Bass

```python
import math
import jax, jax.numpy as jnp
from jax import lax
import numpy as np

D_MODEL = 1024
BATCH = 8
SEQ = 4096
DEPTH = 4

HEAD_DIM = 64
N_A_LAYERS = DEPTH // 2
N_B_LAYERS = DEPTH - N_A_LAYERS
MEM_LEN = 256
MEM_HEADS = 4
MEM_W = MEM_HEADS * HEAD_DIM
RWKV_HEADS = (D_MODEL - MEM_W) // HEAD_DIM
RWKV_W = RWKV_HEADS * HEAD_DIM
DECAY_LORA = 64
AAA_LORA = 64
GATE_LORA = 128
RWKV_SHIFT_W = 3 * RWKV_W + DECAY_LORA + AAA_LORA + GATE_LORA
A_IN_W = RWKV_SHIFT_W + MEM_W
DIL_GROUPS = ((128, 1), (512, 4), (2048, 16))
DIL_GROUP_HEADS = 4
DIL_HEADS = len(DIL_GROUPS) * DIL_GROUP_HEADS
DIL_W = DIL_HEADS * HEAD_DIM
DIL_OUT_W = DIL_GROUP_HEADS * HEAD_DIM
BLOCK = 128
NUM_BUCKETS = 32
MAX_DISTANCE = 2048
D_FF = 2816
NORM_EPS = 1e-6
LNX_EPS = 64e-5
NEG_INF = -1e30

kernel_name = 'yoco_rwkv7_dilated_macaron_hybrid'


def rms_norm(x, g):
    xf = x.astype(jnp.float32)
    y = xf * lax.rsqrt(jnp.mean(xf * xf, axis=-1, keepdims=True) + NORM_EPS)
    return (y * g.astype(jnp.float32)).astype(x.dtype)


def split_heads(t):
    return t.reshape(t.shape[:-1] + (t.shape[-1] // HEAD_DIM, HEAD_DIM))


def swiglu_ffn(x, norm_g, w_in, w_out):
    gate, up = jnp.split(rms_norm(x, norm_g) @ w_in, 2, axis=-1)
    return (jax.nn.silu(gate) * up) @ w_out


def token_shift(p, mu):
    prev = jnp.pad(p, ((0, 0), (1, 0), (0, 0)))[:, :-1]
    return p + mu * (prev - p)


def rwkv7_scan(r, w, k, v, kk, a):
    b, _, h, n = r.shape

    def step(state, inp):
        r_t, w_t, k_t, v_t, kk_t, a_t = inp
        sa = jnp.einsum('bhvk,bhk->bhv', state, -kk_t)
        state = (state * w_t[:, :, None, :]
                 + sa[..., None] * (kk_t * a_t)[:, :, None, :]
                 + v_t[..., None] * k_t[:, :, None, :])
        return state, jnp.einsum('bhvk,bhk->bhv', state, r_t)

    xs = tuple(jnp.moveaxis(t.astype(jnp.float32), 1, 0) for t in (r, w, k, v, kk, a))
    _, ys = lax.scan(step, jnp.zeros((b, h, n, n), jnp.float32), xs)
    return jnp.moveaxis(ys, 0, 1)


def rwkv7_mix(p, mu, w0, w_up, a0, a_up, g_up, kk_scale, k_a, r_k, lnx_g, lnx_b):
    out_dtype = p.dtype
    b, s = p.shape[:2]
    p = token_shift(p, mu).astype(jnp.float32)
    cuts = [RWKV_W, 2 * RWKV_W, 3 * RWKV_W, 3 * RWKV_W + DECAY_LORA,
            3 * RWKV_W + DECAY_LORA + AAA_LORA]
    r, k, v, w_lo, a_lo, g_lo = jnp.split(p, cuts, axis=-1)
    w_log = -jax.nn.softplus(-(w0 + jnp.tanh(w_lo) @ w_up)) - 0.5
    decay = jnp.exp(-jnp.exp(w_log))
    a = jax.nn.sigmoid(a0 + a_lo @ a_up)
    g = jax.nn.sigmoid(g_lo) @ g_up
    kk = split_heads(k * kk_scale)
    kk = kk / jnp.maximum(jnp.linalg.norm(kk, axis=-1, keepdims=True), 1e-12)
    k = k * (1.0 + (a - 1.0) * k_a)
    r_h, k_h, v_h, a_h, w_h = (split_heads(t) for t in (r, k, v, a, decay))
    y = rwkv7_scan(r_h, w_h, k_h, v_h, kk, a_h)
    mean = jnp.mean(y, axis=-1, keepdims=True)
    var = jnp.mean(jnp.square(y - mean), axis=-1, keepdims=True)
    y = ((y - mean) * lax.rsqrt(var + LNX_EPS)).reshape(b, s, RWKV_W) * lnx_g + lnx_b
    bonus = jnp.sum(r_h * k_h * r_k, axis=-1, keepdims=True) * v_h
    y = (y + bonus.reshape(b, s, RWKV_W)) * g
    return y.astype(out_dtype)


def memory_attention(q, mem, mem_norm_g, w_kv, q_gain, k_gain):
    k, v = jnp.split(rms_norm(mem, mem_norm_g) @ w_kv, 2, axis=-1)
    k = rms_norm(split_heads(k), k_gain)
    v = split_heads(v)
    q = rms_norm(q, q_gain)
    logits = jnp.einsum('bshd,bmhd->bhsm', q, k).astype(jnp.float32) / math.sqrt(HEAD_DIM)
    probs = jax.nn.softmax(logits, axis=-1)
    return jnp.einsum('bhsm,bmhd->bshd', probs.astype(v.dtype), v)


def padded_len(seq_len, dil):
    unit = dil * BLOCK
    return -(-seq_len // unit) * unit


def strided_blocks(t, dil, s_pad):
    b, s = t.shape[:2]
    t = jnp.pad(t, ((0, 0), (0, s_pad - s)) + ((0, 0),) * (t.ndim - 2))
    t = t.reshape((b, s_pad // dil, dil) + t.shape[2:])
    t = jnp.moveaxis(t, 2, 1)
    return t.reshape((b, dil, s_pad // (dil * BLOCK), BLOCK) + t.shape[3:])


def unstride_blocks(t, seq_len):
    b, dil, nb, blk = t.shape[:4]
    t = t.reshape((b, dil, nb * blk) + t.shape[4:])
    t = jnp.moveaxis(t, 1, 2)
    return t.reshape((b, dil * nb * blk) + t.shape[3:])[:, :seq_len]


def with_prev_block(t):
    prev = jnp.pad(t[:, :, :-1], ((0, 0), (0, 0), (1, 0), (0, 0), (0, 0), (0, 0)))
    return jnp.concatenate([prev, t], axis=3)


def t5_bucket(dist):
    max_exact = NUM_BUCKETS // 2
    d_f = jnp.maximum(dist, 1).astype(jnp.float32)
    large = max_exact + (jnp.log(d_f / max_exact) / math.log(MAX_DISTANCE / max_exact)
                         * (NUM_BUCKETS - max_exact)).astype(jnp.int32)
    large = jnp.minimum(large, NUM_BUCKETS - 1)
    return jnp.where(dist < max_exact, dist, large)


def band_mask_and_bias(window, dil, n_blocks, table):
    span = window // dil
    qi = jnp.arange(BLOCK)[:, None]
    kj = jnp.arange(2 * BLOCK)[None, :]
    dsub = BLOCK + qi - kj
    band = (dsub >= 0) & (dsub <= span)
    first = (jnp.arange(n_blocks)[:, None, None] > 0) | (kj[None] >= BLOCK)
    mask = band[None] & first
    bias = jnp.transpose(table[t5_bucket(jnp.maximum(dsub, 0) * dil)], (2, 0, 1))
    return mask, bias


def shared_kv_blocks(x, kv_norm, kv_w, k_gain):
    k, v = jnp.split(rms_norm(x, kv_norm) @ kv_w, 2, axis=-1)
    k = rms_norm(split_heads(k), k_gain)
    v = split_heads(v)
    seq_len = x.shape[1]
    k_blocks, v_blocks = [], []
    for g, (_, dil) in enumerate(DIL_GROUPS):
        hs = slice(g * DIL_GROUP_HEADS, (g + 1) * DIL_GROUP_HEADS)
        s_pad = padded_len(seq_len, dil)
        k_blocks.append(strided_blocks(k[:, :, hs], dil, s_pad))
        v_blocks.append(strided_blocks(v[:, :, hs], dil, s_pad))
    return k_blocks, v_blocks


def dilated_attention(q, k_blocks, v_blocks, q_gain, rel_bias, seq_len):
    q = rms_norm(q, q_gain)
    scale = 1.0 / math.sqrt(HEAD_DIM)
    outs, lses = [], []
    for g, (window, dil) in enumerate(DIL_GROUPS):
        hs = slice(g * DIL_GROUP_HEADS, (g + 1) * DIL_GROUP_HEADS)
        qb = strided_blocks(q[:, :, hs], dil, padded_len(seq_len, dil))
        kw = with_prev_block(k_blocks[g])
        vw = with_prev_block(v_blocks[g]).astype(jnp.float32)
        mask, bias = band_mask_and_bias(window, dil, qb.shape[2], rel_bias[:, hs])
        logits = jnp.einsum('brnqhd,brnkhd->brnhqk', qb, kw).astype(jnp.float32) * scale
        logits = jnp.where(mask[:, None], logits + bias, NEG_INF)
        m = jnp.max(logits, axis=-1, keepdims=True)
        p = jnp.exp(logits - m)
        l = jnp.sum(p, axis=-1, keepdims=True)
        o = jnp.einsum('brnhqk,brnkhd->brnqhd', p, vw) / jnp.swapaxes(l, 3, 4)
        lse = jnp.swapaxes(m + jnp.log(l), 3, 4)
        outs.append(unstride_blocks(o, seq_len))
        lses.append(unstride_blocks(lse, seq_len))
    weights = jax.nn.softmax(jnp.stack(lses), axis=0)
    return jnp.sum(weights * jnp.stack(outs), axis=0).astype(q.dtype)


def setup_inputs(seed: int = 0) -> dict:
    key = jax.random.key(seed)
    keys = iter(jax.random.split(key, 48))

    def normal(shape, scale):
        return jax.random.normal(next(keys), shape, jnp.float32) * scale

    def gain(shape):
        return 1.0 + normal(shape, 0.02)

    def uniform(shape, lo, hi):
        return jax.random.uniform(next(keys), shape, jnp.float32, lo, hi)

    d, f = D_MODEL, D_FF
    return {
        'x': normal((BATCH, SEQ, d), 1.0),
        'mem': normal((BATCH, MEM_LEN, d), 1.0),
        'ffn_pre_norm': gain((DEPTH, d)),
        'ffn_pre_w_in': normal((DEPTH, d, 2 * f), d ** -0.5),
        'ffn_pre_w_out': normal((DEPTH, f, d), f ** -0.5),
        'mix_norm': gain((DEPTH, d)),
        'ffn_post_norm': gain((DEPTH, d)),
        'ffn_post_w_in': normal((DEPTH, d, 2 * f), d ** -0.5),
        'ffn_post_w_out': normal((DEPTH, f, d), f ** -0.5),
        'mem_norm': gain((DEPTH, d)),
        'mem_w_kv': normal((DEPTH, d, 2 * MEM_W), d ** -0.5),
        'mem_q_norm': gain((DEPTH, HEAD_DIM)),
        'mem_k_norm': gain((DEPTH, HEAD_DIM)),
        'a_w_in': normal((N_A_LAYERS, d, A_IN_W), d ** -0.5),
        'a_shift_mu': uniform((N_A_LAYERS, RWKV_SHIFT_W), 0.0, 1.0),
        'a_w0': uniform((N_A_LAYERS, RWKV_W), -6.0, -1.0),
        'a_w_up': normal((N_A_LAYERS, DECAY_LORA, RWKV_W), 0.5 * DECAY_LORA ** -0.5),
        'a_a0': normal((N_A_LAYERS, RWKV_W), 0.1),
        'a_a_up': normal((N_A_LAYERS, AAA_LORA, RWKV_W), AAA_LORA ** -0.5),
        'a_g_up': normal((N_A_LAYERS, GATE_LORA, RWKV_W), GATE_LORA ** -0.5),
        'a_kk_scale': 0.85 + normal((N_A_LAYERS, RWKV_W), 0.02),
        'a_k_a': gain((N_A_LAYERS, RWKV_W)),
        'a_r_k': normal((N_A_LAYERS, RWKV_HEADS, HEAD_DIM), 0.1),
        'a_lnx_g': gain((N_A_LAYERS, RWKV_W)),
        'a_lnx_b': normal((N_A_LAYERS, RWKV_W), 0.01),
        'a_w_out': normal((N_A_LAYERS, RWKV_W + MEM_W, d), (RWKV_W + MEM_W) ** -0.5),
        'b_w_q': normal((N_B_LAYERS, d, DIL_W + MEM_W), d ** -0.5),
        'b_q_norm': gain((N_B_LAYERS, HEAD_DIM)),
        'b_w_out': normal((N_B_LAYERS, DIL_OUT_W + MEM_W, d), (DIL_OUT_W + MEM_W) ** -0.5),
        'kv_norm': gain((d,)),
        'kv_w': normal((d, 2 * DIL_W), d ** -0.5),
        'kv_k_norm': gain((HEAD_DIM,)),
        'rel_bias': normal((NUM_BUCKETS, DIL_HEADS), 0.2),
    }


def reference(x, mem, ffn_pre_norm, ffn_pre_w_in, ffn_pre_w_out, mix_norm,
              ffn_post_norm, ffn_post_w_in, ffn_post_w_out,
              mem_norm, mem_w_kv, mem_q_norm, mem_k_norm,
              a_w_in, a_shift_mu, a_w0, a_w_up, a_a0, a_a_up, a_g_up,
              a_kk_scale, a_k_a, a_r_k, a_lnx_g, a_lnx_b, a_w_out,
              b_w_q, b_q_norm, b_w_out, kv_norm, kv_w, kv_k_norm, rel_bias):
    b, s = x.shape[:2]
    k_blocks, v_blocks = None, None
    for layer in range(DEPTH):
        x = x + 0.5 * swiglu_ffn(x, ffn_pre_norm[layer], ffn_pre_w_in[layer], ffn_pre_w_out[layer])
        u = rms_norm(x, mix_norm[layer])
        if layer < N_A_LAYERS:
            i = layer
            proj = u @ a_w_in[i]
            y_main = rwkv7_mix(proj[..., :RWKV_SHIFT_W], a_shift_mu[i], a_w0[i], a_w_up[i],
                               a_a0[i], a_a_up[i], a_g_up[i], a_kk_scale[i], a_k_a[i],
                               a_r_k[i], a_lnx_g[i], a_lnx_b[i])
            y_mem = memory_attention(split_heads(proj[..., RWKV_SHIFT_W:]), mem, mem_norm[layer],
                                     mem_w_kv[layer], mem_q_norm[layer], mem_k_norm[layer])
            y = jnp.concatenate([y_main, y_mem.reshape(b, s, MEM_W)], axis=-1) @ a_w_out[i]
        else:
            j = layer - N_A_LAYERS
            q_all = u @ b_w_q[j]
            y_dil = dilated_attention(split_heads(q_all[..., :DIL_W]), k_blocks, v_blocks,
                                      b_q_norm[j], rel_bias, s)
            y_mem = memory_attention(split_heads(q_all[..., DIL_W:]), mem, mem_norm[layer],
                                     mem_w_kv[layer], mem_q_norm[layer], mem_k_norm[layer])
            y = jnp.concatenate([y_dil.reshape(b, s, DIL_OUT_W),
                                 y_mem.reshape(b, s, MEM_W)], axis=-1) @ b_w_out[j]
        x = x + y
        x = x + 0.5 * swiglu_ffn(x, ffn_post_norm[layer], ffn_post_w_in[layer], ffn_post_w_out[layer])
        if layer == N_A_LAYERS - 1:
            k_blocks, v_blocks = shared_kv_blocks(x, kv_norm, kv_w, kv_k_norm)
    return x
```

```python
import os
import numpy as np
from contextlib import ExitStack
import concourse.bass as bass
import concourse.mybir as mybir
from concourse.bass_utils import run_bass_kernel_spmd

F32 = mybir.dt.float32
BF16 = mybir.dt.bfloat16
AF = mybir.ActivationFunctionType
ALU = mybir.AluOpType
AX = mybir.AxisListType

D = 1024
KD = 8
S = 4096
FF = 2816
KF = 22
TT = 512
NT = S // TT
NORM_EPS = 1e-6

COMPUTE = ("pe", "act", "dve", "pool")


class Prog:
    def __init__(self, nc, es):
        self.nc = nc
        self.eng = dict(pe=nc.tensor, act=nc.scalar, dve=nc.vector, pool=nc.gpsimd, sp=nc.sync)
        self.sem = {e: es.enter_context(nc.semaphore("s_" + e)) for e in COMPUTE}
        self.cnt = {e: 0 for e in COMPUTE}
        self.es = es
        self.dsem = {}
        self.dcnt = {}
        self.pending = []
        self.last_w = {}
        self.readers = {}
        self.waited = {e: {} for e in self.eng}
        self.done = {}
        self.sigs = {e: [] for e in COMPUTE}
        self.nops = 0
        self.ninstr = 0

    def add(self, eng, fn, r=(), w=(), dma=None):
        idx = self.nops
        self.nops += 1
        deps = set()
        for k in r:
            j = self.last_w.get(k)
            if j is not None:
                deps.add(j)
        for k in w:
            j = self.last_w.get(k)
            if j is not None:
                deps.add(j)
            rd = self.readers.get(k)
            if rd:
                deps.update(rd.values())
        for k in w:
            self.last_w[k] = idx
            self.readers[k] = {}
        tag = ("d", dma) if dma else ("c", eng)
        for k in r:
            if k not in w:
                self.readers.setdefault(k, {})[tag] = idx
        self.pending.append(dict(idx=idx, eng=eng, fn=fn, deps=deps, dma=dma, sig=False))
        return idx

    def _wait(self, eng, semname, semh, val):
        if self.waited[eng].get(semname, 0) >= val:
            return
        self.waited[eng][semname] = val
        self.eng[eng].wait_ge(semh, val)
        self.ninstr += 1

    def flush(self):
        pend = self.pending
        self.pending = []
        byidx = {op["idx"]: op for op in pend}
        for op in pend:
            for j in op["deps"]:
                d = byidx.get(j)
                if d is not None and d["dma"] is None:
                    if not (d["eng"] == "pe" and op["eng"] == "pe" and op["dma"] is None):
                        d["sig"] = True
        last = {}
        for op in pend:
            if op["dma"] is None:
                last[op["eng"]] = op
        for op in last.values():
            op["sig"] = True
        for op in pend:
            eng = op["eng"]
            for j in sorted(op["deps"]):
                if j in self.done:
                    kind, name, val = self.done[j]
                    if kind == "d":
                        self._wait(eng, "d_" + name, self.dsem[name], val)
                    else:
                        if name == "pe" and eng == "pe" and op["dma"] is None:
                            continue
                        if val is None:
                            lst = self.sigs[name]
                            lo, hi = 0, len(lst)
                            while lo < hi:
                                mid = (lo + hi) // 2
                                if lst[mid][0] < j:
                                    lo = mid + 1
                                else:
                                    hi = mid
                            val = lst[lo][1]
                        self._wait(eng, "c_" + name, self.sem[name], val)
                else:
                    raise RuntimeError("dep on unemitted op")
            if op["dma"]:
                name = op["dma"]
                if name not in self.dsem:
                    self.dsem[name] = self.es.enter_context(self.nc.semaphore("d_" + name))
                    self.dcnt[name] = 0
                if self.dcnt[name] > 0:
                    self._wait(eng, "d_" + name, self.dsem[name], self.dcnt[name])
                ins = op["fn"](self.eng[eng])
                self.dcnt[name] += 16
                ins.then_inc(self.dsem[name], 16)
                self.done[op["idx"]] = ("d", name, self.dcnt[name])
            else:
                ins = op["fn"](self.eng[eng])
                if op["sig"]:
                    self.cnt[eng] += 1
                    ins.then_inc(self.sem[eng], 1)
                    self.done[op["idx"]] = ("c", eng, self.cnt[eng])
                    self.sigs[eng].append((op["idx"], self.cnt[eng]))
                else:
                    self.done[op["idx"]] = ("c", eng, None)
            self.ninstr += 1

    def barrier(self, dma_only_on=("sp",)):
        self.flush()
        for f in self.eng:
            for e in COMPUTE:
                if e != f and self.cnt[e] > 0:
                    self._wait(f, "c_" + e, self.sem[e], self.cnt[e])
            for name, h in self.dsem.items():
                if self.dcnt[name] > 0:
                    self._wait(f, "d_" + name, h, self.dcnt[name])
        self.last_w = {}
        self.readers = {}


    def mm(self, out, lhsT, rhs, start=True, stop=True, r=(), w=()):
        self.add("pe", lambda e: e.matmul(out, lhsT, rhs, start=start, stop=stop), r=r, w=w)

    def tr(self, out, in_, ident, r=(), w=()):
        self.add("pe", lambda e: e.matmul(out, in_, ident, start=True, stop=True), r=r, w=w)

    def act(self, out, in_, func, r=(), w=(), bias=None, scale=None):
        kw = {}
        if bias is not None:
            kw["bias"] = bias
        if scale is not None:
            kw["scale"] = scale
        self.add("act", lambda e: e.activation(out=out, in_=in_, func=func, **kw), r=r, w=w)

    def tt(self, eng, out, in0, in1, op, r=(), w=()):
        self.add(eng, lambda e: e.tensor_tensor(out, in0, in1, op), r=r, w=w)

    def ts(self, eng, out, in0, s1, s2, op0, op1, r=(), w=()):
        self.add(eng, lambda e: e.tensor_scalar(out, in0, s1, s2, op0, op1), r=r, w=w)

    def tsmul(self, eng, out, in0, s1, r=(), w=()):
        self.add(eng, lambda e: e.tensor_scalar_mul(out, in0, s1), r=r, w=w)

    def stt(self, eng, out, in0, scalar, in1, op0, op1, r=(), w=()):
        self.add(eng, lambda e: e.scalar_tensor_tensor(out, in0, scalar, in1, op0, op1), r=r, w=w)

    def copy(self, eng, out, in_, r=(), w=()):
        if eng == "act":
            self.add("act", lambda e: e.activation(out=out, in_=in_, func=AF.Copy), r=r, w=w)
        elif os.environ.get("KCOPY", "mul") == "mul":
            self.add(eng, lambda e: e.tensor_scalar_mul(out, in_, 1.0), r=r, w=w)
        else:
            self.add(eng, lambda e: e.tensor_copy(out, in_), r=r, w=w)

    def dma(self, q, out, in_, sem, r=(), w=()):
        self.add(q, lambda e: e.dma_start(out=out, in_=in_), r=r, w=w, dma=sem)


def _slots_in(w):
    K, M = w.shape
    return np.ascontiguousarray(w.reshape(K // 128, 128, M // 128, 128).transpose(2, 1, 0, 3))


def _vec_pk(v):
    return np.ascontiguousarray(v.reshape(-1, 128).T)


class Ctx:
    pass


_UID = [0]


def _u(name):
    _UID[0] += 1
    return "%s_%d" % (name, _UID[0])


def ffn_phase(P, nc, c, src, dst, w_in_d, w_out_d, gcol):
    with ExitStack() as es:
        def sb(name, shape, dt):
            return es.enter_context(nc.sbuf_tensor(_u(name), shape, dt))

        def ps(name, shape, dt=F32):
            return es.enter_context(nc.psum_tensor(_u(name), shape, dt))
        WIN = sb("f_win", [128, 2 * KF, KD, 128], BF16)
        WOUT = sb("f_wout", [128, KF, D], BF16)
        XT = [sb("f_xt%d" % i, [128, KD, TT], F32) for i in range(2)]
        XN = sb("f_xn", [128, KD, TT], BF16)
        ACTB = sb("f_act", [128, KF, TT], BF16)
        SQ = [sb("f_sq%d" % i, [128, TT], BF16) for i in range(1)]
        SG = [sb("f_sg%d" % i, [128, TT], BF16) for i in range(2)]
        RSTD = sb("f_rstd", [128, TT], F32)
        PH = [ps("f_ph%d" % i, [128, TT]) for i in range(4)]
        PY = [ps("f_py%d" % i, [128, TT]) for i in range(2)]
        PSS = ps("f_pss", [128, TT])

        G = 2
        for j0 in range(0, 2 * KF, G):
            P.add("pool", lambda e, j0=j0: e.dma_start(
                out=WIN[:, j0:j0 + G], in_=w_in_d[j0:j0 + G].rearrange("j p k c -> p j k c")),
                w=[("win", j) for j in range(j0, j0 + G)], dma="wq%d" % ((j0 // G) % 4))
        for k0 in range(0, KF, G):
            P.add("pool", lambda e, k0=k0: e.dma_start(
                out=WOUT[:, k0:k0 + G], in_=w_out_d[k0 * 128:(k0 + G) * 128, :].rearrange("(k p) c -> p k c", p=128)),
                w=[("wout", k) for k in range(k0, k0 + G)], dma="wq%d" % ((k0 // G) % 4))

        def load(t):
            b = t % 2
            P.add("sp", lambda e: e.dma_start(
                out=XT[b][:], in_=src[:, t * TT:(t + 1) * TT].rearrange("(k p) n -> p k n", p=128)),
                r=[("X", src.tensor.name, t)], w=[("xt", b)], dma="xl%d" % b)

        def do_tile(t):
            b = t % 2
            if t + 1 < NT:
                load(t + 1)
            xt = XT[b]
            for k in range(KD):
                q = 0
                P.add("act", lambda e, k=k, q=q: e.activation(out=SQ[q][:], in_=xt[:, k, :], func=AF.Square),
                      r=[("xt", b)], w=[("sq", q)])
                P.add("pe", lambda e, k=k, q=q: e.matmul(PSS[:], c.ones_bf[:], SQ[q][:], start=(k == 0), stop=(k == KD - 1)),
                      r=[("sq", q)], w=[("pss",)])
            P.add("act", lambda e: e.activation(out=RSTD[:], in_=PSS[:], func=AF.Sqrt, bias=c.eps[:, 0:1], scale=1.0 / D),
                  r=[("pss",)], w=[("rstd",)])
            P.add("dve", lambda e: e.reciprocal(RSTD[:], RSTD[:]),
                  r=[("rstd",)], w=[("rstd",)])
            for k in range(KD):
                P.add("dve", lambda e, k=k: e.scalar_tensor_tensor(
                    XN[:, k, :], xt[:, k, :], c.vecs[:, gcol + k:gcol + k + 1], RSTD[:], ALU.mult, ALU.mult),
                    r=[("xt", b), ("rstd",)], w=[("xn", k)])
            for j in range(KF):
                pg = PH[(2 * j) % 4]
                pu = PH[(2 * j + 1) % 4]
                kg, ku = ("ph", (2 * j) % 4), ("ph", (2 * j + 1) % 4)
                for k in range(KD):
                    P.add("pe", lambda e, j=j, k=k, pg=pg: e.matmul(pg[:], WIN[:, j, k, :], XN[:, k, :], start=(k == 0), stop=(k == KD - 1)),
                          r=[("win", j), ("xn", k)], w=[kg])
                for k in range(KD):
                    P.add("pe", lambda e, j=j, k=k, pu=pu: e.matmul(pu[:], WIN[:, KF + j, k, :], XN[:, k, :], start=(k == 0), stop=(k == KD - 1)),
                          r=[("win", KF + j), ("xn", k)], w=[ku])
                q = j % 2
                P.add("act", lambda e, pg=pg, q=q: e.activation(out=SG[q][:], in_=pg[:], func=AF.Silu),
                      r=[kg], w=[("sg", q)])
                P.add("dve", lambda e, j=j, pu=pu, q=q: e.tensor_tensor(ACTB[:, j, :], pu[:], SG[q][:], ALU.mult),
                      r=[ku, ("sg", q)], w=[("act", j)])
            for m in range(KD):
                py = PY[m % 2]
                ky = ("py", m % 2)
                for k in range(KF):
                    P.add("pe", lambda e, m=m, k=k, py=py: e.matmul(py[:], WOUT[:, k, m * 128:(m + 1) * 128], ACTB[:, k, :], start=(k == 0), stop=(k == KF - 1)),
                          r=[("wout", k), ("act", k)], w=[ky])
                P.add("dve", lambda e, m=m, py=py: e.scalar_tensor_tensor(
                    xt[:, m, :], py[:], 0.5, xt[:, m, :], ALU.mult, ALU.add),
                    r=[ky, ("xt", b)], w=[("xt", b)])
            P.add("sp", lambda e, t=t: e.dma_start(
                out=dst[:, t * TT:(t + 1) * TT].rearrange("(k p) n -> p k n", p=128), in_=xt[:]),
                r=[("xt", b)], w=[("X", dst.tensor.name, t)], dma="xs%d" % b)
        load(0)
        for t in range(NT):
            do_tile(t)
        P.barrier()


import os
DBG_NTR = int(os.environ.get('KDBG_NTR', '999'))
TMZENG = os.environ.get('KTMZ', 'act')
DBG_STOP = float(os.environ.get('KDBG_STOP', '99'))
TR = 256
HG = [[0, 2, 4, 6], [1, 3, 5, 7], [8, 10], [9, 11]]


def hgk(h):
    return 2 * (h // 8) + (h % 2)

CH = 128
NCH = TR // CH
NTR = S // TR
LNX_EPS = 64e-5
DEC_SCALE = -0.6065306597126334


class Banks:
    def __init__(self, tiles, tag):
        self.tiles = tiles
        self.tag = tag
        self.i = 0

    def next(self):
        i = self.i
        self.i = (self.i + 1) % len(self.tiles)
        return self.tiles[i], (self.tag, i)


def rms_tile(P, c, xt, xkey, xn, xnkey, gcol, n, PSS, psskey, SQ, RSTD, tag):
    for k in range(KD):
        q = k % 2
        P.act(SQ[q][:, :n], xt[:, k, :n], AF.Square, r=[xkey], w=[(tag, "sq", q)])
        P.mm(PSS[:, :n], c.ones_b[:], SQ[q][:, :n], start=(k == 0), stop=(k == KD - 1), r=[(tag, "sq", q)], w=[psskey])
    P.act(RSTD[:, :n], PSS[:, :n], AF.Sqrt, r=[psskey], w=[(tag, "rstd")], bias=c.eps[:, 0:1], scale=1.0 / D)
    P.add("dve", lambda e: e.reciprocal(RSTD[:, :n], RSTD[:, :n]), r=[(tag, "rstd")], w=[(tag, "rstd")])
    for k in range(KD):
        P.stt("dve", xn[:, k, :n], xt[:, k, :n], c.vecs[:, gcol + k:gcol + k + 1], RSTD[:, :n], ALU.mult, ALU.mult,
              r=[xkey, (tag, "rstd")], w=[xnkey + (k,)])


def head_rms(P, c, out, src, srckey, gain_col, n, bank, bkey, SQ, TMP, tag, outkey):
    P.act(SQ[:, :n], src, AF.Square, r=[srckey], w=[(tag, "hsq")])
    P.mm(bank[:, :n], c.bd_b[:], SQ[:, :n], r=[(tag, "hsq")], w=[bkey])
    P.act(TMP[:, :n], bank[:, :n], AF.Sqrt, r=[bkey], w=[(tag, "htmp")], bias=c.eps[:, 0:1], scale=1.0 / 64)
    P.add("dve", lambda e: e.reciprocal(TMP[:, :n], TMP[:, :n]), r=[(tag, "htmp")], w=[(tag, "htmp")])
    P.stt("dve", out, src, c.vecs[:, gain_col:gain_col + 1], TMP[:, :n], ALU.mult, ALU.mult,
          r=[srckey, (tag, "htmp")], w=[outkey])


def mem_prep(P, nc, c, memT_d, wkv_d, layer, V):
    with ExitStack() as es:
        def sb(name, shape, dt):
            return es.enter_context(nc.sbuf_tensor(_u(name), shape, dt))
        WKV = sb("mp_wkv", [128, KD, 512], BF16)
        MT = sb("mp_mt", [128, KD, 256], F32)
        MN = sb("mp_mn", [128, KD, 256], BF16)
        SQ = [sb("mp_sq%d" % i, [128, 256], BF16) for i in range(2)]
        RSTD = sb("mp_rstd", [128, 256], F32)
        KF32 = sb("mp_kf", [128, 256], F32)
        TMP = sb("mp_tmp", [128, 256], F32)
        pb = [es.enter_context(nc.psum_tensor(_u("mp_pb%d" % i), [128, 512], F32)) for i in range(3)]
        P.dma("pool", WKV[:], wkv_d.rearrange("(k p) c -> p k c", p=128), "wq0", w=[("mp", "wkv")])
        P.dma("sp", MT[:], memT_d.rearrange("(k p) n -> p k n", p=128), "xl0", w=[("mp", "mt")])
        rms_tile(P, c, MT, ("mp", "mt"), MN, ("mp", "mn"), V["mem_norm%d" % layer], 256, pb[0], ("mp", "pb", 0), SQ, RSTD, "mp")
        mnkeys = [("mp", "mn", k) for k in range(KD)]
        for hp in range(2):
            for k in range(KD):
                P.mm(pb[1][:, :256], WKV[:, k, hp * 128:(hp + 1) * 128], MN[:, k, :], start=(k == 0), stop=(k == KD - 1),
                     r=[("mp", "wkv"), mnkeys[k]], w=[("mp", "pb", 1)])
            P.copy("act", KF32[:], pb[1][:, :256], r=[("mp", "pb", 1)], w=[("mp", "kf")])
            head_rms(P, c, c.MK[:, hp, :], KF32[:], ("mp", "kf"), V["mem_k_norm%d" % layer], 256, pb[2], ("mp", "pb", 2), SQ[0], TMP, "mpk",
                     ("MK", hp))
        for mc in range(2):
            for k in range(KD):
                P.mm(pb[1][:, :256], MN[:, k, mc * 128:(mc + 1) * 128], WKV[:, k, 256:512], start=(k == 0), stop=(k == KD - 1),
                     r=[("mp", "wkv"), mnkeys[k]], w=[("mp", "pb", 1)])
            for h in range(4):
                hh = h % 2
                P.copy("act", c.MVZ[:, mc, h, hh * 64:(hh + 1) * 64], pb[1][:, h * 64:(h + 1) * 64], r=[("mp", "pb", 1)], w=[("MVZ",)])
        P.barrier()


def mem_attn_tile(P, c, QM, qkeys, YM, ymkeys, n, layer, V, banks, SQ, TMP, QN, PT, RD, tag):
    for hp in range(2):
        bank, bk = banks.next()
        head_rms(P, c, QN[:, hp, :n], QM[:, hp, :n], qkeys[hp], V["mem_q_norm%d" % layer], n, bank, bk, SQ, TMP, tag, (tag, "qn", hp))
    for hp in range(2):
        for hh in range(2):
            off = hh * 64
            for mc in range(2):
                bl, kl = banks.next()
                P.mm(bl[:, :n], c.MK[off:off + 64, hp, mc * 128:(mc + 1) * 128], QN[off:off + 64, hp, :n],
                     r=[("MK", hp), (tag, "qn", hp)], w=[kl])
                P.act(PT[hh * 2 + mc][:, :n], bl[:, :n], AF.Exp, r=[kl], w=[(tag, "pt", hh * 2 + mc)], scale=0.125)
        bnum, knum = banks.next()
        bden, kden = banks.next()
        for i in range(4):
            hh, mc = i // 2, i % 2
            h = hp * 2 + hh
            P.mm(bnum[:, :n], c.MVZ[:, mc, h, :], PT[i][:, :n], start=(i == 0), stop=(i == 3), r=[("MVZ",), (tag, "pt", i)], w=[knum])
        for i in range(4):
            hh, mc = i // 2, i % 2
            P.mm(bden[:, :n], c.onesh[hh], PT[i][:, :n], start=(i == 0), stop=(i == 3), r=[(tag, "pt", i)], w=[kden])
        P.add("dve", lambda e, bden=bden: e.reciprocal(RD[:, :n], bden[:, :n]), r=[kden], w=[(tag, "rd")])
        P.tt("dve", YM[:, hp, :n], bnum[:, :n], RD[:, :n], ALU.mult, r=[knum, (tag, "rd")], w=[ymkeys[hp]])


def rwkv_phase(P, nc, c, src, dst, li, layer, w_in_d, w_out_d, wup_d, aup_d, gup_d, V, dbg=None):
    with ExitStack() as es:
        def sb(name, shape, dt):
            return es.enter_context(nc.sbuf_tensor(_u("rk_" + name), shape, dt))
        WA = sb("wa", [128, 22, KD, 128], BF16)
        WO = sb("wo", [128, KD, D], BF16)
        WAUP = sb("waup", [128, 768], BF16)
        GUP = sb("gup", [128, 768], BF16)
        XT = sb("xt", [128, KD, TR], F32)
        XN = sb("xn", [128, KD, TR], BF16)
        SQ = [sb("sq%d" % i, [128, TR], BF16) for i in range(2)]
        RSTD = sb("rstd", [128, TR], F32)
        CARRY = sb("carry", [128, 20], F32)
        OMM = sb("omm", [128, 20], F32)
        OMKA = sb("omka", [128, 6], F32)
        TA = sb("ta", [128, TR], F32)
        P18 = sb("p18", [128, TR], F32)
        P19 = sb("p19", [128, TR], F32)
        TW = sb("tw", [128, TR], BF16)
        AL = sb("al", [128, TR], BF16)
        SGG = sb("sgg", [128, TR], BF16)
        QM = sb("qm", [128, 2, TR], F32)
        Rf = sb("rf", [128, TR], F32)
        Kf = sb("kf", [128, TR], F32)
        Vf = sb("vf", [128, TR], F32)
        T = [sb("t%d" % i, [128, TR], F32) for i in range(10)]
        HB = [sb("hb%d" % i, [128, TR], BF16) for i in range(4)]
        G = sb("g", [128, 6, TR], BF16)
        BON = sb("bon", [128, 6, TR], BF16)
        RT = sb("rt", [128, 6, TR], BF16)
        KT = sb("kt", [128, 6, TR], BF16)
        BT = sb("bt", [128, 6, TR], BF16)
        AT = sb("at", [128, 6, TR], BF16)
        PC = sb("pc", [128, 6, NCH], F32)
        TMV = [sb("tmv%d" % i, [128, 6, 128], BF16) for i in range(NCH)]
        TMZ = [sb("tmz%d" % i, [128, 2, 13, 128], BF16) for i in range(NCH)]
        W = [sb("w%d" % i, [128, 2, 768], BF16) for i in range(2 * NCH)]
        AK = [sb("ak%d" % i, [128, 12, 128], BF16) for i in range(2)]
        BK = [sb("bk%d" % i, [128, 12, 128], BF16) for i in range(2)]
        AAK = sb("aak", [128, 12, 128], BF16)
        ARK = sb("ark", [128, 12, 128], BF16)
        ARB = sb("arb", [128, 12, 128], BF16)
        AHT = sb("aht", [128, 6, 128], BF16)
        UU = sb("uu", [128, 12, 64], BF16)
        SF = sb("sf", [128, 6, 64], F32)
        SB = sb("sb", [128, 6, 64], BF16)
        YB = sb("yb", [128, 6, TR], F32)
        YO = sb("yo", [128, 6, TR], BF16)
        YM = sb("ym", [128, 2, TR], BF16)
        QN = sb("qn", [128, 2, TR], BF16)
        PT = [sb("pt%d" % i, [128, TR], BF16) for i in range(4)]
        RD = sb("rd", [128, TR], F32)
        MSQ = sb("msq", [128, TR], BF16)
        MTMP = sb("mtmp", [128, TR], F32)
        pbt = [es.enter_context(nc.psum_tensor(_u("rk_pb%d" % i), [128, 512], F32)) for i in range(8)]
        banks = Banks(pbt, "rpb")
        tbanks = banks

        for j0 in range(0, 22, 2):
            P.dma("pool", WA[:, j0:j0 + 2], w_in_d[j0:j0 + 2].rearrange("j p k c -> p j k c"), "wq%d" % ((j0 // 2) % 4),
                  w=[("wa", j) for j in range(j0, j0 + 2)])
        P.dma("pool", WAUP[0:64, :], wup_d[:, :], "wq0", w=[("waup", 0)])
        P.dma("pool", WAUP[64:128, :], aup_d[:, :], "wq1", w=[("waup", 1)])
        P.dma("pool", GUP[:], gup_d[:, :], "wq2", w=[("gup",)])
        for k0 in range(0, KD, 2):
            P.dma("pool", WO[:, k0:k0 + 2], w_out_d[k0 * 128:(k0 + 2) * 128, :].rearrange("(k p) c -> p k c", p=128), "wq%d" % ((k0 // 2) % 4),
                  w=[("wo", k) for k in range(k0, k0 + 2)])
        mu0 = V["a_mu%d" % li]
        P.ts("dve", OMM[:], c.vecs[:, mu0:mu0 + 20], -1.0, 1.0, ALU.mult, ALU.add, w=[("omm",)])
        ka0 = V["a_k_a%d" % li]
        P.ts("dve", OMKA[:], c.vecs[:, ka0:ka0 + 6], -1.0, 1.0, ALU.mult, ALU.add, w=[("omka",)])
        P.add("dve", lambda e: e.memset(CARRY[:], 0.0), w=[("carry", m) for m in range(20)])
        P.add("dve", lambda e: e.memset(SF[:], 0.0), w=[("sf",)])
        P.add("dve", lambda e: e.memset(SB[:], 0.0), w=[("sbk",)])
        for i in range(NCH):
            P.add("dve", lambda e, i=i: e.memset(TMZ[i][:], 0.0), w=[("tmz", i, hp, ty, hh) for hp in range(6) for ty in range(2) for hh in range(2)])

        def vcol(name, i):
            o = V[name + "%d" % li] + i
            return c.vecs[:, o:o + 1]

        def proj(m, n):
            bank, bk = banks.next()
            for k in range(KD):
                P.mm(bank[:, :n], WA[:, m, k, :], XN[:, k, :n], start=(k == 0), stop=(k == KD - 1), r=[("wa", m), ("xn", k)], w=[bk])
            return bank, bk

        def shift_evac(m, bank, bk, out, outkey):
            n = TR
            P.act(TA[:, :n], bank[:, :n], AF.Copy, r=[bk, ("omm",)], w=[("ta",)], scale=OMM[:, m:m + 1])
            P.stt("dve", out[:, 1:n], bank[:, 0:n - 1], c.vecs[:, mu0 + m:mu0 + m + 1], TA[:, 1:n], ALU.mult, ALU.add,
                  r=[bk, ("ta",)], w=[outkey])
            P.stt("dve", out[:, 0:1], CARRY[:, m:m + 1], c.vecs[:, mu0 + m:mu0 + m + 1], TA[:, 0:1], ALU.mult, ALU.add,
                  r=[("carry", m), ("ta",)], w=[outkey + ("c0",)])
            P.copy("dve", CARRY[:, m:m + 1], bank[:, n - 1:n], r=[bk], w=[("carry", m)])

        def do_tile(t):
            n = TR
            P.dma("sp", XT[:], src[:, t * TR:(t + 1) * TR].rearrange("(k p) n -> p k n", p=128), "xl0",
                  r=[("X", src.tensor.name, t)], w=[("xt",)])
            rb, rbk = banks.next()
            rms_tile(P, c, XT, ("xt",), XN, ("xn",), V["mix_norm%d" % layer], n, rb, rbk, SQ, RSTD, "rk")
            b18, k18 = proj(18, n)
            shift_evac(18, b18, k18, P18, ("p18",))
            b19, k19 = proj(19, n)
            shift_evac(19, b19, k19, P19, ("p19",))
            p18k = [("p18",), ("p18", "c0")]
            p19k = [("p19",), ("p19", "c0")]
            P.act(TW[0:64, :], P18[0:64, :], AF.Tanh, r=p18k, w=[("tw",)])
            P.copy("dve", AL[64:128, :], P18[64:128, :], r=p18k, w=[("al",)])
            P.act(SGG[:], P19[:], AF.Sigmoid, r=p19k, w=[("sgg",)])
            for q in range(2):
                bq, kq = proj(20 + q, n)
                P.copy("act", QM[:, q, :], bq[:, :n], r=[kq], w=[("qm", q)])
            if DBG_STOP <= 1:
                return
            for hp in range(6):
                cs = slice(hp * 128, (hp + 1) * 128)
                bz, kz = banks.next()
                P.mm(bz[:, :n], WAUP[0:64, cs], TW[0:64, :], r=[("waup", 0), ("tw",)], w=[kz])
                SW, LOGW, ALR = T[0], T[1], T[2]
                P.act(SW[:], bz[:, :n], AF.Sigmoid, r=[kz], w=[("sw",)], bias=vcol("a_w0", hp))
                P.tsmul("dve", LOGW[:], SW[:], DEC_SCALE, r=[("sw",)], w=[("logw",)])
                bz, kz = banks.next()
                P.mm(bz[:, :n], WAUP[64:128, cs], AL[64:128, :], r=[("waup", 1), ("al",)], w=[kz])
                P.act(ALR[:], bz[:, :n], AF.Sigmoid, r=[kz], w=[("alr",)], bias=vcol("a_a0", hp))
                bz, kz = banks.next()
                P.mm(bz[:, :n], GUP[:, cs], SGG[:], r=[("gup",), ("sgg",)], w=[kz])
                P.copy("act", G[:, hp, :], bz[:, :n], r=[kz], w=[("g", hp)])
                if DBG_STOP <= 1.1:
                    continue
                br, kr = proj(hp, n)
                shift_evac(hp, br, kr, Rf, ("rf",))
                bk_, kk_ = proj(6 + hp, n)
                shift_evac(6 + hp, bk_, kk_, Kf, ("kf",))
                bv, kv = proj(12 + hp, n)
                shift_evac(12 + hp, bv, kv, Vf, ("vf",))
                rfk = [("rf",), ("rf", "c0")]
                kfk = [("kf",), ("kf", "c0")]
                vfk = [("vf",), ("vf", "c0")]
                if DBG_STOP <= 1.2:
                    continue
                KS, NRM, KK, TMv, KM, Bv, L = T[3], T[4], T[5], T[6], T[7], T[8], T[9]
                P.tsmul("dve", KS[:], Kf[:], vcol("a_kk_scale", hp), r=kfk, w=[("ks",)])
                P.act(HB[0][:], KS[:], AF.Square, r=[("ks",)], w=[("hb", 0)])
                bz, kz = banks.next()
                P.mm(bz[:, :n], c.bd_b[:], HB[0][:], r=[("hb", 0)], w=[kz])
                P.act(NRM[:], bz[:, :n], AF.Sqrt, r=[kz], w=[("nrm",)])
                P.add("dve", lambda e: e.tensor_scalar_max(NRM[:], NRM[:], 1e-12), r=[("nrm",)], w=[("nrm",)])
                P.add("dve", lambda e: e.reciprocal(NRM[:], NRM[:]), r=[("nrm",)], w=[("nrm",)])
                P.tt("dve", KK[:], KS[:], NRM[:], ALU.mult, r=[("ks",), ("nrm",)], w=[("kk",)])
                P.ts("dve", TMv[:], ALR[:], vcol("a_k_a", hp), OMKA[:, hp:hp + 1], ALU.mult, ALU.add, r=[("alr",), ("omka",)], w=[("tmv",)])
                P.tt("dve", KM[:], Kf[:], TMv[:], ALU.mult, r=kfk + [("tmv",)], w=[("km",)])
                P.tt("dve", Bv[:], KK[:], ALR[:], ALU.mult, r=[("kk",), ("alr",)], w=[("bv",)])
                P.stt("dve", HB[1][:], Rf[:], vcol("a_r_k", hp), KM[:], ALU.mult, ALU.mult, r=rfk + [("km",)], w=[("hb", 1)])
                bz, kz = banks.next()
                P.mm(bz[:, :n], c.bd_b[:], HB[1][:], r=[("hb", 1)], w=[kz])
                P.tt("dve", BON[:, hp, :], bz[:, :n], Vf[:], ALU.mult, r=[kz] + vfk, w=[("bon", hp)])
                if DBG_STOP <= 1.3:
                    continue
                P.add("dve", lambda e, L=L, LOGW=LOGW: e.tensor_tensor_scan(L[:], c.cmask[:, :n], LOGW[:], 0.0, ALU.mult, ALU.add),
                      r=[("logw",)], w=[("L",)])
                E1 = T[0]
                P.act(E1[:], L[:], AF.Exp, r=[("L",)], w=[("sw",)])
                P.tt("dve", RT[:, hp, :], Rf[:], E1[:], ALU.mult, r=rfk + [("sw",)], w=[("rt", hp)])
                E2 = T[3]
                P.act(E2[:], L[:], AF.Exp, r=[("L",), ("kk",)], w=[("ks",)], scale=-1.0)
                P.tt("dve", KT[:, hp, :], KM[:], E2[:], ALU.mult, r=[("km",), ("ks",)], w=[("kt", hp)])
                P.tt("dve", BT[:, hp, :], Bv[:], E2[:], ALU.mult, r=[("bv",), ("ks",)], w=[("bt", hp)])
                LX = T[4]
                P.tt("dve", LX[:], L[:], LOGW[:], ALU.subtract, r=[("L",), ("logw",), ("kk",)], w=[("nrm",)])
                P.act(LX[:], LX[:], AF.Exp, r=[("nrm",)], w=[("nrm",)])
                P.stt("dve", AT[:, hp, :], KK[:], -1.0, LX[:], ALU.mult, ALU.mult, r=[("kk",), ("nrm",)], w=[("at", hp)])
                DEC = T[6]
                for cc in range(NCH):
                    ce = (cc + 1) * CH - 1
                    P.act(DEC[:, cc * CH:(cc + 1) * CH], L[:, cc * CH:(cc + 1) * CH], AF.Exp, r=[("L",), ("km",)], w=[("tmv",)],
                          bias=L[:, ce:ce + 1], scale=-1.0)
                    P.act(PC[:, hp, cc:cc + 1], L[:, ce:ce + 1], AF.Exp, r=[("L",)], w=[("pc", cc)])
                P.tt("dve", HB[2][:], KM[:], DEC[:], ALU.mult, r=[("km",), ("tmv",)], w=[("hb", 2)])
                P.tt("dve", HB[3][:], Bv[:], DEC[:], ALU.mult, r=[("bv",), ("tmv",)], w=[("hb", 3)])
                P.copy("act", HB[0][:], Vf[:], r=vfk, w=[("hb", 0)])
                if DBG_STOP <= 1.4:
                    continue
                for cc in range(NCH):
                    tb, tk = tbanks.next()
                    csl = slice(cc * CH, (cc + 1) * CH)
                    P.tr(tb[:, 0:128], HB[0][:, csl], c.ident_b[:], r=[("hb", 0)], w=[tk])
                    P.tr(tb[:, 128:256], HB[2][:, csl], c.ident_b[:], r=[("hb", 2)], w=[tk])
                    P.tr(tb[:, 256:384], HB[3][:, csl], c.ident_b[:], r=[("hb", 3)], w=[tk])
                    P.tr(tb[:, 384:512], AT[:, hp, csl], c.ident_b[:], r=[("at", hp)], w=[tk])
                    if DBG_STOP <= 1.45:
                        continue
                    evq = "act" if (hp * NCH + cc) % 2 == 0 else "dve"
                    P.copy(evq, TMV[cc][:, hp, :], tb[:, 0:128], r=[tk], w=[("tmv", cc, hp)])
                    if DBG_STOP <= 1.46:
                        continue
                    for ty in range(2):
                        for hh in range(2):
                            P.copy(evq, TMZ[cc][:, ty, 2 * hp + hh, hh * 64:(hh + 1) * 64],
                                   tb[:, 128 + ty * 128 + hh * 64:128 + ty * 128 + (hh + 1) * 64], r=[tk], w=[("tmz", cc, hp, ty, hh)])
                    if DBG_STOP <= 1.47:
                        continue
                    P.copy(evq, W[2 * cc][:, 0, hp * 128:(hp + 1) * 128], tb[:, 384:512], r=[tk], w=[("w", 2 * cc, hp)])
            if DBG_STOP <= 2:
                return
            for cc in range(NCH):
                csl = slice(cc * CH, (cc + 1) * CH)

                def amat(lhs, lkey, rhs, rkey, mask, dest, dkey):
                    for g in range(4):
                        heads = HG[g]
                        bank, bk = banks.next()
                        for hi, h in enumerate(heads):
                            hp, off = h // 2, (h % 2) * 64
                            P.mm(bank[:, hi * 128:(hi + 1) * 128], lhs[off:off + 64, hp, csl], rhs[off:off + 64, hp, csl],
                                 r=[(lkey, hp), (rkey, hp)], w=[bk])
                        nh = len(heads)
                        P.tt("dve", dest[:, heads[0]:heads[-1] + 1:2, :], bank[:, 0:nh * 128].rearrange("p (a b) -> p a b", a=nh),
                             mask[:].unsqueeze(1).to_broadcast([128, nh, 128]), ALU.mult, r=[bk], w=[(dkey, g)])
                amat(BT, "bt", AT, "at", c.msu, BK[0], "bk0")
                amat(AT, "at", BT, "bt", c.msl, AK[0], "ak0")
                amat(KT, "kt", AT, "at", c.msu, AAK, "aak")
                amat(KT, "kt", RT, "rt", c.mui, ARK, "ark")
                amat(BT, "bt", RT, "rt", c.mui, ARB, "arb")
                if DBG_STOP <= 2.1:
                    continue
                for h0, nh in ((0, 8), (8, 4)):
                    bank, bk = banks.next()
                    for hi in range(nh):
                        h = h0 + hi
                        hp, off = h // 2, (h % 2) * 64
                        P.mm(bank[:, hi * 64:(hi + 1) * 64], AAK[:, h, :], TMV[cc][:, hp, off:off + 64],
                             r=[("aak", hgk(h)), ("tmv", cc, hp)], w=[bk])
                    P.copy("act", W[2 * cc][:, 1, h0 * 64:(h0 + nh) * 64], bank[:, 0:nh * 64], r=[bk], w=[("wu", 2 * cc, h0)])
                if DBG_STOP <= 2.2:
                    continue
                wcur, wnxt = 2 * cc, 2 * cc + 1
                wkeys = lambda wi: [("w", wi, hp) for hp in range(6)] + [("wu", wi, 0), ("wu", wi, 8)]
                for lev in range(7):
                    ai, ao = lev % 2, (lev + 1) % 2
                    for hg in range(3):
                        bank, bk = banks.next()
                        for hi in range(4):
                            h = hg * 4 + hi
                            for part in range(2):
                                o = bank[:, part * 256 + hi * 64:part * 256 + (hi + 1) * 64]
                                rhs = W[wcur][:, part, h * 64:(h + 1) * 64]
                                P.mm(o, c.ident_b[:], rhs, start=True, stop=False, r=wkeys(wcur), w=[bk])
                                P.mm(o, BK[ai][:, h, :], rhs, start=False, stop=True, r=[("bk%d" % ai, hgk(h))], w=[bk])
                        P.copy("act" if hg % 2 == 0 else "dve", W[wnxt][:, :, hg * 256:(hg + 1) * 256],
                               bank[:, :].rearrange("p (a b) -> p a b", a=2), r=[bk], w=[("wl", wnxt, hg)])
                    if lev < 6:
                        for g in range(4):
                            heads = HG[g]
                            nh = len(heads)
                            bank, bk = banks.next()
                            for hi, h in enumerate(heads):
                                P.mm(bank[:, hi * 128:(hi + 1) * 128], BK[ai][:, h, :], AK[ai][:, h, :],
                                     r=[("bk%d" % ai, g), ("ak%d" % ai, g)], w=[bk])
                            P.copy("dve" if g % 2 == 0 else "act", AK[ao][:, heads[0]:heads[-1] + 1:2, :],
                                   bank[:, 0:nh * 128].rearrange("p (a b) -> p a b", a=nh), r=[bk], w=[("ak%d" % ao, g)])
                            bank, bk = banks.next()
                            for hi, h in enumerate(heads):
                                P.mm(bank[:, hi * 128:(hi + 1) * 128], AK[ai][:, h, :], BK[ai][:, h, :],
                                     r=[("bk%d" % ai, g), ("ak%d" % ai, g)], w=[bk])
                            P.copy("act" if g % 2 == 0 else "dve", BK[ao][:, heads[0]:heads[-1] + 1:2, :],
                                   bank[:, 0:nh * 128].rearrange("p (a b) -> p a b", a=nh), r=[bk], w=[("bk%d" % ao, g)])
                    wcur, wnxt = wnxt, wcur
                    wkeys = lambda wi: [("wl", wi, hg) for hg in range(3)]
                if DBG_STOP <= 2.3:
                    continue
                wf = wcur
                wfk = [("wl", wf, hg) for hg in range(3)]
                for p0, npp in ((0, 4), (4, 2)):
                    tb, tk = tbanks.next()
                    for pi in range(npp):
                        hp = p0 + pi
                        P.tr(tb[:, pi * 128:(pi + 1) * 128], W[wf][:, 0, hp * 128:(hp + 1) * 128], c.ident_b[:], r=wfk, w=[tk])
                    P.copy("dve", AHT[:, p0:p0 + npp, :], tb[:, 0:npp * 128].rearrange("p (a b) -> p a b", a=npp), r=[tk], w=[("aht", p0)])
                if DBG_STOP <= 2.4:
                    continue
                ahk = [("aht", 0), ("aht", 4)]
                for g in range(4):
                    heads = HG[g]
                    nh = len(heads)
                    bank, bk = banks.next()
                    for hi, h in enumerate(heads):
                        hp, off = h // 2, (h % 2) * 64
                        P.mm(bank[:, hi * 64:(hi + 1) * 64], AHT[off:off + 64, hp, :], SB[off:off + 64, hp, :], start=True, stop=False,
                             r=ahk + [("sbk",)], w=[bk])
                        P.mm(bank[:, hi * 64:(hi + 1) * 64], c.ident_b[:], W[wf][:, 1, h * 64:(h + 1) * 64], start=False, stop=True, r=wfk, w=[bk])
                    P.copy("act", UU[:, heads[0]:heads[-1] + 1:2, :], bank[:, 0:nh * 64].rearrange("p (a b) -> p a b", a=nh), r=[bk], w=[("uu", g)])
                uuk = [("uu", g) for g in range(4)]
                if DBG_STOP <= 2.5:
                    continue
                for p0, npp in ((0, 4), (4, 2)):
                    for hh in range(2):
                        off = hh * 64
                        bank, bk = banks.next()
                        for pi in range(npp):
                            hp = p0 + pi
                            h = 2 * hp + hh
                            o = bank[off:off + 64, pi * 128:(pi + 1) * 128]
                            P.mm(o, SB[off:off + 64, hp, :], RT[off:off + 64, hp, csl], start=True, stop=False, r=[("sbk",), ("rt", hp)], w=[bk])
                            P.mm(o, TMV[cc][:, hp, off:off + 64], ARK[:, h, :], start=False, stop=False, r=[("tmv", cc, hp), ("ark", hgk(h))], w=[bk])
                            P.mm(o, UU[:, h, :], ARB[:, h, :], start=False, stop=True, r=uuk + [("arb", hgk(h))], w=[bk])
                        P.copy("act", YB[off:off + 64, p0:p0 + npp, csl], bank[off:off + 64, 0:npp * 128].rearrange("p (a b) -> p a b", a=npp),
                               r=[bk], w=[("yb", cc, p0, hh)])
                if DBG_STOP <= 2.6:
                    continue
                bank, bk = banks.next()
                for hp in range(6):
                    o = bank[:, hp * 64:(hp + 1) * 64]
                    for hh in range(2):
                        off = hh * 64
                        h = 2 * hp + hh
                        P.mm(o, TMZ[cc][:, 0, h, :], TMV[cc][:, hp, off:off + 64], start=(hh == 0), stop=False,
                             r=[("tmz", cc, hp, 0, hh), ("tmz", cc, hp, 1, hh), ("tmv", cc, hp)], w=[bk])
                        P.mm(o, TMZ[cc][:, 1, h, :], UU[:, h, :], start=False, stop=(hh == 1), r=uuk, w=[bk])
                P.tt("dve", SF[:], SF[:], PC[:, :, cc:cc + 1].to_broadcast([128, 6, 64]), ALU.mult, r=[("pc", cc), ("sf",)], w=[("sf",)])
                P.tt("dve", SF[:], SF[:], bank[:, 0:384].rearrange("p (a b) -> p a b", a=6), ALU.add, r=[bk, ("sf",)], w=[("sf",)])
                P.copy("act", SB[:], SF[:], r=[("sf",)], w=[("sbk",)])
            if DBG_STOP <= 3:
                return
            ybk = [("yb", cc, p0, hh) for cc in range(NCH) for p0 in (0, 4) for hh in range(2)]
            for hp in range(6):
                YC, SQf, SD = T[0], T[1], T[2]
                bank, bk = banks.next()
                P.mm(bank[:, :n], c.bda_f[:], YB[:, hp, :], r=ybk, w=[bk])
                P.tt("dve", YC[:], YB[:, hp, :], bank[:, :n], ALU.subtract, r=ybk + [bk], w=[("sw",)])
                P.act(SQf[:], YC[:], AF.Square, r=[("sw",)], w=[("logw",)])
                bank, bk = banks.next()
                P.mm(bank[:, :n], c.bda_f[:], SQf[:], r=[("logw",)], w=[bk])
                P.act(SD[:], bank[:, :n], AF.Sqrt, r=[bk], w=[("alr",)], bias=c.eps[:, 1:2])
                P.add("dve", lambda e, SD=SD: e.reciprocal(SD[:], SD[:]), r=[("alr",)], w=[("alr",)])
                P.tt("dve", YC[:], YC[:], SD[:], ALU.mult, r=[("sw",), ("alr",)], w=[("sw",)])
                P.ts("dve", YC[:], YC[:], vcol("a_lnx_g", hp), vcol("a_lnx_b", hp), ALU.mult, ALU.add, r=[("sw",)], w=[("sw",)])
                P.tt("dve", YC[:], YC[:], BON[:, hp, :], ALU.add, r=[("sw",), ("bon", hp)], w=[("sw",)])
                P.tt("dve", YO[:, hp, :], YC[:], G[:, hp, :], ALU.mult, r=[("sw",), ("g", hp)], w=[("yo", hp)])
            if DBG_STOP <= 4:
                return
            mem_attn_tile(P, c, QM, [("qm", 0), ("qm", 1)], YM, [("ym", 0), ("ym", 1)], n, layer, V, banks, MSQ, MTMP, QN, PT, RD, "rma")
            if dbg is not None:
                P.copy("dve", YB[:], YO[:], r=[("yo", hp) for hp in range(6)], w=ybk)
                P.dma("sp", dbg[0:768, t * TR:(t + 1) * TR].rearrange("(k p) n -> p k n", p=128), YB[:], "dbg0", r=ybk)
                P.copy("dve", QM[:], YM[:], r=[("ym", 0), ("ym", 1)], w=[("qm", 0), ("qm", 1)])
                P.dma("sp", dbg[768:1024, t * TR:(t + 1) * TR].rearrange("(k p) n -> p k n", p=128), QM[:], "dbg1", r=[("qm", 0), ("qm", 1)])
            for m in range(KD):
                bank, bk = banks.next()
                for k in range(KD):
                    rhs = YO[:, k, :] if k < 6 else YM[:, k - 6, :]
                    rk = ("yo", k) if k < 6 else ("ym", k - 6)
                    P.mm(bank[:, :n], WO[:, k, m * 128:(m + 1) * 128], rhs, start=(k == 0), stop=(k == KD - 1), r=[("wo", k), rk], w=[bk])
                P.tt("dve", XT[:, m, :], XT[:, m, :], bank[:, :n], ALU.add, r=[bk, ("xt",)], w=[("xt",)])
            P.dma("sp", dst[:, t * TR:(t + 1) * TR].rearrange("(k p) n -> p k n", p=128), XT[:], "xs0",
                  r=[("xt",)], w=[("X", dst.tensor.name, t)])

        for t in range(min(NTR, DBG_NTR)):
            do_tile(t)
        P.barrier()


DIL = (1, 4, 16)
MASKV = -30000.0


def flat_head_rms(P, c, dst, bank, bk, gain_col, n, KFt, SQt, RS, banks, tag, dkey):
    P.copy("act", KFt[0:64, :n], bank[0:64, :n], r=[bk], w=[(tag, "kf")])
    P.act(SQt[0:64, :n], KFt[0:64, :n], AF.Square, r=[(tag, "kf")], w=[(tag, "sq")])
    b2, k2 = banks.next()
    P.mm(b2[0:64, :n], c.ones_b[0:64, 0:64], SQt[0:64, :n], r=[(tag, "sq")], w=[k2])
    P.act(RS[0:64, :n], b2[0:64, :n], AF.Sqrt, r=[k2], w=[(tag, "rs")], bias=c.eps[0:64, 0:1], scale=1.0 / 64)
    P.add("dve", lambda e: e.reciprocal(RS[0:64, :n], RS[0:64, :n]), r=[(tag, "rs")], w=[(tag, "rs")])
    return KFt, RS


def kv_phase(P, nc, c, src, kvw_d, KT_d, V_d, V):
    with ExitStack() as es:
        def sb(name, shape, dt):
            return es.enter_context(nc.sbuf_tensor(_u("kv_" + name), shape, dt))
        WKV = sb("w", [128, KD, 1536], BF16)
        XT = sb("xt", [128, KD, TT], F32)
        XN = sb("xn", [128, KD, TT], BF16)
        SQ = [sb("sq%d" % i, [128, TT], BF16) for i in range(2)]
        RSTD = sb("rstd", [128, TT], F32)
        KFt = sb("kf", [128, TT], F32)
        SQt = sb("sqt", [128, TT], BF16)
        RS = sb("rs", [128, TT], F32)
        KTA = sb("kta", [64, 12, S], BF16)
        VT = [sb("vt%d" % i, [128, 768], BF16) for i in range(2)]
        pbt = [es.enter_context(nc.psum_tensor(_u("kv_pb%d" % i), [128, 512], F32)) for i in range(8)]
        banks = Banks(pbt, "kpb")
        for k0 in range(0, KD, 2):
            P.dma("pool", WKV[:, k0:k0 + 2], kvw_d[k0 * 128:(k0 + 2) * 128, :].rearrange("(k p) c -> p k c", p=128), "wq%d" % (k0 // 2),
                  w=[("kvw", k) for k in range(k0, k0 + 2)])
        gk = V["kv_k_norm"]
        for t in range(NT):
            n = TT
            P.dma("sp", XT[:], src[:, t * TT:(t + 1) * TT].rearrange("(k p) n -> p k n", p=128), "xl0", w=[("xt",)])
            rb, rbk = banks.next()
            rms_tile(P, c, XT, ("xt",), XN, ("xn",), V["kv_norm"], n, rb, rbk, SQ, RSTD, "kv")
            for h in range(12):
                dil = DIL[h // 4]
                bank, bk = banks.next()
                for k in range(KD):
                    P.mm(bank[0:64, :n], WKV[:, k, h * 64:(h + 1) * 64], XN[:, k, :], start=(k == 0), stop=(k == KD - 1),
                         r=[("kvw", k), ("xn", k)], w=[bk])
                flat_head_rms(P, c, None, bank, bk, gk, n, KFt, SQt, RS, banks, "kvh", None)
                dst = KTA[0:64, h, :].rearrange("p (c i) -> p c i", c=dil)[:, :, t * (TT // dil):(t + 1) * (TT // dil)]
                P.stt("dve", dst, KFt[0:64, :n].rearrange("p (i c) -> p c i", c=dil), c.vecs[0:64, gk:gk + 1],
                      RS[0:64, :n].rearrange("p (i c) -> p c i", c=dil), ALU.mult, ALU.mult,
                      r=[("kvh", "kf"), ("kvh", "rs")], w=[("kta", h, t)])
            for s4 in range(TT // 128):
                vb = VT[s4 % 2]
                b1, k1 = banks.next()
                for k in range(KD):
                    P.mm(b1[:, 0:512], XN[:, k, s4 * 128:(s4 + 1) * 128], WKV[:, k, 768:1280], start=(k == 0), stop=(k == KD - 1),
                         r=[("kvw", k), ("xn", k)], w=[k1])
                P.copy("act", vb[:, 0:512], b1[:, 0:512], r=[k1], w=[("vt", s4 % 2, 0)])
                b2, k2 = banks.next()
                for k in range(KD):
                    P.mm(b2[:, 0:256], XN[:, k, s4 * 128:(s4 + 1) * 128], WKV[:, k, 1280:1536], start=(k == 0), stop=(k == KD - 1),
                         r=[("kvw", k), ("xn", k)], w=[k2])
                P.copy("dve", vb[:, 512:768], b2[:, 0:256], r=[k2], w=[("vt", s4 % 2, 1)])
                P.dma("sp", V_d[t * TT + s4 * 128:t * TT + (s4 + 1) * 128, :], vb[:], "vs%d" % (s4 % 2),
                      r=[("vt", s4 % 2, 0), ("vt", s4 % 2, 1)])
        for h in range(12):
            P.dma("sp", KT_d[:, h, :], KTA[0:64, h, :], "ks%d" % (h % 2), r=[("kta", h, t) for t in range(NT)])
        P.barrier()


def attn_phase(P, nc, c, src, dst, j, layer, wq_d, wo_d, KT_d, V_d, relb_d, sel_d, E_d, V):
    HALF = S // 2
    NTH = HALF // TR
    with ExitStack() as es:
        def sb(name, shape, dt):
            return es.enter_context(nc.sbuf_tensor(_u("at_" + name), shape, dt))
        WQ = sb("wq", [128, KD, D], BF16)
        WO = sb("wo", [128, 4, D], BF16)
        KTG = sb("ktg", [64, 4, S], BF16)
        VZ = sb("vz", [128, 32, 4, 128], BF16)
        QG = sb("qg", [64, 4, HALF], BF16)
        ACN = sb("acn", [128, 2, HALF], F32)
        ACD = sb("acd", [128, 2, HALF], F32)
        XT = sb("xt", [128, KD, TR], F32)
        XN = sb("xn", [128, KD, TR], BF16)
        SQ = [sb("sq%d" % i, [128, TR], BF16) for i in range(2)]
        RSTD = sb("rstd", [128, TR], F32)
        KFt = sb("kf", [128, TR], F32)
        SQt = sb("sqt", [128, TR], BF16)
        RS = sb("rs", [128, TR], F32)
        BM = sb("bm", [128, 12, 256], F32)
        TAB = sb("tab", [33, 12], F32)
        SEL = sb("sel", [33, 3, 510], F32)
        ESB = sb("esb", [12, 510], F32)
        HK = [sb("hk%d" % i, [128, 128], F32) for i in range(2)]
        LG = [sb("lg%d" % i, [128, 256], F32) for i in range(2)]
        PTb = [sb("ptb%d" % i, [128, 256], BF16) for i in range(4)]
        OB = sb("ob", [128, 2, TR], BF16)
        QM = sb("qm", [128, 2, TR], F32)
        YM = sb("ym", [128, 2, TR], BF16)
        QN = sb("qn", [128, 2, TR], BF16)
        PT = [sb("pt%d" % i, [128, TR], BF16) for i in range(4)]
        RD = sb("rd", [128, TR], F32)
        MSQ = sb("msq", [128, TR], BF16)
        MTMP = sb("mtmp", [128, TR], F32)
        pbt = [es.enter_context(nc.psum_tensor(_u("at_pb%d" % i), [128, 512], F32)) for i in range(8)]
        banks = Banks(pbt, "apb")
        for k0 in range(0, KD, 2):
            P.dma("pool", WQ[:, k0:k0 + 2], wq_d[k0 * 128:(k0 + 2) * 128, :].rearrange("(k p) c -> p k c", p=128), "wq%d" % (k0 // 2),
                  w=[("wq", k) for k in range(k0, k0 + 2)])
        P.dma("pool", WO[:], wo_d.rearrange("(k p) c -> p k c", p=128), "wq0", w=[("wo",)])
        P.add("dve", lambda e: e.memset(TAB[:], MASKV), w=[("tab",)])
        P.dma("sp", TAB[0:32, :], relb_d[:, :], "xl1", w=[("tab", 1)], r=[("tab",)])
        P.dma("sp", SEL[:], sel_d.rearrange("g b n -> b g n"), "xs1", w=[("sel",)])
        for g in range(3):
            bank, bk = banks.next()
            P.mm(bank[0:12, 0:510], TAB[:, :], SEL[:, g, :], r=[("tab",), ("tab", 1), ("sel",)], w=[bk])
            P.copy("act", ESB[:], bank[0:12, 0:510], r=[bk], w=[("esb",)])
            P.dma("sp", E_d[g], ESB[:], "vs0", r=[("esb",)], w=[("E", g)])
            for h in range(4):
                for role in range(2):
                    i = (h * 2 + role) % 2
                    srcap = bass.AP(tensor=E_d.tensor, offset=g * 12 * 510 + (4 * g + h) * 510 + role * 255, ap=[[1, 128], [1, 128]])
                    P.dma("sp", HK[i][:], srcap, "xl%d" % i, r=[("E", g)], w=[("hk", i)])
                    bank, bk = banks.next()
                    P.mm(bank[:, 0:128], c.jf[:], HK[i][:], r=[("hk", i)], w=[bk])
                    P.copy("act", BM[:, 4 * g + h, role * 128:(role + 1) * 128], bank[:, 0:128], r=[bk], w=[("bm", g, h, role)])
        P.add("dve", lambda e: e.memset(VZ[:], 0.0), w=[("vz", 0), ("vz", 1)])
        gq = V["b_q_norm%d" % j]
        for H in range(2):
            P.add("dve", lambda e: e.memset(ACN[:], 0.0), w=[("acn",)])
            P.add("dve", lambda e: e.memset(ACD[:], 0.0), w=[("acd",)])
            for g in range(3):
                dil = DIL[g]
                nb = S // (dil * 128)
                nbh = nb // 2
                SL = S // dil
                HL = HALF // dil
                P.dma("sp", KTG[:], KT_d[:, 4 * g:4 * g + 4, :], "xl0", w=[("ktg",)])
                for h in range(4):
                    hh = h % 2
                    vsrc = bass.AP(tensor=V_d.tensor, offset=(4 * g + h) * 64,
                                   ap=[[dil * 768, 128], [768, dil], [128 * dil * 768, nb], [1, 64]])
                    P.dma("sp" if h % 2 == 0 else "act", VZ[:, 0:dil * nb, h, hh * 64:(hh + 1) * 64].rearrange("p (c n) d -> p c n d", c=dil),
                          vsrc, "vl%d" % h, w=[("vzh", h)], r=[("vz", 0)])
                vzk = [("vzh", h) for h in range(4)]
                for tt in range(NTH):
                    t = H * NTH + tt
                    n = TR
                    P.dma("sp", XT[:], src[:, t * TR:(t + 1) * TR].rearrange("(k p) n -> p k n", p=128), "xl0",
                          r=[("X", src.tensor.name, t)], w=[("xt",)])
                    rb, rbk = banks.next()
                    rms_tile(P, c, XT, ("xt",), XN, ("xn",), V["mix_norm%d" % layer], n, rb, rbk, SQ, RSTD, "at")
                    for h in range(4):
                        hq = 4 * g + h
                        bank, bk = banks.next()
                        for k in range(KD):
                            P.mm(bank[0:64, :n], WQ[:, k, hq * 64:(hq + 1) * 64], XN[:, k, :], start=(k == 0), stop=(k == KD - 1),
                                 r=[("wq", k), ("xn", k)], w=[bk])
                        flat_head_rms(P, c, None, bank, bk, gq, n, KFt, SQt, RS, banks, "ath", None)
                        dq = QG[0:64, h, :].rearrange("p (c i) -> p c i", c=dil)[:, :, tt * (TR // dil):(tt + 1) * (TR // dil)]
                        P.stt("dve", dq, KFt[0:64, :n].rearrange("p (i c) -> p c i", c=dil), c.vecs[0:64, gq:gq + 1],
                              RS[0:64, :n].rearrange("p (i c) -> p c i", c=dil), ALU.mult, ALU.mult,
                              r=[("ath", "kf"), ("ath", "rs")], w=[("qg", h, tt)])
                qgk = [("qg", h, tt) for h in range(4) for tt in range(NTH)]
                for cidx in range(dil):
                    for nl in range(nbh):
                        nblk = H * nbh + nl
                        qcol = cidx * HL + nl * 128
                        for hp in range(2):
                            pts = []
                            for hh in range(2):
                                h = 2 * hp + hh
                                bank, bk = banks.next()
                                kcol = cidx * SL + nblk * 128
                                P.mm(bank[:, 0:128], KTG[0:64, h, kcol:kcol + 128], QG[0:64, h, qcol:qcol + 128], r=[("ktg",)] + qgk, w=[bk])
                                wcols = 128
                                if nblk > 0:
                                    P.mm(bank[:, 128:256], KTG[0:64, h, kcol - 128:kcol], QG[0:64, h, qcol:qcol + 128], r=[("ktg",)] + qgk, w=[bk])
                                    wcols = 256
                                li = (hp * 2 + hh) % 2
                                P.stt("dve", LG[li][:, 0:wcols], bank[:, 0:wcols], 0.125, BM[:, 4 * g + h, 0:wcols], ALU.mult, ALU.add,
                                      r=[bk] + [("bm", g, h, r_) for r_ in range(2)], w=[("lg", li)])
                                pi = hp * 2 + hh
                                P.act(PTb[pi][:, 0:wcols], LG[li][:, 0:wcols], AF.Exp, r=[("lg", li)], w=[("ptb", pi)])
                                pts.append((pi, h, hh, wcols))
                            bn, kn = banks.next()
                            bd, kd = banks.next()
                            mms = []
                            for (pi, h, hh, wcols) in pts:
                                mms.append((VZ[:, cidx * nb + nblk, h, :], c.onesh[hh], PTb[pi][:, 0:128], pi, h))
                                if wcols == 256:
                                    mms.append((VZ[:, cidx * nb + nblk - 1, h, :], c.onesh[hh], PTb[pi][:, 128:256], pi, h))
                            for i, (vz, oh, rhs, pi, h) in enumerate(mms):
                                P.mm(bn[:, 0:128], vz, rhs, start=(i == 0), stop=(i == len(mms) - 1), r=[("ptb", pi), ("vzh", h)], w=[kn])
                            for i, (vz, oh, rhs, pi, h) in enumerate(mms):
                                P.mm(bd[:, 0:128], oh, rhs, start=(i == 0), stop=(i == len(mms) - 1), r=[("ptb", pi)], w=[kd])
                            an = ACN[:, hp, :].rearrange("p (i c) -> p c i", c=dil)[:, cidx, nl * 128:(nl + 1) * 128]
                            ad = ACD[:, hp, :].rearrange("p (i c) -> p c i", c=dil)[:, cidx, nl * 128:(nl + 1) * 128]
                            P.tt("dve", an, an, bn[:, 0:128], ALU.add, r=[kn, ("acn",)], w=[("acn",)])
                            P.tt("act" if False else "dve", ad, ad, bd[:, 0:128], ALU.add, r=[kd, ("acd",)], w=[("acd",)])
            for tt in range(NTH):
                t = H * NTH + tt
                n = TR
                tsl = slice(tt * TR, (tt + 1) * TR)
                P.add("dve", lambda e, tsl=tsl: e.reciprocal(ACD[:, :, tsl], ACD[:, :, tsl]), r=[("acd",)], w=[("acd",)])
                P.tt("dve", OB[:], ACN[:, :, tsl], ACD[:, :, tsl], ALU.mult, r=[("acn",), ("acd",)], w=[("ob",)])
                P.dma("sp", XT[:], src[:, t * TR:(t + 1) * TR].rearrange("(k p) n -> p k n", p=128), "xl0",
                      r=[("X", src.tensor.name, t)], w=[("xt",)])
                rb, rbk = banks.next()
                rms_tile(P, c, XT, ("xt",), XN, ("xn",), V["mix_norm%d" % layer], n, rb, rbk, SQ, RSTD, "at")
                for q in range(2):
                    bq, kq = banks.next()
                    for k in range(KD):
                        P.mm(bq[:, :n], WQ[:, k, 768 + q * 128:768 + (q + 1) * 128], XN[:, k, :], start=(k == 0), stop=(k == KD - 1),
                             r=[("wq", k), ("xn", k)], w=[kq])
                    P.copy("act", QM[:, q, :], bq[:, :n], r=[kq], w=[("qm", q)])
                mem_attn_tile(P, c, QM, [("qm", 0), ("qm", 1)], YM, [("ym", 0), ("ym", 1)], n, layer, V, banks, MSQ, MTMP, QN, PT, RD, "ama")
                for m in range(KD):
                    bank, bk = banks.next()
                    for k in range(4):
                        rhs = OB[:, k, :] if k < 2 else YM[:, k - 2, :]
                        rk = ("ob",) if k < 2 else ("ym", k - 2)
                        P.mm(bank[:, :n], WO[:, k, m * 128:(m + 1) * 128], rhs, start=(k == 0), stop=(k == 3), r=[("wo",), rk], w=[bk])
                    P.tt("dve", XT[:, m, :], XT[:, m, :], bank[:, :n], ALU.add, r=[bk, ("xt",)], w=[("xt",)])
                P.dma("sp", dst[:, t * TR:(t + 1) * TR].rearrange("(k p) n -> p k n", p=128), XT[:], "xs0",
                      r=[("xt",)], w=[("X", dst.tensor.name, t)])
        P.barrier()


def vec_layout():
    V = {}
    off = 0

    def put(name, n):
        nonlocal off
        V[name] = off
        off += n
    for i in range(8):
        put("ffn%d" % i, KD)
    for l in range(4):
        put("mix_norm%d" % l, KD)
        put("mem_norm%d" % l, KD)
        put("mem_q_norm%d" % l, 1)
        put("mem_k_norm%d" % l, 1)
    for i in range(2):
        put("a_mu%d" % i, 20)
        for nm in ("a_w0", "a_a0", "a_kk_scale", "a_k_a", "a_r_k", "a_lnx_g", "a_lnx_b"):
            put(nm + "%d" % i, 6)
    for j in range(2):
        put("b_q_norm%d" % j, 1)
    put("kv_norm", KD)
    put("kv_k_norm", 1)
    return V, off


NCONST = 128 * 8 + 256
NKF = NCONST - 384


def build(n_stages=99, dbg_on=False):
    nc = bass.Bass("TRN2", target_bir_lowering=False)
    es = ExitStack()
    c = Ctx()
    V, NV = vec_layout()
    xT = nc.dram_tensor("xT", [D, S], F32, kind="ExternalInput").ap()
    memT = nc.dram_tensor("memT", [D, 256], F32, kind="ExternalInput").ap()
    outT = nc.dram_tensor("outT", [D, S], F32, kind="ExternalOutput").ap()
    dbg = nc.dram_tensor("dbg", [D, S], F32, kind="ExternalOutput").ap() if dbg_on else None
    XS = nc.dram_tensor("xs_scratch", [D, S], F32, kind="Internal").ap()
    vecs_d = nc.dram_tensor("vecs", [128, NV], F32, kind="ExternalInput").ap()
    consts_d = nc.dram_tensor("consts", [128, NCONST + 256], F32, kind="ExternalInput").ap()
    w_in_d = [nc.dram_tensor("w_in%d" % i, [2 * KF, 128, KD, 128], F32, kind="ExternalInput").ap() for i in range(8)]
    w_out_d = [nc.dram_tensor("w_out%d" % i, [FF, D], F32, kind="ExternalInput").ap() for i in range(8)]
    a_w_in_d = [nc.dram_tensor("a_w_in%d" % i, [22, 128, KD, 128], F32, kind="ExternalInput").ap() for i in range(2)]
    a_w_out_d = [nc.dram_tensor("a_w_out%d" % i, [D, D], F32, kind="ExternalInput").ap() for i in range(2)]
    a_wup_d = [nc.dram_tensor("a_w_up%d" % i, [64, 768], F32, kind="ExternalInput").ap() for i in range(2)]
    a_aup_d = [nc.dram_tensor("a_a_up%d" % i, [64, 768], F32, kind="ExternalInput").ap() for i in range(2)]
    a_gup_d = [nc.dram_tensor("a_g_up%d" % i, [128, 768], F32, kind="ExternalInput").ap() for i in range(2)]
    wkv_d = [nc.dram_tensor("mem_w_kv%d" % l, [D, 512], F32, kind="ExternalInput").ap() for l in range(4)]
    b_wq_d = [nc.dram_tensor("b_w_q%d" % i, [D, D], F32, kind="ExternalInput").ap() for i in range(2)]
    b_wo_d = [nc.dram_tensor("b_w_out%d" % i, [512, D], F32, kind="ExternalInput").ap() for i in range(2)]
    kvw_d = nc.dram_tensor("kv_w", [D, 1536], F32, kind="ExternalInput").ap()
    relb_d = nc.dram_tensor("rel_bias", [32, 12], F32, kind="ExternalInput").ap()
    sel_d = nc.dram_tensor("sel", [3, 33, 510], F32, kind="ExternalInput").ap()
    KT_d = nc.dram_tensor("kt_scratch", [64, 12, S], BF16, kind="Internal").ap()
    V_d = nc.dram_tensor("v_scratch", [S, 768], BF16, kind="Internal").ap()
    E_d = nc.dram_tensor("e_scratch", [3, 12, 510], F32, kind="Internal").ap()

    P = Prog(nc, es)

    def sbt(name, shape, dt):
        return es.enter_context(nc.sbuf_tensor(_u(name), shape, dt))
    c.vecs = sbt("c_vecs", [128, NV], F32)
    c.KF = sbt("c_kf", [128, NKF], F32)
    c.KB = sbt("c_kb", [128, 5 * 128], BF16)
    c.eps = sbt("c_eps", [128, 4], F32)
    c.MK = sbt("c_mk", [128, 2, 256], BF16)
    c.MVZ = sbt("c_mvz", [128, 2, 4, 128], BF16)
    P.dma("sp", c.vecs[:], vecs_d[:, :], "c0", w=[("vecs",)])
    P.dma("sp", c.KF[:], consts_d[:, 384:NCONST], "c1", w=[("kf",)])
    P.dma("pool", c.KB[:, 0:384], consts_d[:, 0:384], "c2", w=[("kb",)])
    P.dma("pool", c.KB[:, 384:640], consts_d[:, NCONST:NCONST + 256], "c3", w=[("kb2",)])
    P.add("dve", lambda e: e.memset(c.eps[:, 0:1], NORM_EPS), w=[("eps", 0)])
    P.add("dve", lambda e: e.memset(c.eps[:, 1:2], LNX_EPS), w=[("eps", 1)])
    P.add("dve", lambda e: e.memset(c.MVZ[:], 0.0), w=[("MVZ",)])
    c.bda_f = c.KF[:, 0:128]
    c.msu = c.KF[:, 128:256]
    c.msl = c.KF[:, 256:384]
    c.mui = c.KF[:, 384:512]
    c.cmask = c.KF[:, 512:768]
    c.jf = c.KF[:, 768:896]
    c.ident_b = c.KB[:, 0:128]
    c.ones_b = c.KB[:, 128:256]
    c.bd_b = c.KB[:, 256:384]
    c.ones_bf = c.ones_b
    c.onesh = [c.KB[:, 384:512], c.KB[:, 512:640]]
    P.barrier()

    stages = []
    for layer in range(4):
        stages.append(("ffn", 2 * layer))
        stages.append(("mix", layer))
        stages.append(("ffn", 2 * layer + 1))
        if layer == 1:
            stages.append(("kv", 0))
    stages = stages[:n_stages]
    cur = xT
    for si, (kind, i) in enumerate(stages):
        last = si == len(stages) - 1
        dst = outT if last else XS
        if kind == "ffn":
            ffn_phase(P, nc, c, cur, dst, w_in_d[i], w_out_d[i], gcol=V["ffn%d" % i])
        elif kind == "kv":
            kv_phase(P, nc, c, cur, kvw_d, KT_d, V_d, V)
            continue
        else:
            layer = i
            mem_prep(P, nc, c, memT, wkv_d[layer], layer, V)
            if layer < 2:
                rwkv_phase(P, nc, c, cur, dst, layer, layer, a_w_in_d[layer], a_w_out_d[layer], a_wup_d[layer], a_aup_d[layer],
                           a_gup_d[layer], V, dbg=dbg if last else None)
            else:
                attn_phase(P, nc, c, cur, dst, layer - 2, layer, b_wq_d[layer - 2], b_wo_d[layer - 2], KT_d, V_d, relb_d, sel_d, E_d, V)
        cur = dst
    P.barrier()
    es.close()
    print("[kernel] ops=%d instr=%d" % (P.nops, P.ninstr))
    return nc


def _rep2(v):
    return np.ascontiguousarray(np.concatenate([v, v]).reshape(128, 1))


def _t5_bucket(dist):
    dist = np.asarray(dist, np.int64)
    d_f = np.maximum(dist, 1).astype(np.float32)
    large = 16 + (np.log(d_f / np.float32(16)) / np.float32(np.log(2048 / 16)) * np.float32(16)).astype(np.int32)
    large = np.minimum(large, 31)
    return np.where(dist < 16, dist, large)


def make_sel():
    sel = np.zeros((3, 33, 510), np.float32)
    n = np.arange(255)
    for g, dil in enumerate((1, 4, 16)):
        own_valid = n >= 127
        bo = np.where(own_valid, _t5_bucket(np.maximum(n - 127, 0) * dil), 32)
        prev_valid = n <= 127
        bp = np.where(prev_valid, _t5_bucket((n + 1) * dil), 32)
        sel[g, bo, n] = 1.0
        sel[g, bp, 255 + n] = 1.0
    return sel


def make_consts():
    K = np.zeros((128, NCONST + 256), np.float32)
    K[:, NCONST:NCONST + 64] = 1.0
    K[:, NCONST + 192:NCONST + 256] = 1.0
    K[:, 0:128] = np.eye(128)
    K[:, 128:256] = 1.0
    bd = np.zeros((128, 128), np.float32)
    bd[:64, :64] = 1.0
    bd[64:, 64:] = 1.0
    K[:, 256:384] = bd
    K[:, 384:512] = bd / 64.0
    i = np.arange(128)
    K[:, 512:640] = (i[:, None] < i[None, :])
    K[:, 640:768] = (i[:, None] > i[None, :])
    K[:, 768:896] = (i[:, None] <= i[None, :])
    cm = np.ones(256, np.float32)
    cm[::128] = 0.0
    K[:, 896:896 + 256] = cm[None, :]
    K[:, 1152:1280] = np.eye(128)[::-1]
    return K


def kernel(**inputs):
    n_stages = int(inputs.pop("_n_stages", 99))
    cores = inputs.pop("_cores", list(range(8)))
    trace = inputs.pop("_trace", False)
    dbg_on = inputs.pop("_dbg", False)
    f = lambda a: np.asarray(a, dtype=np.float32)
    x = f(inputs["x"])
    mem = f(inputs["mem"])
    V, NV = vec_layout()
    vecs = np.zeros((128, NV), np.float32)

    def put(name, arr):
        arr = np.asarray(arr, np.float32)
        vecs[:, V[name]:V[name] + arr.shape[1]] = arr
    shared = {}
    for l in range(4):
        for nm in ("ffn_pre", "ffn_post"):
            i = 2 * l + (0 if nm == "ffn_pre" else 1)
            shared["w_in%d" % i] = _slots_in(f(inputs[nm + "_w_in"][l]))
            shared["w_out%d" % i] = np.ascontiguousarray(f(inputs[nm + "_w_out"][l]))
            put("ffn%d" % i, _vec_pk(f(inputs[nm + "_norm"][l])))
        put("mix_norm%d" % l, _vec_pk(f(inputs["mix_norm"][l])))
        put("mem_norm%d" % l, _vec_pk(f(inputs["mem_norm"][l])))
        put("mem_q_norm%d" % l, _rep2(f(inputs["mem_q_norm"][l])))
        put("mem_k_norm%d" % l, _rep2(f(inputs["mem_k_norm"][l])))
        shared["mem_w_kv%d" % l] = np.ascontiguousarray(f(inputs["mem_w_kv"][l]))
    for i in range(2):
        shared["a_w_in%d" % i] = _slots_in(f(inputs["a_w_in"][i]))
        shared["a_w_out%d" % i] = np.ascontiguousarray(f(inputs["a_w_out"][i]))
        shared["a_w_up%d" % i] = np.ascontiguousarray(f(inputs["a_w_up"][i]))
        shared["a_a_up%d" % i] = np.ascontiguousarray(f(inputs["a_a_up"][i]))
        shared["a_g_up%d" % i] = np.ascontiguousarray(f(inputs["a_g_up"][i]))
        put("a_mu%d" % i, _vec_pk(f(inputs["a_shift_mu"][i])))
        put("a_w0%d" % i, _vec_pk(f(inputs["a_w0"][i])))
        put("a_a0%d" % i, _vec_pk(f(inputs["a_a0"][i])))
        put("a_kk_scale%d" % i, _vec_pk(f(inputs["a_kk_scale"][i])))
        put("a_k_a%d" % i, _vec_pk(f(inputs["a_k_a"][i])))
        put("a_r_k%d" % i, _vec_pk(f(inputs["a_r_k"][i]).reshape(-1)))
        put("a_lnx_g%d" % i, _vec_pk(f(inputs["a_lnx_g"][i])))
        put("a_lnx_b%d" % i, _vec_pk(f(inputs["a_lnx_b"][i])))
    for j in range(2):
        put("b_q_norm%d" % j, _rep2(f(inputs["b_q_norm"][j])))
    put("kv_norm", _vec_pk(f(inputs["kv_norm"])))
    put("kv_k_norm", _rep2(f(inputs["kv_k_norm"])))
    for jj in range(2):
        shared["b_w_q%d" % jj] = np.ascontiguousarray(f(inputs["b_w_q"][jj]))
        shared["b_w_out%d" % jj] = np.ascontiguousarray(f(inputs["b_w_out"][jj]))
    shared["kv_w"] = np.ascontiguousarray(f(inputs["kv_w"]))
    shared["rel_bias"] = np.ascontiguousarray(f(inputs["rel_bias"]))
    shared["sel"] = make_sel()
    shared["vecs"] = vecs
    shared["consts"] = make_consts()
    nc = build(n_stages, dbg_on)
    in_maps = []
    for b in cores:
        m = dict(shared)
        m["xT"] = np.ascontiguousarray(x[b].T)
        m["memT"] = np.ascontiguousarray(mem[b].T)
        in_maps.append(m)
    if trace:
        res = run_bass_kernel_spmd(nc, in_maps, core_ids=list(range(len(cores))), trace=True)
        print("[kernel] exec_time_ns", res.exec_time_ns)
    else:
        res = run_bass_kernel_spmd(nc, in_maps, core_ids=list(range(len(cores))))
    if dbg_on:
        kernel.dbg = [np.ascontiguousarray(r["dbg"].T) for r in res.results]
    out = np.stack([np.ascontiguousarray(r["outT"].T) for r in res.results], axis=0)
    return out.astype(np.float32)
```

```python
import os
import numpy as np
from contextlib import ExitStack
import concourse.bass as bass
import concourse.mybir as mybir
from concourse.bass_utils import run_bass_kernel_spmd

F32 = mybir.dt.float32
BF16 = mybir.dt.bfloat16
AF = mybir.ActivationFunctionType
ALU = mybir.AluOpType
AX = mybir.AxisListType

D = 1024
KD = 8
S = 4096
FF = 2816
KF = 22
TT = 512
NT = S // TT
NORM_EPS = 1e-6

COMPUTE = ("pe", "act", "dve", "pool")


class Prog:
    def __init__(self, nc, es):
        self.nc = nc
        self.eng = dict(pe=nc.tensor, act=nc.scalar, dve=nc.vector, pool=nc.gpsimd, sp=nc.sync)
        self.sem = {e: es.enter_context(nc.semaphore("s_" + e)) for e in COMPUTE}
        self.cnt = {e: 0 for e in COMPUTE}
        self.es = es
        self.dsem = {}
        self.dcnt = {}
        self.pending = []
        self.last_w = {}
        self.readers = {}
        self.waited = {e: {} for e in self.eng}
        self.done = {}
        self.sigs = {e: [] for e in COMPUTE}
        self.nops = 0
        self.ninstr = 0

    def add(self, eng, fn, r=(), w=(), dma=None):
        idx = self.nops
        self.nops += 1
        deps = set()
        for k in r:
            j = self.last_w.get(k)
            if j is not None:
                deps.add(j)
        for k in w:
            j = self.last_w.get(k)
            if j is not None:
                deps.add(j)
            rd = self.readers.get(k)
            if rd:
                deps.update(rd.values())
        for k in w:
            self.last_w[k] = idx
            self.readers[k] = {}
        tag = ("d", dma) if dma else ("c", eng)
        for k in r:
            if k not in w:
                self.readers.setdefault(k, {})[tag] = idx
        self.pending.append(dict(idx=idx, eng=eng, fn=fn, deps=deps, dma=dma, sig=False))
        return idx

    def _wait(self, eng, semname, semh, val):
        if self.waited[eng].get(semname, 0) >= val:
            return
        self.waited[eng][semname] = val
        self.eng[eng].wait_ge(semh, val)
        self.ninstr += 1

    def flush(self):
        pend = self.pending
        self.pending = []
        byidx = {op["idx"]: op for op in pend}
        for op in pend:
            for j in op["deps"]:
                d = byidx.get(j)
                if d is not None and d["dma"] is None:
                    if not (d["eng"] == "pe" and op["eng"] == "pe" and op["dma"] is None):
                        d["sig"] = True
        last = {}
        for op in pend:
            if op["dma"] is None:
                last[op["eng"]] = op
        for op in last.values():
            op["sig"] = True
        for op in pend:
            eng = op["eng"]
            for j in sorted(op["deps"]):
                if j in self.done:
                    kind, name, val = self.done[j]
                    if kind == "d":
                        self._wait(eng, "d_" + name, self.dsem[name], val)
                    else:
                        if name == "pe" and eng == "pe" and op["dma"] is None:
                            continue
                        if val is None:
                            lst = self.sigs[name]
                            lo, hi = 0, len(lst)
                            while lo < hi:
                                mid = (lo + hi) // 2
                                if lst[mid][0] < j:
                                    lo = mid + 1
                                else:
                                    hi = mid
                            val = lst[lo][1]
                        self._wait(eng, "c_" + name, self.sem[name], val)
                else:
                    raise RuntimeError("dep on unemitted op")
            if op["dma"]:
                name = op["dma"]
                if name not in self.dsem:
                    self.dsem[name] = self.es.enter_context(self.nc.semaphore("d_" + name))
                    self.dcnt[name] = 0
                if self.dcnt[name] > 0:
                    self._wait(eng, "d_" + name, self.dsem[name], self.dcnt[name])
                ins = op["fn"](self.eng[eng])
                self.dcnt[name] += 16
                ins.then_inc(self.dsem[name], 16)
                self.done[op["idx"]] = ("d", name, self.dcnt[name])
            else:
                ins = op["fn"](self.eng[eng])
                if op["sig"]:
                    self.cnt[eng] += 1
                    ins.then_inc(self.sem[eng], 1)
                    self.done[op["idx"]] = ("c", eng, self.cnt[eng])
                    self.sigs[eng].append((op["idx"], self.cnt[eng]))
                else:
                    self.done[op["idx"]] = ("c", eng, None)
            self.ninstr += 1

    def barrier(self, dma_only_on=("sp",)):
        self.flush()
        for f in self.eng:
            for e in COMPUTE:
                if e != f and self.cnt[e] > 0:
                    self._wait(f, "c_" + e, self.sem[e], self.cnt[e])
            for name, h in self.dsem.items():
                if self.dcnt[name] > 0:
                    self._wait(f, "d_" + name, h, self.dcnt[name])
        self.last_w = {}
        self.readers = {}


    def mm(self, out, lhsT, rhs, start=True, stop=True, r=(), w=()):
        self.add("pe", lambda e: e.matmul(out, lhsT, rhs, start=start, stop=stop), r=r, w=w)

    def tr(self, out, in_, ident, r=(), w=()):
        self.add("pe", lambda e: e.matmul(out, in_, ident, start=True, stop=True), r=r, w=w)

    def act(self, out, in_, func, r=(), w=(), bias=None, scale=None):
        kw = {}
        if bias is not None:
            kw["bias"] = bias
        if scale is not None:
            kw["scale"] = scale
        self.add("act", lambda e: e.activation(out=out, in_=in_, func=func, **kw), r=r, w=w)

    def tt(self, eng, out, in0, in1, op, r=(), w=()):
        self.add(eng, lambda e: e.tensor_tensor(out, in0, in1, op), r=r, w=w)

    def ts(self, eng, out, in0, s1, s2, op0, op1, r=(), w=()):
        self.add(eng, lambda e: e.tensor_scalar(out, in0, s1, s2, op0, op1), r=r, w=w)

    def tsmul(self, eng, out, in0, s1, r=(), w=()):
        self.add(eng, lambda e: e.tensor_scalar_mul(out, in0, s1), r=r, w=w)

    def stt(self, eng, out, in0, scalar, in1, op0, op1, r=(), w=()):
        self.add(eng, lambda e: e.scalar_tensor_tensor(out, in0, scalar, in1, op0, op1), r=r, w=w)

    def copy(self, eng, out, in_, r=(), w=()):
        if eng == "act":
            self.add("act", lambda e: e.activation(out=out, in_=in_, func=AF.Copy), r=r, w=w)
        elif os.environ.get("KCOPY", "mul") == "mul":
            self.add(eng, lambda e: e.tensor_scalar_mul(out, in_, 1.0), r=r, w=w)
        else:
            self.add(eng, lambda e: e.tensor_copy(out, in_), r=r, w=w)

    def dma(self, q, out, in_, sem, r=(), w=()):
        self.add(q, lambda e: e.dma_start(out=out, in_=in_), r=r, w=w, dma=sem)


def _slots_in(w):
    K, M = w.shape
    return np.ascontiguousarray(w.reshape(K // 128, 128, M // 128, 128).transpose(2, 1, 0, 3))


def _vec_pk(v):
    return np.ascontiguousarray(v.reshape(-1, 128).T)


class Ctx:
    pass


_UID = [0]


def _u(name):
    _UID[0] += 1
    return "%s_%d" % (name, _UID[0])


def ffn_phase(P, nc, c, src, dst, w_in_d, w_out_d, gcol):
    with ExitStack() as es:
        def sb(name, shape, dt):
            return es.enter_context(nc.sbuf_tensor(_u(name), shape, dt))

        def ps(name, shape, dt=F32):
            return es.enter_context(nc.psum_tensor(_u(name), shape, dt))
        WIN = sb("f_win", [128, 2 * KF, KD, 128], BF16)
        WOUT = sb("f_wout", [128, KF, D], BF16)
        XT = [sb("f_xt%d" % i, [128, KD, TT], F32) for i in range(2)]
        XN = sb("f_xn", [128, KD, TT], BF16)
        ACTB = sb("f_act", [128, KF, TT], BF16)
        SQ = [sb("f_sq%d" % i, [128, TT], BF16) for i in range(1)]
        SG = [sb("f_sg%d" % i, [128, TT], BF16) for i in range(2)]
        RSTD = sb("f_rstd", [128, TT], F32)
        PH = [ps("f_ph%d" % i, [128, TT]) for i in range(4)]
        PY = [ps("f_py%d" % i, [128, TT]) for i in range(2)]
        PSS = ps("f_pss", [128, TT])

        G = 2
        for j0 in range(0, 2 * KF, G):
            P.add("pool", lambda e, j0=j0: e.dma_start(
                out=WIN[:, j0:j0 + G], in_=w_in_d[j0:j0 + G].rearrange("j p k c -> p j k c")),
                w=[("win", j) for j in range(j0, j0 + G)], dma="wq%d" % ((j0 // G) % 4))
        for k0 in range(0, KF, G):
            P.add("pool", lambda e, k0=k0: e.dma_start(
                out=WOUT[:, k0:k0 + G], in_=w_out_d[k0 * 128:(k0 + G) * 128, :].rearrange("(k p) c -> p k c", p=128)),
                w=[("wout", k) for k in range(k0, k0 + G)], dma="wq%d" % ((k0 // G) % 4))

        def load(t):
            b = t % 2
            P.add("sp", lambda e: e.dma_start(
                out=XT[b][:], in_=src[:, t * TT:(t + 1) * TT].rearrange("(k p) n -> p k n", p=128)),
                r=[("X", src.tensor.name, t)], w=[("xt", b)], dma="xl%d" % b)

        def do_tile(t):
            b = t % 2
            if t + 1 < NT:
                load(t + 1)
            xt = XT[b]
            for k in range(KD):
                q = 0
                P.add("act", lambda e, k=k, q=q: e.activation(out=SQ[q][:], in_=xt[:, k, :], func=AF.Square),
                      r=[("xt", b)], w=[("sq", q)])
                P.add("pe", lambda e, k=k, q=q: e.matmul(PSS[:], c.ones_bf[:], SQ[q][:], start=(k == 0), stop=(k == KD - 1)),
                      r=[("sq", q)], w=[("pss",)])
            P.add("act", lambda e: e.activation(out=RSTD[:], in_=PSS[:], func=AF.Sqrt, bias=c.eps[:, 0:1], scale=1.0 / D),
                  r=[("pss",)], w=[("rstd",)])
            P.add("dve", lambda e: e.reciprocal(RSTD[:], RSTD[:]),
                  r=[("rstd",)], w=[("rstd",)])
            for k in range(KD):
                P.add("dve", lambda e, k=k: e.scalar_tensor_tensor(
                    XN[:, k, :], xt[:, k, :], c.vecs[:, gcol + k:gcol + k + 1], RSTD[:], ALU.mult, ALU.mult),
                    r=[("xt", b), ("rstd",)], w=[("xn", k)])
            for j in range(KF):
                pg = PH[(2 * j) % 4]
                pu = PH[(2 * j + 1) % 4]
                kg, ku = ("ph", (2 * j) % 4), ("ph", (2 * j + 1) % 4)
                for k in range(KD):
                    P.add("pe", lambda e, j=j, k=k, pg=pg: e.matmul(pg[:], WIN[:, j, k, :], XN[:, k, :], start=(k == 0), stop=(k == KD - 1)),
                          r=[("win", j), ("xn", k)], w=[kg])
                for k in range(KD):
                    P.add("pe", lambda e, j=j, k=k, pu=pu: e.matmul(pu[:], WIN[:, KF + j, k, :], XN[:, k, :], start=(k == 0), stop=(k == KD - 1)),
                          r=[("win", KF + j), ("xn", k)], w=[ku])
                q = j % 2
                P.add("act", lambda e, pg=pg, q=q: e.activation(out=SG[q][:], in_=pg[:], func=AF.Silu),
                      r=[kg], w=[("sg", q)])
                P.add("dve", lambda e, j=j, pu=pu, q=q: e.tensor_tensor(ACTB[:, j, :], pu[:], SG[q][:], ALU.mult),
                      r=[ku, ("sg", q)], w=[("act", j)])
            for m in range(KD):
                py = PY[m % 2]
                ky = ("py", m % 2)
                for k in range(KF):
                    P.add("pe", lambda e, m=m, k=k, py=py: e.matmul(py[:], WOUT[:, k, m * 128:(m + 1) * 128], ACTB[:, k, :], start=(k == 0), stop=(k == KF - 1)),
                          r=[("wout", k), ("act", k)], w=[ky])
                P.add("dve", lambda e, m=m, py=py: e.scalar_tensor_tensor(
                    xt[:, m, :], py[:], 0.5, xt[:, m, :], ALU.mult, ALU.add),
                    r=[ky, ("xt", b)], w=[("xt", b)])
            P.add("sp", lambda e, t=t: e.dma_start(
                out=dst[:, t * TT:(t + 1) * TT].rearrange("(k p) n -> p k n", p=128), in_=xt[:]),
                r=[("xt", b)], w=[("X", dst.tensor.name, t)], dma="xs%d" % b)
        load(0)
        for t in range(NT):
            do_tile(t)
        P.barrier()


import os
DBG_NTR = int(os.environ.get('KDBG_NTR', '999'))
TMZENG = os.environ.get('KTMZ', 'act')
DBG_STOP = float(os.environ.get('KDBG_STOP', '99'))
TR = 256
HG = [[0, 2, 4, 6], [1, 3, 5, 7], [8, 10], [9, 11]]


def hgk(h):
    return 2 * (h // 8) + (h % 2)

CH = 128
NCH = TR // CH
NTR = S // TR
LNX_EPS = 64e-5
DEC_SCALE = -0.6065306597126334


class Banks:
    def __init__(self, tiles, tag):
        self.tiles = tiles
        self.tag = tag
        self.i = 0

    def next(self):
        i = self.i
        self.i = (self.i + 1) % len(self.tiles)
        return self.tiles[i], (self.tag, i)


def rms_tile(P, c, xt, xkey, xn, xnkey, gcol, n, PSS, psskey, SQ, RSTD, tag):
    for k in range(KD):
        q = k % 2
        P.act(SQ[q][:, :n], xt[:, k, :n], AF.Square, r=[xkey], w=[(tag, "sq", q)])
        P.mm(PSS[:, :n], c.ones_b[:], SQ[q][:, :n], start=(k == 0), stop=(k == KD - 1), r=[(tag, "sq", q)], w=[psskey])
    P.act(RSTD[:, :n], PSS[:, :n], AF.Sqrt, r=[psskey], w=[(tag, "rstd")], bias=c.eps[:, 0:1], scale=1.0 / D)
    P.add("dve", lambda e: e.reciprocal(RSTD[:, :n], RSTD[:, :n]), r=[(tag, "rstd")], w=[(tag, "rstd")])
    for k in range(KD):
        P.stt("dve", xn[:, k, :n], xt[:, k, :n], c.vecs[:, gcol + k:gcol + k + 1], RSTD[:, :n], ALU.mult, ALU.mult,
              r=[xkey, (tag, "rstd")], w=[xnkey + (k,)])


def head_rms(P, c, out, src, srckey, gain_col, n, bank, bkey, SQ, TMP, tag, outkey):
    P.act(SQ[:, :n], src, AF.Square, r=[srckey], w=[(tag, "hsq")])
    P.mm(bank[:, :n], c.bd_b[:], SQ[:, :n], r=[(tag, "hsq")], w=[bkey])
    P.act(TMP[:, :n], bank[:, :n], AF.Sqrt, r=[bkey], w=[(tag, "htmp")], bias=c.eps[:, 0:1], scale=1.0 / 64)
    P.add("dve", lambda e: e.reciprocal(TMP[:, :n], TMP[:, :n]), r=[(tag, "htmp")], w=[(tag, "htmp")])
    P.stt("dve", out, src, c.vecs[:, gain_col:gain_col + 1], TMP[:, :n], ALU.mult, ALU.mult,
          r=[srckey, (tag, "htmp")], w=[outkey])


def mem_prep(P, nc, c, memT_d, wkv_d, layer, V):
    with ExitStack() as es:
        def sb(name, shape, dt):
            return es.enter_context(nc.sbuf_tensor(_u(name), shape, dt))
        WKV = sb("mp_wkv", [128, KD, 512], BF16)
        MT = sb("mp_mt", [128, KD, 256], F32)
        MN = sb("mp_mn", [128, KD, 256], BF16)
        SQ = [sb("mp_sq%d" % i, [128, 256], BF16) for i in range(2)]
        RSTD = sb("mp_rstd", [128, 256], F32)
        KF32 = sb("mp_kf", [128, 256], F32)
        TMP = sb("mp_tmp", [128, 256], F32)
        pb = [es.enter_context(nc.psum_tensor(_u("mp_pb%d" % i), [128, 512], F32)) for i in range(3)]
        P.dma("pool", WKV[:], wkv_d.rearrange("(k p) c -> p k c", p=128), "wq0", w=[("mp", "wkv")])
        P.dma("sp", MT[:], memT_d.rearrange("(k p) n -> p k n", p=128), "xl0", w=[("mp", "mt")])
        rms_tile(P, c, MT, ("mp", "mt"), MN, ("mp", "mn"), V["mem_norm%d" % layer], 256, pb[0], ("mp", "pb", 0), SQ, RSTD, "mp")
        mnkeys = [("mp", "mn", k) for k in range(KD)]
        for hp in range(2):
            for k in range(KD):
                P.mm(pb[1][:, :256], WKV[:, k, hp * 128:(hp + 1) * 128], MN[:, k, :], start=(k == 0), stop=(k == KD - 1),
                     r=[("mp", "wkv"), mnkeys[k]], w=[("mp", "pb", 1)])
            P.copy("act", KF32[:], pb[1][:, :256], r=[("mp", "pb", 1)], w=[("mp", "kf")])
            head_rms(P, c, c.MK[:, hp, :], KF32[:], ("mp", "kf"), V["mem_k_norm%d" % layer], 256, pb[2], ("mp", "pb", 2), SQ[0], TMP, "mpk",
                     ("MK", hp))
        for mc in range(2):
            for k in range(KD):
                P.mm(pb[1][:, :256], MN[:, k, mc * 128:(mc + 1) * 128], WKV[:, k, 256:512], start=(k == 0), stop=(k == KD - 1),
                     r=[("mp", "wkv"), mnkeys[k]], w=[("mp", "pb", 1)])
            for h in range(4):
                hh = h % 2
                P.copy("act", c.MVZ[:, mc, h, hh * 64:(hh + 1) * 64], pb[1][:, h * 64:(h + 1) * 64], r=[("mp", "pb", 1)], w=[("MVZ",)])
        P.barrier()


def mem_attn_tile(P, c, QM, qkeys, YM, ymkeys, n, layer, V, banks, SQ, TMP, QN, PT, RD, tag):
    for hp in range(2):
        bank, bk = banks.next()
        head_rms(P, c, QN[:, hp, :n], QM[:, hp, :n], qkeys[hp], V["mem_q_norm%d" % layer], n, bank, bk, SQ, TMP, tag, (tag, "qn", hp))
    for hp in range(2):
        for hh in range(2):
            off = hh * 64
            for mc in range(2):
                bl, kl = banks.next()
                P.mm(bl[:, :n], c.MK[off:off + 64, hp, mc * 128:(mc + 1) * 128], QN[off:off + 64, hp, :n],
                     r=[("MK", hp), (tag, "qn", hp)], w=[kl])
                P.act(PT[hh * 2 + mc][:, :n], bl[:, :n], AF.Exp, r=[kl], w=[(tag, "pt", hh * 2 + mc)], scale=0.125)
        bnum, knum = banks.next()
        bden, kden = banks.next()
        for i in range(4):
            hh, mc = i // 2, i % 2
            h = hp * 2 + hh
            P.mm(bnum[:, :n], c.MVZ[:, mc, h, :], PT[i][:, :n], start=(i == 0), stop=(i == 3), r=[("MVZ",), (tag, "pt", i)], w=[knum])
        for i in range(4):
            hh, mc = i // 2, i % 2
            P.mm(bden[:, :n], c.onesh[hh], PT[i][:, :n], start=(i == 0), stop=(i == 3), r=[(tag, "pt", i)], w=[kden])
        P.add("dve", lambda e, bden=bden: e.reciprocal(RD[:, :n], bden[:, :n]), r=[kden], w=[(tag, "rd")])
        P.tt("dve", YM[:, hp, :n], bnum[:, :n], RD[:, :n], ALU.mult, r=[knum, (tag, "rd")], w=[ymkeys[hp]])


def rwkv_phase(P, nc, c, src, dst, li, layer, w_in_d, w_out_d, wup_d, aup_d, gup_d, V, dbg=None):
    with ExitStack() as es:
        def sb(name, shape, dt):
            return es.enter_context(nc.sbuf_tensor(_u("rk_" + name), shape, dt))
        WA = sb("wa", [128, 22, KD, 128], BF16)
        WO = sb("wo", [128, KD, D], BF16)
        WAUP = sb("waup", [128, 768], BF16)
        GUP = sb("gup", [128, 768], BF16)
        XT = sb("xt", [128, KD, TR], F32)
        XN = sb("xn", [128, KD, TR], BF16)
        SQ = [sb("sq%d" % i, [128, TR], BF16) for i in range(2)]
        RSTD = sb("rstd", [128, TR], F32)
        CARRY = sb("carry", [128, 20], F32)
        OMM = sb("omm", [128, 20], F32)
        OMKA = sb("omka", [128, 6], F32)
        TA = sb("ta", [128, TR], F32)
        P18 = sb("p18", [128, TR], F32)
        P19 = sb("p19", [128, TR], F32)
        TW = sb("tw", [128, TR], BF16)
        AL = sb("al", [128, TR], BF16)
        SGG = sb("sgg", [128, TR], BF16)
        QM = sb("qm", [128, 2, TR], F32)
        Rf = sb("rf", [128, TR], F32)
        Kf = sb("kf", [128, TR], F32)
        Vf = sb("vf", [128, TR], F32)
        T = [sb("t%d" % i, [128, TR], F32) for i in range(10)]
        HB = [sb("hb%d" % i, [128, TR], BF16) for i in range(4)]
        G = sb("g", [128, 6, TR], BF16)
        BON = sb("bon", [128, 6, TR], BF16)
        RT = sb("rt", [128, 6, TR], BF16)
        KT = sb("kt", [128, 6, TR], BF16)
        BT = sb("bt", [128, 6, TR], BF16)
        AT = sb("at", [128, 6, TR], BF16)
        PC = sb("pc", [128, 6, NCH], F32)
        TMV = [sb("tmv%d" % i, [128, 6, 128], BF16) for i in range(NCH)]
        TMZ = [sb("tmz%d" % i, [128, 2, 13, 128], BF16) for i in range(NCH)]
        W = [sb("w%d" % i, [128, 2, 768], BF16) for i in range(2 * NCH)]
        AK = [sb("ak%d" % i, [128, 12, 128], BF16) for i in range(2 * NCH)]
        BK = [sb("bk%d" % i, [128, 12, 128], BF16) for i in range(2 * NCH)]
        AAK = sb("aak", [128, 12, 128], BF16)
        ARK = sb("ark", [128, 12, 128], BF16)
        ARB = sb("arb", [128, 12, 128], BF16)
        AHT = sb("aht", [128, 6, 128], BF16)
        UU = sb("uu", [128, 12, 64], BF16)
        SF = sb("sf", [128, 6, 64], F32)
        SB = sb("sb", [128, 6, 64], BF16)
        YB = sb("yb", [128, 6, TR], F32)
        YO = sb("yo", [128, 6, TR], BF16)
        YM = sb("ym", [128, 2, TR], BF16)
        QN = sb("qn", [128, 2, TR], BF16)
        PT = [sb("pt%d" % i, [128, TR], BF16) for i in range(4)]
        RD = sb("rd", [128, TR], F32)
        MSQ = sb("msq", [128, TR], BF16)
        MTMP = sb("mtmp", [128, TR], F32)
        pbt = [es.enter_context(nc.psum_tensor(_u("rk_pb%d" % i), [128, 512], F32)) for i in range(8)]
        banks = Banks(pbt, "rpb")
        tbanks = banks
        print("[kernel] rwkv sbuf free", nc.sbuf_bytes_remaining)

        for j0 in range(0, 22, 2):
            P.dma("pool", WA[:, j0:j0 + 2], w_in_d[j0:j0 + 2].rearrange("j p k c -> p j k c"), "wq%d" % ((j0 // 2) % 4),
                  w=[("wa", j) for j in range(j0, j0 + 2)])
        P.dma("pool", WAUP[0:64, :], wup_d[:, :], "wq0", w=[("waup", 0)])
        P.dma("pool", WAUP[64:128, :], aup_d[:, :], "wq1", w=[("waup", 1)])
        P.dma("pool", GUP[:], gup_d[:, :], "wq2", w=[("gup",)])
        for k0 in range(0, KD, 2):
            P.dma("pool", WO[:, k0:k0 + 2], w_out_d[k0 * 128:(k0 + 2) * 128, :].rearrange("(k p) c -> p k c", p=128), "wq%d" % ((k0 // 2) % 4),
                  w=[("wo", k) for k in range(k0, k0 + 2)])
        mu0 = V["a_mu%d" % li]
        P.ts("dve", OMM[:], c.vecs[:, mu0:mu0 + 20], -1.0, 1.0, ALU.mult, ALU.add, w=[("omm",)])
        ka0 = V["a_k_a%d" % li]
        P.ts("dve", OMKA[:], c.vecs[:, ka0:ka0 + 6], -1.0, 1.0, ALU.mult, ALU.add, w=[("omka",)])
        P.add("dve", lambda e: e.memset(CARRY[:], 0.0), w=[("carry", m) for m in range(20)])
        P.add("dve", lambda e: e.memset(SF[:], 0.0), w=[("sf",)])
        P.add("dve", lambda e: e.memset(SB[:], 0.0), w=[("sbk",)])
        for i in range(NCH):
            P.add("dve", lambda e, i=i: e.memset(TMZ[i][:], 0.0), w=[("tmz", i, hp, ty, hh) for hp in range(6) for ty in range(2) for hh in range(2)])

        def vcol(name, i):
            o = V[name + "%d" % li] + i
            return c.vecs[:, o:o + 1]

        def proj(m, n):
            bank, bk = banks.next()
            for k in range(KD):
                P.mm(bank[:, :n], WA[:, m, k, :], XN[:, k, :n], start=(k == 0), stop=(k == KD - 1), r=[("wa", m), ("xn", k)], w=[bk])
            return bank, bk

        def shift_evac(m, bank, bk, out, outkey):
            n = TR
            P.act(TA[:, :n], bank[:, :n], AF.Copy, r=[bk, ("omm",)], w=[("ta",)], scale=OMM[:, m:m + 1])
            P.stt("dve", out[:, 1:n], bank[:, 0:n - 1], c.vecs[:, mu0 + m:mu0 + m + 1], TA[:, 1:n], ALU.mult, ALU.add,
                  r=[bk, ("ta",)], w=[outkey])
            P.stt("dve", out[:, 0:1], CARRY[:, m:m + 1], c.vecs[:, mu0 + m:mu0 + m + 1], TA[:, 0:1], ALU.mult, ALU.add,
                  r=[("carry", m), ("ta",)], w=[outkey + ("c0",)])
            P.copy("dve", CARRY[:, m:m + 1], bank[:, n - 1:n], r=[bk], w=[("carry", m)])

        def do_tile(t):
            n = TR
            P.dma("sp", XT[:], src[:, t * TR:(t + 1) * TR].rearrange("(k p) n -> p k n", p=128), "xl0",
                  r=[("X", src.tensor.name, t)], w=[("xt",)])
            rb, rbk = banks.next()
            rms_tile(P, c, XT, ("xt",), XN, ("xn",), V["mix_norm%d" % layer], n, rb, rbk, SQ, RSTD, "rk")
            b18, k18 = proj(18, n)
            shift_evac(18, b18, k18, P18, ("p18",))
            b19, k19 = proj(19, n)
            shift_evac(19, b19, k19, P19, ("p19",))
            p18k = [("p18",), ("p18", "c0")]
            p19k = [("p19",), ("p19", "c0")]
            P.act(TW[0:64, :], P18[0:64, :], AF.Tanh, r=p18k, w=[("tw",)])
            P.copy("dve", AL[64:128, :], P18[64:128, :], r=p18k, w=[("al",)])
            P.act(SGG[:], P19[:], AF.Sigmoid, r=p19k, w=[("sgg",)])
            for q in range(2):
                bq, kq = proj(20 + q, n)
                P.copy("act", QM[:, q, :], bq[:, :n], r=[kq], w=[("qm", q)])
            if DBG_STOP <= 1:
                return
            for hp in range(6):
                cs = slice(hp * 128, (hp + 1) * 128)
                bz, kz = banks.next()
                P.mm(bz[:, :n], WAUP[0:64, cs], TW[0:64, :], r=[("waup", 0), ("tw",)], w=[kz])
                SW, LOGW, ALR = T[0], T[1], T[2]
                P.act(SW[:], bz[:, :n], AF.Sigmoid, r=[kz], w=[("sw",)], bias=vcol("a_w0", hp))
                P.tsmul("dve", LOGW[:], SW[:], DEC_SCALE, r=[("sw",)], w=[("logw",)])
                bz, kz = banks.next()
                P.mm(bz[:, :n], WAUP[64:128, cs], AL[64:128, :], r=[("waup", 1), ("al",)], w=[kz])
                P.act(ALR[:], bz[:, :n], AF.Sigmoid, r=[kz], w=[("alr",)], bias=vcol("a_a0", hp))
                bz, kz = banks.next()
                P.mm(bz[:, :n], GUP[:, cs], SGG[:], r=[("gup",), ("sgg",)], w=[kz])
                P.copy("act", G[:, hp, :], bz[:, :n], r=[kz], w=[("g", hp)])
                if DBG_STOP <= 1.1:
                    continue
                br, kr = proj(hp, n)
                shift_evac(hp, br, kr, Rf, ("rf",))
                bk_, kk_ = proj(6 + hp, n)
                shift_evac(6 + hp, bk_, kk_, Kf, ("kf",))
                bv, kv = proj(12 + hp, n)
                shift_evac(12 + hp, bv, kv, Vf, ("vf",))
                rfk = [("rf",), ("rf", "c0")]
                kfk = [("kf",), ("kf", "c0")]
                vfk = [("vf",), ("vf", "c0")]
                if DBG_STOP <= 1.2:
                    continue
                KS, NRM, KK, TMv, KM, Bv, L = T[3], T[4], T[5], T[6], T[7], T[8], T[9]
                P.tsmul("dve", KS[:], Kf[:], vcol("a_kk_scale", hp), r=kfk, w=[("ks",)])
                P.act(HB[0][:], KS[:], AF.Square, r=[("ks",)], w=[("hb", 0)])
                bz, kz = banks.next()
                P.mm(bz[:, :n], c.bd_b[:], HB[0][:], r=[("hb", 0)], w=[kz])
                P.act(NRM[:], bz[:, :n], AF.Sqrt, r=[kz], w=[("nrm",)])
                P.add("dve", lambda e: e.tensor_scalar_max(NRM[:], NRM[:], 1e-12), r=[("nrm",)], w=[("nrm",)])
                P.add("dve", lambda e: e.reciprocal(NRM[:], NRM[:]), r=[("nrm",)], w=[("nrm",)])
                P.tt("dve", KK[:], KS[:], NRM[:], ALU.mult, r=[("ks",), ("nrm",)], w=[("kk",)])
                P.ts("dve", TMv[:], ALR[:], vcol("a_k_a", hp), OMKA[:, hp:hp + 1], ALU.mult, ALU.add, r=[("alr",), ("omka",)], w=[("tmv",)])
                P.tt("dve", KM[:], Kf[:], TMv[:], ALU.mult, r=kfk + [("tmv",)], w=[("km",)])
                P.tt("dve", Bv[:], KK[:], ALR[:], ALU.mult, r=[("kk",), ("alr",)], w=[("bv",)])
                P.stt("dve", HB[1][:], Rf[:], vcol("a_r_k", hp), KM[:], ALU.mult, ALU.mult, r=rfk + [("km",)], w=[("hb", 1)])
                bz, kz = banks.next()
                P.mm(bz[:, :n], c.bd_b[:], HB[1][:], r=[("hb", 1)], w=[kz])
                P.tt("dve", BON[:, hp, :], bz[:, :n], Vf[:], ALU.mult, r=[kz] + vfk, w=[("bon", hp)])
                if DBG_STOP <= 1.3:
                    continue
                P.add("dve", lambda e, L=L, LOGW=LOGW: e.tensor_tensor_scan(L[:], c.cmask[:, :n], LOGW[:], 0.0, ALU.mult, ALU.add),
                      r=[("logw",)], w=[("L",)])
                E1 = T[0]
                P.act(E1[:], L[:], AF.Exp, r=[("L",)], w=[("sw",)])
                P.tt("dve", RT[:, hp, :], Rf[:], E1[:], ALU.mult, r=rfk + [("sw",)], w=[("rt", hp)])
                E2 = T[3]
                P.act(E2[:], L[:], AF.Exp, r=[("L",), ("kk",)], w=[("ks",)], scale=-1.0)
                P.tt("dve", KT[:, hp, :], KM[:], E2[:], ALU.mult, r=[("km",), ("ks",)], w=[("kt", hp)])
                P.tt("dve", BT[:, hp, :], Bv[:], E2[:], ALU.mult, r=[("bv",), ("ks",)], w=[("bt", hp)])
                LX = T[4]
                P.tt("dve", LX[:], L[:], LOGW[:], ALU.subtract, r=[("L",), ("logw",), ("kk",)], w=[("nrm",)])
                P.act(LX[:], LX[:], AF.Exp, r=[("nrm",)], w=[("nrm",)])
                P.stt("dve", AT[:, hp, :], KK[:], -1.0, LX[:], ALU.mult, ALU.mult, r=[("kk",), ("nrm",)], w=[("at", hp)])
                DEC = T[6]
                for cc in range(NCH):
                    ce = (cc + 1) * CH - 1
                    P.act(DEC[:, cc * CH:(cc + 1) * CH], L[:, cc * CH:(cc + 1) * CH], AF.Exp, r=[("L",), ("km",)], w=[("tmv",)],
                          bias=L[:, ce:ce + 1], scale=-1.0)
                    P.act(PC[:, hp, cc:cc + 1], L[:, ce:ce + 1], AF.Exp, r=[("L",)], w=[("pc", cc)])
                P.tt("dve", HB[2][:], KM[:], DEC[:], ALU.mult, r=[("km",), ("tmv",)], w=[("hb", 2)])
                P.tt("dve", HB[3][:], Bv[:], DEC[:], ALU.mult, r=[("bv",), ("tmv",)], w=[("hb", 3)])
                P.copy("act", HB[0][:], Vf[:], r=vfk, w=[("hb", 0)])
                if DBG_STOP <= 1.4:
                    continue
                for cc in range(NCH):
                    tb, tk = tbanks.next()
                    csl = slice(cc * CH, (cc + 1) * CH)
                    P.tr(tb[:, 0:128], HB[0][:, csl], c.ident_b[:], r=[("hb", 0)], w=[tk])
                    P.tr(tb[:, 128:256], HB[2][:, csl], c.ident_b[:], r=[("hb", 2)], w=[tk])
                    P.tr(tb[:, 256:384], HB[3][:, csl], c.ident_b[:], r=[("hb", 3)], w=[tk])
                    P.tr(tb[:, 384:512], AT[:, hp, csl], c.ident_b[:], r=[("at", hp)], w=[tk])
                    if DBG_STOP <= 1.45:
                        continue
                    evq = "act" if (hp * NCH + cc) % 2 == 0 else "dve"
                    P.copy(evq, TMV[cc][:, hp, :], tb[:, 0:128], r=[tk], w=[("tmv", cc, hp)])
                    if DBG_STOP <= 1.46:
                        continue
                    for ty in range(2):
                        for hh in range(2):
                            P.copy(evq, TMZ[cc][:, ty, 2 * hp + hh, hh * 64:(hh + 1) * 64],
                                   tb[:, 128 + ty * 128 + hh * 64:128 + ty * 128 + (hh + 1) * 64], r=[tk], w=[("tmz", cc, hp, ty, hh)])
                    if DBG_STOP <= 1.47:
                        continue
                    P.copy(evq, W[2 * cc][:, 0, hp * 128:(hp + 1) * 128], tb[:, 384:512], r=[tk], w=[("w", 2 * cc, hp)])
            if DBG_STOP <= 2:
                return
            def amat(cc, lhs, lkey, rhs, rkey, mask, dest, dkey):
                csl = slice(cc * CH, (cc + 1) * CH)
                for g in range(4):
                    heads = HG[g]
                    bank, bk = banks.next()
                    for hi, h in enumerate(heads):
                        hp, off = h // 2, (h % 2) * 64
                        P.mm(bank[:, hi * 128:(hi + 1) * 128], lhs[off:off + 64, hp, csl], rhs[off:off + 64, hp, csl],
                             r=[(lkey, hp), (rkey, hp)], w=[bk])
                    nh = len(heads)
                    P.tt("dve", dest[:, heads[0]:heads[-1] + 1:2, :], bank[:, 0:nh * 128].rearrange("p (a b) -> p a b", a=nh),
                         mask[:].unsqueeze(1).to_broadcast([128, nh, 128]), ALU.mult, r=[bk], w=[dkey + (g,)])

            wst = {}
            for cc in range(NCH):
                amat(cc, BT, "bt", AT, "at", c.msu, BK[2 * cc], ("bk", cc, 0))
                amat(cc, AT, "at", BT, "bt", c.msl, AK[2 * cc], ("ak", cc, 0))
                amat(cc, KT, "kt", AT, "at", c.msu, AAK, ("aak",))
                for h0, nh in ((0, 8), (8, 4)):
                    bank, bk = banks.next()
                    for hi in range(nh):
                        h = h0 + hi
                        hp, off = h // 2, (h % 2) * 64
                        P.mm(bank[:, hi * 64:(hi + 1) * 64], AAK[:, h, :], TMV[cc][:, hp, off:off + 64],
                             r=[("aak", hgk(h)), ("tmv", cc, hp)], w=[bk])
                    P.copy("act", W[2 * cc][:, 1, h0 * 64:(h0 + nh) * 64], bank[:, 0:nh * 64], r=[bk], w=[("wu", 2 * cc, h0)])
                wst[cc] = [2 * cc, 2 * cc + 1, [("w", 2 * cc, hp) for hp in range(6)] + [("wu", 2 * cc, 0), ("wu", 2 * cc, 8)]]
            for lev in range(7):
                ai, ao = lev % 2, (lev + 1) % 2
                for cc in range(NCH):
                    wcur, wnxt, wk = wst[cc]
                    BKi, AKi, BKo, AKo = BK[2 * cc + ai], AK[2 * cc + ai], BK[2 * cc + ao], AK[2 * cc + ao]
                    for hg in range(3):
                        bank, bk = banks.next()
                        for hi in range(4):
                            h = hg * 4 + hi
                            for part in range(2):
                                o = bank[:, part * 256 + hi * 64:part * 256 + (hi + 1) * 64]
                                rhs = W[wcur][:, part, h * 64:(h + 1) * 64]
                                P.mm(o, c.ident_b[:], rhs, start=True, stop=False, r=wk, w=[bk])
                                P.mm(o, BKi[:, h, :], rhs, start=False, stop=True, r=[("bk", cc, ai, hgk(h))], w=[bk])
                        P.copy("act" if hg % 2 == 0 else "dve", W[wnxt][:, :, hg * 256:(hg + 1) * 256],
                               bank[:, :].rearrange("p (a b) -> p a b", a=2), r=[bk], w=[("wl", wnxt, hg)])
                    if lev < 6:
                        for g in range(4):
                            heads = HG[g]
                            nh = len(heads)
                            bank, bk = banks.next()
                            for hi, h in enumerate(heads):
                                P.mm(bank[:, hi * 128:(hi + 1) * 128], BKi[:, h, :], AKi[:, h, :],
                                     r=[("bk", cc, ai, g), ("ak", cc, ai, g)], w=[bk])
                            P.copy("dve" if g % 2 == 0 else "act", AKo[:, heads[0]:heads[-1] + 1:2, :],
                                   bank[:, 0:nh * 128].rearrange("p (a b) -> p a b", a=nh), r=[bk], w=[("ak", cc, ao, g)])
                            bank, bk = banks.next()
                            for hi, h in enumerate(heads):
                                P.mm(bank[:, hi * 128:(hi + 1) * 128], AKi[:, h, :], BKi[:, h, :],
                                     r=[("bk", cc, ai, g), ("ak", cc, ai, g)], w=[bk])
                            P.copy("act" if g % 2 == 0 else "dve", BKo[:, heads[0]:heads[-1] + 1:2, :],
                                   bank[:, 0:nh * 128].rearrange("p (a b) -> p a b", a=nh), r=[bk], w=[("bk", cc, ao, g)])
                    wst[cc] = [wnxt, wcur, [("wl", wnxt, hg) for hg in range(3)]]
            for cc in range(NCH):
                csl = slice(cc * CH, (cc + 1) * CH)
                wf, _, wfk = wst[cc]
                amat(cc, KT, "kt", RT, "rt", c.mui, ARK, ("ark",))
                amat(cc, BT, "bt", RT, "rt", c.mui, ARB, ("arb",))
                for p0, npp in ((0, 4), (4, 2)):
                    tb, tk = tbanks.next()
                    for pi in range(npp):
                        hp = p0 + pi
                        P.tr(tb[:, pi * 128:(pi + 1) * 128], W[wf][:, 0, hp * 128:(hp + 1) * 128], c.ident_b[:], r=wfk, w=[tk])
                    P.copy("dve", AHT[:, p0:p0 + npp, :], tb[:, 0:npp * 128].rearrange("p (a b) -> p a b", a=npp), r=[tk], w=[("aht", p0)])
                ahk = [("aht", 0), ("aht", 4)]
                for g in range(4):
                    heads = HG[g]
                    nh = len(heads)
                    bank, bk = banks.next()
                    for hi, h in enumerate(heads):
                        hp, off = h // 2, (h % 2) * 64
                        P.mm(bank[:, hi * 64:(hi + 1) * 64], AHT[off:off + 64, hp, :], SB[off:off + 64, hp, :], start=True, stop=False,
                             r=ahk + [("sbk",)], w=[bk])
                        P.mm(bank[:, hi * 64:(hi + 1) * 64], c.ident_b[:], W[wf][:, 1, h * 64:(h + 1) * 64], start=False, stop=True, r=wfk, w=[bk])
                    P.copy("act", UU[:, heads[0]:heads[-1] + 1:2, :], bank[:, 0:nh * 64].rearrange("p (a b) -> p a b", a=nh), r=[bk], w=[("uu", g)])
                uuk = [("uu", g) for g in range(4)]
                for p0, npp in ((0, 4), (4, 2)):
                    for hh in range(2):
                        off = hh * 64
                        bank, bk = banks.next()
                        for pi in range(npp):
                            hp = p0 + pi
                            h = 2 * hp + hh
                            o = bank[off:off + 64, pi * 128:(pi + 1) * 128]
                            P.mm(o, SB[off:off + 64, hp, :], RT[off:off + 64, hp, csl], start=True, stop=False, r=[("sbk",), ("rt", hp)], w=[bk])
                            P.mm(o, TMV[cc][:, hp, off:off + 64], ARK[:, h, :], start=False, stop=False, r=[("tmv", cc, hp), ("ark", hgk(h))], w=[bk])
                            P.mm(o, UU[:, h, :], ARB[:, h, :], start=False, stop=True, r=uuk + [("arb", hgk(h))], w=[bk])
                        P.copy("act", YB[off:off + 64, p0:p0 + npp, csl], bank[off:off + 64, 0:npp * 128].rearrange("p (a b) -> p a b", a=npp),
                               r=[bk], w=[("yb", cc, p0, hh)])
                bank, bk = banks.next()
                for hp in range(6):
                    o = bank[:, hp * 64:(hp + 1) * 64]
                    for hh in range(2):
                        off = hh * 64
                        h = 2 * hp + hh
                        P.mm(o, TMZ[cc][:, 0, h, :], TMV[cc][:, hp, off:off + 64], start=(hh == 0), stop=False,
                             r=[("tmz", cc, hp, 0, hh), ("tmz", cc, hp, 1, hh), ("tmv", cc, hp)], w=[bk])
                        P.mm(o, TMZ[cc][:, 1, h, :], UU[:, h, :], start=False, stop=(hh == 1), r=uuk, w=[bk])
                P.tt("dve", SF[:], SF[:], PC[:, :, cc:cc + 1].to_broadcast([128, 6, 64]), ALU.mult, r=[("pc", cc), ("sf",)], w=[("sf",)])
                P.tt("dve", SF[:], SF[:], bank[:, 0:384].rearrange("p (a b) -> p a b", a=6), ALU.add, r=[bk, ("sf",)], w=[("sf",)])
                P.copy("act", SB[:], SF[:], r=[("sf",)], w=[("sbk",)])
            if DBG_STOP <= 3:
                return
            ybk = [("yb", cc, p0, hh) for cc in range(NCH) for p0 in (0, 4) for hh in range(2)]
            for hp in range(6):
                YC, SQf, SD = T[0], T[1], T[2]
                bank, bk = banks.next()
                P.mm(bank[:, :n], c.bda_f[:], YB[:, hp, :], r=ybk, w=[bk])
                P.tt("dve", YC[:], YB[:, hp, :], bank[:, :n], ALU.subtract, r=ybk + [bk], w=[("sw",)])
                P.act(SQf[:], YC[:], AF.Square, r=[("sw",)], w=[("logw",)])
                bank, bk = banks.next()
                P.mm(bank[:, :n], c.bda_f[:], SQf[:], r=[("logw",)], w=[bk])
                P.act(SD[:], bank[:, :n], AF.Sqrt, r=[bk], w=[("alr",)], bias=c.eps[:, 1:2])
                P.add("dve", lambda e, SD=SD: e.reciprocal(SD[:], SD[:]), r=[("alr",)], w=[("alr",)])
                P.tt("dve", YC[:], YC[:], SD[:], ALU.mult, r=[("sw",), ("alr",)], w=[("sw",)])
                P.ts("dve", YC[:], YC[:], vcol("a_lnx_g", hp), vcol("a_lnx_b", hp), ALU.mult, ALU.add, r=[("sw",)], w=[("sw",)])
                P.tt("dve", YC[:], YC[:], BON[:, hp, :], ALU.add, r=[("sw",), ("bon", hp)], w=[("sw",)])
                P.tt("dve", YO[:, hp, :], YC[:], G[:, hp, :], ALU.mult, r=[("sw",), ("g", hp)], w=[("yo", hp)])
            if DBG_STOP <= 4:
                return
            mem_attn_tile(P, c, QM, [("qm", 0), ("qm", 1)], YM, [("ym", 0), ("ym", 1)], n, layer, V, banks, MSQ, MTMP, QN, PT, RD, "rma")
            if dbg is not None:
                P.copy("dve", YB[:], YO[:], r=[("yo", hp) for hp in range(6)], w=ybk)
                P.dma("sp", dbg[0:768, t * TR:(t + 1) * TR].rearrange("(k p) n -> p k n", p=128), YB[:], "dbg0", r=ybk)
                P.copy("dve", QM[:], YM[:], r=[("ym", 0), ("ym", 1)], w=[("qm", 0), ("qm", 1)])
                P.dma("sp", dbg[768:1024, t * TR:(t + 1) * TR].rearrange("(k p) n -> p k n", p=128), QM[:], "dbg1", r=[("qm", 0), ("qm", 1)])
            for m in range(KD):
                bank, bk = banks.next()
                for k in range(KD):
                    rhs = YO[:, k, :] if k < 6 else YM[:, k - 6, :]
                    rk = ("yo", k) if k < 6 else ("ym", k - 6)
                    P.mm(bank[:, :n], WO[:, k, m * 128:(m + 1) * 128], rhs, start=(k == 0), stop=(k == KD - 1), r=[("wo", k), rk], w=[bk])
                P.tt("dve", XT[:, m, :], XT[:, m, :], bank[:, :n], ALU.add, r=[bk, ("xt",)], w=[("xt",)])
            P.dma("sp", dst[:, t * TR:(t + 1) * TR].rearrange("(k p) n -> p k n", p=128), XT[:], "xs0",
                  r=[("xt",)], w=[("X", dst.tensor.name, t)])

        for t in range(min(NTR, DBG_NTR)):
            do_tile(t)
        P.barrier()


DIL = (1, 4, 16)
MASKV = -30000.0


def flat_head_rms(P, c, bank, bk, n, KFt, SQt, RS, banks, tag, pb):
    P.copy("act", KFt[pb][0:64, :n], bank[0:64, :n], r=[bk], w=[(tag, "kf", pb)])
    P.act(SQt[pb][0:64, :n], KFt[pb][0:64, :n], AF.Square, r=[(tag, "kf", pb)], w=[(tag, "sq", pb)])
    b2, k2 = banks.next()
    P.mm(b2[0:64, :n], c.ones_b[0:64, 0:64], SQt[pb][0:64, :n], r=[(tag, "sq", pb)], w=[k2])
    P.act(RS[pb][0:64, :n], b2[0:64, :n], AF.Sqrt, r=[k2], w=[(tag, "rs", pb)], bias=c.eps[0:64, 0:1], scale=1.0 / 64)
    P.add("dve", lambda e: e.reciprocal(RS[pb][0:64, :n], RS[pb][0:64, :n]), r=[(tag, "rs", pb)], w=[(tag, "rs", pb)])


def kv_phase(P, nc, c, src, kvw_d, KT_d, V_d, V):
    with ExitStack() as es:
        def sb(name, shape, dt):
            return es.enter_context(nc.sbuf_tensor(_u("kv_" + name), shape, dt))
        WKV = sb("w", [128, KD, 1536], BF16)
        XTL = [sb("xt%d" % i, [128, KD, TT], F32) for i in range(2)]
        XN = sb("xn", [128, KD, TT], BF16)
        SQ = [sb("sq%d" % i, [128, TT], BF16) for i in range(2)]
        RSTD = sb("rstd", [128, TT], F32)
        KFt = [sb("kf%d" % i, [64, TT], F32) for i in range(2)]
        SQt = [sb("sqt%d" % i, [64, TT], BF16) for i in range(2)]
        RS = [sb("rs%d" % i, [64, TT], F32) for i in range(2)]
        KTA = sb("kta", [64, 12, S], BF16)
        VT = [sb("vt%d" % i, [128, 768], BF16) for i in range(2)]
        pbt = [es.enter_context(nc.psum_tensor(_u("kv_pb%d" % i), [128, 512], F32)) for i in range(8)]
        banks = Banks(pbt, "kpb")
        for k0 in range(0, KD, 2):
            P.dma("pool", WKV[:, k0:k0 + 2], kvw_d[k0 * 128:(k0 + 2) * 128, :].rearrange("(k p) c -> p k c", p=128), "wq%d" % (k0 // 2),
                  w=[("kvw", k) for k in range(k0, k0 + 2)])
        gk = V["kv_k_norm"]

        def kload(t):
            P.dma("sp", XTL[t % 2][:], src[:, t * TT:(t + 1) * TT].rearrange("(k p) n -> p k n", p=128), "xl%d" % (t % 2), w=[("xt", t % 2)])
        kload(0)
        for t in range(NT):
            n = TT
            if t + 1 < NT:
                kload(t + 1)
            XT = XTL[t % 2]
            rb, rbk = banks.next()
            rms_tile(P, c, XT, ("xt", t % 2), XN, ("xn",), V["kv_norm"], n, rb, rbk, SQ, RSTD, "kv")
            for h in range(12):
                dil = DIL[h // 4]
                pb = h % 2
                bank, bk = banks.next()
                for k in range(KD):
                    P.mm(bank[0:64, :n], WKV[:, k, h * 64:(h + 1) * 64], XN[:, k, :], start=(k == 0), stop=(k == KD - 1),
                         r=[("kvw", k), ("xn", k)], w=[bk])
                flat_head_rms(P, c, bank, bk, n, KFt, SQt, RS, banks, "kvh", pb)
                dst = KTA[0:64, h, :].rearrange("p (c i) -> p c i", c=dil)[:, :, t * (TT // dil):(t + 1) * (TT // dil)]
                P.stt("dve", dst, KFt[pb][0:64, :n].rearrange("p (i c) -> p c i", c=dil), c.vecs[0:64, gk:gk + 1],
                      RS[pb][0:64, :n].rearrange("p (i c) -> p c i", c=dil), ALU.mult, ALU.mult,
                      r=[("kvh", "kf", pb), ("kvh", "rs", pb)], w=[("kta", h, t)])
            for s4 in range(TT // 128):
                vb = VT[s4 % 2]
                b1, k1 = banks.next()
                for k in range(KD):
                    P.mm(b1[:, 0:512], XN[:, k, s4 * 128:(s4 + 1) * 128], WKV[:, k, 768:1280], start=(k == 0), stop=(k == KD - 1),
                         r=[("kvw", k), ("xn", k)], w=[k1])
                P.copy("act", vb[:, 0:512], b1[:, 0:512], r=[k1], w=[("vt", s4 % 2, 0)])
                b2, k2 = banks.next()
                for k in range(KD):
                    P.mm(b2[:, 0:256], XN[:, k, s4 * 128:(s4 + 1) * 128], WKV[:, k, 1280:1536], start=(k == 0), stop=(k == KD - 1),
                         r=[("kvw", k), ("xn", k)], w=[k2])
                P.copy("dve", vb[:, 512:768], b2[:, 0:256], r=[k2], w=[("vt", s4 % 2, 1)])
                P.dma("sp", V_d[t * TT + s4 * 128:t * TT + (s4 + 1) * 128, :], vb[:], "vs%d" % (s4 % 2),
                      r=[("vt", s4 % 2, 0), ("vt", s4 % 2, 1)])
        for h in range(12):
            P.dma("sp", KT_d[:, h, :], KTA[0:64, h, :], "ks%d" % (h % 2), r=[("kta", h, t) for t in range(NT)])
        P.barrier()


def attn_phase(P, nc, c, src, dst, j, layer, wq_d, wo_d, KT_d, V_d, relb_d, sel_d, E_d, V):
    HALF = S // 2
    NTH = HALF // TR
    with ExitStack() as es:
        def sb(name, shape, dt):
            return es.enter_context(nc.sbuf_tensor(_u("at_" + name), shape, dt))
        WQ = sb("wq", [128, KD, D], BF16)
        WO = sb("wo", [128, 4, D], BF16)
        KTG = sb("ktg", [64, 4, S], BF16)
        VZ = sb("vz", [128, 32, 4, 128], BF16)
        QG = sb("qg", [64, 4, HALF], BF16)
        ACN = sb("acn", [128, 2, HALF], F32)
        ACD = sb("acd", [128, 2, HALF], F32)
        XTL = [sb("xt%d" % i, [128, KD, TR], F32) for i in range(2)]
        XN = sb("xn", [128, KD, TR], BF16)
        SQ = [sb("sq%d" % i, [128, TR], BF16) for i in range(2)]
        RSTD = sb("rstd", [128, TR], F32)
        KFt = [sb("kf%d" % i, [64, TR], F32) for i in range(2)]
        SQt = [sb("sqt%d" % i, [64, TR], BF16) for i in range(2)]
        RS = [sb("rs%d" % i, [64, TR], F32) for i in range(2)]
        BM = sb("bm", [128, 12, 256], F32)
        TAB = sb("tab", [33, 12], F32)
        SEL = sb("sel", [33, 3, 510], F32)
        ESB = sb("esb", [12, 510], F32)
        HK = [sb("hk%d" % i, [128, 128], F32) for i in range(2)]
        LG = [sb("lg%d" % i, [128, 256], F32) for i in range(2)]
        PTb = [sb("ptb%d" % i, [128, 256], BF16) for i in range(4)]
        OB = sb("ob", [128, 2, TR], BF16)
        QM = sb("qm", [128, 2, TR], F32)
        YM = sb("ym", [128, 2, TR], BF16)
        QN = sb("qn", [128, 2, TR], BF16)
        PT = [sb("pt%d" % i, [128, TR], BF16) for i in range(4)]
        RD = sb("rd", [128, TR], F32)
        MSQ = sb("msq", [128, TR], BF16)
        MTMP = sb("mtmp", [128, TR], F32)
        pbt = [es.enter_context(nc.psum_tensor(_u("at_pb%d" % i), [128, 512], F32)) for i in range(8)]
        banks = Banks(pbt, "apb")
        print("[kernel] attn sbuf free", nc.sbuf_bytes_remaining)
        xcnt = [0]

        def xload(t):
            b = xcnt[0] % 2
            xcnt[0] += 1
            P.dma("sp", XTL[b][:], src[:, t * TR:(t + 1) * TR].rearrange("(k p) n -> p k n", p=128), "xl%d" % b,
                  r=[("X", src.tensor.name, t)], w=[("xt", b)])
            return b
        for k0 in range(0, KD, 2):
            P.dma("pool", WQ[:, k0:k0 + 2], wq_d[k0 * 128:(k0 + 2) * 128, :].rearrange("(k p) c -> p k c", p=128), "wq%d" % (k0 // 2),
                  w=[("wq", k) for k in range(k0, k0 + 2)])
        P.dma("pool", WO[:], wo_d.rearrange("(k p) c -> p k c", p=128), "wq0", w=[("wo",)])
        P.add("dve", lambda e: e.memset(TAB[:], MASKV), w=[("tab",)])
        P.dma("sp", TAB[0:32, :], relb_d[:, :], "xl1", w=[("tab", 1)], r=[("tab",)])
        P.dma("sp", SEL[:], sel_d.rearrange("g b n -> b g n"), "xs1", w=[("sel",)])
        for g in range(3):
            bank, bk = banks.next()
            P.mm(bank[0:12, 0:510], TAB[:, :], SEL[:, g, :], r=[("tab",), ("tab", 1), ("sel",)], w=[bk])
            P.copy("act", ESB[:], bank[0:12, 0:510], r=[bk], w=[("esb",)])
            P.dma("sp", E_d[g], ESB[:], "vs0", r=[("esb",)], w=[("E", g)])
            for h in range(4):
                for role in range(2):
                    i = (h * 2 + role) % 2
                    srcap = bass.AP(tensor=E_d.tensor, offset=g * 12 * 510 + (4 * g + h) * 510 + role * 255, ap=[[1, 128], [1, 128]])
                    P.dma("sp", HK[i][:], srcap, "xl%d" % i, r=[("E", g)], w=[("hk", i)])
                    bank, bk = banks.next()
                    P.mm(bank[:, 0:128], c.jf[:], HK[i][:], r=[("hk", i)], w=[bk])
                    P.copy("act", BM[:, 4 * g + h, role * 128:(role + 1) * 128], bank[:, 0:128], r=[bk], w=[("bm", g, h, role)])
        P.add("dve", lambda e: e.memset(VZ[:], 0.0), w=[("vz", 0), ("vz", 1)])
        gq = V["b_q_norm%d" % j]
        for H in range(2):
            P.add("dve", lambda e: e.memset(ACN[:], 0.0), w=[("acn",)])
            P.add("dve", lambda e: e.memset(ACD[:], 0.0), w=[("acd",)])
            for g in range(3):
                dil = DIL[g]
                nb = S // (dil * 128)
                nbh = nb // 2
                SL = S // dil
                HL = HALF // dil
                P.dma("sp", KTG[:], KT_d[:, 4 * g:4 * g + 4, :], "xl0", w=[("ktg",)])
                for h in range(4):
                    hh = h % 2
                    vsrc = bass.AP(tensor=V_d.tensor, offset=(4 * g + h) * 64,
                                   ap=[[dil * 768, 128], [768, dil], [128 * dil * 768, nb], [1, 64]])
                    P.dma("sp" if h % 2 == 0 else "act", VZ[:, 0:dil * nb, h, hh * 64:(hh + 1) * 64].rearrange("p (c n) d -> p c n d", c=dil),
                          vsrc, "vl%d" % h, w=[("vzh", h)], r=[("vz", 0)])
                vzk = [("vzh", h) for h in range(4)]
                nxt = xload(H * NTH)
                for tt in range(NTH):
                    t = H * NTH + tt
                    n = TR
                    b = nxt
                    if tt + 1 < NTH:
                        nxt = xload(t + 1)
                    XT = XTL[b]
                    rb, rbk = banks.next()
                    rms_tile(P, c, XT, ("xt", b), XN, ("xn",), V["mix_norm%d" % layer], n, rb, rbk, SQ, RSTD, "at")
                    for h in range(4):
                        hq = 4 * g + h
                        pb = h % 2
                        bank, bk = banks.next()
                        for k in range(KD):
                            P.mm(bank[0:64, :n], WQ[:, k, hq * 64:(hq + 1) * 64], XN[:, k, :], start=(k == 0), stop=(k == KD - 1),
                                 r=[("wq", k), ("xn", k)], w=[bk])
                        flat_head_rms(P, c, bank, bk, n, KFt, SQt, RS, banks, "ath", pb)
                        dq = QG[0:64, h, :].rearrange("p (c i) -> p c i", c=dil)[:, :, tt * (TR // dil):(tt + 1) * (TR // dil)]
                        P.stt("dve", dq, KFt[pb][0:64, :n].rearrange("p (i c) -> p c i", c=dil), c.vecs[0:64, gq:gq + 1],
                              RS[pb][0:64, :n].rearrange("p (i c) -> p c i", c=dil), ALU.mult, ALU.mult,
                              r=[("ath", "kf", pb), ("ath", "rs", pb)], w=[("qg", h, tt)])
                qgk = [("qg", h, tt) for h in range(4) for tt in range(NTH)]
                for cidx in range(dil):
                    for nl in range(nbh):
                        nblk = H * nbh + nl
                        qcol = cidx * HL + nl * 128
                        for hp in range(2):
                            pts = []
                            for hh in range(2):
                                h = 2 * hp + hh
                                bank, bk = banks.next()
                                kcol = cidx * SL + nblk * 128
                                P.mm(bank[:, 0:128], KTG[0:64, h, kcol:kcol + 128], QG[0:64, h, qcol:qcol + 128], r=[("ktg",)] + qgk, w=[bk])
                                wcols = 128
                                if nblk > 0:
                                    P.mm(bank[:, 128:256], KTG[0:64, h, kcol - 128:kcol], QG[0:64, h, qcol:qcol + 128], r=[("ktg",)] + qgk, w=[bk])
                                    wcols = 256
                                li = (hp * 2 + hh) % 2
                                P.stt("dve", LG[li][:, 0:wcols], bank[:, 0:wcols], 0.125, BM[:, 4 * g + h, 0:wcols], ALU.mult, ALU.add,
                                      r=[bk] + [("bm", g, h, r_) for r_ in range(2)], w=[("lg", li)])
                                pi = hp * 2 + hh
                                P.act(PTb[pi][:, 0:wcols], LG[li][:, 0:wcols], AF.Exp, r=[("lg", li)], w=[("ptb", pi)])
                                pts.append((pi, h, hh, wcols))
                            bn, kn = banks.next()
                            bd, kd = banks.next()
                            mms = []
                            for (pi, h, hh, wcols) in pts:
                                mms.append((VZ[:, cidx * nb + nblk, h, :], c.onesh[hh], PTb[pi][:, 0:128], pi, h))
                                if wcols == 256:
                                    mms.append((VZ[:, cidx * nb + nblk - 1, h, :], c.onesh[hh], PTb[pi][:, 128:256], pi, h))
                            for i, (vz, oh, rhs, pi, h) in enumerate(mms):
                                P.mm(bn[:, 0:128], vz, rhs, start=(i == 0), stop=(i == len(mms) - 1), r=[("ptb", pi), ("vzh", h)], w=[kn])
                            for i, (vz, oh, rhs, pi, h) in enumerate(mms):
                                P.mm(bd[:, 0:128], oh, rhs, start=(i == 0), stop=(i == len(mms) - 1), r=[("ptb", pi)], w=[kd])
                            an = ACN[:, hp, :].rearrange("p (i c) -> p c i", c=dil)[:, cidx, nl * 128:(nl + 1) * 128]
                            ad = ACD[:, hp, :].rearrange("p (i c) -> p c i", c=dil)[:, cidx, nl * 128:(nl + 1) * 128]
                            P.tt("dve", an, an, bn[:, 0:128], ALU.add, r=[kn, ("acn",)], w=[("acn",)])
                            P.tt("act" if False else "dve", ad, ad, bd[:, 0:128], ALU.add, r=[kd, ("acd",)], w=[("acd",)])
            nxt = xload(H * NTH)
            for tt in range(NTH):
                t = H * NTH + tt
                n = TR
                b = nxt
                if tt + 1 < NTH:
                    nxt = xload(t + 1)
                XT = XTL[b]
                xk = ("xt", b)
                tsl = slice(tt * TR, (tt + 1) * TR)
                P.add("dve", lambda e, tsl=tsl: e.reciprocal(ACD[:, :, tsl], ACD[:, :, tsl]), r=[("acd",)], w=[("acd",)])
                P.tt("dve", OB[:], ACN[:, :, tsl], ACD[:, :, tsl], ALU.mult, r=[("acn",), ("acd",)], w=[("ob",)])
                rb, rbk = banks.next()
                rms_tile(P, c, XT, xk, XN, ("xn",), V["mix_norm%d" % layer], n, rb, rbk, SQ, RSTD, "at")
                for q in range(2):
                    bq, kq = banks.next()
                    for k in range(KD):
                        P.mm(bq[:, :n], WQ[:, k, 768 + q * 128:768 + (q + 1) * 128], XN[:, k, :], start=(k == 0), stop=(k == KD - 1),
                             r=[("wq", k), ("xn", k)], w=[kq])
                    P.copy("act", QM[:, q, :], bq[:, :n], r=[kq], w=[("qm", q)])
                mem_attn_tile(P, c, QM, [("qm", 0), ("qm", 1)], YM, [("ym", 0), ("ym", 1)], n, layer, V, banks, MSQ, MTMP, QN, PT, RD, "ama")
                for m in range(KD):
                    bank, bk = banks.next()
                    for k in range(4):
                        rhs = OB[:, k, :] if k < 2 else YM[:, k - 2, :]
                        rk = ("ob",) if k < 2 else ("ym", k - 2)
                        P.mm(bank[:, :n], WO[:, k, m * 128:(m + 1) * 128], rhs, start=(k == 0), stop=(k == 3), r=[("wo",), rk], w=[bk])
                    P.tt("dve", XT[:, m, :], XT[:, m, :], bank[:, :n], ALU.add, r=[bk, xk], w=[xk])
                P.dma("sp", dst[:, t * TR:(t + 1) * TR].rearrange("(k p) n -> p k n", p=128), XT[:], "xs%d" % b,
                      r=[xk], w=[("X", dst.tensor.name, t)])
        P.barrier()


def vec_layout():
    V = {}
    off = 0

    def put(name, n):
        nonlocal off
        V[name] = off
        off += n
    for i in range(8):
        put("ffn%d" % i, KD)
    for l in range(4):
        put("mix_norm%d" % l, KD)
        put("mem_norm%d" % l, KD)
        put("mem_q_norm%d" % l, 1)
        put("mem_k_norm%d" % l, 1)
    for i in range(2):
        put("a_mu%d" % i, 20)
        for nm in ("a_w0", "a_a0", "a_kk_scale", "a_k_a", "a_r_k", "a_lnx_g", "a_lnx_b"):
            put(nm + "%d" % i, 6)
    for j in range(2):
        put("b_q_norm%d" % j, 1)
    put("kv_norm", KD)
    put("kv_k_norm", 1)
    return V, off


NCONST = 128 * 8 + 256
NKF = NCONST - 384


def build(n_stages=99, dbg_on=False):
    nc = bass.Bass("TRN2", target_bir_lowering=False)
    es = ExitStack()
    c = Ctx()
    V, NV = vec_layout()
    xT = nc.dram_tensor("xT", [D, S], F32, kind="ExternalInput").ap()
    memT = nc.dram_tensor("memT", [D, 256], F32, kind="ExternalInput").ap()
    outT = nc.dram_tensor("outT", [D, S], F32, kind="ExternalOutput").ap()
    dbg = nc.dram_tensor("dbg", [D, S], F32, kind="ExternalOutput").ap() if dbg_on else None
    XS = nc.dram_tensor("xs_scratch", [D, S], F32, kind="Internal").ap()
    vecs_d = nc.dram_tensor("vecs", [128, NV], F32, kind="ExternalInput").ap()
    consts_d = nc.dram_tensor("consts", [128, NCONST + 256], F32, kind="ExternalInput").ap()
    w_in_d = [nc.dram_tensor("w_in%d" % i, [2 * KF, 128, KD, 128], F32, kind="ExternalInput").ap() for i in range(8)]
    w_out_d = [nc.dram_tensor("w_out%d" % i, [FF, D], F32, kind="ExternalInput").ap() for i in range(8)]
    a_w_in_d = [nc.dram_tensor("a_w_in%d" % i, [22, 128, KD, 128], F32, kind="ExternalInput").ap() for i in range(2)]
    a_w_out_d = [nc.dram_tensor("a_w_out%d" % i, [D, D], F32, kind="ExternalInput").ap() for i in range(2)]
    a_wup_d = [nc.dram_tensor("a_w_up%d" % i, [64, 768], F32, kind="ExternalInput").ap() for i in range(2)]
    a_aup_d = [nc.dram_tensor("a_a_up%d" % i, [64, 768], F32, kind="ExternalInput").ap() for i in range(2)]
    a_gup_d = [nc.dram_tensor("a_g_up%d" % i, [128, 768], F32, kind="ExternalInput").ap() for i in range(2)]
    wkv_d = [nc.dram_tensor("mem_w_kv%d" % l, [D, 512], F32, kind="ExternalInput").ap() for l in range(4)]
    b_wq_d = [nc.dram_tensor("b_w_q%d" % i, [D, D], F32, kind="ExternalInput").ap() for i in range(2)]
    b_wo_d = [nc.dram_tensor("b_w_out%d" % i, [512, D], F32, kind="ExternalInput").ap() for i in range(2)]
    kvw_d = nc.dram_tensor("kv_w", [D, 1536], F32, kind="ExternalInput").ap()
    relb_d = nc.dram_tensor("rel_bias", [32, 12], F32, kind="ExternalInput").ap()
    sel_d = nc.dram_tensor("sel", [3, 33, 510], F32, kind="ExternalInput").ap()
    KT_d = nc.dram_tensor("kt_scratch", [64, 12, S], BF16, kind="Internal").ap()
    V_d = nc.dram_tensor("v_scratch", [S, 768], BF16, kind="Internal").ap()
    E_d = nc.dram_tensor("e_scratch", [3, 12, 510], F32, kind="Internal").ap()

    P = Prog(nc, es)

    def sbt(name, shape, dt):
        return es.enter_context(nc.sbuf_tensor(_u(name), shape, dt))
    c.vecs = sbt("c_vecs", [128, NV], F32)
    c.KF = sbt("c_kf", [128, NKF], F32)
    c.KB = sbt("c_kb", [128, 5 * 128], BF16)
    c.eps = sbt("c_eps", [128, 4], F32)
    c.MK = sbt("c_mk", [128, 2, 256], BF16)
    c.MVZ = sbt("c_mvz", [128, 2, 4, 128], BF16)
    P.dma("sp", c.vecs[:], vecs_d[:, :], "c0", w=[("vecs",)])
    P.dma("sp", c.KF[:], consts_d[:, 384:NCONST], "c1", w=[("kf",)])
    P.dma("pool", c.KB[:, 0:384], consts_d[:, 0:384], "c2", w=[("kb",)])
    P.dma("pool", c.KB[:, 384:640], consts_d[:, NCONST:NCONST + 256], "c3", w=[("kb2",)])
    P.add("dve", lambda e: e.memset(c.eps[:, 0:1], NORM_EPS), w=[("eps", 0)])
    P.add("dve", lambda e: e.memset(c.eps[:, 1:2], LNX_EPS), w=[("eps", 1)])
    P.add("dve", lambda e: e.memset(c.MVZ[:], 0.0), w=[("MVZ",)])
    c.bda_f = c.KF[:, 0:128]
    c.msu = c.KF[:, 128:256]
    c.msl = c.KF[:, 256:384]
    c.mui = c.KF[:, 384:512]
    c.cmask = c.KF[:, 512:768]
    c.jf = c.KF[:, 768:896]
    c.ident_b = c.KB[:, 0:128]
    c.ones_b = c.KB[:, 128:256]
    c.bd_b = c.KB[:, 256:384]
    c.ones_bf = c.ones_b
    c.onesh = [c.KB[:, 384:512], c.KB[:, 512:640]]
    P.barrier()

    stages = []
    for layer in range(4):
        stages.append(("ffn", 2 * layer))
        stages.append(("mix", layer))
        stages.append(("ffn", 2 * layer + 1))
        if layer == 1:
            stages.append(("kv", 0))
    stages = stages[:n_stages]
    cur = xT
    for si, (kind, i) in enumerate(stages):
        last = si == len(stages) - 1
        dst = outT if last else XS
        if kind == "ffn":
            ffn_phase(P, nc, c, cur, dst, w_in_d[i], w_out_d[i], gcol=V["ffn%d" % i])
        elif kind == "kv":
            kv_phase(P, nc, c, cur, kvw_d, KT_d, V_d, V)
            continue
        else:
            layer = i
            mem_prep(P, nc, c, memT, wkv_d[layer], layer, V)
            if layer < 2:
                rwkv_phase(P, nc, c, cur, dst, layer, layer, a_w_in_d[layer], a_w_out_d[layer], a_wup_d[layer], a_aup_d[layer],
                           a_gup_d[layer], V, dbg=dbg if last else None)
            else:
                attn_phase(P, nc, c, cur, dst, layer - 2, layer, b_wq_d[layer - 2], b_wo_d[layer - 2], KT_d, V_d, relb_d, sel_d, E_d, V)
        cur = dst
    P.barrier()
    es.close()
    print("[kernel] ops=%d instr=%d" % (P.nops, P.ninstr))
    return nc


def _rep2(v):
    return np.ascontiguousarray(np.concatenate([v, v]).reshape(128, 1))


def _t5_bucket(dist):
    dist = np.asarray(dist, np.int64)
    d_f = np.maximum(dist, 1).astype(np.float32)
    large = 16 + (np.log(d_f / np.float32(16)) / np.float32(np.log(2048 / 16)) * np.float32(16)).astype(np.int32)
    large = np.minimum(large, 31)
    return np.where(dist < 16, dist, large)


def make_sel():
    sel = np.zeros((3, 33, 510), np.float32)
    n = np.arange(255)
    for g, dil in enumerate((1, 4, 16)):
        own_valid = n >= 127
        bo = np.where(own_valid, _t5_bucket(np.maximum(n - 127, 0) * dil), 32)
        prev_valid = n <= 127
        bp = np.where(prev_valid, _t5_bucket((n + 1) * dil), 32)
        sel[g, bo, n] = 1.0
        sel[g, bp, 255 + n] = 1.0
    return sel


def make_consts():
    K = np.zeros((128, NCONST + 256), np.float32)
    K[:, NCONST:NCONST + 64] = 1.0
    K[:, NCONST + 192:NCONST + 256] = 1.0
    K[:, 0:128] = np.eye(128)
    K[:, 128:256] = 1.0
    bd = np.zeros((128, 128), np.float32)
    bd[:64, :64] = 1.0
    bd[64:, 64:] = 1.0
    K[:, 256:384] = bd
    K[:, 384:512] = bd / 64.0
    i = np.arange(128)
    K[:, 512:640] = (i[:, None] < i[None, :])
    K[:, 640:768] = (i[:, None] > i[None, :])
    K[:, 768:896] = (i[:, None] <= i[None, :])
    cm = np.ones(256, np.float32)
    cm[::128] = 0.0
    K[:, 896:896 + 256] = cm[None, :]
    K[:, 1152:1280] = np.eye(128)[::-1]
    return K


def kernel(**inputs):
    n_stages = int(inputs.pop("_n_stages", 99))
    cores = inputs.pop("_cores", list(range(8)))
    trace = inputs.pop("_trace", False)
    dbg_on = inputs.pop("_dbg", False)
    f = lambda a: np.asarray(a, dtype=np.float32)
    x = f(inputs["x"])
    mem = f(inputs["mem"])
    V, NV = vec_layout()
    vecs = np.zeros((128, NV), np.float32)

    def put(name, arr):
        arr = np.asarray(arr, np.float32)
        vecs[:, V[name]:V[name] + arr.shape[1]] = arr
    shared = {}
    for l in range(4):
        for nm in ("ffn_pre", "ffn_post"):
            i = 2 * l + (0 if nm == "ffn_pre" else 1)
            shared["w_in%d" % i] = _slots_in(f(inputs[nm + "_w_in"][l]))
            shared["w_out%d" % i] = np.ascontiguousarray(f(inputs[nm + "_w_out"][l]))
            put("ffn%d" % i, _vec_pk(f(inputs[nm + "_norm"][l])))
        put("mix_norm%d" % l, _vec_pk(f(inputs["mix_norm"][l])))
        put("mem_norm%d" % l, _vec_pk(f(inputs["mem_norm"][l])))
        put("mem_q_norm%d" % l, _rep2(f(inputs["mem_q_norm"][l])))
        put("mem_k_norm%d" % l, _rep2(f(inputs["mem_k_norm"][l])))
        shared["mem_w_kv%d" % l] = np.ascontiguousarray(f(inputs["mem_w_kv"][l]))
    for i in range(2):
        shared["a_w_in%d" % i] = _slots_in(f(inputs["a_w_in"][i]))
        shared["a_w_out%d" % i] = np.ascontiguousarray(f(inputs["a_w_out"][i]))
        shared["a_w_up%d" % i] = np.ascontiguousarray(f(inputs["a_w_up"][i]))
        shared["a_a_up%d" % i] = np.ascontiguousarray(f(inputs["a_a_up"][i]))
        shared["a_g_up%d" % i] = np.ascontiguousarray(f(inputs["a_g_up"][i]))
        put("a_mu%d" % i, _vec_pk(f(inputs["a_shift_mu"][i])))
        put("a_w0%d" % i, _vec_pk(f(inputs["a_w0"][i])))
        put("a_a0%d" % i, _vec_pk(f(inputs["a_a0"][i])))
        put("a_kk_scale%d" % i, _vec_pk(f(inputs["a_kk_scale"][i])))
        put("a_k_a%d" % i, _vec_pk(f(inputs["a_k_a"][i])))
        put("a_r_k%d" % i, _vec_pk(f(inputs["a_r_k"][i]).reshape(-1)))
        put("a_lnx_g%d" % i, _vec_pk(f(inputs["a_lnx_g"][i])))
        put("a_lnx_b%d" % i, _vec_pk(f(inputs["a_lnx_b"][i])))
    for j in range(2):
        put("b_q_norm%d" % j, _rep2(f(inputs["b_q_norm"][j])))
    put("kv_norm", _vec_pk(f(inputs["kv_norm"])))
    put("kv_k_norm", _rep2(f(inputs["kv_k_norm"])))
    for jj in range(2):
        shared["b_w_q%d" % jj] = np.ascontiguousarray(f(inputs["b_w_q"][jj]))
        shared["b_w_out%d" % jj] = np.ascontiguousarray(f(inputs["b_w_out"][jj]))
    shared["kv_w"] = np.ascontiguousarray(f(inputs["kv_w"]))
    shared["rel_bias"] = np.ascontiguousarray(f(inputs["rel_bias"]))
    shared["sel"] = make_sel()
    shared["vecs"] = vecs
    shared["consts"] = make_consts()
    nc = build(n_stages, dbg_on)
    in_maps = []
    for b in cores:
        m = dict(shared)
        m["xT"] = np.ascontiguousarray(x[b].T)
        m["memT"] = np.ascontiguousarray(mem[b].T)
        in_maps.append(m)
    if trace:
        res = run_bass_kernel_spmd(nc, in_maps, core_ids=list(range(len(cores))), trace=True)
        print("[kernel] exec_time_ns", res.exec_time_ns)
    else:
        res = run_bass_kernel_spmd(nc, in_maps, core_ids=list(range(len(cores))))
    if dbg_on:
        kernel.dbg = [np.ascontiguousarray(r["dbg"].T) for r in res.results]
    out = np.stack([np.ascontiguousarray(r["outT"].T) for r in res.results], axis=0)
    return out.astype(np.float32)
```

```python
import os
import numpy as np
from contextlib import ExitStack
import concourse.bass as bass
import concourse.mybir as mybir
from concourse.bass_utils import run_bass_kernel_spmd

F32 = mybir.dt.float32
BF16 = mybir.dt.bfloat16
AF = mybir.ActivationFunctionType
ALU = mybir.AluOpType
AX = mybir.AxisListType

D = 1024
KD = 8
S = 4096
FF = 2816
KF = 22
TT = 512
NT = S // TT
NORM_EPS = 1e-6

COMPUTE = ("pe", "act", "dve", "pool")


class Prog:
    def __init__(self, nc, es):
        self.nc = nc
        self.eng = dict(pe=nc.tensor, act=nc.scalar, dve=nc.vector, pool=nc.gpsimd, sp=nc.sync)
        self.sem = {e: es.enter_context(nc.semaphore("s_" + e)) for e in COMPUTE}
        self.cnt = {e: 0 for e in COMPUTE}
        self.es = es
        self.dsem = {}
        self.dcnt = {}
        self.pending = []
        self.last_w = {}
        self.readers = {}
        self.waited = {e: {} for e in self.eng}
        self.done = {}
        self.sigs = {e: [] for e in COMPUTE}
        self.nops = 0
        self.ninstr = 0

    def add(self, eng, fn, r=(), w=(), dma=None):
        idx = self.nops
        self.nops += 1
        deps = set()
        for k in r:
            j = self.last_w.get(k)
            if j is not None:
                deps.add(j)
        for k in w:
            j = self.last_w.get(k)
            if j is not None:
                deps.add(j)
            rd = self.readers.get(k)
            if rd:
                deps.update(rd.values())
        for k in w:
            self.last_w[k] = idx
            self.readers[k] = {}
        tag = ("d", dma) if dma else ("c", eng)
        for k in r:
            if k not in w:
                self.readers.setdefault(k, {})[tag] = idx
        self.pending.append(dict(idx=idx, eng=eng, fn=fn, deps=deps, dma=dma, sig=False))
        return idx

    def _wait(self, eng, semname, semh, val):
        if self.waited[eng].get(semname, 0) >= val:
            return
        self.waited[eng][semname] = val
        self.eng[eng].wait_ge(semh, val)
        self.ninstr += 1

    def flush(self):
        pend = self.pending
        self.pending = []
        byidx = {op["idx"]: op for op in pend}
        for op in pend:
            for j in op["deps"]:
                d = byidx.get(j)
                if d is not None and d["dma"] is None:
                    if not (d["eng"] == "pe" and op["eng"] == "pe" and op["dma"] is None):
                        d["sig"] = True
        last = {}
        for op in pend:
            if op["dma"] is None:
                last[op["eng"]] = op
        for op in last.values():
            op["sig"] = True
        for op in pend:
            eng = op["eng"]
            for j in sorted(op["deps"]):
                if j in self.done:
                    kind, name, val = self.done[j]
                    if kind == "d":
                        self._wait(eng, "d_" + name, self.dsem[name], val)
                    else:
                        if name == "pe" and eng == "pe" and op["dma"] is None:
                            continue
                        if val is None:
                            lst = self.sigs[name]
                            lo, hi = 0, len(lst)
                            while lo < hi:
                                mid = (lo + hi) // 2
                                if lst[mid][0] < j:
                                    lo = mid + 1
                                else:
                                    hi = mid
                            val = lst[lo][1]
                        self._wait(eng, "c_" + name, self.sem[name], val)
                else:
                    raise RuntimeError("dep on unemitted op")
            if op["dma"]:
                name = op["dma"]
                if name not in self.dsem:
                    self.dsem[name] = self.es.enter_context(self.nc.semaphore("d_" + name))
                    self.dcnt[name] = 0
                if self.dcnt[name] > 0:
                    self._wait(eng, "d_" + name, self.dsem[name], self.dcnt[name])
                ins = op["fn"](self.eng[eng])
                self.dcnt[name] += 16
                ins.then_inc(self.dsem[name], 16)
                self.done[op["idx"]] = ("d", name, self.dcnt[name])
            else:
                ins = op["fn"](self.eng[eng])
                if op["sig"]:
                    self.cnt[eng] += 1
                    ins.then_inc(self.sem[eng], 1)
                    self.done[op["idx"]] = ("c", eng, self.cnt[eng])
                    self.sigs[eng].append((op["idx"], self.cnt[eng]))
                else:
                    self.done[op["idx"]] = ("c", eng, None)
            self.ninstr += 1

    def barrier(self, dma_only_on=("sp",)):
        self.flush()
        for f in self.eng:
            for e in COMPUTE:
                if e != f and self.cnt[e] > 0:
                    self._wait(f, "c_" + e, self.sem[e], self.cnt[e])
            for name, h in self.dsem.items():
                if self.dcnt[name] > 0:
                    self._wait(f, "d_" + name, h, self.dcnt[name])
        self.last_w = {}
        self.readers = {}


    def mm(self, out, lhsT, rhs, start=True, stop=True, r=(), w=()):
        self.add("pe", lambda e: e.matmul(out, lhsT, rhs, start=start, stop=stop), r=r, w=w)

    def tr(self, out, in_, ident, r=(), w=()):
        self.add("pe", lambda e: e.matmul(out, in_, ident, start=True, stop=True), r=r, w=w)

    def act(self, out, in_, func, r=(), w=(), bias=None, scale=None):
        kw = {}
        if bias is not None:
            kw["bias"] = bias
        if scale is not None:
            kw["scale"] = scale
        self.add("act", lambda e: e.activation(out=out, in_=in_, func=func, **kw), r=r, w=w)

    def tt(self, eng, out, in0, in1, op, r=(), w=()):
        self.add(eng, lambda e: e.tensor_tensor(out, in0, in1, op), r=r, w=w)

    def ts(self, eng, out, in0, s1, s2, op0, op1, r=(), w=()):
        self.add(eng, lambda e: e.tensor_scalar(out, in0, s1, s2, op0, op1), r=r, w=w)

    def tsmul(self, eng, out, in0, s1, r=(), w=()):
        self.add(eng, lambda e: e.tensor_scalar_mul(out, in0, s1), r=r, w=w)

    def stt(self, eng, out, in0, scalar, in1, op0, op1, r=(), w=()):
        self.add(eng, lambda e: e.scalar_tensor_tensor(out, in0, scalar, in1, op0, op1), r=r, w=w)

    def copy(self, eng, out, in_, r=(), w=()):
        if eng == "act":
            self.add("act", lambda e: e.activation(out=out, in_=in_, func=AF.Copy), r=r, w=w)
        elif os.environ.get("KCOPY", "mul") == "mul":
            self.add(eng, lambda e: e.tensor_scalar_mul(out, in_, 1.0), r=r, w=w)
        else:
            self.add(eng, lambda e: e.tensor_copy(out, in_), r=r, w=w)

    def dma(self, q, out, in_, sem, r=(), w=()):
        self.add(q, lambda e: e.dma_start(out=out, in_=in_), r=r, w=w, dma=sem)


def _slots_in(w):
    K, M = w.shape
    return np.ascontiguousarray(w.reshape(K // 128, 128, M // 128, 128).transpose(2, 1, 0, 3))


def _vec_pk(v):
    return np.ascontiguousarray(v.reshape(-1, 128).T)


class Ctx:
    pass


_UID = [0]


def _u(name):
    _UID[0] += 1
    return "%s_%d" % (name, _UID[0])


def ffn_phase(P, nc, c, src, dst, w_in_d, w_out_d, gcol):
    with ExitStack() as es:
        def sb(name, shape, dt):
            return es.enter_context(nc.sbuf_tensor(_u(name), shape, dt))

        def ps(name, shape, dt=F32):
            return es.enter_context(nc.psum_tensor(_u(name), shape, dt))
        WIN = sb("f_win", [128, 2 * KF, KD, 128], BF16)
        WOUT = sb("f_wout", [128, KF, D], BF16)
        XT = [sb("f_xt%d" % i, [128, KD, TT], F32) for i in range(2)]
        XN = sb("f_xn", [128, KD, TT], BF16)
        ACTB = sb("f_act", [128, KF, TT], BF16)
        SQ = [sb("f_sq%d" % i, [128, TT], BF16) for i in range(1)]
        SG = [sb("f_sg%d" % i, [128, TT], BF16) for i in range(2)]
        RSTD = sb("f_rstd", [128, TT], F32)
        PH = [ps("f_ph%d" % i, [128, TT]) for i in range(4)]
        PY = [ps("f_py%d" % i, [128, TT]) for i in range(2)]
        PSS = ps("f_pss", [128, TT])

        G = 2
        for j0 in range(0, 2 * KF, G):
            P.add("pool", lambda e, j0=j0: e.dma_start(
                out=WIN[:, j0:j0 + G], in_=w_in_d[j0:j0 + G].rearrange("j p k c -> p j k c")),
                w=[("win", j) for j in range(j0, j0 + G)], dma="wq%d" % ((j0 // G) % 4))
        for k0 in range(0, KF, G):
            P.add("pool", lambda e, k0=k0: e.dma_start(
                out=WOUT[:, k0:k0 + G], in_=w_out_d[k0 * 128:(k0 + G) * 128, :].rearrange("(k p) c -> p k c", p=128)),
                w=[("wout", k) for k in range(k0, k0 + G)], dma="wq%d" % ((k0 // G) % 4))

        def load(t):
            b = t % 2
            P.add("sp", lambda e: e.dma_start(
                out=XT[b][:], in_=src[:, t * TT:(t + 1) * TT].rearrange("(k p) n -> p k n", p=128)),
                r=[("X", src.tensor.name, t)], w=[("xt", b)], dma="xl%d" % b)

        def do_tile(t):
            b = t % 2
            if t + 1 < NT:
                load(t + 1)
            xt = XT[b]
            for k in range(KD):
                q = 0
                P.add("act", lambda e, k=k, q=q: e.activation(out=SQ[q][:], in_=xt[:, k, :], func=AF.Square),
                      r=[("xt", b)], w=[("sq", q)])
                P.add("pe", lambda e, k=k, q=q: e.matmul(PSS[:], c.ones_bf[:], SQ[q][:], start=(k == 0), stop=(k == KD - 1)),
                      r=[("sq", q)], w=[("pss",)])
            P.add("act", lambda e: e.activation(out=RSTD[:], in_=PSS[:], func=AF.Ln, bias=c.eps[:, 0:1], scale=1.0 / D),
                  r=[("pss",)], w=[("rstd",)])
            P.add("act", lambda e: e.activation(out=RSTD[:], in_=RSTD[:], func=AF.Exp, scale=-0.5),
                  r=[("rstd",)], w=[("rstd",)])
            for k in range(KD):
                P.add("dve", lambda e, k=k: e.scalar_tensor_tensor(
                    XN[:, k, :], xt[:, k, :], c.vecs[:, gcol + k:gcol + k + 1], RSTD[:], ALU.mult, ALU.mult),
                    r=[("xt", b), ("rstd",)], w=[("xn", k)])
            for j in range(KF):
                pg = PH[(2 * j) % 4]
                pu = PH[(2 * j + 1) % 4]
                kg, ku = ("ph", (2 * j) % 4), ("ph", (2 * j + 1) % 4)
                for k in range(KD):
                    P.add("pe", lambda e, j=j, k=k, pg=pg: e.matmul(pg[:], WIN[:, j, k, :], XN[:, k, :], start=(k == 0), stop=(k == KD - 1)),
                          r=[("win", j), ("xn", k)], w=[kg])
                for k in range(KD):
                    P.add("pe", lambda e, j=j, k=k, pu=pu: e.matmul(pu[:], WIN[:, KF + j, k, :], XN[:, k, :], start=(k == 0), stop=(k == KD - 1)),
                          r=[("win", KF + j), ("xn", k)], w=[ku])
                q = j % 2
                P.add("act", lambda e, pg=pg, q=q: e.activation(out=SG[q][:], in_=pg[:], func=AF.Silu),
                      r=[kg], w=[("sg", q)])
                P.add("dve", lambda e, j=j, pu=pu, q=q: e.tensor_tensor(ACTB[:, j, :], pu[:], SG[q][:], ALU.mult),
                      r=[ku, ("sg", q)], w=[("act", j)])
            for m in range(KD):
                py = PY[m % 2]
                ky = ("py", m % 2)
                for k in range(KF):
                    P.add("pe", lambda e, m=m, k=k, py=py: e.matmul(py[:], WOUT[:, k, m * 128:(m + 1) * 128], ACTB[:, k, :], start=(k == 0), stop=(k == KF - 1)),
                          r=[("wout", k), ("act", k)], w=[ky])
                P.add("dve", lambda e, m=m, py=py: e.scalar_tensor_tensor(
                    xt[:, m, :], py[:], 0.5, xt[:, m, :], ALU.mult, ALU.add),
                    r=[ky, ("xt", b)], w=[("xt", b)])
            P.add("sp", lambda e, t=t: e.dma_start(
                out=dst[:, t * TT:(t + 1) * TT].rearrange("(k p) n -> p k n", p=128), in_=xt[:]),
                r=[("xt", b)], w=[("X", dst.tensor.name, t)], dma="xs%d" % b)
        load(0)
        for t in range(NT):
            do_tile(t)
        P.barrier()


import os
DBG_NTR = int(os.environ.get('KDBG_NTR', '999'))
TMZENG = os.environ.get('KTMZ', 'act')
DBG_STOP = float(os.environ.get('KDBG_STOP', '99'))
TR = 256
HG = [[0, 2, 4, 6], [1, 3, 5, 7], [8, 10], [9, 11]]


def hgk(h):
    return 2 * (h // 8) + (h % 2)

CH = 128
NCH = TR // CH
NTR = S // TR
LNX_EPS = 64e-5
DEC_SCALE = -0.6065306597126334


class Banks:
    def __init__(self, tiles, tag):
        self.tiles = tiles
        self.tag = tag
        self.i = 0

    def next(self):
        i = self.i
        self.i = (self.i + 1) % len(self.tiles)
        return self.tiles[i], (self.tag, i)


def rms_tile(P, c, xt, xkey, xn, xnkey, gcol, n, PSS, psskey, SQ, RSTD, tag):
    for k in range(KD):
        q = k % 2
        P.act(SQ[q][:, :n], xt[:, k, :n], AF.Square, r=[xkey], w=[(tag, "sq", q)])
        P.mm(PSS[:, :n], c.ones_b[:], SQ[q][:, :n], start=(k == 0), stop=(k == KD - 1), r=[(tag, "sq", q)], w=[psskey])
    P.act(RSTD[:, :n], PSS[:, :n], AF.Ln, r=[psskey], w=[(tag, "rstd")], bias=c.eps[:, 0:1], scale=1.0 / D)
    P.act(RSTD[:, :n], RSTD[:, :n], AF.Exp, r=[(tag, "rstd")], w=[(tag, "rstd")], scale=-0.5)
    for k in range(KD):
        P.stt("dve", xn[:, k, :n], xt[:, k, :n], c.vecs[:, gcol + k:gcol + k + 1], RSTD[:, :n], ALU.mult, ALU.mult,
              r=[xkey, (tag, "rstd")], w=[xnkey + (k,)])


def head_rms(P, c, out, src, srckey, gain_col, n, bank, bkey, SQ, TMP, tag, outkey):
    P.act(SQ[:, :n], src, AF.Square, r=[srckey], w=[(tag, "hsq")])
    P.mm(bank[:, :n], c.bd_b[:], SQ[:, :n], r=[(tag, "hsq")], w=[bkey])
    P.act(TMP[:, :n], bank[:, :n], AF.Ln, r=[bkey], w=[(tag, "htmp")], bias=c.eps[:, 0:1], scale=1.0 / 64)
    P.act(TMP[:, :n], TMP[:, :n], AF.Exp, r=[(tag, "htmp")], w=[(tag, "htmp")], scale=-0.5)
    P.stt("dve", out, src, c.vecs[:, gain_col:gain_col + 1], TMP[:, :n], ALU.mult, ALU.mult,
          r=[srckey, (tag, "htmp")], w=[outkey])


def mem_prep(P, nc, c, memT_d, wkv_d, layer, V):
    with ExitStack() as es:
        def sb(name, shape, dt):
            return es.enter_context(nc.sbuf_tensor(_u(name), shape, dt))
        WKV = sb("mp_wkv", [128, KD, 512], BF16)
        MT = sb("mp_mt", [128, KD, 256], F32)
        MN = sb("mp_mn", [128, KD, 256], BF16)
        SQ = [sb("mp_sq%d" % i, [128, 256], BF16) for i in range(2)]
        RSTD = sb("mp_rstd", [128, 256], F32)
        KF32 = sb("mp_kf", [128, 256], F32)
        TMP = sb("mp_tmp", [128, 256], F32)
        pb = [es.enter_context(nc.psum_tensor(_u("mp_pb%d" % i), [128, 512], F32)) for i in range(3)]
        P.dma("pool", WKV[:], wkv_d.rearrange("(k p) c -> p k c", p=128), "wq0", w=[("mp", "wkv")])
        P.dma("sp", MT[:], memT_d.rearrange("(k p) n -> p k n", p=128), "xl0", w=[("mp", "mt")])
        rms_tile(P, c, MT, ("mp", "mt"), MN, ("mp", "mn"), V["mem_norm%d" % layer], 256, pb[0], ("mp", "pb", 0), SQ, RSTD, "mp")
        mnkeys = [("mp", "mn", k) for k in range(KD)]
        for hp in range(2):
            for k in range(KD):
                P.mm(pb[1][:, :256], WKV[:, k, hp * 128:(hp + 1) * 128], MN[:, k, :], start=(k == 0), stop=(k == KD - 1),
                     r=[("mp", "wkv"), mnkeys[k]], w=[("mp", "pb", 1)])
            P.copy("act", KF32[:], pb[1][:, :256], r=[("mp", "pb", 1)], w=[("mp", "kf")])
            head_rms(P, c, c.MK[:, hp, :], KF32[:], ("mp", "kf"), V["mem_k_norm%d" % layer], 256, pb[2], ("mp", "pb", 2), SQ[0], TMP, "mpk",
                     ("MK", hp))
        for mc in range(2):
            for k in range(KD):
                P.mm(pb[1][:, :256], MN[:, k, mc * 128:(mc + 1) * 128], WKV[:, k, 256:512], start=(k == 0), stop=(k == KD - 1),
                     r=[("mp", "wkv"), mnkeys[k]], w=[("mp", "pb", 1)])
            for h in range(4):
                hh = h % 2
                P.copy("act", c.MVZ[:, mc, h, hh * 64:(hh + 1) * 64], pb[1][:, h * 64:(h + 1) * 64], r=[("mp", "pb", 1)], w=[("MVZ",)])
        P.barrier()


def mem_attn_tile(P, c, QM, qkeys, YM, ymkeys, n, layer, V, banks, SQ, TMP, QN, PT, RD, tag):
    for hp in range(2):
        bank, bk = banks.next()
        head_rms(P, c, QN[:, hp, :n], QM[:, hp, :n], qkeys[hp], V["mem_q_norm%d" % layer], n, bank, bk, SQ, TMP, tag, (tag, "qn", hp))
    for hp in range(2):
        for hh in range(2):
            off = hh * 64
            for mc in range(2):
                bl, kl = banks.next()
                P.mm(bl[:, :n], c.MK[off:off + 64, hp, mc * 128:(mc + 1) * 128], QN[off:off + 64, hp, :n],
                     r=[("MK", hp), (tag, "qn", hp)], w=[kl])
                P.act(PT[hh * 2 + mc][:, :n], bl[:, :n], AF.Exp, r=[kl], w=[(tag, "pt", hh * 2 + mc)], scale=0.125)
        bnum, knum = banks.next()
        bden, kden = banks.next()
        for i in range(4):
            hh, mc = i // 2, i % 2
            h = hp * 2 + hh
            P.mm(bnum[:, :n], c.MVZ[:, mc, h, :], PT[i][:, :n], start=(i == 0), stop=(i == 3), r=[("MVZ",), (tag, "pt", i)], w=[knum])
        for i in range(4):
            hh, mc = i // 2, i % 2
            P.mm(bden[:, :n], c.onesh[hh], PT[i][:, :n], start=(i == 0), stop=(i == 3), r=[(tag, "pt", i)], w=[kden])
        P.act(RD[:, :n], bden[:, :n], AF.Ln, r=[kden], w=[(tag, "rd")])
        P.act(RD[:, :n], RD[:, :n], AF.Exp, r=[(tag, "rd")], w=[(tag, "rd")], scale=-1.0)
        P.tt("dve", YM[:, hp, :n], bnum[:, :n], RD[:, :n], ALU.mult, r=[knum, (tag, "rd")], w=[ymkeys[hp]])


def rwkv_phase(P, nc, c, src, dst, li, layer, w_in_d, w_out_d, wup_d, aup_d, gup_d, V, dbg=None):
    with ExitStack() as es:
        def sb(name, shape, dt):
            return es.enter_context(nc.sbuf_tensor(_u("rk_" + name), shape, dt))
        WA = sb("wa", [128, 22, KD, 128], BF16)
        WO = sb("wo", [128, KD, D], BF16)
        WAUP = sb("waup", [128, 768], BF16)
        GUP = sb("gup", [128, 768], BF16)
        XT = sb("xt", [128, KD, TR], F32)
        XN = sb("xn", [128, KD, TR], BF16)
        SQ = [sb("sq%d" % i, [128, TR], BF16) for i in range(2)]
        RSTD = sb("rstd", [128, TR], F32)
        CARRY = sb("carry", [128, 20], F32)
        OMM = sb("omm", [128, 20], F32)
        OMKA = sb("omka", [128, 6], F32)
        TA = sb("ta", [128, TR], F32)
        P18 = sb("p18", [128, TR], F32)
        P19 = sb("p19", [128, TR], F32)
        TW = sb("tw", [128, TR], BF16)
        AL = sb("al", [128, TR], BF16)
        SGG = sb("sgg", [128, TR], BF16)
        QM = sb("qm", [128, 2, TR], F32)
        Rf = sb("rf", [128, TR], F32)
        Kf = sb("kf", [128, TR], F32)
        Vf = sb("vf", [128, TR], F32)
        T = [sb("t%d" % i, [128, TR], F32) for i in range(10)]
        HB = [sb("hb%d" % i, [128, TR], BF16) for i in range(4)]
        G = sb("g", [128, 6, TR], BF16)
        BON = sb("bon", [128, 6, TR], BF16)
        RT = sb("rt", [128, 6, TR], BF16)
        KT = sb("kt", [128, 6, TR], BF16)
        BT = sb("bt", [128, 6, TR], BF16)
        AT = sb("at", [128, 6, TR], BF16)
        PC = sb("pc", [128, 6, NCH], F32)
        TMV = [sb("tmv%d" % i, [128, 6, 128], BF16) for i in range(NCH)]
        TMZ = [sb("tmz%d" % i, [128, 2, 13, 128], BF16) for i in range(NCH)]
        W = [sb("w%d" % i, [128, 2, 768], BF16) for i in range(2 * NCH)]
        AK = [sb("ak%d" % i, [128, 12, 128], BF16) for i in range(2 * NCH)]
        BK = [sb("bk%d" % i, [128, 12, 128], BF16) for i in range(2 * NCH)]
        AAK = sb("aak", [128, 12, 128], BF16)
        ARK = sb("ark", [128, 12, 128], BF16)
        ARB = sb("arb", [128, 12, 128], BF16)
        AHT = sb("aht", [128, 6, 128], BF16)
        UU = sb("uu", [128, 12, 64], BF16)
        SF = sb("sf", [128, 6, 64], F32)
        SB = sb("sb", [128, 6, 64], BF16)
        YB = sb("yb", [128, 6, TR], F32)
        YO = sb("yo", [128, 6, TR], BF16)
        YM = sb("ym", [128, 2, TR], BF16)
        QN = sb("qn", [128, 2, TR], BF16)
        PT = [sb("pt%d" % i, [128, TR], BF16) for i in range(4)]
        RD = sb("rd", [128, TR], F32)
        MSQ = sb("msq", [128, TR], BF16)
        MTMP = sb("mtmp", [128, TR], F32)
        pbt = [es.enter_context(nc.psum_tensor(_u("rk_pb%d" % i), [128, 512], F32)) for i in range(8)]
        banks = Banks(pbt, "rpb")
        tbanks = banks
        print("[kernel] rwkv sbuf free", nc.sbuf_bytes_remaining)

        for j0 in range(0, 22, 2):
            P.dma("pool", WA[:, j0:j0 + 2], w_in_d[j0:j0 + 2].rearrange("j p k c -> p j k c"), "wq%d" % ((j0 // 2) % 4),
                  w=[("wa", j) for j in range(j0, j0 + 2)])
        P.dma("pool", WAUP[0:64, :], wup_d[:, :], "wq0", w=[("waup", 0)])
        P.dma("pool", WAUP[64:128, :], aup_d[:, :], "wq1", w=[("waup", 1)])
        P.dma("pool", GUP[:], gup_d[:, :], "wq2", w=[("gup",)])
        for k0 in range(0, KD, 2):
            P.dma("pool", WO[:, k0:k0 + 2], w_out_d[k0 * 128:(k0 + 2) * 128, :].rearrange("(k p) c -> p k c", p=128), "wq%d" % ((k0 // 2) % 4),
                  w=[("wo", k) for k in range(k0, k0 + 2)])
        mu0 = V["a_mu%d" % li]
        P.ts("dve", OMM[:], c.vecs[:, mu0:mu0 + 20], -1.0, 1.0, ALU.mult, ALU.add, w=[("omm",)])
        ka0 = V["a_k_a%d" % li]
        P.ts("dve", OMKA[:], c.vecs[:, ka0:ka0 + 6], -1.0, 1.0, ALU.mult, ALU.add, w=[("omka",)])
        P.add("dve", lambda e: e.memset(CARRY[:], 0.0), w=[("carry", m) for m in range(20)])
        P.add("dve", lambda e: e.memset(SF[:], 0.0), w=[("sf",)])
        P.add("dve", lambda e: e.memset(SB[:], 0.0), w=[("sbk",)])
        for i in range(NCH):
            P.add("dve", lambda e, i=i: e.memset(TMZ[i][:], 0.0), w=[("tmz", i, hp, ty, hh) for hp in range(6) for ty in range(2) for hh in range(2)])

        def vcol(name, i):
            o = V[name + "%d" % li] + i
            return c.vecs[:, o:o + 1]

        def proj(m, n):
            bank, bk = banks.next()
            for k in range(KD):
                P.mm(bank[:, :n], WA[:, m, k, :], XN[:, k, :n], start=(k == 0), stop=(k == KD - 1), r=[("wa", m), ("xn", k)], w=[bk])
            return bank, bk

        def shift_evac(m, bank, bk, out, outkey):
            n = TR
            P.act(TA[:, :n], bank[:, :n], AF.Copy, r=[bk, ("omm",)], w=[("ta",)], scale=OMM[:, m:m + 1])
            P.stt("dve", out[:, 1:n], bank[:, 0:n - 1], c.vecs[:, mu0 + m:mu0 + m + 1], TA[:, 1:n], ALU.mult, ALU.add,
                  r=[bk, ("ta",)], w=[outkey])
            P.stt("dve", out[:, 0:1], CARRY[:, m:m + 1], c.vecs[:, mu0 + m:mu0 + m + 1], TA[:, 0:1], ALU.mult, ALU.add,
                  r=[("carry", m), ("ta",)], w=[outkey + ("c0",)])
            P.copy("dve", CARRY[:, m:m + 1], bank[:, n - 1:n], r=[bk], w=[("carry", m)])

        def do_tile(t):
            n = TR
            P.dma("sp", XT[:], src[:, t * TR:(t + 1) * TR].rearrange("(k p) n -> p k n", p=128), "xl0",
                  r=[("X", src.tensor.name, t)], w=[("xt",)])
            rb, rbk = banks.next()
            rms_tile(P, c, XT, ("xt",), XN, ("xn",), V["mix_norm%d" % layer], n, rb, rbk, SQ, RSTD, "rk")
            b18, k18 = proj(18, n)
            shift_evac(18, b18, k18, P18, ("p18",))
            b19, k19 = proj(19, n)
            shift_evac(19, b19, k19, P19, ("p19",))
            p18k = [("p18",), ("p18", "c0")]
            p19k = [("p19",), ("p19", "c0")]
            P.act(TW[0:64, :], P18[0:64, :], AF.Tanh, r=p18k, w=[("tw",)])
            P.copy("dve", AL[64:128, :], P18[64:128, :], r=p18k, w=[("al",)])
            P.act(SGG[:], P19[:], AF.Sigmoid, r=p19k, w=[("sgg",)])
            for q in range(2):
                bq, kq = proj(20 + q, n)
                P.copy("act", QM[:, q, :], bq[:, :n], r=[kq], w=[("qm", q)])
            if DBG_STOP <= 1:
                return
            for hp in range(6):
                cs = slice(hp * 128, (hp + 1) * 128)
                bz, kz = banks.next()
                P.mm(bz[:, :n], WAUP[0:64, cs], TW[0:64, :], r=[("waup", 0), ("tw",)], w=[kz])
                SW, LOGW, ALR = T[0], T[1], T[2]
                P.act(SW[:], bz[:, :n], AF.Sigmoid, r=[kz], w=[("sw",)], bias=vcol("a_w0", hp))
                P.tsmul("dve", LOGW[:], SW[:], DEC_SCALE, r=[("sw",)], w=[("logw",)])
                bz, kz = banks.next()
                P.mm(bz[:, :n], WAUP[64:128, cs], AL[64:128, :], r=[("waup", 1), ("al",)], w=[kz])
                P.act(ALR[:], bz[:, :n], AF.Sigmoid, r=[kz], w=[("alr",)], bias=vcol("a_a0", hp))
                bz, kz = banks.next()
                P.mm(bz[:, :n], GUP[:, cs], SGG[:], r=[("gup",), ("sgg",)], w=[kz])
                P.copy("act", G[:, hp, :], bz[:, :n], r=[kz], w=[("g", hp)])
                if DBG_STOP <= 1.1:
                    continue
                br, kr = proj(hp, n)
                shift_evac(hp, br, kr, Rf, ("rf",))
                bk_, kk_ = proj(6 + hp, n)
                shift_evac(6 + hp, bk_, kk_, Kf, ("kf",))
                bv, kv = proj(12 + hp, n)
                shift_evac(12 + hp, bv, kv, Vf, ("vf",))
                rfk = [("rf",), ("rf", "c0")]
                kfk = [("kf",), ("kf", "c0")]
                vfk = [("vf",), ("vf", "c0")]
                if DBG_STOP <= 1.2:
                    continue
                KS, NRM, KK, TMv, KM, Bv, L = T[3], T[4], T[5], T[6], T[7], T[8], T[9]
                P.tsmul("dve", KS[:], Kf[:], vcol("a_kk_scale", hp), r=kfk, w=[("ks",)])
                P.act(HB[0][:], KS[:], AF.Square, r=[("ks",)], w=[("hb", 0)])
                bz, kz = banks.next()
                P.mm(bz[:, :n], c.bd_b[:], HB[0][:], r=[("hb", 0)], w=[kz])
                P.act(NRM[:], bz[:, :n], AF.Ln, r=[kz], w=[("nrm",)], bias=c.eps[:, 2:3])
                P.act(NRM[:], NRM[:], AF.Exp, r=[("nrm",)], w=[("nrm",)], scale=-0.5)
                P.tt("dve", KK[:], KS[:], NRM[:], ALU.mult, r=[("ks",), ("nrm",)], w=[("kk",)])
                P.ts("dve", TMv[:], ALR[:], vcol("a_k_a", hp), OMKA[:, hp:hp + 1], ALU.mult, ALU.add, r=[("alr",), ("omka",)], w=[("tmv",)])
                P.tt("dve", KM[:], Kf[:], TMv[:], ALU.mult, r=kfk + [("tmv",)], w=[("km",)])
                P.tt("dve", Bv[:], KK[:], ALR[:], ALU.mult, r=[("kk",), ("alr",)], w=[("bv",)])
                P.stt("dve", HB[1][:], Rf[:], vcol("a_r_k", hp), KM[:], ALU.mult, ALU.mult, r=rfk + [("km",)], w=[("hb", 1)])
                bz, kz = banks.next()
                P.mm(bz[:, :n], c.bd_b[:], HB[1][:], r=[("hb", 1)], w=[kz])
                P.tt("dve", BON[:, hp, :], bz[:, :n], Vf[:], ALU.mult, r=[kz] + vfk, w=[("bon", hp)])
                if DBG_STOP <= 1.3:
                    continue
                P.add("dve", lambda e, L=L, LOGW=LOGW: e.tensor_tensor_scan(L[:], c.cmask[:, :n], LOGW[:], 0.0, ALU.mult, ALU.add),
                      r=[("logw",)], w=[("L",)])
                E1 = T[0]
                P.act(E1[:], L[:], AF.Exp, r=[("L",)], w=[("sw",)])
                P.tt("dve", RT[:, hp, :], Rf[:], E1[:], ALU.mult, r=rfk + [("sw",)], w=[("rt", hp)])
                E2 = T[3]
                P.act(E2[:], L[:], AF.Exp, r=[("L",), ("kk",)], w=[("ks",)], scale=-1.0)
                P.tt("dve", KT[:, hp, :], KM[:], E2[:], ALU.mult, r=[("km",), ("ks",)], w=[("kt", hp)])
                P.tt("dve", BT[:, hp, :], Bv[:], E2[:], ALU.mult, r=[("bv",), ("ks",)], w=[("bt", hp)])
                LX = T[4]
                P.tt("dve", LX[:], L[:], LOGW[:], ALU.subtract, r=[("L",), ("logw",), ("kk",)], w=[("nrm",)])
                P.act(LX[:], LX[:], AF.Exp, r=[("nrm",)], w=[("nrm",)])
                P.stt("dve", AT[:, hp, :], KK[:], -1.0, LX[:], ALU.mult, ALU.mult, r=[("kk",), ("nrm",)], w=[("at", hp)])
                DEC = T[6]
                for cc in range(NCH):
                    ce = (cc + 1) * CH - 1
                    P.act(DEC[:, cc * CH:(cc + 1) * CH], L[:, cc * CH:(cc + 1) * CH], AF.Exp, r=[("L",), ("km",)], w=[("tmv",)],
                          bias=L[:, ce:ce + 1], scale=-1.0)
                    P.act(PC[:, hp, cc:cc + 1], L[:, ce:ce + 1], AF.Exp, r=[("L",)], w=[("pc", cc)])
                P.tt("dve", HB[2][:], KM[:], DEC[:], ALU.mult, r=[("km",), ("tmv",)], w=[("hb", 2)])
                P.tt("dve", HB[3][:], Bv[:], DEC[:], ALU.mult, r=[("bv",), ("tmv",)], w=[("hb", 3)])
                P.copy("act", HB[0][:], Vf[:], r=vfk, w=[("hb", 0)])
                if DBG_STOP <= 1.4:
                    continue
                for cc in range(NCH):
                    tb, tk = tbanks.next()
                    csl = slice(cc * CH, (cc + 1) * CH)
                    P.tr(tb[:, 0:128], HB[0][:, csl], c.ident_b[:], r=[("hb", 0)], w=[tk])
                    P.tr(tb[:, 128:256], HB[2][:, csl], c.ident_b[:], r=[("hb", 2)], w=[tk])
                    P.tr(tb[:, 256:384], HB[3][:, csl], c.ident_b[:], r=[("hb", 3)], w=[tk])
                    P.tr(tb[:, 384:512], AT[:, hp, csl], c.ident_b[:], r=[("at", hp)], w=[tk])
                    if DBG_STOP <= 1.45:
                        continue
                    evq = "act" if (hp * NCH + cc) % 2 == 0 else "dve"
                    P.copy(evq, TMV[cc][:, hp, :], tb[:, 0:128], r=[tk], w=[("tmv", cc, hp)])
                    if DBG_STOP <= 1.46:
                        continue
                    for ty in range(2):
                        for hh in range(2):
                            P.copy(evq, TMZ[cc][:, ty, 2 * hp + hh, hh * 64:(hh + 1) * 64],
                                   tb[:, 128 + ty * 128 + hh * 64:128 + ty * 128 + (hh + 1) * 64], r=[tk], w=[("tmz", cc, hp, ty, hh)])
                    if DBG_STOP <= 1.47:
                        continue
                    P.copy(evq, W[2 * cc][:, 0, hp * 128:(hp + 1) * 128], tb[:, 384:512], r=[tk], w=[("w", 2 * cc, hp)])
            if DBG_STOP <= 2:
                return
            def amat(cc, lhs, lkey, rhs, rkey, mask, dest, dkey):
                csl = slice(cc * CH, (cc + 1) * CH)
                for g in range(4):
                    heads = HG[g]
                    bank, bk = banks.next()
                    for hi, h in enumerate(heads):
                        hp, off = h // 2, (h % 2) * 64
                        P.mm(bank[:, hi * 128:(hi + 1) * 128], lhs[off:off + 64, hp, csl], rhs[off:off + 64, hp, csl],
                             r=[(lkey, hp), (rkey, hp)], w=[bk])
                    nh = len(heads)
                    P.tt("dve", dest[:, heads[0]:heads[-1] + 1:2, :], bank[:, 0:nh * 128].rearrange("p (a b) -> p a b", a=nh),
                         mask[:].unsqueeze(1).to_broadcast([128, nh, 128]), ALU.mult, r=[bk], w=[dkey + (g,)])

            wst = {}
            for cc in range(NCH):
                amat(cc, BT, "bt", AT, "at", c.msu, BK[2 * cc], ("bk", cc, 0))
                amat(cc, AT, "at", BT, "bt", c.msl, AK[2 * cc], ("ak", cc, 0))
                amat(cc, KT, "kt", AT, "at", c.msu, AAK, ("aak",))
                for h0, nh in ((0, 8), (8, 4)):
                    bank, bk = banks.next()
                    for hi in range(nh):
                        h = h0 + hi
                        hp, off = h // 2, (h % 2) * 64
                        P.mm(bank[:, hi * 64:(hi + 1) * 64], AAK[:, h, :], TMV[cc][:, hp, off:off + 64],
                             r=[("aak", hgk(h)), ("tmv", cc, hp)], w=[bk])
                    P.copy("act", W[2 * cc][:, 1, h0 * 64:(h0 + nh) * 64], bank[:, 0:nh * 64], r=[bk], w=[("wu", 2 * cc, h0)])
                wst[cc] = [2 * cc, 2 * cc + 1, [("w", 2 * cc, hp) for hp in range(6)] + [("wu", 2 * cc, 0), ("wu", 2 * cc, 8)]]
            for lev in range(7):
                ai, ao = lev % 2, (lev + 1) % 2
                for cc in range(NCH):
                    wcur, wnxt, wk = wst[cc]
                    BKi, AKi, BKo, AKo = BK[2 * cc + ai], AK[2 * cc + ai], BK[2 * cc + ao], AK[2 * cc + ao]
                    for hg in range(3):
                        bank, bk = banks.next()
                        for hi in range(4):
                            h = hg * 4 + hi
                            for part in range(2):
                                o = bank[:, part * 256 + hi * 64:part * 256 + (hi + 1) * 64]
                                rhs = W[wcur][:, part, h * 64:(h + 1) * 64]
                                P.mm(o, c.ident_b[:], rhs, start=True, stop=False, r=wk, w=[bk])
                                P.mm(o, BKi[:, h, :], rhs, start=False, stop=True, r=[("bk", cc, ai, hgk(h))], w=[bk])
                        P.copy("act" if hg % 2 == 0 else "dve", W[wnxt][:, :, hg * 256:(hg + 1) * 256],
                               bank[:, :].rearrange("p (a b) -> p a b", a=2), r=[bk], w=[("wl", wnxt, hg)])
                    if lev < 6:
                        for g in range(4):
                            heads = HG[g]
                            nh = len(heads)
                            bank, bk = banks.next()
                            for hi, h in enumerate(heads):
                                P.mm(bank[:, hi * 128:(hi + 1) * 128], BKi[:, h, :], AKi[:, h, :],
                                     r=[("bk", cc, ai, g), ("ak", cc, ai, g)], w=[bk])
                            P.copy("dve" if g % 2 == 0 else "act", AKo[:, heads[0]:heads[-1] + 1:2, :],
                                   bank[:, 0:nh * 128].rearrange("p (a b) -> p a b", a=nh), r=[bk], w=[("ak", cc, ao, g)])
                            bank, bk = banks.next()
                            for hi, h in enumerate(heads):
                                P.mm(bank[:, hi * 128:(hi + 1) * 128], AKi[:, h, :], BKi[:, h, :],
                                     r=[("bk", cc, ai, g), ("ak", cc, ai, g)], w=[bk])
                            P.copy("act" if g % 2 == 0 else "dve", BKo[:, heads[0]:heads[-1] + 1:2, :],
                                   bank[:, 0:nh * 128].rearrange("p (a b) -> p a b", a=nh), r=[bk], w=[("bk", cc, ao, g)])
                    wst[cc] = [wnxt, wcur, [("wl", wnxt, hg) for hg in range(3)]]
            for cc in range(NCH):
                csl = slice(cc * CH, (cc + 1) * CH)
                wf, _, wfk = wst[cc]
                amat(cc, KT, "kt", RT, "rt", c.mui, ARK, ("ark",))
                amat(cc, BT, "bt", RT, "rt", c.mui, ARB, ("arb",))
                for p0, npp in ((0, 4), (4, 2)):
                    tb, tk = tbanks.next()
                    for pi in range(npp):
                        hp = p0 + pi
                        P.tr(tb[:, pi * 128:(pi + 1) * 128], W[wf][:, 0, hp * 128:(hp + 1) * 128], c.ident_b[:], r=wfk, w=[tk])
                    P.copy("dve", AHT[:, p0:p0 + npp, :], tb[:, 0:npp * 128].rearrange("p (a b) -> p a b", a=npp), r=[tk], w=[("aht", p0)])
                ahk = [("aht", 0), ("aht", 4)]
                for g in range(4):
                    heads = HG[g]
                    nh = len(heads)
                    bank, bk = banks.next()
                    for hi, h in enumerate(heads):
                        hp, off = h // 2, (h % 2) * 64
                        P.mm(bank[:, hi * 64:(hi + 1) * 64], AHT[off:off + 64, hp, :], SB[off:off + 64, hp, :], start=True, stop=False,
                             r=ahk + [("sbk",)], w=[bk])
                        P.mm(bank[:, hi * 64:(hi + 1) * 64], c.ident_b[:], W[wf][:, 1, h * 64:(h + 1) * 64], start=False, stop=True, r=wfk, w=[bk])
                    P.copy("act", UU[:, heads[0]:heads[-1] + 1:2, :], bank[:, 0:nh * 64].rearrange("p (a b) -> p a b", a=nh), r=[bk], w=[("uu", g)])
                uuk = [("uu", g) for g in range(4)]
                for p0, npp in ((0, 4), (4, 2)):
                    for hh in range(2):
                        off = hh * 64
                        bank, bk = banks.next()
                        for pi in range(npp):
                            hp = p0 + pi
                            h = 2 * hp + hh
                            o = bank[off:off + 64, pi * 128:(pi + 1) * 128]
                            P.mm(o, SB[off:off + 64, hp, :], RT[off:off + 64, hp, csl], start=True, stop=False, r=[("sbk",), ("rt", hp)], w=[bk])
                            P.mm(o, TMV[cc][:, hp, off:off + 64], ARK[:, h, :], start=False, stop=False, r=[("tmv", cc, hp), ("ark", hgk(h))], w=[bk])
                            P.mm(o, UU[:, h, :], ARB[:, h, :], start=False, stop=True, r=uuk + [("arb", hgk(h))], w=[bk])
                        P.copy("act", YB[off:off + 64, p0:p0 + npp, csl], bank[off:off + 64, 0:npp * 128].rearrange("p (a b) -> p a b", a=npp),
                               r=[bk], w=[("yb", cc, p0, hh)])
                bank, bk = banks.next()
                for hp in range(6):
                    o = bank[:, hp * 64:(hp + 1) * 64]
                    for hh in range(2):
                        off = hh * 64
                        h = 2 * hp + hh
                        P.mm(o, TMZ[cc][:, 0, h, :], TMV[cc][:, hp, off:off + 64], start=(hh == 0), stop=False,
                             r=[("tmz", cc, hp, 0, hh), ("tmz", cc, hp, 1, hh), ("tmv", cc, hp)], w=[bk])
                        P.mm(o, TMZ[cc][:, 1, h, :], UU[:, h, :], start=False, stop=(hh == 1), r=uuk, w=[bk])
                P.tt("dve", SF[:], SF[:], PC[:, :, cc:cc + 1].to_broadcast([128, 6, 64]), ALU.mult, r=[("pc", cc), ("sf",)], w=[("sf",)])
                P.tt("dve", SF[:], SF[:], bank[:, 0:384].rearrange("p (a b) -> p a b", a=6), ALU.add, r=[bk, ("sf",)], w=[("sf",)])
                P.copy("act", SB[:], SF[:], r=[("sf",)], w=[("sbk",)])
            if DBG_STOP <= 3:
                return
            ybk = [("yb", cc, p0, hh) for cc in range(NCH) for p0 in (0, 4) for hh in range(2)]
            for hp in range(6):
                YC, SQf, SD = T[0], T[1], T[2]
                bank, bk = banks.next()
                P.mm(bank[:, :n], c.bda_f[:], YB[:, hp, :], r=ybk, w=[bk])
                P.tt("dve", YC[:], YB[:, hp, :], bank[:, :n], ALU.subtract, r=ybk + [bk], w=[("sw",)])
                P.act(SQf[:], YC[:], AF.Square, r=[("sw",)], w=[("logw",)])
                bank, bk = banks.next()
                P.mm(bank[:, :n], c.bda_f[:], SQf[:], r=[("logw",)], w=[bk])
                P.act(SD[:], bank[:, :n], AF.Ln, r=[bk], w=[("alr",)], bias=c.eps[:, 1:2])
                P.act(SD[:], SD[:], AF.Exp, r=[("alr",)], w=[("alr",)], scale=-0.5)
                P.tt("dve", YC[:], YC[:], SD[:], ALU.mult, r=[("sw",), ("alr",)], w=[("sw",)])
                P.ts("dve", YC[:], YC[:], vcol("a_lnx_g", hp), vcol("a_lnx_b", hp), ALU.mult, ALU.add, r=[("sw",)], w=[("sw",)])
                P.tt("dve", YC[:], YC[:], BON[:, hp, :], ALU.add, r=[("sw",), ("bon", hp)], w=[("sw",)])
                P.tt("dve", YO[:, hp, :], YC[:], G[:, hp, :], ALU.mult, r=[("sw",), ("g", hp)], w=[("yo", hp)])
            if DBG_STOP <= 4:
                return
            mem_attn_tile(P, c, QM, [("qm", 0), ("qm", 1)], YM, [("ym", 0), ("ym", 1)], n, layer, V, banks, MSQ, MTMP, QN, PT, RD, "rma")
            if dbg is not None:
                P.copy("dve", YB[:], YO[:], r=[("yo", hp) for hp in range(6)], w=ybk)
                P.dma("sp", dbg[0:768, t * TR:(t + 1) * TR].rearrange("(k p) n -> p k n", p=128), YB[:], "dbg0", r=ybk)
                P.copy("dve", QM[:], YM[:], r=[("ym", 0), ("ym", 1)], w=[("qm", 0), ("qm", 1)])
                P.dma("sp", dbg[768:1024, t * TR:(t + 1) * TR].rearrange("(k p) n -> p k n", p=128), QM[:], "dbg1", r=[("qm", 0), ("qm", 1)])
            for m in range(KD):
                bank, bk = banks.next()
                for k in range(KD):
                    rhs = YO[:, k, :] if k < 6 else YM[:, k - 6, :]
                    rk = ("yo", k) if k < 6 else ("ym", k - 6)
                    P.mm(bank[:, :n], WO[:, k, m * 128:(m + 1) * 128], rhs, start=(k == 0), stop=(k == KD - 1), r=[("wo", k), rk], w=[bk])
                P.tt("dve", XT[:, m, :], XT[:, m, :], bank[:, :n], ALU.add, r=[bk, ("xt",)], w=[("xt",)])
            P.dma("sp", dst[:, t * TR:(t + 1) * TR].rearrange("(k p) n -> p k n", p=128), XT[:], "xs0",
                  r=[("xt",)], w=[("X", dst.tensor.name, t)])

        for t in range(min(NTR, DBG_NTR)):
            do_tile(t)
        P.barrier()


DIL = (1, 4, 16)
MASKV = -30000.0


def flat_head_rms(P, c, bank, bk, n, KFt, SQt, RS, banks, tag, pb):
    P.copy("act", KFt[pb][0:64, :n], bank[0:64, :n], r=[bk], w=[(tag, "kf", pb)])
    P.act(SQt[pb][0:64, :n], KFt[pb][0:64, :n], AF.Square, r=[(tag, "kf", pb)], w=[(tag, "sq", pb)])
    b2, k2 = banks.next()
    P.mm(b2[0:64, :n], c.ones_b[0:64, 0:64], SQt[pb][0:64, :n], r=[(tag, "sq", pb)], w=[k2])
    P.act(RS[pb][0:64, :n], b2[0:64, :n], AF.Ln, r=[k2], w=[(tag, "rs", pb)], bias=c.eps[0:64, 0:1], scale=1.0 / 64)
    P.act(RS[pb][0:64, :n], RS[pb][0:64, :n], AF.Exp, r=[(tag, "rs", pb)], w=[(tag, "rs", pb)], scale=-0.5)


def kv_phase(P, nc, c, src, kvw_d, KT_d, V_d, V):
    with ExitStack() as es:
        def sb(name, shape, dt):
            return es.enter_context(nc.sbuf_tensor(_u("kv_" + name), shape, dt))
        WKV = sb("w", [128, KD, 1536], BF16)
        XTL = [sb("xt%d" % i, [128, KD, TT], F32) for i in range(2)]
        XN = sb("xn", [128, KD, TT], BF16)
        SQ = [sb("sq%d" % i, [128, TT], BF16) for i in range(2)]
        RSTD = sb("rstd", [128, TT], F32)
        KFt = [sb("kf%d" % i, [64, TT], F32) for i in range(2)]
        SQt = [sb("sqt%d" % i, [64, TT], BF16) for i in range(2)]
        RS = [sb("rs%d" % i, [64, TT], F32) for i in range(2)]
        KTA = sb("kta", [64, 12, S], BF16)
        VT = [sb("vt%d" % i, [128, 768], BF16) for i in range(2)]
        pbt = [es.enter_context(nc.psum_tensor(_u("kv_pb%d" % i), [128, 512], F32)) for i in range(8)]
        banks = Banks(pbt, "kpb")
        for k0 in range(0, KD, 2):
            P.dma("pool", WKV[:, k0:k0 + 2], kvw_d[k0 * 128:(k0 + 2) * 128, :].rearrange("(k p) c -> p k c", p=128), "wq%d" % (k0 // 2),
                  w=[("kvw", k) for k in range(k0, k0 + 2)])
        gk = V["kv_k_norm"]

        def kload(t):
            P.dma("sp", XTL[t % 2][:], src[:, t * TT:(t + 1) * TT].rearrange("(k p) n -> p k n", p=128), "xl%d" % (t % 2), w=[("xt", t % 2)])
        kload(0)
        for t in range(NT):
            n = TT
            if t + 1 < NT:
                kload(t + 1)
            XT = XTL[t % 2]
            rb, rbk = banks.next()
            rms_tile(P, c, XT, ("xt", t % 2), XN, ("xn",), V["kv_norm"], n, rb, rbk, SQ, RSTD, "kv")
            for h in range(12):
                dil = DIL[h // 4]
                pb = h % 2
                bank, bk = banks.next()
                for k in range(KD):
                    P.mm(bank[0:64, :n], WKV[:, k, h * 64:(h + 1) * 64], XN[:, k, :], start=(k == 0), stop=(k == KD - 1),
                         r=[("kvw", k), ("xn", k)], w=[bk])
                flat_head_rms(P, c, bank, bk, n, KFt, SQt, RS, banks, "kvh", pb)
                dst = KTA[0:64, h, :].rearrange("p (c i) -> p c i", c=dil)[:, :, t * (TT // dil):(t + 1) * (TT // dil)]
                P.stt("dve", dst, KFt[pb][0:64, :n].rearrange("p (i c) -> p c i", c=dil), c.vecs[0:64, gk:gk + 1],
                      RS[pb][0:64, :n].rearrange("p (i c) -> p c i", c=dil), ALU.mult, ALU.mult,
                      r=[("kvh", "kf", pb), ("kvh", "rs", pb)], w=[("kta", h, t)])
            for s4 in range(TT // 128):
                vb = VT[s4 % 2]
                b1, k1 = banks.next()
                for k in range(KD):
                    P.mm(b1[:, 0:512], XN[:, k, s4 * 128:(s4 + 1) * 128], WKV[:, k, 768:1280], start=(k == 0), stop=(k == KD - 1),
                         r=[("kvw", k), ("xn", k)], w=[k1])
                P.copy("act", vb[:, 0:512], b1[:, 0:512], r=[k1], w=[("vt", s4 % 2, 0)])
                b2, k2 = banks.next()
                for k in range(KD):
                    P.mm(b2[:, 0:256], XN[:, k, s4 * 128:(s4 + 1) * 128], WKV[:, k, 1280:1536], start=(k == 0), stop=(k == KD - 1),
                         r=[("kvw", k), ("xn", k)], w=[k2])
                P.copy("dve", vb[:, 512:768], b2[:, 0:256], r=[k2], w=[("vt", s4 % 2, 1)])
                P.dma("sp", V_d[t * TT + s4 * 128:t * TT + (s4 + 1) * 128, :], vb[:], "vs%d" % (s4 % 2),
                      r=[("vt", s4 % 2, 0), ("vt", s4 % 2, 1)])
        for h in range(12):
            P.dma("sp", KT_d[:, h, :], KTA[0:64, h, :], "ks%d" % (h % 2), r=[("kta", h, t) for t in range(NT)])
        P.barrier()


def attn_phase(P, nc, c, src, dst, j, layer, wq_d, wo_d, KT_d, V_d, relb_d, sel_d, E_d, V):
    HALF = S // 2
    NTH = HALF // TR
    with ExitStack() as es:
        def sb(name, shape, dt):
            return es.enter_context(nc.sbuf_tensor(_u("at_" + name), shape, dt))
        WQ = sb("wq", [128, KD, D], BF16)
        WO = sb("wo", [128, 4, D], BF16)
        KTG = sb("ktg", [64, 4, S], BF16)
        VZ = sb("vz", [128, 32, 4, 128], BF16)
        QG = sb("qg", [64, 4, HALF], BF16)
        ACN = sb("acn", [128, 2, HALF], F32)
        ACD = sb("acd", [128, 2, HALF], F32)
        XTL = [sb("xt%d" % i, [128, KD, TR], F32) for i in range(2)]
        XN = sb("xn", [128, KD, TR], BF16)
        SQ = [sb("sq%d" % i, [128, TR], BF16) for i in range(2)]
        RSTD = sb("rstd", [128, TR], F32)
        KFt = [sb("kf%d" % i, [64, TR], F32) for i in range(2)]
        SQt = [sb("sqt%d" % i, [64, TR], BF16) for i in range(2)]
        RS = [sb("rs%d" % i, [64, TR], F32) for i in range(2)]
        BM = sb("bm", [128, 12, 256], F32)
        TAB = sb("tab", [33, 12], F32)
        SEL = sb("sel", [33, 3, 510], F32)
        ESB = sb("esb", [12, 510], F32)
        HK = [sb("hk%d" % i, [128, 128], F32) for i in range(2)]
        LG = [sb("lg%d" % i, [128, 256], F32) for i in range(2)]
        PTb = [sb("ptb%d" % i, [128, 256], BF16) for i in range(4)]
        OB = sb("ob", [128, 2, TR], BF16)
        QM = sb("qm", [128, 2, TR], F32)
        YM = sb("ym", [128, 2, TR], BF16)
        QN = sb("qn", [128, 2, TR], BF16)
        PT = [sb("pt%d" % i, [128, TR], BF16) for i in range(4)]
        RD = sb("rd", [128, TR], F32)
        MSQ = sb("msq", [128, TR], BF16)
        MTMP = sb("mtmp", [128, TR], F32)
        pbt = [es.enter_context(nc.psum_tensor(_u("at_pb%d" % i), [128, 512], F32)) for i in range(8)]
        banks = Banks(pbt, "apb")
        print("[kernel] attn sbuf free", nc.sbuf_bytes_remaining)
        xcnt = [0]

        def xload(t):
            b = xcnt[0] % 2
            xcnt[0] += 1
            P.dma("sp", XTL[b][:], src[:, t * TR:(t + 1) * TR].rearrange("(k p) n -> p k n", p=128), "xl%d" % b,
                  r=[("X", src.tensor.name, t)], w=[("xt", b)])
            return b
        for k0 in range(0, KD, 2):
            P.dma("pool", WQ[:, k0:k0 + 2], wq_d[k0 * 128:(k0 + 2) * 128, :].rearrange("(k p) c -> p k c", p=128), "wq%d" % (k0 // 2),
                  w=[("wq", k) for k in range(k0, k0 + 2)])
        P.dma("pool", WO[:], wo_d.rearrange("(k p) c -> p k c", p=128), "wq0", w=[("wo",)])
        P.add("dve", lambda e: e.memset(TAB[:], MASKV), w=[("tab",)])
        P.dma("sp", TAB[0:32, :], relb_d[:, :], "xl1", w=[("tab", 1)], r=[("tab",)])
        P.dma("sp", SEL[:], sel_d.rearrange("g b n -> b g n"), "xs1", w=[("sel",)])
        for g in range(3):
            bank, bk = banks.next()
            P.mm(bank[0:12, 0:510], TAB[:, :], SEL[:, g, :], r=[("tab",), ("tab", 1), ("sel",)], w=[bk])
            P.copy("act", ESB[:], bank[0:12, 0:510], r=[bk], w=[("esb",)])
            P.dma("sp", E_d[g], ESB[:], "vs0", r=[("esb",)], w=[("E", g)])
            for h in range(4):
                for role in range(2):
                    i = (h * 2 + role) % 2
                    srcap = bass.AP(tensor=E_d.tensor, offset=g * 12 * 510 + (4 * g + h) * 510 + role * 255, ap=[[1, 128], [1, 128]])
                    P.dma("sp", HK[i][:], srcap, "xl%d" % i, r=[("E", g)], w=[("hk", i)])
                    bank, bk = banks.next()
                    P.mm(bank[:, 0:128], c.jf[:], HK[i][:], r=[("hk", i)], w=[bk])
                    P.copy("act", BM[:, 4 * g + h, role * 128:(role + 1) * 128], bank[:, 0:128], r=[bk], w=[("bm", g, h, role)])
        P.add("dve", lambda e: e.memset(VZ[:], 0.0), w=[("vz", 0), ("vz", 1)])
        gq = V["b_q_norm%d" % j]
        for H in range(2):
            P.add("dve", lambda e: e.memset(ACN[:], 0.0), w=[("acn",)])
            P.add("dve", lambda e: e.memset(ACD[:], 0.0), w=[("acd",)])
            for g in range(3):
                dil = DIL[g]
                nb = S // (dil * 128)
                nbh = nb // 2
                SL = S // dil
                HL = HALF // dil
                P.dma("sp", KTG[:], KT_d[:, 4 * g:4 * g + 4, :], "xl0", w=[("ktg",)])
                for h in range(4):
                    hh = h % 2
                    vsrc = bass.AP(tensor=V_d.tensor, offset=(4 * g + h) * 64,
                                   ap=[[dil * 768, 128], [768, dil], [128 * dil * 768, nb], [1, 64]])
                    P.dma("sp" if h % 2 == 0 else "act", VZ[:, 0:dil * nb, h, hh * 64:(hh + 1) * 64].rearrange("p (c n) d -> p c n d", c=dil),
                          vsrc, "vl%d" % h, w=[("vzh", h)], r=[("vz", 0)])
                vzk = [("vzh", h) for h in range(4)]
                nxt = xload(H * NTH)
                for tt in range(NTH):
                    t = H * NTH + tt
                    n = TR
                    b = nxt
                    if tt + 1 < NTH:
                        nxt = xload(t + 1)
                    XT = XTL[b]
                    rb, rbk = banks.next()
                    rms_tile(P, c, XT, ("xt", b), XN, ("xn",), V["mix_norm%d" % layer], n, rb, rbk, SQ, RSTD, "at")
                    for h in range(4):
                        hq = 4 * g + h
                        pb = h % 2
                        bank, bk = banks.next()
                        for k in range(KD):
                            P.mm(bank[0:64, :n], WQ[:, k, hq * 64:(hq + 1) * 64], XN[:, k, :], start=(k == 0), stop=(k == KD - 1),
                                 r=[("wq", k), ("xn", k)], w=[bk])
                        flat_head_rms(P, c, bank, bk, n, KFt, SQt, RS, banks, "ath", pb)
                        dq = QG[0:64, h, :].rearrange("p (c i) -> p c i", c=dil)[:, :, tt * (TR // dil):(tt + 1) * (TR // dil)]
                        P.stt("dve", dq, KFt[pb][0:64, :n].rearrange("p (i c) -> p c i", c=dil), c.vecs[0:64, gq:gq + 1],
                              RS[pb][0:64, :n].rearrange("p (i c) -> p c i", c=dil), ALU.mult, ALU.mult,
                              r=[("ath", "kf", pb), ("ath", "rs", pb)], w=[("qg", h, tt)])
                qgk = [("qg", h, tt) for h in range(4) for tt in range(NTH)]
                for cidx in range(dil):
                    for nl in range(nbh):
                        nblk = H * nbh + nl
                        qcol = cidx * HL + nl * 128
                        for hp in range(2):
                            pts = []
                            for hh in range(2):
                                h = 2 * hp + hh
                                bank, bk = banks.next()
                                kcol = cidx * SL + nblk * 128
                                P.mm(bank[:, 0:128], KTG[0:64, h, kcol:kcol + 128], QG[0:64, h, qcol:qcol + 128], r=[("ktg",)] + qgk, w=[bk])
                                wcols = 128
                                if nblk > 0:
                                    P.mm(bank[:, 128:256], KTG[0:64, h, kcol - 128:kcol], QG[0:64, h, qcol:qcol + 128], r=[("ktg",)] + qgk, w=[bk])
                                    wcols = 256
                                li = (hp * 2 + hh) % 2
                                P.stt("dve", LG[li][:, 0:wcols], bank[:, 0:wcols], 0.125, BM[:, 4 * g + h, 0:wcols], ALU.mult, ALU.add,
                                      r=[bk] + [("bm", g, h, r_) for r_ in range(2)], w=[("lg", li)])
                                pi = hp * 2 + hh
                                P.act(PTb[pi][:, 0:wcols], LG[li][:, 0:wcols], AF.Exp, r=[("lg", li)], w=[("ptb", pi)])
                                pts.append((pi, h, hh, wcols))
                            bn, kn = banks.next()
                            bd, kd = banks.next()
                            mms = []
                            for (pi, h, hh, wcols) in pts:
                                mms.append((VZ[:, cidx * nb + nblk, h, :], c.onesh[hh], PTb[pi][:, 0:128], pi, h))
                                if wcols == 256:
                                    mms.append((VZ[:, cidx * nb + nblk - 1, h, :], c.onesh[hh], PTb[pi][:, 128:256], pi, h))
                            for i, (vz, oh, rhs, pi, h) in enumerate(mms):
                                P.mm(bn[:, 0:128], vz, rhs, start=(i == 0), stop=(i == len(mms) - 1), r=[("ptb", pi), ("vzh", h)], w=[kn])
                            for i, (vz, oh, rhs, pi, h) in enumerate(mms):
                                P.mm(bd[:, 0:128], oh, rhs, start=(i == 0), stop=(i == len(mms) - 1), r=[("ptb", pi)], w=[kd])
                            an = ACN[:, hp, :].rearrange("p (i c) -> p c i", c=dil)[:, cidx, nl * 128:(nl + 1) * 128]
                            ad = ACD[:, hp, :].rearrange("p (i c) -> p c i", c=dil)[:, cidx, nl * 128:(nl + 1) * 128]
                            P.tt("dve", an, an, bn[:, 0:128], ALU.add, r=[kn, ("acn",)], w=[("acn",)])
                            P.tt("act" if False else "dve", ad, ad, bd[:, 0:128], ALU.add, r=[kd, ("acd",)], w=[("acd",)])
            nxt = xload(H * NTH)
            for tt in range(NTH):
                t = H * NTH + tt
                n = TR
                b = nxt
                if tt + 1 < NTH:
                    nxt = xload(t + 1)
                XT = XTL[b]
                xk = ("xt", b)
                tsl = slice(tt * TR, (tt + 1) * TR)
                P.act(ACD[:, :, tsl], ACD[:, :, tsl], AF.Ln, r=[("acd",)], w=[("acd",)])
                P.act(ACD[:, :, tsl], ACD[:, :, tsl], AF.Exp, r=[("acd",)], w=[("acd",)], scale=-1.0)
                P.tt("dve", OB[:], ACN[:, :, tsl], ACD[:, :, tsl], ALU.mult, r=[("acn",), ("acd",)], w=[("ob",)])
                rb, rbk = banks.next()
                rms_tile(P, c, XT, xk, XN, ("xn",), V["mix_norm%d" % layer], n, rb, rbk, SQ, RSTD, "at")
                for q in range(2):
                    bq, kq = banks.next()
                    for k in range(KD):
                        P.mm(bq[:, :n], WQ[:, k, 768 + q * 128:768 + (q + 1) * 128], XN[:, k, :], start=(k == 0), stop=(k == KD - 1),
                             r=[("wq", k), ("xn", k)], w=[kq])
                    P.copy("act", QM[:, q, :], bq[:, :n], r=[kq], w=[("qm", q)])
                mem_attn_tile(P, c, QM, [("qm", 0), ("qm", 1)], YM, [("ym", 0), ("ym", 1)], n, layer, V, banks, MSQ, MTMP, QN, PT, RD, "ama")
                for m in range(KD):
                    bank, bk = banks.next()
                    for k in range(4):
                        rhs = OB[:, k, :] if k < 2 else YM[:, k - 2, :]
                        rk = ("ob",) if k < 2 else ("ym", k - 2)
                        P.mm(bank[:, :n], WO[:, k, m * 128:(m + 1) * 128], rhs, start=(k == 0), stop=(k == 3), r=[("wo",), rk], w=[bk])
                    P.tt("dve", XT[:, m, :], XT[:, m, :], bank[:, :n], ALU.add, r=[bk, xk], w=[xk])
                P.dma("sp", dst[:, t * TR:(t + 1) * TR].rearrange("(k p) n -> p k n", p=128), XT[:], "xs%d" % b,
                      r=[xk], w=[("X", dst.tensor.name, t)])
        P.barrier()


def vec_layout():
    V = {}
    off = 0

    def put(name, n):
        nonlocal off
        V[name] = off
        off += n
    for i in range(8):
        put("ffn%d" % i, KD)
    for l in range(4):
        put("mix_norm%d" % l, KD)
        put("mem_norm%d" % l, KD)
        put("mem_q_norm%d" % l, 1)
        put("mem_k_norm%d" % l, 1)
    for i in range(2):
        put("a_mu%d" % i, 20)
        for nm in ("a_w0", "a_a0", "a_kk_scale", "a_k_a", "a_r_k", "a_lnx_g", "a_lnx_b"):
            put(nm + "%d" % i, 6)
    for j in range(2):
        put("b_q_norm%d" % j, 1)
    put("kv_norm", KD)
    put("kv_k_norm", 1)
    return V, off


NCONST = 128 * 8 + 256
NKF = NCONST - 384


def build(n_stages=99, dbg_on=False):
    nc = bass.Bass("TRN2", target_bir_lowering=False)
    es = ExitStack()
    c = Ctx()
    V, NV = vec_layout()
    xT = nc.dram_tensor("xT", [D, S], F32, kind="ExternalInput").ap()
    memT = nc.dram_tensor("memT", [D, 256], F32, kind="ExternalInput").ap()
    outT = nc.dram_tensor("outT", [D, S], F32, kind="ExternalOutput").ap()
    dbg = nc.dram_tensor("dbg", [D, S], F32, kind="ExternalOutput").ap() if dbg_on else None
    XS = nc.dram_tensor("xs_scratch", [D, S], F32, kind="Internal").ap()
    vecs_d = nc.dram_tensor("vecs", [128, NV], F32, kind="ExternalInput").ap()
    consts_d = nc.dram_tensor("consts", [128, NCONST + 256], F32, kind="ExternalInput").ap()
    w_in_d = [nc.dram_tensor("w_in%d" % i, [2 * KF, 128, KD, 128], F32, kind="ExternalInput").ap() for i in range(8)]
    w_out_d = [nc.dram_tensor("w_out%d" % i, [FF, D], F32, kind="ExternalInput").ap() for i in range(8)]
    a_w_in_d = [nc.dram_tensor("a_w_in%d" % i, [22, 128, KD, 128], F32, kind="ExternalInput").ap() for i in range(2)]
    a_w_out_d = [nc.dram_tensor("a_w_out%d" % i, [D, D], F32, kind="ExternalInput").ap() for i in range(2)]
    a_wup_d = [nc.dram_tensor("a_w_up%d" % i, [64, 768], F32, kind="ExternalInput").ap() for i in range(2)]
    a_aup_d = [nc.dram_tensor("a_a_up%d" % i, [64, 768], F32, kind="ExternalInput").ap() for i in range(2)]
    a_gup_d = [nc.dram_tensor("a_g_up%d" % i, [128, 768], F32, kind="ExternalInput").ap() for i in range(2)]
    wkv_d = [nc.dram_tensor("mem_w_kv%d" % l, [D, 512], F32, kind="ExternalInput").ap() for l in range(4)]
    b_wq_d = [nc.dram_tensor("b_w_q%d" % i, [D, D], F32, kind="ExternalInput").ap() for i in range(2)]
    b_wo_d = [nc.dram_tensor("b_w_out%d" % i, [512, D], F32, kind="ExternalInput").ap() for i in range(2)]
    kvw_d = nc.dram_tensor("kv_w", [D, 1536], F32, kind="ExternalInput").ap()
    relb_d = nc.dram_tensor("rel_bias", [32, 12], F32, kind="ExternalInput").ap()
    sel_d = nc.dram_tensor("sel", [3, 33, 510], F32, kind="ExternalInput").ap()
    KT_d = nc.dram_tensor("kt_scratch", [64, 12, S], BF16, kind="Internal").ap()
    V_d = nc.dram_tensor("v_scratch", [S, 768], BF16, kind="Internal").ap()
    E_d = nc.dram_tensor("e_scratch", [3, 12, 510], F32, kind="Internal").ap()

    P = Prog(nc, es)

    def sbt(name, shape, dt):
        return es.enter_context(nc.sbuf_tensor(_u(name), shape, dt))
    c.vecs = sbt("c_vecs", [128, NV], F32)
    c.KF = sbt("c_kf", [128, NKF], F32)
    c.KB = sbt("c_kb", [128, 5 * 128], BF16)
    c.eps = sbt("c_eps", [128, 4], F32)
    c.MK = sbt("c_mk", [128, 2, 256], BF16)
    c.MVZ = sbt("c_mvz", [128, 2, 4, 128], BF16)
    P.dma("sp", c.vecs[:], vecs_d[:, :], "c0", w=[("vecs",)])
    P.dma("sp", c.KF[:], consts_d[:, 384:NCONST], "c1", w=[("kf",)])
    P.dma("pool", c.KB[:, 0:384], consts_d[:, 0:384], "c2", w=[("kb",)])
    P.dma("pool", c.KB[:, 384:640], consts_d[:, NCONST:NCONST + 256], "c3", w=[("kb2",)])
    P.add("dve", lambda e: e.memset(c.eps[:, 0:1], NORM_EPS), w=[("eps", 0)])
    P.add("dve", lambda e: e.memset(c.eps[:, 1:2], LNX_EPS), w=[("eps", 1)])
    P.add("dve", lambda e: e.memset(c.eps[:, 2:3], 1e-30), w=[("eps", 2)])
    P.add("dve", lambda e: e.memset(c.MVZ[:], 0.0), w=[("MVZ",)])
    c.bda_f = c.KF[:, 0:128]
    c.msu = c.KF[:, 128:256]
    c.msl = c.KF[:, 256:384]
    c.mui = c.KF[:, 384:512]
    c.cmask = c.KF[:, 512:768]
    c.jf = c.KF[:, 768:896]
    c.ident_b = c.KB[:, 0:128]
    c.ones_b = c.KB[:, 128:256]
    c.bd_b = c.KB[:, 256:384]
    c.ones_bf = c.ones_b
    c.onesh = [c.KB[:, 384:512], c.KB[:, 512:640]]
    P.barrier()

    stages = []
    for layer in range(4):
        stages.append(("ffn", 2 * layer))
        stages.append(("mix", layer))
        stages.append(("ffn", 2 * layer + 1))
        if layer == 1:
            stages.append(("kv", 0))
    stages = stages[:n_stages]
    cur = xT
    for si, (kind, i) in enumerate(stages):
        last = si == len(stages) - 1
        dst = outT if last else XS
        if kind == "ffn":
            ffn_phase(P, nc, c, cur, dst, w_in_d[i], w_out_d[i], gcol=V["ffn%d" % i])
        elif kind == "kv":
            kv_phase(P, nc, c, cur, kvw_d, KT_d, V_d, V)
            continue
        else:
            layer = i
            mem_prep(P, nc, c, memT, wkv_d[layer], layer, V)
            if layer < 2:
                rwkv_phase(P, nc, c, cur, dst, layer, layer, a_w_in_d[layer], a_w_out_d[layer], a_wup_d[layer], a_aup_d[layer],
                           a_gup_d[layer], V, dbg=dbg if last else None)
            else:
                attn_phase(P, nc, c, cur, dst, layer - 2, layer, b_wq_d[layer - 2], b_wo_d[layer - 2], KT_d, V_d, relb_d, sel_d, E_d, V)
        cur = dst
    P.barrier()
    es.close()
    print("[kernel] ops=%d instr=%d" % (P.nops, P.ninstr))
    return nc


def _rep2(v):
    return np.ascontiguousarray(np.concatenate([v, v]).reshape(128, 1))


def _t5_bucket(dist):
    dist = np.asarray(dist, np.int64)
    d_f = np.maximum(dist, 1).astype(np.float32)
    large = 16 + (np.log(d_f / np.float32(16)) / np.float32(np.log(2048 / 16)) * np.float32(16)).astype(np.int32)
    large = np.minimum(large, 31)
    return np.where(dist < 16, dist, large)


def make_sel():
    sel = np.zeros((3, 33, 510), np.float32)
    n = np.arange(255)
    for g, dil in enumerate((1, 4, 16)):
        own_valid = n >= 127
        bo = np.where(own_valid, _t5_bucket(np.maximum(n - 127, 0) * dil), 32)
        prev_valid = n <= 127
        bp = np.where(prev_valid, _t5_bucket((n + 1) * dil), 32)
        sel[g, bo, n] = 1.0
        sel[g, bp, 255 + n] = 1.0
    return sel


def make_consts():
    K = np.zeros((128, NCONST + 256), np.float32)
    K[:, NCONST:NCONST + 64] = 1.0
    K[:, NCONST + 192:NCONST + 256] = 1.0
    K[:, 0:128] = np.eye(128)
    K[:, 128:256] = 1.0
    bd = np.zeros((128, 128), np.float32)
    bd[:64, :64] = 1.0
    bd[64:, 64:] = 1.0
    K[:, 256:384] = bd
    K[:, 384:512] = bd / 64.0
    i = np.arange(128)
    K[:, 512:640] = (i[:, None] < i[None, :])
    K[:, 640:768] = (i[:, None] > i[None, :])
    K[:, 768:896] = (i[:, None] <= i[None, :])
    cm = np.ones(256, np.float32)
    cm[::128] = 0.0
    K[:, 896:896 + 256] = cm[None, :]
    K[:, 1152:1280] = np.eye(128)[::-1]
    return K


def kernel(**inputs):
    n_stages = int(inputs.pop("_n_stages", 99))
    cores = inputs.pop("_cores", list(range(8)))
    trace = inputs.pop("_trace", False)
    dbg_on = inputs.pop("_dbg", False)
    f = lambda a: np.asarray(a, dtype=np.float32)
    x = f(inputs["x"])
    mem = f(inputs["mem"])
    V, NV = vec_layout()
    vecs = np.zeros((128, NV), np.float32)

    def put(name, arr):
        arr = np.asarray(arr, np.float32)
        vecs[:, V[name]:V[name] + arr.shape[1]] = arr
    shared = {}
    for l in range(4):
        for nm in ("ffn_pre", "ffn_post"):
            i = 2 * l + (0 if nm == "ffn_pre" else 1)
            shared["w_in%d" % i] = _slots_in(f(inputs[nm + "_w_in"][l]))
            shared["w_out%d" % i] = np.ascontiguousarray(f(inputs[nm + "_w_out"][l]))
            put("ffn%d" % i, _vec_pk(f(inputs[nm + "_norm"][l])))
        put("mix_norm%d" % l, _vec_pk(f(inputs["mix_norm"][l])))
        put("mem_norm%d" % l, _vec_pk(f(inputs["mem_norm"][l])))
        put("mem_q_norm%d" % l, _rep2(f(inputs["mem_q_norm"][l])))
        put("mem_k_norm%d" % l, _rep2(f(inputs["mem_k_norm"][l])))
        shared["mem_w_kv%d" % l] = np.ascontiguousarray(f(inputs["mem_w_kv"][l]))
    for i in range(2):
        shared["a_w_in%d" % i] = _slots_in(f(inputs["a_w_in"][i]))
        shared["a_w_out%d" % i] = np.ascontiguousarray(f(inputs["a_w_out"][i]))
        shared["a_w_up%d" % i] = np.ascontiguousarray(f(inputs["a_w_up"][i]))
        shared["a_a_up%d" % i] = np.ascontiguousarray(f(inputs["a_a_up"][i]))
        shared["a_g_up%d" % i] = np.ascontiguousarray(f(inputs["a_g_up"][i]))
        put("a_mu%d" % i, _vec_pk(f(inputs["a_shift_mu"][i])))
        put("a_w0%d" % i, _vec_pk(f(inputs["a_w0"][i])))
        put("a_a0%d" % i, _vec_pk(f(inputs["a_a0"][i])))
        put("a_kk_scale%d" % i, _vec_pk(f(inputs["a_kk_scale"][i])))
        put("a_k_a%d" % i, _vec_pk(f(inputs["a_k_a"][i])))
        put("a_r_k%d" % i, _vec_pk(f(inputs["a_r_k"][i]).reshape(-1)))
        put("a_lnx_g%d" % i, _vec_pk(f(inputs["a_lnx_g"][i])))
        put("a_lnx_b%d" % i, _vec_pk(f(inputs["a_lnx_b"][i])))
    for j in range(2):
        put("b_q_norm%d" % j, _rep2(f(inputs["b_q_norm"][j])))
    put("kv_norm", _vec_pk(f(inputs["kv_norm"])))
    put("kv_k_norm", _rep2(f(inputs["kv_k_norm"])))
    for jj in range(2):
        shared["b_w_q%d" % jj] = np.ascontiguousarray(f(inputs["b_w_q"][jj]))
        shared["b_w_out%d" % jj] = np.ascontiguousarray(f(inputs["b_w_out"][jj]))
    shared["kv_w"] = np.ascontiguousarray(f(inputs["kv_w"]))
    shared["rel_bias"] = np.ascontiguousarray(f(inputs["rel_bias"]))
    shared["sel"] = make_sel()
    shared["vecs"] = vecs
    shared["consts"] = make_consts()
    nc = build(n_stages, dbg_on)
    in_maps = []
    for b in cores:
        m = dict(shared)
        m["xT"] = np.ascontiguousarray(x[b].T)
        m["memT"] = np.ascontiguousarray(mem[b].T)
        in_maps.append(m)
    if trace:
        res = run_bass_kernel_spmd(nc, in_maps, core_ids=list(range(len(cores))), trace=True)
        print("[kernel] exec_time_ns", res.exec_time_ns)
    else:
        res = run_bass_kernel_spmd(nc, in_maps, core_ids=list(range(len(cores))))
    if dbg_on:
        kernel.dbg = [np.ascontiguousarray(r["dbg"].T) for r in res.results]
    out = np.stack([np.ascontiguousarray(r["outT"].T) for r in res.results], axis=0)
    return out.astype(np.float32)
```

```python
import os
import numpy as np
from contextlib import ExitStack
import concourse.bass as bass
import concourse.mybir as mybir
from concourse.bass_utils import run_bass_kernel_spmd

F32 = mybir.dt.float32
BF16 = mybir.dt.bfloat16
AF = mybir.ActivationFunctionType
ALU = mybir.AluOpType
AX = mybir.AxisListType

D = 1024
KD = 8
S = 4096
FF = 2816
KF = 22
TT = 512
NT = S // TT
NORM_EPS = 1e-6

COMPUTE = ("pe", "act", "dve", "pool")


class Prog:
    def __init__(self, nc, es):
        self.nc = nc
        self.eng = dict(pe=nc.tensor, act=nc.scalar, dve=nc.vector, pool=nc.gpsimd, sp=nc.sync)
        self.sem = {e: es.enter_context(nc.semaphore("s_" + e)) for e in COMPUTE}
        self.cnt = {e: 0 for e in COMPUTE}
        self.es = es
        self.dsem = {}
        self.dcnt = {}
        self.pending = []
        self.last_w = {}
        self.readers = {}
        self.waited = {e: {} for e in self.eng}
        self.done = {}
        self.sigs = {e: [] for e in COMPUTE}
        self.nops = 0
        self.ninstr = 0
        self.defer = None
        self.attach = os.environ.get('KATTACH', '1') == '1'

    def add(self, eng, fn, r=(), w=(), dma=None):
        idx = self.nops
        self.nops += 1
        deps = set()
        for k in r:
            j = self.last_w.get(k)
            if j is not None:
                deps.add(j)
        for k in w:
            j = self.last_w.get(k)
            if j is not None:
                deps.add(j)
            rd = self.readers.get(k)
            if rd:
                deps.update(rd.values())
        for k in w:
            self.last_w[k] = idx
            self.readers[k] = {}
        tag = ("d", dma) if dma else ("c", eng)
        for k in r:
            if k not in w:
                self.readers.setdefault(k, {})[tag] = idx
        self.pending.append(dict(idx=idx, eng=eng, fn=fn, deps=deps, dma=dma, sig=False))
        return idx

    def _wait(self, eng, semname, semh, val):
        if self.waited[eng].get(semname, 0) >= val:
            return
        self.waited[eng][semname] = val
        if self.defer is not None:
            self.defer[semname] = (semh, max(val, self.defer.get(semname, (None, 0))[1]))
            return
        self.eng[eng].wait_ge(semh, val)
        self.ninstr += 1

    def flush(self):
        pend = self.pending
        self.pending = []
        byidx = {op["idx"]: op for op in pend}
        for op in pend:
            for j in op["deps"]:
                d = byidx.get(j)
                if d is not None and d["dma"] is None:
                    if not (d["eng"] == "pe" and op["eng"] == "pe" and op["dma"] is None):
                        d["sig"] = True
        last = {}
        for op in pend:
            if op["dma"] is None:
                last[op["eng"]] = op
        for op in last.values():
            op["sig"] = True
        for op in pend:
            eng = op["eng"]
            self.defer = {} if op["dma"] is None else None
            for j in sorted(op["deps"]):
                if j in self.done:
                    kind, name, val = self.done[j]
                    if kind == "d":
                        self._wait(eng, "d_" + name, self.dsem[name], val)
                    else:
                        if name == "pe" and eng == "pe" and op["dma"] is None:
                            continue
                        if val is None:
                            lst = self.sigs[name]
                            lo, hi = 0, len(lst)
                            while lo < hi:
                                mid = (lo + hi) // 2
                                if lst[mid][0] < j:
                                    lo = mid + 1
                                else:
                                    hi = mid
                            val = lst[lo][1]
                        self._wait(eng, "c_" + name, self.sem[name], val)
                else:
                    raise RuntimeError("dep on unemitted op")
            if op["dma"]:
                name = op["dma"]
                if name not in self.dsem:
                    self.dsem[name] = self.es.enter_context(self.nc.semaphore("d_" + name))
                    self.dcnt[name] = 0
                if self.dcnt[name] > 0:
                    self._wait(eng, "d_" + name, self.dsem[name], self.dcnt[name])
                ins = op["fn"](self.eng[eng])
                self.dcnt[name] += 16
                ins.then_inc(self.dsem[name], 16)
                self.done[op["idx"]] = ("d", name, self.dcnt[name])
            else:
                ws = list(self.defer.values()) if self.defer else []
                self.defer = None
                att = None
                if len(ws) == 1 and self.attach:
                    att = ws[0]
                else:
                    for (semh, val) in ws:
                        self.eng[eng].wait_ge(semh, val)
                        self.ninstr += 1
                ins = op["fn"](self.eng[eng])
                if att is not None:
                    ins._wait_ge(att[0], att[1])
                if op["sig"]:
                    self.cnt[eng] += 1
                    ins.then_inc(self.sem[eng], 1)
                    self.done[op["idx"]] = ("c", eng, self.cnt[eng])
                    self.sigs[eng].append((op["idx"], self.cnt[eng]))
                else:
                    self.done[op["idx"]] = ("c", eng, None)
            self.ninstr += 1

    def barrier(self, dma_only_on=("sp",)):
        self.flush()
        for f in self.eng:
            for e in COMPUTE:
                if e != f and self.cnt[e] > 0:
                    self._wait(f, "c_" + e, self.sem[e], self.cnt[e])
            for name, h in self.dsem.items():
                if self.dcnt[name] > 0:
                    self._wait(f, "d_" + name, h, self.dcnt[name])
        self.last_w = {}
        self.readers = {}


    def mm(self, out, lhsT, rhs, start=True, stop=True, r=(), w=()):
        self.add("pe", lambda e: e.matmul(out, lhsT, rhs, start=start, stop=stop), r=r, w=w)

    def tr(self, out, in_, ident, r=(), w=()):
        self.add("pe", lambda e: e.matmul(out, in_, ident, start=True, stop=True), r=r, w=w)

    def act(self, out, in_, func, r=(), w=(), bias=None, scale=None):
        kw = {}
        if bias is not None:
            kw["bias"] = bias
        if scale is not None:
            kw["scale"] = scale
        self.add("act", lambda e: e.activation(out=out, in_=in_, func=func, **kw), r=r, w=w)

    def tt(self, eng, out, in0, in1, op, r=(), w=()):
        self.add(eng, lambda e: e.tensor_tensor(out, in0, in1, op), r=r, w=w)

    def ts(self, eng, out, in0, s1, s2, op0, op1, r=(), w=()):
        self.add(eng, lambda e: e.tensor_scalar(out, in0, s1, s2, op0, op1), r=r, w=w)

    def tsmul(self, eng, out, in0, s1, r=(), w=()):
        self.add(eng, lambda e: e.tensor_scalar_mul(out, in0, s1), r=r, w=w)

    def stt(self, eng, out, in0, scalar, in1, op0, op1, r=(), w=()):
        self.add(eng, lambda e: e.scalar_tensor_tensor(out, in0, scalar, in1, op0, op1), r=r, w=w)

    def copy(self, eng, out, in_, r=(), w=()):
        if eng == "act":
            self.add("act", lambda e: e.activation(out=out, in_=in_, func=AF.Copy), r=r, w=w)
        elif os.environ.get("KCOPY", "mul") == "mul":
            self.add(eng, lambda e: e.tensor_scalar_mul(out, in_, 1.0), r=r, w=w)
        else:
            self.add(eng, lambda e: e.tensor_copy(out, in_), r=r, w=w)

    def dma(self, q, out, in_, sem, r=(), w=()):
        self.add(q, lambda e: e.dma_start(out=out, in_=in_), r=r, w=w, dma=sem)


def _slots_in(w):
    K, M = w.shape
    return np.ascontiguousarray(w.reshape(K // 128, 128, M // 128, 128).transpose(2, 1, 0, 3))


def _vec_pk(v):
    return np.ascontiguousarray(v.reshape(-1, 128).T)


class Ctx:
    pass


_UID = [0]


def _u(name):
    _UID[0] += 1
    return "%s_%d" % (name, _UID[0])


def ffn_phase(P, nc, c, src, dst, w_in_d, w_out_d, gcol):
    with ExitStack() as es:
        def sb(name, shape, dt):
            return es.enter_context(nc.sbuf_tensor(_u(name), shape, dt))

        def ps(name, shape, dt=F32):
            return es.enter_context(nc.psum_tensor(_u(name), shape, dt))
        WIN = sb("f_win", [128, 2 * KF, KD, 128], BF16)
        WOUT = sb("f_wout", [128, KF, D], BF16)
        XT = [sb("f_xt%d" % i, [128, KD, TT], F32) for i in range(2)]
        XN = sb("f_xn", [128, KD, TT], BF16)
        ACTB = sb("f_act", [128, KF, TT], BF16)
        SQ = [sb("f_sq%d" % i, [128, TT], BF16) for i in range(1)]
        SG = [sb("f_sg%d" % i, [128, TT], BF16) for i in range(2)]
        RSTD = sb("f_rstd", [128, TT], F32)
        PH = [ps("f_ph%d" % i, [128, TT]) for i in range(4)]
        PY = [ps("f_py%d" % i, [128, TT]) for i in range(2)]
        PSS = ps("f_pss", [128, TT])

        G = 2
        for j0 in range(0, 2 * KF, G):
            P.add("pool", lambda e, j0=j0: e.dma_start(
                out=WIN[:, j0:j0 + G], in_=w_in_d[j0:j0 + G].rearrange("j p k c -> p j k c")),
                w=[("win", j) for j in range(j0, j0 + G)], dma="wq%d" % ((j0 // G) % 4))
        for k0 in range(0, KF, G):
            P.add("pool", lambda e, k0=k0: e.dma_start(
                out=WOUT[:, k0:k0 + G], in_=w_out_d[k0 * 128:(k0 + G) * 128, :].rearrange("(k p) c -> p k c", p=128)),
                w=[("wout", k) for k in range(k0, k0 + G)], dma="wq%d" % ((k0 // G) % 4))

        def load(t):
            b = t % 2
            P.add("sp", lambda e: e.dma_start(
                out=XT[b][:], in_=src[:, t * TT:(t + 1) * TT].rearrange("(k p) n -> p k n", p=128)),
                r=[("X", src.tensor.name, t)], w=[("xt", b)], dma="xl%d" % b)

        def do_tile(t):
            b = t % 2
            if t + 1 < NT:
                load(t + 1)
            xt = XT[b]
            for k in range(KD):
                q = 0
                P.add("act", lambda e, k=k, q=q: e.activation(out=SQ[q][:], in_=xt[:, k, :], func=AF.Square),
                      r=[("xt", b)], w=[("sq", q)])
                P.add("pe", lambda e, k=k, q=q: e.matmul(PSS[:], c.ones_bf[:], SQ[q][:], start=(k == 0), stop=(k == KD - 1)),
                      r=[("sq", q)], w=[("pss",)])
            P.add("act", lambda e: e.activation(out=RSTD[:], in_=PSS[:], func=AF.Ln, bias=c.eps[:, 0:1], scale=1.0 / D),
                  r=[("pss",)], w=[("rstd",)])
            P.add("act", lambda e: e.activation(out=RSTD[:], in_=RSTD[:], func=AF.Exp, scale=-0.5),
                  r=[("rstd",)], w=[("rstd",)])
            for k in range(KD):
                P.add("dve", lambda e, k=k: e.scalar_tensor_tensor(
                    XN[:, k, :], xt[:, k, :], c.vecs[:, gcol + k:gcol + k + 1], RSTD[:], ALU.mult, ALU.mult),
                    r=[("xt", b), ("rstd",)], w=[("xn", k)])
            for j in range(KF):
                pg = PH[(2 * j) % 4]
                pu = PH[(2 * j + 1) % 4]
                kg, ku = ("ph", (2 * j) % 4), ("ph", (2 * j + 1) % 4)
                for k in range(KD):
                    P.add("pe", lambda e, j=j, k=k, pg=pg: e.matmul(pg[:], WIN[:, j, k, :], XN[:, k, :], start=(k == 0), stop=(k == KD - 1)),
                          r=[("win", j), ("xn", k)], w=[kg])
                for k in range(KD):
                    P.add("pe", lambda e, j=j, k=k, pu=pu: e.matmul(pu[:], WIN[:, KF + j, k, :], XN[:, k, :], start=(k == 0), stop=(k == KD - 1)),
                          r=[("win", KF + j), ("xn", k)], w=[ku])
                q = j % 2
                P.add("act", lambda e, pg=pg, q=q: e.activation(out=SG[q][:], in_=pg[:], func=AF.Silu),
                      r=[kg], w=[("sg", q)])
                P.add("dve", lambda e, j=j, pu=pu, q=q: e.tensor_tensor(ACTB[:, j, :], pu[:], SG[q][:], ALU.mult),
                      r=[ku, ("sg", q)], w=[("act", j)])
            for m in range(KD):
                py = PY[m % 2]
                ky = ("py", m % 2)
                for k in range(KF):
                    P.add("pe", lambda e, m=m, k=k, py=py: e.matmul(py[:], WOUT[:, k, m * 128:(m + 1) * 128], ACTB[:, k, :], start=(k == 0), stop=(k == KF - 1)),
                          r=[("wout", k), ("act", k)], w=[ky])
                P.add("dve", lambda e, m=m, py=py: e.scalar_tensor_tensor(
                    xt[:, m, :], py[:], 0.5, xt[:, m, :], ALU.mult, ALU.add),
                    r=[ky, ("xt", b)], w=[("xt", b)])
            P.add("sp", lambda e, t=t: e.dma_start(
                out=dst[:, t * TT:(t + 1) * TT].rearrange("(k p) n -> p k n", p=128), in_=xt[:]),
                r=[("xt", b)], w=[("X", dst.tensor.name, t)], dma="xs%d" % b)
        load(0)
        for t in range(NT):
            do_tile(t)
        P.barrier()


import os
DBG_NTR = int(os.environ.get('KDBG_NTR', '999'))
TMZENG = os.environ.get('KTMZ', 'act')
DBG_STOP = float(os.environ.get('KDBG_STOP', '99'))
TR = 256
HG = [[0, 2, 4, 6], [1, 3, 5, 7], [8, 10], [9, 11]]


def hgk(h):
    return 2 * (h // 8) + (h % 2)

CH = 128
NCH = TR // CH
NTR = S // TR
LNX_EPS = 64e-5
DEC_SCALE = -0.6065306597126334


class Banks:
    def __init__(self, tiles, tag):
        self.tiles = tiles
        self.tag = tag
        self.i = 0

    def next(self):
        i = self.i
        self.i = (self.i + 1) % len(self.tiles)
        return self.tiles[i], (self.tag, i)


def rms_tile(P, c, xt, xkey, xn, xnkey, gcol, n, PSS, psskey, SQ, RSTD, tag):
    for k in range(KD):
        q = k % 2
        P.act(SQ[q][:, :n], xt[:, k, :n], AF.Square, r=[xkey], w=[(tag, "sq", q)])
        P.mm(PSS[:, :n], c.ones_b[:], SQ[q][:, :n], start=(k == 0), stop=(k == KD - 1), r=[(tag, "sq", q)], w=[psskey])
    P.act(RSTD[:, :n], PSS[:, :n], AF.Ln, r=[psskey], w=[(tag, "rstd")], bias=c.eps[:, 0:1], scale=1.0 / D)
    P.act(RSTD[:, :n], RSTD[:, :n], AF.Exp, r=[(tag, "rstd")], w=[(tag, "rstd")], scale=-0.5)
    for k in range(KD):
        P.stt("dve", xn[:, k, :n], xt[:, k, :n], c.vecs[:, gcol + k:gcol + k + 1], RSTD[:, :n], ALU.mult, ALU.mult,
              r=[xkey, (tag, "rstd")], w=[xnkey + (k,)])


def head_rms(P, c, out, src, srckey, gain_col, n, bank, bkey, SQ, TMP, tag, outkey):
    P.act(SQ[:, :n], src, AF.Square, r=[srckey], w=[(tag, "hsq")])
    P.mm(bank[:, :n], c.bd_b[:], SQ[:, :n], r=[(tag, "hsq")], w=[bkey])
    P.act(TMP[:, :n], bank[:, :n], AF.Ln, r=[bkey], w=[(tag, "htmp")], bias=c.eps[:, 0:1], scale=1.0 / 64)
    P.act(TMP[:, :n], TMP[:, :n], AF.Exp, r=[(tag, "htmp")], w=[(tag, "htmp")], scale=-0.5)
    P.stt("dve", out, src, c.vecs[:, gain_col:gain_col + 1], TMP[:, :n], ALU.mult, ALU.mult,
          r=[srckey, (tag, "htmp")], w=[outkey])


def mem_prep(P, nc, c, memT_d, wkv_d, layer, V):
    with ExitStack() as es:
        def sb(name, shape, dt):
            return es.enter_context(nc.sbuf_tensor(_u(name), shape, dt))
        WKV = sb("mp_wkv", [128, KD, 512], BF16)
        MT = sb("mp_mt", [128, KD, 256], F32)
        MN = sb("mp_mn", [128, KD, 256], BF16)
        SQ = [sb("mp_sq%d" % i, [128, 256], BF16) for i in range(2)]
        RSTD = sb("mp_rstd", [128, 256], F32)
        KF32 = sb("mp_kf", [128, 256], F32)
        TMP = sb("mp_tmp", [128, 256], F32)
        pb = [es.enter_context(nc.psum_tensor(_u("mp_pb%d" % i), [128, 512], F32)) for i in range(3)]
        P.dma("pool", WKV[:], wkv_d.rearrange("(k p) c -> p k c", p=128), "wq0", w=[("mp", "wkv")])
        P.dma("sp", MT[:], memT_d.rearrange("(k p) n -> p k n", p=128), "xl0", w=[("mp", "mt")])
        rms_tile(P, c, MT, ("mp", "mt"), MN, ("mp", "mn"), V["mem_norm%d" % layer], 256, pb[0], ("mp", "pb", 0), SQ, RSTD, "mp")
        mnkeys = [("mp", "mn", k) for k in range(KD)]
        for hp in range(2):
            for k in range(KD):
                P.mm(pb[1][:, :256], WKV[:, k, hp * 128:(hp + 1) * 128], MN[:, k, :], start=(k == 0), stop=(k == KD - 1),
                     r=[("mp", "wkv"), mnkeys[k]], w=[("mp", "pb", 1)])
            P.copy("act", KF32[:], pb[1][:, :256], r=[("mp", "pb", 1)], w=[("mp", "kf")])
            head_rms(P, c, c.MK[:, hp, :], KF32[:], ("mp", "kf"), V["mem_k_norm%d" % layer], 256, pb[2], ("mp", "pb", 2), SQ[0], TMP, "mpk",
                     ("MK", hp))
        for mc in range(2):
            for k in range(KD):
                P.mm(pb[1][:, :256], MN[:, k, mc * 128:(mc + 1) * 128], WKV[:, k, 256:512], start=(k == 0), stop=(k == KD - 1),
                     r=[("mp", "wkv"), mnkeys[k]], w=[("mp", "pb", 1)])
            for h in range(4):
                hh = h % 2
                P.copy("act", c.MVZ[:, mc, h, hh * 64:(hh + 1) * 64], pb[1][:, h * 64:(h + 1) * 64], r=[("mp", "pb", 1)], w=[("MVZ",)])
        P.barrier()


def mem_attn_tile(P, c, QM, qkeys, YM, ymkeys, n, layer, V, banks, SQ, TMP, QN, PT, RD, tag):
    for hp in range(2):
        bank, bk = banks.next()
        head_rms(P, c, QN[:, hp, :n], QM[:, hp, :n], qkeys[hp], V["mem_q_norm%d" % layer], n, bank, bk, SQ, TMP, tag, (tag, "qn", hp))
    for hp in range(2):
        for hh in range(2):
            off = hh * 64
            for mc in range(2):
                bl, kl = banks.next()
                P.mm(bl[:, :n], c.MK[off:off + 64, hp, mc * 128:(mc + 1) * 128], QN[off:off + 64, hp, :n],
                     r=[("MK", hp), (tag, "qn", hp)], w=[kl])
                P.act(PT[hh * 2 + mc][:, :n], bl[:, :n], AF.Exp, r=[kl], w=[(tag, "pt", hh * 2 + mc)], scale=0.125)
        bnum, knum = banks.next()
        bden, kden = banks.next()
        for i in range(4):
            hh, mc = i // 2, i % 2
            h = hp * 2 + hh
            P.mm(bnum[:, :n], c.MVZ[:, mc, h, :], PT[i][:, :n], start=(i == 0), stop=(i == 3), r=[("MVZ",), (tag, "pt", i)], w=[knum])
        for i in range(4):
            hh, mc = i // 2, i % 2
            P.mm(bden[:, :n], c.onesh[hh], PT[i][:, :n], start=(i == 0), stop=(i == 3), r=[(tag, "pt", i)], w=[kden])
        P.act(RD[:, :n], bden[:, :n], AF.Ln, r=[kden], w=[(tag, "rd")])
        P.act(RD[:, :n], RD[:, :n], AF.Exp, r=[(tag, "rd")], w=[(tag, "rd")], scale=-1.0)
        P.tt("dve", YM[:, hp, :n], bnum[:, :n], RD[:, :n], ALU.mult, r=[knum, (tag, "rd")], w=[ymkeys[hp]])


def rwkv_phase(P, nc, c, src, dst, li, layer, w_in_d, w_out_d, wup_d, aup_d, gup_d, V, dbg=None):
    with ExitStack() as es:
        def sb(name, shape, dt):
            return es.enter_context(nc.sbuf_tensor(_u("rk_" + name), shape, dt))
        WA = sb("wa", [128, 22, KD, 128], BF16)
        WO = sb("wo", [128, KD, D], BF16)
        WAUP = sb("waup", [128, 768], BF16)
        GUP = sb("gup", [128, 768], BF16)
        XT = sb("xt", [128, KD, TR], F32)
        XN = sb("xn", [128, KD, TR], BF16)
        SQ = [sb("sq%d" % i, [128, TR], BF16) for i in range(2)]
        RSTD = sb("rstd", [128, TR], F32)
        CARRY = sb("carry", [128, 20], F32)
        OMM = sb("omm", [128, 20], F32)
        OMKA = sb("omka", [128, 6], F32)
        TA = sb("ta", [128, TR], F32)
        P18 = sb("p18", [128, TR], F32)
        P19 = sb("p19", [128, TR], F32)
        TW = sb("tw", [128, TR], BF16)
        AL = sb("al", [128, TR], BF16)
        SGG = sb("sgg", [128, TR], BF16)
        QM = sb("qm", [128, 2, TR], F32)
        Rf = sb("rf", [128, TR], F32)
        Kf = sb("kf", [128, TR], F32)
        Vf = sb("vf", [128, TR], F32)
        T = [sb("t%d" % i, [128, TR], F32) for i in range(10)]
        HB = [sb("hb%d" % i, [128, TR], BF16) for i in range(4)]
        G = sb("g", [128, 6, TR], BF16)
        BON = sb("bon", [128, 6, TR], BF16)
        RT = sb("rt", [128, 6, TR], BF16)
        KT = sb("kt", [128, 6, TR], BF16)
        BT = sb("bt", [128, 6, TR], BF16)
        AT = sb("at", [128, 6, TR], BF16)
        PC = sb("pc", [128, 6, NCH], F32)
        TMV = [sb("tmv%d" % i, [128, 6, 128], BF16) for i in range(NCH)]
        TMZ = [sb("tmz%d" % i, [128, 2, 13, 128], BF16) for i in range(NCH)]
        W = [sb("w%d" % i, [128, 2, 768], BF16) for i in range(2 * NCH)]
        AK = [sb("ak%d" % i, [128, 12, 128], BF16) for i in range(2 * NCH)]
        BK = [sb("bk%d" % i, [128, 12, 128], BF16) for i in range(2 * NCH)]
        AAK = sb("aak", [128, 12, 128], BF16)
        ARK = sb("ark", [128, 12, 128], BF16)
        ARB = sb("arb", [128, 12, 128], BF16)
        AHT = sb("aht", [128, 6, 128], BF16)
        UU = sb("uu", [128, 12, 64], BF16)
        SF = sb("sf", [128, 6, 64], F32)
        SB = sb("sb", [128, 6, 64], BF16)
        YB = sb("yb", [128, 6, TR], F32)
        YO = sb("yo", [128, 6, TR], BF16)
        YM = sb("ym", [128, 2, TR], BF16)
        QN = sb("qn", [128, 2, TR], BF16)
        PT = [sb("pt%d" % i, [128, TR], BF16) for i in range(4)]
        RD = sb("rd", [128, TR], F32)
        MSQ = sb("msq", [128, TR], BF16)
        MTMP = sb("mtmp", [128, TR], F32)
        pbt = [es.enter_context(nc.psum_tensor(_u("rk_pb%d" % i), [128, 512], F32)) for i in range(8)]
        banks = Banks(pbt, "rpb")
        tbanks = banks
        print("[kernel] rwkv sbuf free", nc.sbuf_bytes_remaining)

        for j0 in range(0, 22, 2):
            P.dma("pool", WA[:, j0:j0 + 2], w_in_d[j0:j0 + 2].rearrange("j p k c -> p j k c"), "wq%d" % ((j0 // 2) % 4),
                  w=[("wa", j) for j in range(j0, j0 + 2)])
        P.dma("pool", WAUP[0:64, :], wup_d[:, :], "wq0", w=[("waup", 0)])
        P.dma("pool", WAUP[64:128, :], aup_d[:, :], "wq1", w=[("waup", 1)])
        P.dma("pool", GUP[:], gup_d[:, :], "wq2", w=[("gup",)])
        for k0 in range(0, KD, 2):
            P.dma("pool", WO[:, k0:k0 + 2], w_out_d[k0 * 128:(k0 + 2) * 128, :].rearrange("(k p) c -> p k c", p=128), "wq%d" % ((k0 // 2) % 4),
                  w=[("wo", k) for k in range(k0, k0 + 2)])
        mu0 = V["a_mu%d" % li]
        P.ts("dve", OMM[:], c.vecs[:, mu0:mu0 + 20], -1.0, 1.0, ALU.mult, ALU.add, w=[("omm",)])
        ka0 = V["a_k_a%d" % li]
        P.ts("dve", OMKA[:], c.vecs[:, ka0:ka0 + 6], -1.0, 1.0, ALU.mult, ALU.add, w=[("omka",)])
        P.add("dve", lambda e: e.memset(CARRY[:], 0.0), w=[("carry", m) for m in range(20)])
        P.add("dve", lambda e: e.memset(SF[:], 0.0), w=[("sf",)])
        P.add("dve", lambda e: e.memset(SB[:], 0.0), w=[("sbk",)])
        for i in range(NCH):
            P.add("dve", lambda e, i=i: e.memset(TMZ[i][:], 0.0), w=[("tmz", i, hp, ty, hh) for hp in range(6) for ty in range(2) for hh in range(2)])

        def vcol(name, i):
            o = V[name + "%d" % li] + i
            return c.vecs[:, o:o + 1]

        def proj(m, n):
            bank, bk = banks.next()
            for k in range(KD):
                P.mm(bank[:, :n], WA[:, m, k, :], XN[:, k, :n], start=(k == 0), stop=(k == KD - 1), r=[("wa", m), ("xn", k)], w=[bk])
            return bank, bk

        def shift_evac(m, bank, bk, out, outkey):
            n = TR
            P.act(TA[:, :n], bank[:, :n], AF.Copy, r=[bk, ("omm",)], w=[("ta",)], scale=OMM[:, m:m + 1])
            P.stt("dve", out[:, 1:n], bank[:, 0:n - 1], c.vecs[:, mu0 + m:mu0 + m + 1], TA[:, 1:n], ALU.mult, ALU.add,
                  r=[bk, ("ta",)], w=[outkey])
            P.stt("dve", out[:, 0:1], CARRY[:, m:m + 1], c.vecs[:, mu0 + m:mu0 + m + 1], TA[:, 0:1], ALU.mult, ALU.add,
                  r=[("carry", m), ("ta",)], w=[outkey + ("c0",)])
            P.copy("dve", CARRY[:, m:m + 1], bank[:, n - 1:n], r=[bk], w=[("carry", m)])

        def do_tile(t):
            n = TR
            P.dma("sp", XT[:], src[:, t * TR:(t + 1) * TR].rearrange("(k p) n -> p k n", p=128), "xl0",
                  r=[("X", src.tensor.name, t)], w=[("xt",)])
            rb, rbk = banks.next()
            rms_tile(P, c, XT, ("xt",), XN, ("xn",), V["mix_norm%d" % layer], n, rb, rbk, SQ, RSTD, "rk")
            b18, k18 = proj(18, n)
            shift_evac(18, b18, k18, P18, ("p18",))
            b19, k19 = proj(19, n)
            shift_evac(19, b19, k19, P19, ("p19",))
            p18k = [("p18",), ("p18", "c0")]
            p19k = [("p19",), ("p19", "c0")]
            P.act(TW[0:64, :], P18[0:64, :], AF.Tanh, r=p18k, w=[("tw",)])
            P.copy("dve", AL[64:128, :], P18[64:128, :], r=p18k, w=[("al",)])
            P.act(SGG[:], P19[:], AF.Sigmoid, r=p19k, w=[("sgg",)])
            for q in range(2):
                bq, kq = proj(20 + q, n)
                P.copy("act", QM[:, q, :], bq[:, :n], r=[kq], w=[("qm", q)])
            if DBG_STOP <= 1:
                return
            for hp in range(6):
                cs = slice(hp * 128, (hp + 1) * 128)
                bz, kz = banks.next()
                P.mm(bz[:, :n], WAUP[0:64, cs], TW[0:64, :], r=[("waup", 0), ("tw",)], w=[kz])
                SW, LOGW, ALR = T[0], T[1], T[2]
                P.act(SW[:], bz[:, :n], AF.Sigmoid, r=[kz], w=[("sw",)], bias=vcol("a_w0", hp))
                P.tsmul("dve", LOGW[:], SW[:], DEC_SCALE, r=[("sw",)], w=[("logw",)])
                bz, kz = banks.next()
                P.mm(bz[:, :n], WAUP[64:128, cs], AL[64:128, :], r=[("waup", 1), ("al",)], w=[kz])
                P.act(ALR[:], bz[:, :n], AF.Sigmoid, r=[kz], w=[("alr",)], bias=vcol("a_a0", hp))
                bz, kz = banks.next()
                P.mm(bz[:, :n], GUP[:, cs], SGG[:], r=[("gup",), ("sgg",)], w=[kz])
                P.copy("act", G[:, hp, :], bz[:, :n], r=[kz], w=[("g", hp)])
                if DBG_STOP <= 1.1:
                    continue
                br, kr = proj(hp, n)
                shift_evac(hp, br, kr, Rf, ("rf",))
                bk_, kk_ = proj(6 + hp, n)
                shift_evac(6 + hp, bk_, kk_, Kf, ("kf",))
                bv, kv = proj(12 + hp, n)
                shift_evac(12 + hp, bv, kv, Vf, ("vf",))
                rfk = [("rf",), ("rf", "c0")]
                kfk = [("kf",), ("kf", "c0")]
                vfk = [("vf",), ("vf", "c0")]
                if DBG_STOP <= 1.2:
                    continue
                KS, NRM, KK, TMv, KM, Bv, L = T[3], T[4], T[5], T[6], T[7], T[8], T[9]
                P.tsmul("dve", KS[:], Kf[:], vcol("a_kk_scale", hp), r=kfk, w=[("ks",)])
                P.act(HB[0][:], KS[:], AF.Square, r=[("ks",)], w=[("hb", 0)])
                bz, kz = banks.next()
                P.mm(bz[:, :n], c.bd_b[:], HB[0][:], r=[("hb", 0)], w=[kz])
                P.act(NRM[:], bz[:, :n], AF.Ln, r=[kz], w=[("nrm",)], bias=c.eps[:, 2:3])
                P.act(NRM[:], NRM[:], AF.Exp, r=[("nrm",)], w=[("nrm",)], scale=-0.5)
                P.tt("dve", KK[:], KS[:], NRM[:], ALU.mult, r=[("ks",), ("nrm",)], w=[("kk",)])
                P.ts("dve", TMv[:], ALR[:], vcol("a_k_a", hp), OMKA[:, hp:hp + 1], ALU.mult, ALU.add, r=[("alr",), ("omka",)], w=[("tmv",)])
                P.tt("dve", KM[:], Kf[:], TMv[:], ALU.mult, r=kfk + [("tmv",)], w=[("km",)])
                P.tt("dve", Bv[:], KK[:], ALR[:], ALU.mult, r=[("kk",), ("alr",)], w=[("bv",)])
                P.stt("dve", HB[1][:], Rf[:], vcol("a_r_k", hp), KM[:], ALU.mult, ALU.mult, r=rfk + [("km",)], w=[("hb", 1)])
                bz, kz = banks.next()
                P.mm(bz[:, :n], c.bd_b[:], HB[1][:], r=[("hb", 1)], w=[kz])
                P.tt("dve", BON[:, hp, :], bz[:, :n], Vf[:], ALU.mult, r=[kz] + vfk, w=[("bon", hp)])
                if DBG_STOP <= 1.3:
                    continue
                P.add("dve", lambda e, L=L, LOGW=LOGW: e.tensor_tensor_scan(L[:], c.cmask[:, :n], LOGW[:], 0.0, ALU.mult, ALU.add),
                      r=[("logw",)], w=[("L",)])
                E1 = T[0]
                P.act(E1[:], L[:], AF.Exp, r=[("L",)], w=[("sw",)])
                P.tt("dve", RT[:, hp, :], Rf[:], E1[:], ALU.mult, r=rfk + [("sw",)], w=[("rt", hp)])
                E2 = T[3]
                P.act(E2[:], L[:], AF.Exp, r=[("L",), ("kk",)], w=[("ks",)], scale=-1.0)
                P.tt("dve", KT[:, hp, :], KM[:], E2[:], ALU.mult, r=[("km",), ("ks",)], w=[("kt", hp)])
                P.tt("dve", BT[:, hp, :], Bv[:], E2[:], ALU.mult, r=[("bv",), ("ks",)], w=[("bt", hp)])
                LX = T[4]
                P.tt("dve", LX[:], L[:], LOGW[:], ALU.subtract, r=[("L",), ("logw",), ("kk",)], w=[("nrm",)])
                P.act(LX[:], LX[:], AF.Exp, r=[("nrm",)], w=[("nrm",)])
                P.stt("dve", AT[:, hp, :], KK[:], -1.0, LX[:], ALU.mult, ALU.mult, r=[("kk",), ("nrm",)], w=[("at", hp)])
                DEC = T[6]
                for cc in range(NCH):
                    ce = (cc + 1) * CH - 1
                    P.act(DEC[:, cc * CH:(cc + 1) * CH], L[:, cc * CH:(cc + 1) * CH], AF.Exp, r=[("L",), ("km",)], w=[("tmv",)],
                          bias=L[:, ce:ce + 1], scale=-1.0)
                    P.act(PC[:, hp, cc:cc + 1], L[:, ce:ce + 1], AF.Exp, r=[("L",)], w=[("pc", cc)])
                P.tt("dve", HB[2][:], KM[:], DEC[:], ALU.mult, r=[("km",), ("tmv",)], w=[("hb", 2)])
                P.tt("dve", HB[3][:], Bv[:], DEC[:], ALU.mult, r=[("bv",), ("tmv",)], w=[("hb", 3)])
                P.copy("act", HB[0][:], Vf[:], r=vfk, w=[("hb", 0)])
                if DBG_STOP <= 1.4:
                    continue
                for cc in range(NCH):
                    tb, tk = tbanks.next()
                    csl = slice(cc * CH, (cc + 1) * CH)
                    P.tr(tb[:, 0:128], HB[0][:, csl], c.ident_b[:], r=[("hb", 0)], w=[tk])
                    P.tr(tb[:, 128:256], HB[2][:, csl], c.ident_b[:], r=[("hb", 2)], w=[tk])
                    P.tr(tb[:, 256:384], HB[3][:, csl], c.ident_b[:], r=[("hb", 3)], w=[tk])
                    P.tr(tb[:, 384:512], AT[:, hp, csl], c.ident_b[:], r=[("at", hp)], w=[tk])
                    if DBG_STOP <= 1.45:
                        continue
                    evq = "act" if (hp * NCH + cc) % 2 == 0 else "dve"
                    P.copy(evq, TMV[cc][:, hp, :], tb[:, 0:128], r=[tk], w=[("tmv", cc, hp)])
                    if DBG_STOP <= 1.46:
                        continue
                    for ty in range(2):
                        for hh in range(2):
                            P.copy(evq, TMZ[cc][:, ty, 2 * hp + hh, hh * 64:(hh + 1) * 64],
                                   tb[:, 128 + ty * 128 + hh * 64:128 + ty * 128 + (hh + 1) * 64], r=[tk], w=[("tmz", cc, hp, ty, hh)])
                    if DBG_STOP <= 1.47:
                        continue
                    P.copy(evq, W[2 * cc][:, 0, hp * 128:(hp + 1) * 128], tb[:, 384:512], r=[tk], w=[("w", 2 * cc, hp)])
            if DBG_STOP <= 2:
                return
            def amat(cc, lhs, lkey, rhs, rkey, mask, dest, dkey):
                csl = slice(cc * CH, (cc + 1) * CH)
                for g in range(4):
                    heads = HG[g]
                    bank, bk = banks.next()
                    for hi, h in enumerate(heads):
                        hp, off = h // 2, (h % 2) * 64
                        P.mm(bank[:, hi * 128:(hi + 1) * 128], lhs[off:off + 64, hp, csl], rhs[off:off + 64, hp, csl],
                             r=[(lkey, hp), (rkey, hp)], w=[bk])
                    nh = len(heads)
                    P.tt("dve", dest[:, heads[0]:heads[-1] + 1:2, :], bank[:, 0:nh * 128].rearrange("p (a b) -> p a b", a=nh),
                         mask[:].unsqueeze(1).to_broadcast([128, nh, 128]), ALU.mult, r=[bk], w=[dkey + (g,)])

            wst = {}
            for cc in range(NCH):
                amat(cc, BT, "bt", AT, "at", c.msu, BK[2 * cc], ("bk", cc, 0))
                amat(cc, AT, "at", BT, "bt", c.msl, AK[2 * cc], ("ak", cc, 0))
                amat(cc, KT, "kt", AT, "at", c.msu, AAK, ("aak",))
                for h0, nh in ((0, 8), (8, 4)):
                    bank, bk = banks.next()
                    for hi in range(nh):
                        h = h0 + hi
                        hp, off = h // 2, (h % 2) * 64
                        P.mm(bank[:, hi * 64:(hi + 1) * 64], AAK[:, h, :], TMV[cc][:, hp, off:off + 64],
                             r=[("aak", hgk(h)), ("tmv", cc, hp)], w=[bk])
                    P.copy("act", W[2 * cc][:, 1, h0 * 64:(h0 + nh) * 64], bank[:, 0:nh * 64], r=[bk], w=[("wu", 2 * cc, h0)])
                wst[cc] = [2 * cc, 2 * cc + 1, [("w", 2 * cc, hp) for hp in range(6)] + [("wu", 2 * cc, 0), ("wu", 2 * cc, 8)]]
            for lev in range(7):
                ai, ao = lev % 2, (lev + 1) % 2
                for cc in range(NCH):
                    wcur, wnxt, wk = wst[cc]
                    BKi, AKi, BKo, AKo = BK[2 * cc + ai], AK[2 * cc + ai], BK[2 * cc + ao], AK[2 * cc + ao]
                    for hg in range(3):
                        bank, bk = banks.next()
                        for hi in range(4):
                            h = hg * 4 + hi
                            for part in range(2):
                                o = bank[:, part * 256 + hi * 64:part * 256 + (hi + 1) * 64]
                                rhs = W[wcur][:, part, h * 64:(h + 1) * 64]
                                P.mm(o, c.ident_b[:], rhs, start=True, stop=False, r=wk, w=[bk])
                                P.mm(o, BKi[:, h, :], rhs, start=False, stop=True, r=[("bk", cc, ai, hgk(h))], w=[bk])
                        P.copy("act" if hg % 2 == 0 else "dve", W[wnxt][:, :, hg * 256:(hg + 1) * 256],
                               bank[:, :].rearrange("p (a b) -> p a b", a=2), r=[bk], w=[("wl", wnxt, hg)])
                    if lev < 6:
                        for g in range(4):
                            heads = HG[g]
                            nh = len(heads)
                            bank, bk = banks.next()
                            for hi, h in enumerate(heads):
                                P.mm(bank[:, hi * 128:(hi + 1) * 128], BKi[:, h, :], AKi[:, h, :],
                                     r=[("bk", cc, ai, g), ("ak", cc, ai, g)], w=[bk])
                            P.copy("dve" if g % 2 == 0 else "act", AKo[:, heads[0]:heads[-1] + 1:2, :],
                                   bank[:, 0:nh * 128].rearrange("p (a b) -> p a b", a=nh), r=[bk], w=[("ak", cc, ao, g)])
                            bank, bk = banks.next()
                            for hi, h in enumerate(heads):
                                P.mm(bank[:, hi * 128:(hi + 1) * 128], AKi[:, h, :], BKi[:, h, :],
                                     r=[("bk", cc, ai, g), ("ak", cc, ai, g)], w=[bk])
                            P.copy("act" if g % 2 == 0 else "dve", BKo[:, heads[0]:heads[-1] + 1:2, :],
                                   bank[:, 0:nh * 128].rearrange("p (a b) -> p a b", a=nh), r=[bk], w=[("bk", cc, ao, g)])
                    wst[cc] = [wnxt, wcur, [("wl", wnxt, hg) for hg in range(3)]]
            for cc in range(NCH):
                csl = slice(cc * CH, (cc + 1) * CH)
                wf, _, wfk = wst[cc]
                amat(cc, KT, "kt", RT, "rt", c.mui, ARK, ("ark",))
                amat(cc, BT, "bt", RT, "rt", c.mui, ARB, ("arb",))
                for p0, npp in ((0, 4), (4, 2)):
                    tb, tk = tbanks.next()
                    for pi in range(npp):
                        hp = p0 + pi
                        P.tr(tb[:, pi * 128:(pi + 1) * 128], W[wf][:, 0, hp * 128:(hp + 1) * 128], c.ident_b[:], r=wfk, w=[tk])
                    P.copy("dve", AHT[:, p0:p0 + npp, :], tb[:, 0:npp * 128].rearrange("p (a b) -> p a b", a=npp), r=[tk], w=[("aht", p0)])
                ahk = [("aht", 0), ("aht", 4)]
                for g in range(4):
                    heads = HG[g]
                    nh = len(heads)
                    bank, bk = banks.next()
                    for hi, h in enumerate(heads):
                        hp, off = h // 2, (h % 2) * 64
                        P.mm(bank[:, hi * 64:(hi + 1) * 64], AHT[off:off + 64, hp, :], SB[off:off + 64, hp, :], start=True, stop=False,
                             r=ahk + [("sbk",)], w=[bk])
                        P.mm(bank[:, hi * 64:(hi + 1) * 64], c.ident_b[:], W[wf][:, 1, h * 64:(h + 1) * 64], start=False, stop=True, r=wfk, w=[bk])
                    P.copy("act", UU[:, heads[0]:heads[-1] + 1:2, :], bank[:, 0:nh * 64].rearrange("p (a b) -> p a b", a=nh), r=[bk], w=[("uu", g)])
                uuk = [("uu", g) for g in range(4)]
                for p0, npp in ((0, 4), (4, 2)):
                    for hh in range(2):
                        off = hh * 64
                        bank, bk = banks.next()
                        for pi in range(npp):
                            hp = p0 + pi
                            h = 2 * hp + hh
                            o = bank[off:off + 64, pi * 128:(pi + 1) * 128]
                            P.mm(o, SB[off:off + 64, hp, :], RT[off:off + 64, hp, csl], start=True, stop=False, r=[("sbk",), ("rt", hp)], w=[bk])
                            P.mm(o, TMV[cc][:, hp, off:off + 64], ARK[:, h, :], start=False, stop=False, r=[("tmv", cc, hp), ("ark", hgk(h))], w=[bk])
                            P.mm(o, UU[:, h, :], ARB[:, h, :], start=False, stop=True, r=uuk + [("arb", hgk(h))], w=[bk])
                        P.copy("act", YB[off:off + 64, p0:p0 + npp, csl], bank[off:off + 64, 0:npp * 128].rearrange("p (a b) -> p a b", a=npp),
                               r=[bk], w=[("yb", cc, p0, hh)])
                bank, bk = banks.next()
                for hp in range(6):
                    o = bank[:, hp * 64:(hp + 1) * 64]
                    for hh in range(2):
                        off = hh * 64
                        h = 2 * hp + hh
                        P.mm(o, TMZ[cc][:, 0, h, :], TMV[cc][:, hp, off:off + 64], start=(hh == 0), stop=False,
                             r=[("tmz", cc, hp, 0, hh), ("tmz", cc, hp, 1, hh), ("tmv", cc, hp)], w=[bk])
                        P.mm(o, TMZ[cc][:, 1, h, :], UU[:, h, :], start=False, stop=(hh == 1), r=uuk, w=[bk])
                P.tt("dve", SF[:], SF[:], PC[:, :, cc:cc + 1].to_broadcast([128, 6, 64]), ALU.mult, r=[("pc", cc), ("sf",)], w=[("sf",)])
                P.tt("dve", SF[:], SF[:], bank[:, 0:384].rearrange("p (a b) -> p a b", a=6), ALU.add, r=[bk, ("sf",)], w=[("sf",)])
                P.copy("act", SB[:], SF[:], r=[("sf",)], w=[("sbk",)])
            if DBG_STOP <= 3:
                return
            ybk = [("yb", cc, p0, hh) for cc in range(NCH) for p0 in (0, 4) for hh in range(2)]
            for hp in range(6):
                YC, SQf, SD = T[0], T[1], T[2]
                bank, bk = banks.next()
                P.mm(bank[:, :n], c.bda_f[:], YB[:, hp, :], r=ybk, w=[bk])
                P.tt("dve", YC[:], YB[:, hp, :], bank[:, :n], ALU.subtract, r=ybk + [bk], w=[("sw",)])
                P.act(SQf[:], YC[:], AF.Square, r=[("sw",)], w=[("logw",)])
                bank, bk = banks.next()
                P.mm(bank[:, :n], c.bda_f[:], SQf[:], r=[("logw",)], w=[bk])
                P.act(SD[:], bank[:, :n], AF.Ln, r=[bk], w=[("alr",)], bias=c.eps[:, 1:2])
                P.act(SD[:], SD[:], AF.Exp, r=[("alr",)], w=[("alr",)], scale=-0.5)
                P.tt("dve", YC[:], YC[:], SD[:], ALU.mult, r=[("sw",), ("alr",)], w=[("sw",)])
                P.ts("dve", YC[:], YC[:], vcol("a_lnx_g", hp), vcol("a_lnx_b", hp), ALU.mult, ALU.add, r=[("sw",)], w=[("sw",)])
                P.tt("dve", YC[:], YC[:], BON[:, hp, :], ALU.add, r=[("sw",), ("bon", hp)], w=[("sw",)])
                P.tt("dve", YO[:, hp, :], YC[:], G[:, hp, :], ALU.mult, r=[("sw",), ("g", hp)], w=[("yo", hp)])
            if DBG_STOP <= 4:
                return
            mem_attn_tile(P, c, QM, [("qm", 0), ("qm", 1)], YM, [("ym", 0), ("ym", 1)], n, layer, V, banks, MSQ, MTMP, QN, PT, RD, "rma")
            if dbg is not None:
                P.copy("dve", YB[:], YO[:], r=[("yo", hp) for hp in range(6)], w=ybk)
                P.dma("sp", dbg[0:768, t * TR:(t + 1) * TR].rearrange("(k p) n -> p k n", p=128), YB[:], "dbg0", r=ybk)
                P.copy("dve", QM[:], YM[:], r=[("ym", 0), ("ym", 1)], w=[("qm", 0), ("qm", 1)])
                P.dma("sp", dbg[768:1024, t * TR:(t + 1) * TR].rearrange("(k p) n -> p k n", p=128), QM[:], "dbg1", r=[("qm", 0), ("qm", 1)])
            for m in range(KD):
                bank, bk = banks.next()
                for k in range(KD):
                    rhs = YO[:, k, :] if k < 6 else YM[:, k - 6, :]
                    rk = ("yo", k) if k < 6 else ("ym", k - 6)
                    P.mm(bank[:, :n], WO[:, k, m * 128:(m + 1) * 128], rhs, start=(k == 0), stop=(k == KD - 1), r=[("wo", k), rk], w=[bk])
                P.tt("dve", XT[:, m, :], XT[:, m, :], bank[:, :n], ALU.add, r=[bk, ("xt",)], w=[("xt",)])
            P.dma("sp", dst[:, t * TR:(t + 1) * TR].rearrange("(k p) n -> p k n", p=128), XT[:], "xs0",
                  r=[("xt",)], w=[("X", dst.tensor.name, t)])

        for t in range(min(NTR, DBG_NTR)):
            do_tile(t)
        P.barrier()


DIL = (1, 4, 16)
MASKV = -30000.0


def flat_head_rms(P, c, bank, bk, n, KFt, SQt, RS, banks, tag, pb):
    P.copy("act", KFt[pb][0:64, :n], bank[0:64, :n], r=[bk], w=[(tag, "kf", pb)])
    P.act(SQt[pb][0:64, :n], KFt[pb][0:64, :n], AF.Square, r=[(tag, "kf", pb)], w=[(tag, "sq", pb)])
    b2, k2 = banks.next()
    P.mm(b2[0:64, :n], c.ones_b[0:64, 0:64], SQt[pb][0:64, :n], r=[(tag, "sq", pb)], w=[k2])
    P.act(RS[pb][0:64, :n], b2[0:64, :n], AF.Ln, r=[k2], w=[(tag, "rs", pb)], bias=c.eps[0:64, 0:1], scale=1.0 / 64)
    P.act(RS[pb][0:64, :n], RS[pb][0:64, :n], AF.Exp, r=[(tag, "rs", pb)], w=[(tag, "rs", pb)], scale=-0.5)


def kv_phase(P, nc, c, src, kvw_d, KT_d, V_d, V):
    with ExitStack() as es:
        def sb(name, shape, dt):
            return es.enter_context(nc.sbuf_tensor(_u("kv_" + name), shape, dt))
        WKV = sb("w", [128, KD, 1536], BF16)
        XTL = [sb("xt%d" % i, [128, KD, TT], F32) for i in range(2)]
        XN = sb("xn", [128, KD, TT], BF16)
        SQ = [sb("sq%d" % i, [128, TT], BF16) for i in range(2)]
        RSTD = sb("rstd", [128, TT], F32)
        KFt = [sb("kf%d" % i, [64, TT], F32) for i in range(2)]
        SQt = [sb("sqt%d" % i, [64, TT], BF16) for i in range(2)]
        RS = [sb("rs%d" % i, [64, TT], F32) for i in range(2)]
        KTA = sb("kta", [64, 12, S], BF16)
        VT = [sb("vt%d" % i, [128, 768], BF16) for i in range(2)]
        pbt = [es.enter_context(nc.psum_tensor(_u("kv_pb%d" % i), [128, 512], F32)) for i in range(8)]
        banks = Banks(pbt, "kpb")
        for k0 in range(0, KD, 2):
            P.dma("pool", WKV[:, k0:k0 + 2], kvw_d[k0 * 128:(k0 + 2) * 128, :].rearrange("(k p) c -> p k c", p=128), "wq%d" % (k0 // 2),
                  w=[("kvw", k) for k in range(k0, k0 + 2)])
        gk = V["kv_k_norm"]

        def kload(t):
            P.dma("sp", XTL[t % 2][:], src[:, t * TT:(t + 1) * TT].rearrange("(k p) n -> p k n", p=128), "xl%d" % (t % 2), w=[("xt", t % 2)])
        kload(0)
        for t in range(NT):
            n = TT
            if t + 1 < NT:
                kload(t + 1)
            XT = XTL[t % 2]
            rb, rbk = banks.next()
            rms_tile(P, c, XT, ("xt", t % 2), XN, ("xn",), V["kv_norm"], n, rb, rbk, SQ, RSTD, "kv")
            for h in range(12):
                dil = DIL[h // 4]
                pb = h % 2
                bank, bk = banks.next()
                for k in range(KD):
                    P.mm(bank[0:64, :n], WKV[:, k, h * 64:(h + 1) * 64], XN[:, k, :], start=(k == 0), stop=(k == KD - 1),
                         r=[("kvw", k), ("xn", k)], w=[bk])
                flat_head_rms(P, c, bank, bk, n, KFt, SQt, RS, banks, "kvh", pb)
                dst = KTA[0:64, h, :].rearrange("p (c i) -> p c i", c=dil)[:, :, t * (TT // dil):(t + 1) * (TT // dil)]
                P.stt("dve", dst, KFt[pb][0:64, :n].rearrange("p (i c) -> p c i", c=dil), c.vecs[0:64, gk:gk + 1],
                      RS[pb][0:64, :n].rearrange("p (i c) -> p c i", c=dil), ALU.mult, ALU.mult,
                      r=[("kvh", "kf", pb), ("kvh", "rs", pb)], w=[("kta", h, t)])
            for s4 in range(TT // 128):
                vb = VT[s4 % 2]
                b1, k1 = banks.next()
                for k in range(KD):
                    P.mm(b1[:, 0:512], XN[:, k, s4 * 128:(s4 + 1) * 128], WKV[:, k, 768:1280], start=(k == 0), stop=(k == KD - 1),
                         r=[("kvw", k), ("xn", k)], w=[k1])
                P.copy("act", vb[:, 0:512], b1[:, 0:512], r=[k1], w=[("vt", s4 % 2, 0)])
                b2, k2 = banks.next()
                for k in range(KD):
                    P.mm(b2[:, 0:256], XN[:, k, s4 * 128:(s4 + 1) * 128], WKV[:, k, 1280:1536], start=(k == 0), stop=(k == KD - 1),
                         r=[("kvw", k), ("xn", k)], w=[k2])
                P.copy("dve", vb[:, 512:768], b2[:, 0:256], r=[k2], w=[("vt", s4 % 2, 1)])
                P.dma("sp", V_d[t * TT + s4 * 128:t * TT + (s4 + 1) * 128, :], vb[:], "vs%d" % (s4 % 2),
                      r=[("vt", s4 % 2, 0), ("vt", s4 % 2, 1)])
        for h in range(12):
            P.dma("sp", KT_d[:, h, :], KTA[0:64, h, :], "ks%d" % (h % 2), r=[("kta", h, t) for t in range(NT)])
        P.barrier()


def attn_phase(P, nc, c, src, dst, j, layer, wq_d, wo_d, KT_d, V_d, relb_d, sel_d, E_d, V):
    HALF = S // 2
    NTH = HALF // TR
    with ExitStack() as es:
        def sb(name, shape, dt):
            return es.enter_context(nc.sbuf_tensor(_u("at_" + name), shape, dt))
        WQ = sb("wq", [128, KD, D], BF16)
        WO = sb("wo", [128, 4, D], BF16)
        KTG = sb("ktg", [64, 4, S], BF16)
        VZ = sb("vz", [128, 32, 4, 128], BF16)
        QG = sb("qg", [64, 4, HALF], BF16)
        ACN = sb("acn", [128, 2, HALF], F32)
        ACD = sb("acd", [128, 2, HALF], F32)
        XTL = [sb("xt%d" % i, [128, KD, TR], F32) for i in range(2)]
        XN = sb("xn", [128, KD, TR], BF16)
        SQ = [sb("sq%d" % i, [128, TR], BF16) for i in range(2)]
        RSTD = sb("rstd", [128, TR], F32)
        KFt = [sb("kf%d" % i, [64, TR], F32) for i in range(2)]
        SQt = [sb("sqt%d" % i, [64, TR], BF16) for i in range(2)]
        RS = [sb("rs%d" % i, [64, TR], F32) for i in range(2)]
        BM = sb("bm", [128, 12, 256], F32)
        TAB = sb("tab", [33, 12], F32)
        SEL = sb("sel", [33, 3, 510], F32)
        ESB = sb("esb", [12, 510], F32)
        HK = [sb("hk%d" % i, [128, 128], F32) for i in range(2)]
        LG = [sb("lg%d" % i, [128, 256], F32) for i in range(2)]
        PTb = [sb("ptb%d" % i, [128, 256], BF16) for i in range(4)]
        OB = sb("ob", [128, 2, TR], BF16)
        QM = sb("qm", [128, 2, TR], F32)
        YM = sb("ym", [128, 2, TR], BF16)
        QN = sb("qn", [128, 2, TR], BF16)
        PT = [sb("pt%d" % i, [128, TR], BF16) for i in range(4)]
        RD = sb("rd", [128, TR], F32)
        MSQ = sb("msq", [128, TR], BF16)
        MTMP = sb("mtmp", [128, TR], F32)
        pbt = [es.enter_context(nc.psum_tensor(_u("at_pb%d" % i), [128, 512], F32)) for i in range(8)]
        banks = Banks(pbt, "apb")
        print("[kernel] attn sbuf free", nc.sbuf_bytes_remaining)
        xcnt = [0]

        def xload(t):
            b = xcnt[0] % 2
            xcnt[0] += 1
            P.dma("sp", XTL[b][:], src[:, t * TR:(t + 1) * TR].rearrange("(k p) n -> p k n", p=128), "xl%d" % b,
                  r=[("X", src.tensor.name, t)], w=[("xt", b)])
            return b
        for k0 in range(0, KD, 2):
            P.dma("pool", WQ[:, k0:k0 + 2], wq_d[k0 * 128:(k0 + 2) * 128, :].rearrange("(k p) c -> p k c", p=128), "wq%d" % (k0 // 2),
                  w=[("wq", k) for k in range(k0, k0 + 2)])
        P.dma("pool", WO[:], wo_d.rearrange("(k p) c -> p k c", p=128), "wq0", w=[("wo",)])
        P.add("dve", lambda e: e.memset(TAB[:], MASKV), w=[("tab",)])
        P.dma("sp", TAB[0:32, :], relb_d[:, :], "xl1", w=[("tab", 1)], r=[("tab",)])
        P.dma("sp", SEL[:], sel_d.rearrange("g b n -> b g n"), "xs1", w=[("sel",)])
        for g in range(3):
            bank, bk = banks.next()
            P.mm(bank[0:12, 0:510], TAB[:, :], SEL[:, g, :], r=[("tab",), ("tab", 1), ("sel",)], w=[bk])
            P.copy("act", ESB[:], bank[0:12, 0:510], r=[bk], w=[("esb",)])
            P.dma("sp", E_d[g], ESB[:], "vs0", r=[("esb",)], w=[("E", g)])
            for h in range(4):
                for role in range(2):
                    i = (h * 2 + role) % 2
                    srcap = bass.AP(tensor=E_d.tensor, offset=g * 12 * 510 + (4 * g + h) * 510 + role * 255, ap=[[1, 128], [1, 128]])
                    P.dma("sp", HK[i][:], srcap, "xl%d" % i, r=[("E", g)], w=[("hk", i)])
                    bank, bk = banks.next()
                    P.mm(bank[:, 0:128], c.jf[:], HK[i][:], r=[("hk", i)], w=[bk])
                    P.copy("act", BM[:, 4 * g + h, role * 128:(role + 1) * 128], bank[:, 0:128], r=[bk], w=[("bm", g, h, role)])
        P.add("dve", lambda e: e.memset(VZ[:], 0.0), w=[("vz", 0), ("vz", 1)])
        gq = V["b_q_norm%d" % j]
        for H in range(2):
            P.add("dve", lambda e: e.memset(ACN[:], 0.0), w=[("acn",)])
            P.add("dve", lambda e: e.memset(ACD[:], 0.0), w=[("acd",)])
            for g in range(3):
                dil = DIL[g]
                nb = S // (dil * 128)
                nbh = nb // 2
                SL = S // dil
                HL = HALF // dil
                P.dma("sp", KTG[:], KT_d[:, 4 * g:4 * g + 4, :], "xl0", w=[("ktg",)])
                for h in range(4):
                    hh = h % 2
                    vsrc = bass.AP(tensor=V_d.tensor, offset=(4 * g + h) * 64,
                                   ap=[[dil * 768, 128], [768, dil], [128 * dil * 768, nb], [1, 64]])
                    P.dma("sp" if h % 2 == 0 else "act", VZ[:, 0:dil * nb, h, hh * 64:(hh + 1) * 64].rearrange("p (c n) d -> p c n d", c=dil),
                          vsrc, "vl%d" % h, w=[("vzh", h)], r=[("vz", 0)])
                vzk = [("vzh", h) for h in range(4)]
                nxt = xload(H * NTH)
                for tt in range(NTH):
                    t = H * NTH + tt
                    n = TR
                    b = nxt
                    if tt + 1 < NTH:
                        nxt = xload(t + 1)
                    XT = XTL[b]
                    rb, rbk = banks.next()
                    rms_tile(P, c, XT, ("xt", b), XN, ("xn",), V["mix_norm%d" % layer], n, rb, rbk, SQ, RSTD, "at")
                    for h in range(4):
                        hq = 4 * g + h
                        pb = h % 2
                        bank, bk = banks.next()
                        for k in range(KD):
                            P.mm(bank[0:64, :n], WQ[:, k, hq * 64:(hq + 1) * 64], XN[:, k, :], start=(k == 0), stop=(k == KD - 1),
                                 r=[("wq", k), ("xn", k)], w=[bk])
                        flat_head_rms(P, c, bank, bk, n, KFt, SQt, RS, banks, "ath", pb)
                        dq = QG[0:64, h, :].rearrange("p (c i) -> p c i", c=dil)[:, :, tt * (TR // dil):(tt + 1) * (TR // dil)]
                        P.stt("dve", dq, KFt[pb][0:64, :n].rearrange("p (i c) -> p c i", c=dil), c.vecs[0:64, gq:gq + 1],
                              RS[pb][0:64, :n].rearrange("p (i c) -> p c i", c=dil), ALU.mult, ALU.mult,
                              r=[("ath", "kf", pb), ("ath", "rs", pb)], w=[("qg", h, tt)])
                qgk = [("qg", h, tt) for h in range(4) for tt in range(NTH)]
                for cidx in range(dil):
                    for nl in range(nbh):
                        nblk = H * nbh + nl
                        qcol = cidx * HL + nl * 128
                        for hp in range(2):
                            pts = []
                            for hh in range(2):
                                h = 2 * hp + hh
                                bank, bk = banks.next()
                                kcol = cidx * SL + nblk * 128
                                P.mm(bank[:, 0:128], KTG[0:64, h, kcol:kcol + 128], QG[0:64, h, qcol:qcol + 128], r=[("ktg",)] + qgk, w=[bk])
                                wcols = 128
                                if nblk > 0:
                                    P.mm(bank[:, 128:256], KTG[0:64, h, kcol - 128:kcol], QG[0:64, h, qcol:qcol + 128], r=[("ktg",)] + qgk, w=[bk])
                                    wcols = 256
                                li = (hp * 2 + hh) % 2
                                P.stt("dve", LG[li][:, 0:wcols], bank[:, 0:wcols], 0.125, BM[:, 4 * g + h, 0:wcols], ALU.mult, ALU.add,
                                      r=[bk] + [("bm", g, h, r_) for r_ in range(2)], w=[("lg", li)])
                                pi = hp * 2 + hh
                                P.act(PTb[pi][:, 0:wcols], LG[li][:, 0:wcols], AF.Exp, r=[("lg", li)], w=[("ptb", pi)])
                                pts.append((pi, h, hh, wcols))
                            bn, kn = banks.next()
                            bd, kd = banks.next()
                            mms = []
                            for (pi, h, hh, wcols) in pts:
                                mms.append((VZ[:, cidx * nb + nblk, h, :], c.onesh[hh], PTb[pi][:, 0:128], pi, h))
                                if wcols == 256:
                                    mms.append((VZ[:, cidx * nb + nblk - 1, h, :], c.onesh[hh], PTb[pi][:, 128:256], pi, h))
                            for i, (vz, oh, rhs, pi, h) in enumerate(mms):
                                P.mm(bn[:, 0:128], vz, rhs, start=(i == 0), stop=(i == len(mms) - 1), r=[("ptb", pi), ("vzh", h)], w=[kn])
                            for i, (vz, oh, rhs, pi, h) in enumerate(mms):
                                P.mm(bd[:, 0:128], oh, rhs, start=(i == 0), stop=(i == len(mms) - 1), r=[("ptb", pi)], w=[kd])
                            an = ACN[:, hp, :].rearrange("p (i c) -> p c i", c=dil)[:, cidx, nl * 128:(nl + 1) * 128]
                            ad = ACD[:, hp, :].rearrange("p (i c) -> p c i", c=dil)[:, cidx, nl * 128:(nl + 1) * 128]
                            P.tt("dve", an, an, bn[:, 0:128], ALU.add, r=[kn, ("acn",)], w=[("acn",)])
                            P.tt("act" if False else "dve", ad, ad, bd[:, 0:128], ALU.add, r=[kd, ("acd",)], w=[("acd",)])
            nxt = xload(H * NTH)
            for tt in range(NTH):
                t = H * NTH + tt
                n = TR
                b = nxt
                if tt + 1 < NTH:
                    nxt = xload(t + 1)
                XT = XTL[b]
                xk = ("xt", b)
                tsl = slice(tt * TR, (tt + 1) * TR)
                P.act(ACD[:, :, tsl], ACD[:, :, tsl], AF.Ln, r=[("acd",)], w=[("acd",)])
                P.act(ACD[:, :, tsl], ACD[:, :, tsl], AF.Exp, r=[("acd",)], w=[("acd",)], scale=-1.0)
                P.tt("dve", OB[:], ACN[:, :, tsl], ACD[:, :, tsl], ALU.mult, r=[("acn",), ("acd",)], w=[("ob",)])
                rb, rbk = banks.next()
                rms_tile(P, c, XT, xk, XN, ("xn",), V["mix_norm%d" % layer], n, rb, rbk, SQ, RSTD, "at")
                for q in range(2):
                    bq, kq = banks.next()
                    for k in range(KD):
                        P.mm(bq[:, :n], WQ[:, k, 768 + q * 128:768 + (q + 1) * 128], XN[:, k, :], start=(k == 0), stop=(k == KD - 1),
                             r=[("wq", k), ("xn", k)], w=[kq])
                    P.copy("act", QM[:, q, :], bq[:, :n], r=[kq], w=[("qm", q)])
                mem_attn_tile(P, c, QM, [("qm", 0), ("qm", 1)], YM, [("ym", 0), ("ym", 1)], n, layer, V, banks, MSQ, MTMP, QN, PT, RD, "ama")
                for m in range(KD):
                    bank, bk = banks.next()
                    for k in range(4):
                        rhs = OB[:, k, :] if k < 2 else YM[:, k - 2, :]
                        rk = ("ob",) if k < 2 else ("ym", k - 2)
                        P.mm(bank[:, :n], WO[:, k, m * 128:(m + 1) * 128], rhs, start=(k == 0), stop=(k == 3), r=[("wo",), rk], w=[bk])
                    P.tt("dve", XT[:, m, :], XT[:, m, :], bank[:, :n], ALU.add, r=[bk, xk], w=[xk])
                P.dma("sp", dst[:, t * TR:(t + 1) * TR].rearrange("(k p) n -> p k n", p=128), XT[:], "xs%d" % b,
                      r=[xk], w=[("X", dst.tensor.name, t)])
        P.barrier()


def vec_layout():
    V = {}
    off = 0

    def put(name, n):
        nonlocal off
        V[name] = off
        off += n
    for i in range(8):
        put("ffn%d" % i, KD)
    for l in range(4):
        put("mix_norm%d" % l, KD)
        put("mem_norm%d" % l, KD)
        put("mem_q_norm%d" % l, 1)
        put("mem_k_norm%d" % l, 1)
    for i in range(2):
        put("a_mu%d" % i, 20)
        for nm in ("a_w0", "a_a0", "a_kk_scale", "a_k_a", "a_r_k", "a_lnx_g", "a_lnx_b"):
            put(nm + "%d" % i, 6)
    for j in range(2):
        put("b_q_norm%d" % j, 1)
    put("kv_norm", KD)
    put("kv_k_norm", 1)
    return V, off


NCONST = 128 * 8 + 256
NKF = NCONST - 384


def build(n_stages=99, dbg_on=False):
    nc = bass.Bass("TRN2", target_bir_lowering=False)
    es = ExitStack()
    c = Ctx()
    V, NV = vec_layout()
    xT = nc.dram_tensor("xT", [D, S], F32, kind="ExternalInput").ap()
    memT = nc.dram_tensor("memT", [D, 256], F32, kind="ExternalInput").ap()
    outT = nc.dram_tensor("outT", [D, S], F32, kind="ExternalOutput").ap()
    dbg = nc.dram_tensor("dbg", [D, S], F32, kind="ExternalOutput").ap() if dbg_on else None
    XS = nc.dram_tensor("xs_scratch", [D, S], F32, kind="Internal").ap()
    vecs_d = nc.dram_tensor("vecs", [128, NV], F32, kind="ExternalInput").ap()
    consts_d = nc.dram_tensor("consts", [128, NCONST + 256], F32, kind="ExternalInput").ap()
    w_in_d = [nc.dram_tensor("w_in%d" % i, [2 * KF, 128, KD, 128], F32, kind="ExternalInput").ap() for i in range(8)]
    w_out_d = [nc.dram_tensor("w_out%d" % i, [FF, D], F32, kind="ExternalInput").ap() for i in range(8)]
    a_w_in_d = [nc.dram_tensor("a_w_in%d" % i, [22, 128, KD, 128], F32, kind="ExternalInput").ap() for i in range(2)]
    a_w_out_d = [nc.dram_tensor("a_w_out%d" % i, [D, D], F32, kind="ExternalInput").ap() for i in range(2)]
    a_wup_d = [nc.dram_tensor("a_w_up%d" % i, [64, 768], F32, kind="ExternalInput").ap() for i in range(2)]
    a_aup_d = [nc.dram_tensor("a_a_up%d" % i, [64, 768], F32, kind="ExternalInput").ap() for i in range(2)]
    a_gup_d = [nc.dram_tensor("a_g_up%d" % i, [128, 768], F32, kind="ExternalInput").ap() for i in range(2)]
    wkv_d = [nc.dram_tensor("mem_w_kv%d" % l, [D, 512], F32, kind="ExternalInput").ap() for l in range(4)]
    b_wq_d = [nc.dram_tensor("b_w_q%d" % i, [D, D], F32, kind="ExternalInput").ap() for i in range(2)]
    b_wo_d = [nc.dram_tensor("b_w_out%d" % i, [512, D], F32, kind="ExternalInput").ap() for i in range(2)]
    kvw_d = nc.dram_tensor("kv_w", [D, 1536], F32, kind="ExternalInput").ap()
    relb_d = nc.dram_tensor("rel_bias", [32, 12], F32, kind="ExternalInput").ap()
    sel_d = nc.dram_tensor("sel", [3, 33, 510], F32, kind="ExternalInput").ap()
    KT_d = nc.dram_tensor("kt_scratch", [64, 12, S], BF16, kind="Internal").ap()
    V_d = nc.dram_tensor("v_scratch", [S, 768], BF16, kind="Internal").ap()
    E_d = nc.dram_tensor("e_scratch", [3, 12, 510], F32, kind="Internal").ap()

    P = Prog(nc, es)

    def sbt(name, shape, dt):
        return es.enter_context(nc.sbuf_tensor(_u(name), shape, dt))
    c.vecs = sbt("c_vecs", [128, NV], F32)
    c.KF = sbt("c_kf", [128, NKF], F32)
    c.KB = sbt("c_kb", [128, 5 * 128], BF16)
    c.eps = sbt("c_eps", [128, 4], F32)
    c.MK = sbt("c_mk", [128, 2, 256], BF16)
    c.MVZ = sbt("c_mvz", [128, 2, 4, 128], BF16)
    P.dma("sp", c.vecs[:], vecs_d[:, :], "c0", w=[("vecs",)])
    P.dma("sp", c.KF[:], consts_d[:, 384:NCONST], "c1", w=[("kf",)])
    P.dma("pool", c.KB[:, 0:384], consts_d[:, 0:384], "c2", w=[("kb",)])
    P.dma("pool", c.KB[:, 384:640], consts_d[:, NCONST:NCONST + 256], "c3", w=[("kb2",)])
    P.add("dve", lambda e: e.memset(c.eps[:, 0:1], NORM_EPS), w=[("eps", 0)])
    P.add("dve", lambda e: e.memset(c.eps[:, 1:2], LNX_EPS), w=[("eps", 1)])
    P.add("dve", lambda e: e.memset(c.eps[:, 2:3], 1e-30), w=[("eps", 2)])
    P.add("dve", lambda e: e.memset(c.MVZ[:], 0.0), w=[("MVZ",)])
    c.bda_f = c.KF[:, 0:128]
    c.msu = c.KF[:, 128:256]
    c.msl = c.KF[:, 256:384]
    c.mui = c.KF[:, 384:512]
    c.cmask = c.KF[:, 512:768]
    c.jf = c.KF[:, 768:896]
    c.ident_b = c.KB[:, 0:128]
    c.ones_b = c.KB[:, 128:256]
    c.bd_b = c.KB[:, 256:384]
    c.ones_bf = c.ones_b
    c.onesh = [c.KB[:, 384:512], c.KB[:, 512:640]]
    P.barrier()

    stages = []
    for layer in range(4):
        stages.append(("ffn", 2 * layer))
        stages.append(("mix", layer))
        stages.append(("ffn", 2 * layer + 1))
        if layer == 1:
            stages.append(("kv", 0))
    stages = stages[:n_stages]
    cur = xT
    for si, (kind, i) in enumerate(stages):
        last = si == len(stages) - 1
        dst = outT if last else XS
        if kind == "ffn":
            ffn_phase(P, nc, c, cur, dst, w_in_d[i], w_out_d[i], gcol=V["ffn%d" % i])
        elif kind == "kv":
            kv_phase(P, nc, c, cur, kvw_d, KT_d, V_d, V)
            continue
        else:
            layer = i
            mem_prep(P, nc, c, memT, wkv_d[layer], layer, V)
            if layer < 2:
                rwkv_phase(P, nc, c, cur, dst, layer, layer, a_w_in_d[layer], a_w_out_d[layer], a_wup_d[layer], a_aup_d[layer],
                           a_gup_d[layer], V, dbg=dbg if last else None)
            else:
                attn_phase(P, nc, c, cur, dst, layer - 2, layer, b_wq_d[layer - 2], b_wo_d[layer - 2], KT_d, V_d, relb_d, sel_d, E_d, V)
        cur = dst
    P.barrier()
    es.close()
    print("[kernel] ops=%d instr=%d" % (P.nops, P.ninstr))
    return nc


def _rep2(v):
    return np.ascontiguousarray(np.concatenate([v, v]).reshape(128, 1))


def _t5_bucket(dist):
    dist = np.asarray(dist, np.int64)
    d_f = np.maximum(dist, 1).astype(np.float32)
    large = 16 + (np.log(d_f / np.float32(16)) / np.float32(np.log(2048 / 16)) * np.float32(16)).astype(np.int32)
    large = np.minimum(large, 31)
    return np.where(dist < 16, dist, large)


def make_sel():
    sel = np.zeros((3, 33, 510), np.float32)
    n = np.arange(255)
    for g, dil in enumerate((1, 4, 16)):
        own_valid = n >= 127
        bo = np.where(own_valid, _t5_bucket(np.maximum(n - 127, 0) * dil), 32)
        prev_valid = n <= 127
        bp = np.where(prev_valid, _t5_bucket((n + 1) * dil), 32)
        sel[g, bo, n] = 1.0
        sel[g, bp, 255 + n] = 1.0
    return sel


def make_consts():
    K = np.zeros((128, NCONST + 256), np.float32)
    K[:, NCONST:NCONST + 64] = 1.0
    K[:, NCONST + 192:NCONST + 256] = 1.0
    K[:, 0:128] = np.eye(128)
    K[:, 128:256] = 1.0
    bd = np.zeros((128, 128), np.float32)
    bd[:64, :64] = 1.0
    bd[64:, 64:] = 1.0
    K[:, 256:384] = bd
    K[:, 384:512] = bd / 64.0
    i = np.arange(128)
    K[:, 512:640] = (i[:, None] < i[None, :])
    K[:, 640:768] = (i[:, None] > i[None, :])
    K[:, 768:896] = (i[:, None] <= i[None, :])
    cm = np.ones(256, np.float32)
    cm[::128] = 0.0
    K[:, 896:896 + 256] = cm[None, :]
    K[:, 1152:1280] = np.eye(128)[::-1]
    return K


def kernel(**inputs):
    n_stages = int(inputs.pop("_n_stages", 99))
    cores = inputs.pop("_cores", list(range(8)))
    trace = inputs.pop("_trace", False)
    dbg_on = inputs.pop("_dbg", False)
    f = lambda a: np.asarray(a, dtype=np.float32)
    x = f(inputs["x"])
    mem = f(inputs["mem"])
    V, NV = vec_layout()
    vecs = np.zeros((128, NV), np.float32)

    def put(name, arr):
        arr = np.asarray(arr, np.float32)
        vecs[:, V[name]:V[name] + arr.shape[1]] = arr
    shared = {}
    for l in range(4):
        for nm in ("ffn_pre", "ffn_post"):
            i = 2 * l + (0 if nm == "ffn_pre" else 1)
            shared["w_in%d" % i] = _slots_in(f(inputs[nm + "_w_in"][l]))
            shared["w_out%d" % i] = np.ascontiguousarray(f(inputs[nm + "_w_out"][l]))
            put("ffn%d" % i, _vec_pk(f(inputs[nm + "_norm"][l])))
        put("mix_norm%d" % l, _vec_pk(f(inputs["mix_norm"][l])))
        put("mem_norm%d" % l, _vec_pk(f(inputs["mem_norm"][l])))
        put("mem_q_norm%d" % l, _rep2(f(inputs["mem_q_norm"][l])))
        put("mem_k_norm%d" % l, _rep2(f(inputs["mem_k_norm"][l])))
        shared["mem_w_kv%d" % l] = np.ascontiguousarray(f(inputs["mem_w_kv"][l]))
    for i in range(2):
        shared["a_w_in%d" % i] = _slots_in(f(inputs["a_w_in"][i]))
        shared["a_w_out%d" % i] = np.ascontiguousarray(f(inputs["a_w_out"][i]))
        shared["a_w_up%d" % i] = np.ascontiguousarray(f(inputs["a_w_up"][i]))
        shared["a_a_up%d" % i] = np.ascontiguousarray(f(inputs["a_a_up"][i]))
        shared["a_g_up%d" % i] = np.ascontiguousarray(f(inputs["a_g_up"][i]))
        put("a_mu%d" % i, _vec_pk(f(inputs["a_shift_mu"][i])))
        put("a_w0%d" % i, _vec_pk(f(inputs["a_w0"][i])))
        put("a_a0%d" % i, _vec_pk(f(inputs["a_a0"][i])))
        put("a_kk_scale%d" % i, _vec_pk(f(inputs["a_kk_scale"][i])))
        put("a_k_a%d" % i, _vec_pk(f(inputs["a_k_a"][i])))
        put("a_r_k%d" % i, _vec_pk(f(inputs["a_r_k"][i]).reshape(-1)))
        put("a_lnx_g%d" % i, _vec_pk(f(inputs["a_lnx_g"][i])))
        put("a_lnx_b%d" % i, _vec_pk(f(inputs["a_lnx_b"][i])))
    for j in range(2):
        put("b_q_norm%d" % j, _rep2(f(inputs["b_q_norm"][j])))
    put("kv_norm", _vec_pk(f(inputs["kv_norm"])))
    put("kv_k_norm", _rep2(f(inputs["kv_k_norm"])))
    for jj in range(2):
        shared["b_w_q%d" % jj] = np.ascontiguousarray(f(inputs["b_w_q"][jj]))
        shared["b_w_out%d" % jj] = np.ascontiguousarray(f(inputs["b_w_out"][jj]))
    shared["kv_w"] = np.ascontiguousarray(f(inputs["kv_w"]))
    shared["rel_bias"] = np.ascontiguousarray(f(inputs["rel_bias"]))
    shared["sel"] = make_sel()
    shared["vecs"] = vecs
    shared["consts"] = make_consts()
    nc = build(n_stages, dbg_on)
    in_maps = []
    for b in cores:
        m = dict(shared)
        m["xT"] = np.ascontiguousarray(x[b].T)
        m["memT"] = np.ascontiguousarray(mem[b].T)
        in_maps.append(m)
    if trace:
        res = run_bass_kernel_spmd(nc, in_maps, core_ids=list(range(len(cores))), trace=True)
        print("[kernel] exec_time_ns", res.exec_time_ns)
    else:
        res = run_bass_kernel_spmd(nc, in_maps, core_ids=list(range(len(cores))))
    if dbg_on:
        kernel.dbg = [np.ascontiguousarray(r["dbg"].T) for r in res.results]
    out = np.stack([np.ascontiguousarray(r["outT"].T) for r in res.results], axis=0)
    return out.astype(np.float32)
```

```python
import os
import numpy as np
from contextlib import ExitStack
import concourse.bass as bass
import concourse.mybir as mybir
from concourse.bass_utils import run_bass_kernel_spmd

F32 = mybir.dt.float32
BF16 = mybir.dt.bfloat16
AF = mybir.ActivationFunctionType
ALU = mybir.AluOpType
AX = mybir.AxisListType

D = 1024
KD = 8
S = 4096
FF = 2816
KF = 22
TT = 512
NT = S // TT
NORM_EPS = 1e-6

COMPUTE = ("pe", "act", "dve", "pool")


class Prog:
    def __init__(self, nc, es):
        self.nc = nc
        self.eng = dict(pe=nc.tensor, act=nc.scalar, dve=nc.vector, pool=nc.gpsimd, sp=nc.sync)
        self.sem = {e: es.enter_context(nc.semaphore("s_" + e)) for e in COMPUTE}
        self.cnt = {e: 0 for e in COMPUTE}
        self.es = es
        self.dsem = {}
        self.dcnt = {}
        self.pending = []
        self.last_w = {}
        self.readers = {}
        self.waited = {e: {} for e in self.eng}
        self.done = {}
        self.sigs = {e: [] for e in COMPUTE}
        self.nops = 0
        self.ninstr = 0
        self.defer = None
        self.attach = os.environ.get('KATTACH', '1') == '1'

    def add(self, eng, fn, r=(), w=(), dma=None):
        idx = self.nops
        self.nops += 1
        deps = set()
        for k in r:
            j = self.last_w.get(k)
            if j is not None:
                deps.add(j)
        for k in w:
            j = self.last_w.get(k)
            if j is not None:
                deps.add(j)
            rd = self.readers.get(k)
            if rd:
                deps.update(rd.values())
        for k in w:
            self.last_w[k] = idx
            self.readers[k] = {}
        tag = ("d", dma) if dma else ("c", eng)
        for k in r:
            if k not in w:
                self.readers.setdefault(k, {})[tag] = idx
        self.pending.append(dict(idx=idx, eng=eng, fn=fn, deps=deps, dma=dma, sig=False))
        return idx

    def _wait(self, eng, semname, semh, val):
        if self.waited[eng].get(semname, 0) >= val:
            return
        self.waited[eng][semname] = val
        if self.defer is not None:
            self.defer[semname] = (semh, max(val, self.defer.get(semname, (None, 0))[1]))
            return
        self.eng[eng].wait_ge(semh, val)
        self.ninstr += 1

    def flush(self):
        pend = self.pending
        self.pending = []
        byidx = {op["idx"]: op for op in pend}
        for op in pend:
            for j in op["deps"]:
                d = byidx.get(j)
                if d is not None and d["dma"] is None:
                    if not (d["eng"] == "pe" and op["eng"] == "pe" and op["dma"] is None):
                        d["sig"] = True
        last = {}
        for op in pend:
            if op["dma"] is None:
                last[op["eng"]] = op
        for op in last.values():
            op["sig"] = True
        for op in pend:
            eng = op["eng"]
            self.defer = {} if op["dma"] is None else None
            for j in sorted(op["deps"]):
                if j in self.done:
                    kind, name, val = self.done[j]
                    if kind == "d":
                        self._wait(eng, "d_" + name, self.dsem[name], val)
                    else:
                        if name == "pe" and eng == "pe" and op["dma"] is None:
                            continue
                        if val is None:
                            lst = self.sigs[name]
                            lo, hi = 0, len(lst)
                            while lo < hi:
                                mid = (lo + hi) // 2
                                if lst[mid][0] < j:
                                    lo = mid + 1
                                else:
                                    hi = mid
                            val = lst[lo][1]
                        self._wait(eng, "c_" + name, self.sem[name], val)
                else:
                    raise RuntimeError("dep on unemitted op")
            if op["dma"]:
                name = op["dma"]
                if name not in self.dsem:
                    self.dsem[name] = self.es.enter_context(self.nc.semaphore("d_" + name))
                    self.dcnt[name] = 0
                if self.dcnt[name] > 0:
                    self._wait(eng, "d_" + name, self.dsem[name], self.dcnt[name])
                ins = op["fn"](self.eng[eng])
                self.dcnt[name] += 16
                ins.then_inc(self.dsem[name], 16)
                self.done[op["idx"]] = ("d", name, self.dcnt[name])
            else:
                ws = list(self.defer.values()) if self.defer else []
                self.defer = None
                att = None
                if ws and self.attach:
                    att = ws[-1]
                    ws = ws[:-1]
                for (semh, val) in ws:
                    self.eng[eng].wait_ge(semh, val)
                    self.ninstr += 1
                ins = op["fn"](self.eng[eng])
                if att is not None:
                    ins._wait_ge(att[0], att[1])
                if op["sig"]:
                    self.cnt[eng] += 1
                    ins.then_inc(self.sem[eng], 1)
                    self.done[op["idx"]] = ("c", eng, self.cnt[eng])
                    self.sigs[eng].append((op["idx"], self.cnt[eng]))
                else:
                    self.done[op["idx"]] = ("c", eng, None)
            self.ninstr += 1

    def barrier(self, dma_only_on=("sp",)):
        self.flush()
        for f in self.eng:
            for e in COMPUTE:
                if e != f and self.cnt[e] > 0:
                    self._wait(f, "c_" + e, self.sem[e], self.cnt[e])
            for name, h in self.dsem.items():
                if self.dcnt[name] > 0:
                    self._wait(f, "d_" + name, h, self.dcnt[name])
        self.last_w = {}
        self.readers = {}


    def mm(self, out, lhsT, rhs, start=True, stop=True, r=(), w=()):
        self.add("pe", lambda e: e.matmul(out, lhsT, rhs, start=start, stop=stop), r=r, w=w)

    def tr(self, out, in_, ident, r=(), w=()):
        self.add("pe", lambda e: e.matmul(out, in_, ident, start=True, stop=True), r=r, w=w)

    def act(self, out, in_, func, r=(), w=(), bias=None, scale=None):
        kw = {}
        if bias is not None:
            kw["bias"] = bias
        if scale is not None:
            kw["scale"] = scale
        self.add("act", lambda e: e.activation(out=out, in_=in_, func=func, **kw), r=r, w=w)

    def tt(self, eng, out, in0, in1, op, r=(), w=()):
        self.add(eng, lambda e: e.tensor_tensor(out, in0, in1, op), r=r, w=w)

    def ts(self, eng, out, in0, s1, s2, op0, op1, r=(), w=()):
        self.add(eng, lambda e: e.tensor_scalar(out, in0, s1, s2, op0, op1), r=r, w=w)

    def tsmul(self, eng, out, in0, s1, r=(), w=()):
        self.add(eng, lambda e: e.tensor_scalar_mul(out, in0, s1), r=r, w=w)

    def stt(self, eng, out, in0, scalar, in1, op0, op1, r=(), w=()):
        self.add(eng, lambda e: e.scalar_tensor_tensor(out, in0, scalar, in1, op0, op1), r=r, w=w)

    def copy(self, eng, out, in_, r=(), w=()):
        if eng == "act":
            self.add("act", lambda e: e.activation(out=out, in_=in_, func=AF.Copy), r=r, w=w)
        elif os.environ.get("KCOPY", "mul") == "mul":
            self.add(eng, lambda e: e.tensor_scalar_mul(out, in_, 1.0), r=r, w=w)
        else:
            self.add(eng, lambda e: e.tensor_copy(out, in_), r=r, w=w)

    def dma(self, q, out, in_, sem, r=(), w=()):
        self.add(q, lambda e: e.dma_start(out=out, in_=in_), r=r, w=w, dma=sem)


def _slots_in(w):
    K, M = w.shape
    return np.ascontiguousarray(w.reshape(K // 128, 128, M // 128, 128).transpose(2, 1, 0, 3))


def _vec_pk(v):
    return np.ascontiguousarray(v.reshape(-1, 128).T)


class Ctx:
    pass


_UID = [0]


def _u(name):
    _UID[0] += 1
    return "%s_%d" % (name, _UID[0])


def ffn_phase(P, nc, c, src, dst, w_in_d, w_out_d, gcol):
    with ExitStack() as es:
        def sb(name, shape, dt):
            return es.enter_context(nc.sbuf_tensor(_u(name), shape, dt))

        def ps(name, shape, dt=F32):
            return es.enter_context(nc.psum_tensor(_u(name), shape, dt))
        WIN = sb("f_win", [128, 2 * KF, KD, 128], BF16)
        WOUT = sb("f_wout", [128, KF, D], BF16)
        XT = [sb("f_xt%d" % i, [128, KD, TT], F32) for i in range(2)]
        XN = sb("f_xn", [128, KD, TT], BF16)
        ACTB = sb("f_act", [128, KF, TT], BF16)
        SQ = [sb("f_sq%d" % i, [128, TT], BF16) for i in range(1)]
        SG = [sb("f_sg%d" % i, [128, TT], BF16) for i in range(2)]
        RSTD = sb("f_rstd", [128, TT], F32)
        PH = [ps("f_ph%d" % i, [128, TT]) for i in range(4)]
        PY = [ps("f_py%d" % i, [128, TT]) for i in range(2)]
        PSS = ps("f_pss", [128, TT])

        G = 2
        for j0 in range(0, 2 * KF, G):
            P.add("pool", lambda e, j0=j0: e.dma_start(
                out=WIN[:, j0:j0 + G], in_=w_in_d[j0:j0 + G].rearrange("j p k c -> p j k c")),
                w=[("win", j) for j in range(j0, j0 + G)], dma="wq%d" % ((j0 // G) % 4))
        for k0 in range(0, KF, G):
            P.add("pool", lambda e, k0=k0: e.dma_start(
                out=WOUT[:, k0:k0 + G], in_=w_out_d[k0 * 128:(k0 + G) * 128, :].rearrange("(k p) c -> p k c", p=128)),
                w=[("wout", k) for k in range(k0, k0 + G)], dma="wq%d" % ((k0 // G) % 4))

        def load(t):
            b = t % 2
            P.add("sp", lambda e: e.dma_start(
                out=XT[b][:], in_=src[:, t * TT:(t + 1) * TT].rearrange("(k p) n -> p k n", p=128)),
                r=[("X", src.tensor.name, t)], w=[("xt", b)], dma="xl%d" % b)

        def do_tile(t):
            b = t % 2
            if t + 1 < NT:
                load(t + 1)
            xt = XT[b]
            for k in range(KD):
                q = 0
                P.add("act", lambda e, k=k, q=q: e.activation(out=SQ[q][:], in_=xt[:, k, :], func=AF.Square),
                      r=[("xt", b)], w=[("sq", q)])
                P.add("pe", lambda e, k=k, q=q: e.matmul(PSS[:], c.ones_bf[:], SQ[q][:], start=(k == 0), stop=(k == KD - 1)),
                      r=[("sq", q)], w=[("pss",)])
            P.add("act", lambda e: e.activation(out=RSTD[:], in_=PSS[:], func=AF.Ln, bias=c.eps[:, 0:1], scale=1.0 / D),
                  r=[("pss",)], w=[("rstd",)])
            P.add("act", lambda e: e.activation(out=RSTD[:], in_=RSTD[:], func=AF.Exp, scale=-0.5),
                  r=[("rstd",)], w=[("rstd",)])
            for k in range(KD):
                P.add("dve", lambda e, k=k: e.scalar_tensor_tensor(
                    XN[:, k, :], xt[:, k, :], c.vecs[:, gcol + k:gcol + k + 1], RSTD[:], ALU.mult, ALU.mult),
                    r=[("xt", b), ("rstd",)], w=[("xn", k)])
            for j in range(KF):
                pg = PH[(2 * j) % 4]
                pu = PH[(2 * j + 1) % 4]
                kg, ku = ("ph", (2 * j) % 4), ("ph", (2 * j + 1) % 4)
                for k in range(KD):
                    P.add("pe", lambda e, j=j, k=k, pg=pg: e.matmul(pg[:], WIN[:, j, k, :], XN[:, k, :], start=(k == 0), stop=(k == KD - 1)),
                          r=[("win", j), ("xn", k)], w=[kg])
                for k in range(KD):
                    P.add("pe", lambda e, j=j, k=k, pu=pu: e.matmul(pu[:], WIN[:, KF + j, k, :], XN[:, k, :], start=(k == 0), stop=(k == KD - 1)),
                          r=[("win", KF + j), ("xn", k)], w=[ku])
                q = j % 2
                P.add("act", lambda e, pg=pg, q=q: e.activation(out=SG[q][:], in_=pg[:], func=AF.Silu),
                      r=[kg], w=[("sg", q)])
                P.add("dve", lambda e, j=j, pu=pu, q=q: e.tensor_tensor(ACTB[:, j, :], pu[:], SG[q][:], ALU.mult),
                      r=[ku, ("sg", q)], w=[("act", j)])
            for m in range(KD):
                py = PY[m % 2]
                ky = ("py", m % 2)
                for k in range(KF):
                    P.add("pe", lambda e, m=m, k=k, py=py: e.matmul(py[:], WOUT[:, k, m * 128:(m + 1) * 128], ACTB[:, k, :], start=(k == 0), stop=(k == KF - 1)),
                          r=[("wout", k), ("act", k)], w=[ky])
                P.add("dve", lambda e, m=m, py=py: e.scalar_tensor_tensor(
                    xt[:, m, :], py[:], 0.5, xt[:, m, :], ALU.mult, ALU.add),
                    r=[ky, ("xt", b)], w=[("xt", b)])
            P.add("sp", lambda e, t=t: e.dma_start(
                out=dst[:, t * TT:(t + 1) * TT].rearrange("(k p) n -> p k n", p=128), in_=xt[:]),
                r=[("xt", b)], w=[("X", dst.tensor.name, t)], dma="xs%d" % b)
        load(0)
        for t in range(NT):
            do_tile(t)
        P.barrier()


import os
DBG_NTR = int(os.environ.get('KDBG_NTR', '999'))
TMZENG = os.environ.get('KTMZ', 'act')
DBG_STOP = float(os.environ.get('KDBG_STOP', '99'))
TR = 256
HG = [[0, 2, 4, 6], [1, 3, 5, 7], [8, 10], [9, 11]]


def hgk(h):
    return 2 * (h // 8) + (h % 2)

CH = 128
NCH = TR // CH
NTR = S // TR
LNX_EPS = 64e-5
DEC_SCALE = -0.6065306597126334


class Banks:
    def __init__(self, tiles, tag):
        self.tiles = tiles
        self.tag = tag
        self.i = 0

    def next(self):
        i = self.i
        self.i = (self.i + 1) % len(self.tiles)
        return self.tiles[i], (self.tag, i)


def rms_tile(P, c, xt, xkey, xn, xnkey, gcol, n, PSS, psskey, SQ, RSTD, tag):
    for k in range(KD):
        q = k % 2
        P.act(SQ[q][:, :n], xt[:, k, :n], AF.Square, r=[xkey], w=[(tag, "sq", q)])
        P.mm(PSS[:, :n], c.ones_b[:], SQ[q][:, :n], start=(k == 0), stop=(k == KD - 1), r=[(tag, "sq", q)], w=[psskey])
    P.act(RSTD[:, :n], PSS[:, :n], AF.Ln, r=[psskey], w=[(tag, "rstd")], bias=c.eps[:, 0:1], scale=1.0 / D)
    P.act(RSTD[:, :n], RSTD[:, :n], AF.Exp, r=[(tag, "rstd")], w=[(tag, "rstd")], scale=-0.5)
    for k in range(KD):
        P.stt("dve", xn[:, k, :n], xt[:, k, :n], c.vecs[:, gcol + k:gcol + k + 1], RSTD[:, :n], ALU.mult, ALU.mult,
              r=[xkey, (tag, "rstd")], w=[xnkey + (k,)])


def head_rms(P, c, out, src, srckey, gain_col, n, bank, bkey, SQ, TMP, tag, outkey):
    P.act(SQ[:, :n], src, AF.Square, r=[srckey], w=[(tag, "hsq")])
    P.mm(bank[:, :n], c.bd_b[:], SQ[:, :n], r=[(tag, "hsq")], w=[bkey])
    P.act(TMP[:, :n], bank[:, :n], AF.Ln, r=[bkey], w=[(tag, "htmp")], bias=c.eps[:, 0:1], scale=1.0 / 64)
    P.act(TMP[:, :n], TMP[:, :n], AF.Exp, r=[(tag, "htmp")], w=[(tag, "htmp")], scale=-0.5)
    P.stt("dve", out, src, c.vecs[:, gain_col:gain_col + 1], TMP[:, :n], ALU.mult, ALU.mult,
          r=[srckey, (tag, "htmp")], w=[outkey])


def mem_prep(P, nc, c, memT_d, wkv_d, layer, V):
    with ExitStack() as es:
        def sb(name, shape, dt):
            return es.enter_context(nc.sbuf_tensor(_u(name), shape, dt))
        WKV = sb("mp_wkv", [128, KD, 512], BF16)
        MT = sb("mp_mt", [128, KD, 256], F32)
        MN = sb("mp_mn", [128, KD, 256], BF16)
        SQ = [sb("mp_sq%d" % i, [128, 256], BF16) for i in range(2)]
        RSTD = sb("mp_rstd", [128, 256], F32)
        KF32 = sb("mp_kf", [128, 256], F32)
        TMP = sb("mp_tmp", [128, 256], F32)
        pb = [es.enter_context(nc.psum_tensor(_u("mp_pb%d" % i), [128, 512], F32)) for i in range(3)]
        P.dma("pool", WKV[:], wkv_d.rearrange("(k p) c -> p k c", p=128), "wq0", w=[("mp", "wkv")])
        P.dma("sp", MT[:], memT_d.rearrange("(k p) n -> p k n", p=128), "xl0", w=[("mp", "mt")])
        rms_tile(P, c, MT, ("mp", "mt"), MN, ("mp", "mn"), V["mem_norm%d" % layer], 256, pb[0], ("mp", "pb", 0), SQ, RSTD, "mp")
        mnkeys = [("mp", "mn", k) for k in range(KD)]
        for hp in range(2):
            for k in range(KD):
                P.mm(pb[1][:, :256], WKV[:, k, hp * 128:(hp + 1) * 128], MN[:, k, :], start=(k == 0), stop=(k == KD - 1),
                     r=[("mp", "wkv"), mnkeys[k]], w=[("mp", "pb", 1)])
            P.copy("act", KF32[:], pb[1][:, :256], r=[("mp", "pb", 1)], w=[("mp", "kf")])
            head_rms(P, c, c.MK[:, hp, :], KF32[:], ("mp", "kf"), V["mem_k_norm%d" % layer], 256, pb[2], ("mp", "pb", 2), SQ[0], TMP, "mpk",
                     ("MK", hp))
        for mc in range(2):
            for k in range(KD):
                P.mm(pb[1][:, :256], MN[:, k, mc * 128:(mc + 1) * 128], WKV[:, k, 256:512], start=(k == 0), stop=(k == KD - 1),
                     r=[("mp", "wkv"), mnkeys[k]], w=[("mp", "pb", 1)])
            for h in range(4):
                hh = h % 2
                P.copy("act", c.MVZ[:, mc, h, hh * 64:(hh + 1) * 64], pb[1][:, h * 64:(h + 1) * 64], r=[("mp", "pb", 1)], w=[("MVZ",)])
        P.barrier()


def mem_attn_tile(P, c, QM, qkeys, YM, ymkeys, n, layer, V, banks, SQ, TMP, QN, PT, RD, tag):
    for hp in range(2):
        bank, bk = banks.next()
        head_rms(P, c, QN[:, hp, :n], QM[:, hp, :n], qkeys[hp], V["mem_q_norm%d" % layer], n, bank, bk, SQ, TMP, tag, (tag, "qn", hp))
    for hp in range(2):
        for hh in range(2):
            off = hh * 64
            for mc in range(2):
                bl, kl = banks.next()
                P.mm(bl[:, :n], c.MK[off:off + 64, hp, mc * 128:(mc + 1) * 128], QN[off:off + 64, hp, :n],
                     r=[("MK", hp), (tag, "qn", hp)], w=[kl])
                P.act(PT[hh * 2 + mc][:, :n], bl[:, :n], AF.Exp, r=[kl], w=[(tag, "pt", hh * 2 + mc)], scale=0.125)
        bnum, knum = banks.next()
        bden, kden = banks.next()
        for i in range(4):
            hh, mc = i // 2, i % 2
            h = hp * 2 + hh
            P.mm(bnum[:, :n], c.MVZ[:, mc, h, :], PT[i][:, :n], start=(i == 0), stop=(i == 3), r=[("MVZ",), (tag, "pt", i)], w=[knum])
        for i in range(4):
            hh, mc = i // 2, i % 2
            P.mm(bden[:, :n], c.onesh[hh], PT[i][:, :n], start=(i == 0), stop=(i == 3), r=[(tag, "pt", i)], w=[kden])
        P.act(RD[:, :n], bden[:, :n], AF.Ln, r=[kden], w=[(tag, "rd")])
        P.act(RD[:, :n], RD[:, :n], AF.Exp, r=[(tag, "rd")], w=[(tag, "rd")], scale=-1.0)
        P.tt("dve", YM[:, hp, :n], bnum[:, :n], RD[:, :n], ALU.mult, r=[knum, (tag, "rd")], w=[ymkeys[hp]])


def rwkv_phase(P, nc, c, src, dst, li, layer, w_in_d, w_out_d, wup_d, aup_d, gup_d, V, dbg=None):
    with ExitStack() as es:
        def sb(name, shape, dt):
            return es.enter_context(nc.sbuf_tensor(_u("rk_" + name), shape, dt))
        WA = sb("wa", [128, 22, KD, 128], BF16)
        WO = sb("wo", [128, KD, D], BF16)
        WAUP = sb("waup", [128, 768], BF16)
        GUP = sb("gup", [128, 768], BF16)
        XT = sb("xt", [128, KD, TR], F32)
        XN = sb("xn", [128, KD, TR], BF16)
        SQ = [sb("sq%d" % i, [128, TR], BF16) for i in range(2)]
        RSTD = sb("rstd", [128, TR], F32)
        CARRY = sb("carry", [128, 20], F32)
        OMM = sb("omm", [128, 20], F32)
        OMKA = sb("omka", [128, 6], F32)
        TA = sb("ta", [128, TR], F32)
        P18 = sb("p18", [128, TR], F32)
        P19 = sb("p19", [128, TR], F32)
        TW = sb("tw", [128, TR], BF16)
        AL = sb("al", [128, TR], BF16)
        SGG = sb("sgg", [128, TR], BF16)
        QM = sb("qm", [128, 2, TR], F32)
        Rf = sb("rf", [128, TR], F32)
        Kf = sb("kf", [128, TR], F32)
        Vf = sb("vf", [128, TR], F32)
        T = [sb("t%d" % i, [128, TR], F32) for i in range(10)]
        HB = [sb("hb%d" % i, [128, TR], BF16) for i in range(4)]
        G = sb("g", [128, 6, TR], BF16)
        BON = sb("bon", [128, 6, TR], BF16)
        RT = sb("rt", [128, 6, TR], BF16)
        KT = sb("kt", [128, 6, TR], BF16)
        BT = sb("bt", [128, 6, TR], BF16)
        AT = sb("at", [128, 6, TR], BF16)
        PC = sb("pc", [128, 6, NCH], F32)
        TMV = [sb("tmv%d" % i, [128, 6, 128], BF16) for i in range(NCH)]
        TMZ = [sb("tmz%d" % i, [128, 2, 13, 128], BF16) for i in range(NCH)]
        W = [sb("w%d" % i, [128, 2, 768], BF16) for i in range(2 * NCH)]
        AK = [sb("ak%d" % i, [128, 12, 128], BF16) for i in range(2 * NCH)]
        BK = [sb("bk%d" % i, [128, 12, 128], BF16) for i in range(2 * NCH)]
        AAK = sb("aak", [128, 12, 128], BF16)
        ARK = sb("ark", [128, 12, 128], BF16)
        ARB = sb("arb", [128, 12, 128], BF16)
        AHT = sb("aht", [128, 6, 128], BF16)
        UU = sb("uu", [128, 12, 64], BF16)
        SF = sb("sf", [128, 6, 64], F32)
        SB = sb("sb", [128, 6, 64], BF16)
        YB = sb("yb", [128, 6, TR], F32)
        YO = sb("yo", [128, 6, TR], BF16)
        YM = sb("ym", [128, 2, TR], BF16)
        QN = sb("qn", [128, 2, TR], BF16)
        PT = [sb("pt%d" % i, [128, TR], BF16) for i in range(4)]
        RD = sb("rd", [128, TR], F32)
        MSQ = sb("msq", [128, TR], BF16)
        MTMP = sb("mtmp", [128, TR], F32)
        pbt = [es.enter_context(nc.psum_tensor(_u("rk_pb%d" % i), [128, 512], F32)) for i in range(8)]
        banks = Banks(pbt, "rpb")
        tbanks = banks
        print("[kernel] rwkv sbuf free", nc.sbuf_bytes_remaining)

        for j0 in range(0, 22, 2):
            P.dma("pool", WA[:, j0:j0 + 2], w_in_d[j0:j0 + 2].rearrange("j p k c -> p j k c"), "wq%d" % ((j0 // 2) % 4),
                  w=[("wa", j) for j in range(j0, j0 + 2)])
        P.dma("pool", WAUP[0:64, :], wup_d[:, :], "wq0", w=[("waup", 0)])
        P.dma("pool", WAUP[64:128, :], aup_d[:, :], "wq1", w=[("waup", 1)])
        P.dma("pool", GUP[:], gup_d[:, :], "wq2", w=[("gup",)])
        for k0 in range(0, KD, 2):
            P.dma("pool", WO[:, k0:k0 + 2], w_out_d[k0 * 128:(k0 + 2) * 128, :].rearrange("(k p) c -> p k c", p=128), "wq%d" % ((k0 // 2) % 4),
                  w=[("wo", k) for k in range(k0, k0 + 2)])
        mu0 = V["a_mu%d" % li]
        P.ts("dve", OMM[:], c.vecs[:, mu0:mu0 + 20], -1.0, 1.0, ALU.mult, ALU.add, w=[("omm",)])
        ka0 = V["a_k_a%d" % li]
        P.ts("dve", OMKA[:], c.vecs[:, ka0:ka0 + 6], -1.0, 1.0, ALU.mult, ALU.add, w=[("omka",)])
        P.add("dve", lambda e: e.memset(CARRY[:], 0.0), w=[("carry", m) for m in range(20)])
        P.add("dve", lambda e: e.memset(SF[:], 0.0), w=[("sf",)])
        P.add("dve", lambda e: e.memset(SB[:], 0.0), w=[("sbk",)])
        for i in range(NCH):
            P.add("dve", lambda e, i=i: e.memset(TMZ[i][:], 0.0), w=[("tmz", i, hp, ty, hh) for hp in range(6) for ty in range(2) for hh in range(2)])

        def vcol(name, i):
            o = V[name + "%d" % li] + i
            return c.vecs[:, o:o + 1]

        def proj(m, n):
            bank, bk = banks.next()
            for k in range(KD):
                P.mm(bank[:, :n], WA[:, m, k, :], XN[:, k, :n], start=(k == 0), stop=(k == KD - 1), r=[("wa", m), ("xn", k)], w=[bk])
            return bank, bk

        def shift_evac(m, bank, bk, out, outkey):
            n = TR
            P.act(TA[:, :n], bank[:, :n], AF.Copy, r=[bk, ("omm",)], w=[("ta",)], scale=OMM[:, m:m + 1])
            P.stt("dve", out[:, 1:n], bank[:, 0:n - 1], c.vecs[:, mu0 + m:mu0 + m + 1], TA[:, 1:n], ALU.mult, ALU.add,
                  r=[bk, ("ta",)], w=[outkey])
            P.stt("dve", out[:, 0:1], CARRY[:, m:m + 1], c.vecs[:, mu0 + m:mu0 + m + 1], TA[:, 0:1], ALU.mult, ALU.add,
                  r=[("carry", m), ("ta",)], w=[outkey + ("c0",)])
            P.copy("dve", CARRY[:, m:m + 1], bank[:, n - 1:n], r=[bk], w=[("carry", m)])

        def do_tile(t):
            n = TR
            P.dma("sp", XT[:], src[:, t * TR:(t + 1) * TR].rearrange("(k p) n -> p k n", p=128), "xl0",
                  r=[("X", src.tensor.name, t)], w=[("xt",)])
            rb, rbk = banks.next()
            rms_tile(P, c, XT, ("xt",), XN, ("xn",), V["mix_norm%d" % layer], n, rb, rbk, SQ, RSTD, "rk")
            b18, k18 = proj(18, n)
            shift_evac(18, b18, k18, P18, ("p18",))
            b19, k19 = proj(19, n)
            shift_evac(19, b19, k19, P19, ("p19",))
            p18k = [("p18",), ("p18", "c0")]
            p19k = [("p19",), ("p19", "c0")]
            P.act(TW[0:64, :], P18[0:64, :], AF.Tanh, r=p18k, w=[("tw",)])
            P.copy("dve", AL[64:128, :], P18[64:128, :], r=p18k, w=[("al",)])
            P.act(SGG[:], P19[:], AF.Sigmoid, r=p19k, w=[("sgg",)])
            for q in range(2):
                bq, kq = proj(20 + q, n)
                P.copy("act", QM[:, q, :], bq[:, :n], r=[kq], w=[("qm", q)])
            if DBG_STOP <= 1:
                return
            for hp in range(6):
                cs = slice(hp * 128, (hp + 1) * 128)
                bz, kz = banks.next()
                P.mm(bz[:, :n], WAUP[0:64, cs], TW[0:64, :], r=[("waup", 0), ("tw",)], w=[kz])
                SW, LOGW, ALR = T[0], T[1], T[2]
                P.act(SW[:], bz[:, :n], AF.Sigmoid, r=[kz], w=[("sw",)], bias=vcol("a_w0", hp))
                P.tsmul("dve", LOGW[:], SW[:], DEC_SCALE, r=[("sw",)], w=[("logw",)])
                bz, kz = banks.next()
                P.mm(bz[:, :n], WAUP[64:128, cs], AL[64:128, :], r=[("waup", 1), ("al",)], w=[kz])
                P.act(ALR[:], bz[:, :n], AF.Sigmoid, r=[kz], w=[("alr",)], bias=vcol("a_a0", hp))
                bz, kz = banks.next()
                P.mm(bz[:, :n], GUP[:, cs], SGG[:], r=[("gup",), ("sgg",)], w=[kz])
                P.copy("act", G[:, hp, :], bz[:, :n], r=[kz], w=[("g", hp)])
                if DBG_STOP <= 1.1:
                    continue
                br, kr = proj(hp, n)
                shift_evac(hp, br, kr, Rf, ("rf",))
                bk_, kk_ = proj(6 + hp, n)
                shift_evac(6 + hp, bk_, kk_, Kf, ("kf",))
                bv, kv = proj(12 + hp, n)
                shift_evac(12 + hp, bv, kv, Vf, ("vf",))
                rfk = [("rf",), ("rf", "c0")]
                kfk = [("kf",), ("kf", "c0")]
                vfk = [("vf",), ("vf", "c0")]
                if DBG_STOP <= 1.2:
                    continue
                KS, NRM, KK, TMv, KM, Bv, L = T[3], T[4], T[5], T[6], T[7], T[8], T[9]
                P.tsmul("dve", KS[:], Kf[:], vcol("a_kk_scale", hp), r=kfk, w=[("ks",)])
                P.act(HB[0][:], KS[:], AF.Square, r=[("ks",)], w=[("hb", 0)])
                bz, kz = banks.next()
                P.mm(bz[:, :n], c.bd_b[:], HB[0][:], r=[("hb", 0)], w=[kz])
                P.act(NRM[:], bz[:, :n], AF.Ln, r=[kz], w=[("nrm",)], bias=c.eps[:, 2:3])
                P.act(NRM[:], NRM[:], AF.Exp, r=[("nrm",)], w=[("nrm",)], scale=-0.5)
                P.tt("dve", KK[:], KS[:], NRM[:], ALU.mult, r=[("ks",), ("nrm",)], w=[("kk",)])
                P.ts("dve", TMv[:], ALR[:], vcol("a_k_a", hp), OMKA[:, hp:hp + 1], ALU.mult, ALU.add, r=[("alr",), ("omka",)], w=[("tmv",)])
                P.tt("dve", KM[:], Kf[:], TMv[:], ALU.mult, r=kfk + [("tmv",)], w=[("km",)])
                P.tt("dve", Bv[:], KK[:], ALR[:], ALU.mult, r=[("kk",), ("alr",)], w=[("bv",)])
                P.stt("dve", HB[1][:], Rf[:], vcol("a_r_k", hp), KM[:], ALU.mult, ALU.mult, r=rfk + [("km",)], w=[("hb", 1)])
                bz, kz = banks.next()
                P.mm(bz[:, :n], c.bd_b[:], HB[1][:], r=[("hb", 1)], w=[kz])
                P.tt("dve", BON[:, hp, :], bz[:, :n], Vf[:], ALU.mult, r=[kz] + vfk, w=[("bon", hp)])
                if DBG_STOP <= 1.3:
                    continue
                P.add("dve", lambda e, L=L, LOGW=LOGW: e.tensor_tensor_scan(L[:], c.cmask[:, :n], LOGW[:], 0.0, ALU.mult, ALU.add),
                      r=[("logw",)], w=[("L",)])
                E1 = T[0]
                P.act(E1[:], L[:], AF.Exp, r=[("L",)], w=[("sw",)])
                P.tt("dve", RT[:, hp, :], Rf[:], E1[:], ALU.mult, r=rfk + [("sw",)], w=[("rt", hp)])
                E2 = T[3]
                P.act(E2[:], L[:], AF.Exp, r=[("L",), ("kk",)], w=[("ks",)], scale=-1.0)
                P.tt("dve", KT[:, hp, :], KM[:], E2[:], ALU.mult, r=[("km",), ("ks",)], w=[("kt", hp)])
                P.tt("dve", BT[:, hp, :], Bv[:], E2[:], ALU.mult, r=[("bv",), ("ks",)], w=[("bt", hp)])
                LX = T[4]
                P.tt("dve", LX[:], L[:], LOGW[:], ALU.subtract, r=[("L",), ("logw",), ("kk",)], w=[("nrm",)])
                P.act(LX[:], LX[:], AF.Exp, r=[("nrm",)], w=[("nrm",)])
                P.stt("dve", AT[:, hp, :], KK[:], -1.0, LX[:], ALU.mult, ALU.mult, r=[("kk",), ("nrm",)], w=[("at", hp)])
                DEC = T[6]
                for cc in range(NCH):
                    ce = (cc + 1) * CH - 1
                    P.act(DEC[:, cc * CH:(cc + 1) * CH], L[:, cc * CH:(cc + 1) * CH], AF.Exp, r=[("L",), ("km",)], w=[("tmv",)],
                          bias=L[:, ce:ce + 1], scale=-1.0)
                    P.act(PC[:, hp, cc:cc + 1], L[:, ce:ce + 1], AF.Exp, r=[("L",)], w=[("pc", cc)])
                P.tt("dve", HB[2][:], KM[:], DEC[:], ALU.mult, r=[("km",), ("tmv",)], w=[("hb", 2)])
                P.tt("dve", HB[3][:], Bv[:], DEC[:], ALU.mult, r=[("bv",), ("tmv",)], w=[("hb", 3)])
                P.copy("act", HB[0][:], Vf[:], r=vfk, w=[("hb", 0)])
                if DBG_STOP <= 1.4:
                    continue
                for cc in range(NCH):
                    tb, tk = tbanks.next()
                    csl = slice(cc * CH, (cc + 1) * CH)
                    P.tr(tb[:, 0:128], HB[0][:, csl], c.ident_b[:], r=[("hb", 0)], w=[tk])
                    P.tr(tb[:, 128:256], HB[2][:, csl], c.ident_b[:], r=[("hb", 2)], w=[tk])
                    P.tr(tb[:, 256:384], HB[3][:, csl], c.ident_b[:], r=[("hb", 3)], w=[tk])
                    P.tr(tb[:, 384:512], AT[:, hp, csl], c.ident_b[:], r=[("at", hp)], w=[tk])
                    if DBG_STOP <= 1.45:
                        continue
                    evq = "act" if (hp * NCH + cc) % 2 == 0 else "dve"
                    P.copy(evq, TMV[cc][:, hp, :], tb[:, 0:128], r=[tk], w=[("tmv", cc, hp)])
                    if DBG_STOP <= 1.46:
                        continue
                    for ty in range(2):
                        for hh in range(2):
                            P.copy(evq, TMZ[cc][:, ty, 2 * hp + hh, hh * 64:(hh + 1) * 64],
                                   tb[:, 128 + ty * 128 + hh * 64:128 + ty * 128 + (hh + 1) * 64], r=[tk], w=[("tmz", cc, hp, ty, hh)])
                    if DBG_STOP <= 1.47:
                        continue
                    P.copy(evq, W[2 * cc][:, 0, hp * 128:(hp + 1) * 128], tb[:, 384:512], r=[tk], w=[("w", 2 * cc, hp)])
            if DBG_STOP <= 2:
                return
            def amat(cc, lhs, lkey, rhs, rkey, mask, dest, dkey):
                csl = slice(cc * CH, (cc + 1) * CH)
                for g in range(4):
                    heads = HG[g]
                    bank, bk = banks.next()
                    for hi, h in enumerate(heads):
                        hp, off = h // 2, (h % 2) * 64
                        P.mm(bank[:, hi * 128:(hi + 1) * 128], lhs[off:off + 64, hp, csl], rhs[off:off + 64, hp, csl],
                             r=[(lkey, hp), (rkey, hp)], w=[bk])
                    nh = len(heads)
                    P.tt("dve", dest[:, heads[0]:heads[-1] + 1:2, :], bank[:, 0:nh * 128].rearrange("p (a b) -> p a b", a=nh),
                         mask[:].unsqueeze(1).to_broadcast([128, nh, 128]), ALU.mult, r=[bk], w=[dkey + (g,)])

            wst = {}
            for cc in range(NCH):
                amat(cc, BT, "bt", AT, "at", c.msu, BK[2 * cc], ("bk", cc, 0))
                amat(cc, AT, "at", BT, "bt", c.msl, AK[2 * cc], ("ak", cc, 0))
                amat(cc, KT, "kt", AT, "at", c.msu, AAK, ("aak",))
                for h0, nh in ((0, 8), (8, 4)):
                    bank, bk = banks.next()
                    for hi in range(nh):
                        h = h0 + hi
                        hp, off = h // 2, (h % 2) * 64
                        P.mm(bank[:, hi * 64:(hi + 1) * 64], AAK[:, h, :], TMV[cc][:, hp, off:off + 64],
                             r=[("aak", hgk(h)), ("tmv", cc, hp)], w=[bk])
                    P.copy("act", W[2 * cc][:, 1, h0 * 64:(h0 + nh) * 64], bank[:, 0:nh * 64], r=[bk], w=[("wu", 2 * cc, h0)])
                wst[cc] = [2 * cc, 2 * cc + 1, [("w", 2 * cc, hp) for hp in range(6)] + [("wu", 2 * cc, 0), ("wu", 2 * cc, 8)]]
            for lev in range(7):
                ai, ao = lev % 2, (lev + 1) % 2
                for cc in range(NCH):
                    wcur, wnxt, wk = wst[cc]
                    BKi, AKi, BKo, AKo = BK[2 * cc + ai], AK[2 * cc + ai], BK[2 * cc + ao], AK[2 * cc + ao]
                    for hg in range(3):
                        bank, bk = banks.next()
                        for hi in range(4):
                            h = hg * 4 + hi
                            for part in range(2):
                                o = bank[:, part * 256 + hi * 64:part * 256 + (hi + 1) * 64]
                                rhs = W[wcur][:, part, h * 64:(h + 1) * 64]
                                P.mm(o, c.ident_b[:], rhs, start=True, stop=False, r=wk, w=[bk])
                                P.mm(o, BKi[:, h, :], rhs, start=False, stop=True, r=[("bk", cc, ai, hgk(h))], w=[bk])
                        P.copy("act" if hg % 2 == 0 else "dve", W[wnxt][:, :, hg * 256:(hg + 1) * 256],
                               bank[:, :].rearrange("p (a b) -> p a b", a=2), r=[bk], w=[("wl", wnxt, hg)])
                    if lev < 6:
                        for g in range(4):
                            heads = HG[g]
                            nh = len(heads)
                            bank, bk = banks.next()
                            for hi, h in enumerate(heads):
                                P.mm(bank[:, hi * 128:(hi + 1) * 128], BKi[:, h, :], AKi[:, h, :],
                                     r=[("bk", cc, ai, g), ("ak", cc, ai, g)], w=[bk])
                            P.copy("dve" if g % 2 == 0 else "act", AKo[:, heads[0]:heads[-1] + 1:2, :],
                                   bank[:, 0:nh * 128].rearrange("p (a b) -> p a b", a=nh), r=[bk], w=[("ak", cc, ao, g)])
                            bank, bk = banks.next()
                            for hi, h in enumerate(heads):
                                P.mm(bank[:, hi * 128:(hi + 1) * 128], AKi[:, h, :], BKi[:, h, :],
                                     r=[("bk", cc, ai, g), ("ak", cc, ai, g)], w=[bk])
                            P.copy("act" if g % 2 == 0 else "dve", BKo[:, heads[0]:heads[-1] + 1:2, :],
                                   bank[:, 0:nh * 128].rearrange("p (a b) -> p a b", a=nh), r=[bk], w=[("bk", cc, ao, g)])
                    wst[cc] = [wnxt, wcur, [("wl", wnxt, hg) for hg in range(3)]]
            for cc in range(NCH):
                csl = slice(cc * CH, (cc + 1) * CH)
                wf, _, wfk = wst[cc]
                amat(cc, KT, "kt", RT, "rt", c.mui, ARK, ("ark",))
                amat(cc, BT, "bt", RT, "rt", c.mui, ARB, ("arb",))
                for p0, npp in ((0, 4), (4, 2)):
                    tb, tk = tbanks.next()
                    for pi in range(npp):
                        hp = p0 + pi
                        P.tr(tb[:, pi * 128:(pi + 1) * 128], W[wf][:, 0, hp * 128:(hp + 1) * 128], c.ident_b[:], r=wfk, w=[tk])
                    P.copy("dve", AHT[:, p0:p0 + npp, :], tb[:, 0:npp * 128].rearrange("p (a b) -> p a b", a=npp), r=[tk], w=[("aht", p0)])
                ahk = [("aht", 0), ("aht", 4)]
                for g in range(4):
                    heads = HG[g]
                    nh = len(heads)
                    bank, bk = banks.next()
                    for hi, h in enumerate(heads):
                        hp, off = h // 2, (h % 2) * 64
                        P.mm(bank[:, hi * 64:(hi + 1) * 64], AHT[off:off + 64, hp, :], SB[off:off + 64, hp, :], start=True, stop=False,
                             r=ahk + [("sbk",)], w=[bk])
                        P.mm(bank[:, hi * 64:(hi + 1) * 64], c.ident_b[:], W[wf][:, 1, h * 64:(h + 1) * 64], start=False, stop=True, r=wfk, w=[bk])
                    P.copy("act", UU[:, heads[0]:heads[-1] + 1:2, :], bank[:, 0:nh * 64].rearrange("p (a b) -> p a b", a=nh), r=[bk], w=[("uu", g)])
                uuk = [("uu", g) for g in range(4)]
                for p0, npp in ((0, 4), (4, 2)):
                    for hh in range(2):
                        off = hh * 64
                        bank, bk = banks.next()
                        for pi in range(npp):
                            hp = p0 + pi
                            h = 2 * hp + hh
                            o = bank[off:off + 64, pi * 128:(pi + 1) * 128]
                            P.mm(o, SB[off:off + 64, hp, :], RT[off:off + 64, hp, csl], start=True, stop=False, r=[("sbk",), ("rt", hp)], w=[bk])
                            P.mm(o, TMV[cc][:, hp, off:off + 64], ARK[:, h, :], start=False, stop=False, r=[("tmv", cc, hp), ("ark", hgk(h))], w=[bk])
                            P.mm(o, UU[:, h, :], ARB[:, h, :], start=False, stop=True, r=uuk + [("arb", hgk(h))], w=[bk])
                        P.copy("act", YB[off:off + 64, p0:p0 + npp, csl], bank[off:off + 64, 0:npp * 128].rearrange("p (a b) -> p a b", a=npp),
                               r=[bk], w=[("yb", cc, p0, hh)])
                bank, bk = banks.next()
                for hp in range(6):
                    o = bank[:, hp * 64:(hp + 1) * 64]
                    for hh in range(2):
                        off = hh * 64
                        h = 2 * hp + hh
                        P.mm(o, TMZ[cc][:, 0, h, :], TMV[cc][:, hp, off:off + 64], start=(hh == 0), stop=False,
                             r=[("tmz", cc, hp, 0, hh), ("tmz", cc, hp, 1, hh), ("tmv", cc, hp)], w=[bk])
                        P.mm(o, TMZ[cc][:, 1, h, :], UU[:, h, :], start=False, stop=(hh == 1), r=uuk, w=[bk])
                P.tt("dve", SF[:], SF[:], PC[:, :, cc:cc + 1].to_broadcast([128, 6, 64]), ALU.mult, r=[("pc", cc), ("sf",)], w=[("sf",)])
                P.tt("dve", SF[:], SF[:], bank[:, 0:384].rearrange("p (a b) -> p a b", a=6), ALU.add, r=[bk, ("sf",)], w=[("sf",)])
                P.copy("act", SB[:], SF[:], r=[("sf",)], w=[("sbk",)])
            if DBG_STOP <= 3:
                return
            ybk = [("yb", cc, p0, hh) for cc in range(NCH) for p0 in (0, 4) for hh in range(2)]
            for hp in range(6):
                YC, SQf, SD = T[0], T[1], T[2]
                bank, bk = banks.next()
                P.mm(bank[:, :n], c.bda_f[:], YB[:, hp, :], r=ybk, w=[bk])
                P.tt("dve", YC[:], YB[:, hp, :], bank[:, :n], ALU.subtract, r=ybk + [bk], w=[("sw",)])
                P.act(SQf[:], YC[:], AF.Square, r=[("sw",)], w=[("logw",)])
                bank, bk = banks.next()
                P.mm(bank[:, :n], c.bda_f[:], SQf[:], r=[("logw",)], w=[bk])
                P.act(SD[:], bank[:, :n], AF.Ln, r=[bk], w=[("alr",)], bias=c.eps[:, 1:2])
                P.act(SD[:], SD[:], AF.Exp, r=[("alr",)], w=[("alr",)], scale=-0.5)
                P.tt("dve", YC[:], YC[:], SD[:], ALU.mult, r=[("sw",), ("alr",)], w=[("sw",)])
                P.ts("dve", YC[:], YC[:], vcol("a_lnx_g", hp), vcol("a_lnx_b", hp), ALU.mult, ALU.add, r=[("sw",)], w=[("sw",)])
                P.tt("dve", YC[:], YC[:], BON[:, hp, :], ALU.add, r=[("sw",), ("bon", hp)], w=[("sw",)])
                P.tt("dve", YO[:, hp, :], YC[:], G[:, hp, :], ALU.mult, r=[("sw",), ("g", hp)], w=[("yo", hp)])
            if DBG_STOP <= 4:
                return
            mem_attn_tile(P, c, QM, [("qm", 0), ("qm", 1)], YM, [("ym", 0), ("ym", 1)], n, layer, V, banks, MSQ, MTMP, QN, PT, RD, "rma")
            if dbg is not None:
                P.copy("dve", YB[:], YO[:], r=[("yo", hp) for hp in range(6)], w=ybk)
                P.dma("sp", dbg[0:768, t * TR:(t + 1) * TR].rearrange("(k p) n -> p k n", p=128), YB[:], "dbg0", r=ybk)
                P.copy("dve", QM[:], YM[:], r=[("ym", 0), ("ym", 1)], w=[("qm", 0), ("qm", 1)])
                P.dma("sp", dbg[768:1024, t * TR:(t + 1) * TR].rearrange("(k p) n -> p k n", p=128), QM[:], "dbg1", r=[("qm", 0), ("qm", 1)])
            for m in range(KD):
                bank, bk = banks.next()
                for k in range(KD):
                    rhs = YO[:, k, :] if k < 6 else YM[:, k - 6, :]
                    rk = ("yo", k) if k < 6 else ("ym", k - 6)
                    P.mm(bank[:, :n], WO[:, k, m * 128:(m + 1) * 128], rhs, start=(k == 0), stop=(k == KD - 1), r=[("wo", k), rk], w=[bk])
                P.tt("dve", XT[:, m, :], XT[:, m, :], bank[:, :n], ALU.add, r=[bk, ("xt",)], w=[("xt",)])
            P.dma("sp", dst[:, t * TR:(t + 1) * TR].rearrange("(k p) n -> p k n", p=128), XT[:], "xs0",
                  r=[("xt",)], w=[("X", dst.tensor.name, t)])

        for t in range(min(NTR, DBG_NTR)):
            do_tile(t)
        P.barrier()


DIL = (1, 4, 16)
MASKV = -30000.0


def flat_head_rms(P, c, bank, bk, n, KFt, SQt, RS, banks, tag, pb):
    P.copy("act", KFt[pb][0:64, :n], bank[0:64, :n], r=[bk], w=[(tag, "kf", pb)])
    P.act(SQt[pb][0:64, :n], KFt[pb][0:64, :n], AF.Square, r=[(tag, "kf", pb)], w=[(tag, "sq", pb)])
    b2, k2 = banks.next()
    P.mm(b2[0:64, :n], c.ones_b[0:64, 0:64], SQt[pb][0:64, :n], r=[(tag, "sq", pb)], w=[k2])
    P.act(RS[pb][0:64, :n], b2[0:64, :n], AF.Ln, r=[k2], w=[(tag, "rs", pb)], bias=c.eps[0:64, 0:1], scale=1.0 / 64)
    P.act(RS[pb][0:64, :n], RS[pb][0:64, :n], AF.Exp, r=[(tag, "rs", pb)], w=[(tag, "rs", pb)], scale=-0.5)


def kv_phase(P, nc, c, src, kvw_d, KT_d, V_d, V):
    with ExitStack() as es:
        def sb(name, shape, dt):
            return es.enter_context(nc.sbuf_tensor(_u("kv_" + name), shape, dt))
        WKV = sb("w", [128, KD, 1536], BF16)
        XTL = [sb("xt%d" % i, [128, KD, TT], F32) for i in range(2)]
        XN = sb("xn", [128, KD, TT], BF16)
        SQ = [sb("sq%d" % i, [128, TT], BF16) for i in range(2)]
        RSTD = sb("rstd", [128, TT], F32)
        KFt = [sb("kf%d" % i, [64, TT], F32) for i in range(2)]
        SQt = [sb("sqt%d" % i, [64, TT], BF16) for i in range(2)]
        RS = [sb("rs%d" % i, [64, TT], F32) for i in range(2)]
        KTA = sb("kta", [64, 12, S], BF16)
        VT = [sb("vt%d" % i, [128, 768], BF16) for i in range(2)]
        pbt = [es.enter_context(nc.psum_tensor(_u("kv_pb%d" % i), [128, 512], F32)) for i in range(8)]
        banks = Banks(pbt, "kpb")
        for k0 in range(0, KD, 2):
            P.dma("pool", WKV[:, k0:k0 + 2], kvw_d[k0 * 128:(k0 + 2) * 128, :].rearrange("(k p) c -> p k c", p=128), "wq%d" % (k0 // 2),
                  w=[("kvw", k) for k in range(k0, k0 + 2)])
        gk = V["kv_k_norm"]

        def kload(t):
            P.dma("sp", XTL[t % 2][:], src[:, t * TT:(t + 1) * TT].rearrange("(k p) n -> p k n", p=128), "xl%d" % (t % 2), w=[("xt", t % 2)])
        kload(0)
        for t in range(NT):
            n = TT
            if t + 1 < NT:
                kload(t + 1)
            XT = XTL[t % 2]
            rb, rbk = banks.next()
            rms_tile(P, c, XT, ("xt", t % 2), XN, ("xn",), V["kv_norm"], n, rb, rbk, SQ, RSTD, "kv")
            for h in range(12):
                dil = DIL[h // 4]
                pb = h % 2
                bank, bk = banks.next()
                for k in range(KD):
                    P.mm(bank[0:64, :n], WKV[:, k, h * 64:(h + 1) * 64], XN[:, k, :], start=(k == 0), stop=(k == KD - 1),
                         r=[("kvw", k), ("xn", k)], w=[bk])
                flat_head_rms(P, c, bank, bk, n, KFt, SQt, RS, banks, "kvh", pb)
                dst = KTA[0:64, h, :].rearrange("p (c i) -> p c i", c=dil)[:, :, t * (TT // dil):(t + 1) * (TT // dil)]
                P.stt("dve", dst, KFt[pb][0:64, :n].rearrange("p (i c) -> p c i", c=dil), c.vecs[0:64, gk:gk + 1],
                      RS[pb][0:64, :n].rearrange("p (i c) -> p c i", c=dil), ALU.mult, ALU.mult,
                      r=[("kvh", "kf", pb), ("kvh", "rs", pb)], w=[("kta", h, t)])
            for s4 in range(TT // 128):
                vb = VT[s4 % 2]
                b1, k1 = banks.next()
                for k in range(KD):
                    P.mm(b1[:, 0:512], XN[:, k, s4 * 128:(s4 + 1) * 128], WKV[:, k, 768:1280], start=(k == 0), stop=(k == KD - 1),
                         r=[("kvw", k), ("xn", k)], w=[k1])
                P.copy("act", vb[:, 0:512], b1[:, 0:512], r=[k1], w=[("vt", s4 % 2, 0)])
                b2, k2 = banks.next()
                for k in range(KD):
                    P.mm(b2[:, 0:256], XN[:, k, s4 * 128:(s4 + 1) * 128], WKV[:, k, 1280:1536], start=(k == 0), stop=(k == KD - 1),
                         r=[("kvw", k), ("xn", k)], w=[k2])
                P.copy("dve", vb[:, 512:768], b2[:, 0:256], r=[k2], w=[("vt", s4 % 2, 1)])
                P.dma("sp", V_d[t * TT + s4 * 128:t * TT + (s4 + 1) * 128, :], vb[:], "vs%d" % (s4 % 2),
                      r=[("vt", s4 % 2, 0), ("vt", s4 % 2, 1)])
        for h in range(12):
            P.dma("sp", KT_d[:, h, :], KTA[0:64, h, :], "ks%d" % (h % 2), r=[("kta", h, t) for t in range(NT)])
        P.barrier()


def attn_phase(P, nc, c, src, dst, j, layer, wq_d, wo_d, KT_d, V_d, relb_d, sel_d, E_d, V):
    HALF = S // 2
    NTH = HALF // TR
    with ExitStack() as es:
        def sb(name, shape, dt):
            return es.enter_context(nc.sbuf_tensor(_u("at_" + name), shape, dt))
        WQ = sb("wq", [128, KD, D], BF16)
        WO = sb("wo", [128, 4, D], BF16)
        KTG = sb("ktg", [64, 4, S], BF16)
        VZ = sb("vz", [128, 32, 4, 128], BF16)
        QG = sb("qg", [64, 4, HALF], BF16)
        ACN = sb("acn", [128, 2, HALF], F32)
        ACD = sb("acd", [128, 2, HALF], F32)
        XTL = [sb("xt%d" % i, [128, KD, TR], F32) for i in range(2)]
        XN = sb("xn", [128, KD, TR], BF16)
        SQ = [sb("sq%d" % i, [128, TR], BF16) for i in range(2)]
        RSTD = sb("rstd", [128, TR], F32)
        KFt = [sb("kf%d" % i, [64, TR], F32) for i in range(2)]
        SQt = [sb("sqt%d" % i, [64, TR], BF16) for i in range(2)]
        RS = [sb("rs%d" % i, [64, TR], F32) for i in range(2)]
        BM = sb("bm", [128, 12, 256], F32)
        TAB = sb("tab", [33, 12], F32)
        SEL = sb("sel", [33, 3, 510], F32)
        ESB = sb("esb", [12, 510], F32)
        HK = [sb("hk%d" % i, [128, 128], F32) for i in range(2)]
        LG = [sb("lg%d" % i, [128, 256], F32) for i in range(2)]
        PTb = [sb("ptb%d" % i, [128, 256], BF16) for i in range(4)]
        OB = sb("ob", [128, 2, TR], BF16)
        QM = sb("qm", [128, 2, TR], F32)
        YM = sb("ym", [128, 2, TR], BF16)
        QN = sb("qn", [128, 2, TR], BF16)
        PT = [sb("pt%d" % i, [128, TR], BF16) for i in range(4)]
        RD = sb("rd", [128, TR], F32)
        MSQ = sb("msq", [128, TR], BF16)
        MTMP = sb("mtmp", [128, TR], F32)
        pbt = [es.enter_context(nc.psum_tensor(_u("at_pb%d" % i), [128, 512], F32)) for i in range(8)]
        banks = Banks(pbt, "apb")
        print("[kernel] attn sbuf free", nc.sbuf_bytes_remaining)
        xcnt = [0]

        def xload(t):
            b = xcnt[0] % 2
            xcnt[0] += 1
            P.dma("sp", XTL[b][:], src[:, t * TR:(t + 1) * TR].rearrange("(k p) n -> p k n", p=128), "xl%d" % b,
                  r=[("X", src.tensor.name, t)], w=[("xt", b)])
            return b
        for k0 in range(0, KD, 2):
            P.dma("pool", WQ[:, k0:k0 + 2], wq_d[k0 * 128:(k0 + 2) * 128, :].rearrange("(k p) c -> p k c", p=128), "wq%d" % (k0 // 2),
                  w=[("wq", k) for k in range(k0, k0 + 2)])
        P.dma("pool", WO[:], wo_d.rearrange("(k p) c -> p k c", p=128), "wq0", w=[("wo",)])
        P.add("dve", lambda e: e.memset(TAB[:], MASKV), w=[("tab",)])
        P.dma("sp", TAB[0:32, :], relb_d[:, :], "xl1", w=[("tab", 1)], r=[("tab",)])
        P.dma("sp", SEL[:], sel_d.rearrange("g b n -> b g n"), "xs1", w=[("sel",)])
        for g in range(3):
            bank, bk = banks.next()
            P.mm(bank[0:12, 0:510], TAB[:, :], SEL[:, g, :], r=[("tab",), ("tab", 1), ("sel",)], w=[bk])
            P.copy("act", ESB[:], bank[0:12, 0:510], r=[bk], w=[("esb",)])
            P.dma("sp", E_d[g], ESB[:], "vs0", r=[("esb",)], w=[("E", g)])
            for h in range(4):
                for role in range(2):
                    i = (h * 2 + role) % 2
                    srcap = bass.AP(tensor=E_d.tensor, offset=g * 12 * 510 + (4 * g + h) * 510 + role * 255, ap=[[1, 128], [1, 128]])
                    P.dma("sp", HK[i][:], srcap, "xl%d" % i, r=[("E", g)], w=[("hk", i)])
                    bank, bk = banks.next()
                    P.mm(bank[:, 0:128], c.jf[:], HK[i][:], r=[("hk", i)], w=[bk])
                    P.copy("act", BM[:, 4 * g + h, role * 128:(role + 1) * 128], bank[:, 0:128], r=[bk], w=[("bm", g, h, role)])
        P.add("dve", lambda e: e.memset(VZ[:], 0.0), w=[("vz", 0), ("vz", 1)])
        gq = V["b_q_norm%d" % j]
        for H in range(2):
            P.add("dve", lambda e: e.memset(ACN[:], 0.0), w=[("acn",)])
            P.add("dve", lambda e: e.memset(ACD[:], 0.0), w=[("acd",)])
            for g in range(3):
                dil = DIL[g]
                nb = S // (dil * 128)
                nbh = nb // 2
                SL = S // dil
                HL = HALF // dil
                P.dma("sp", KTG[:], KT_d[:, 4 * g:4 * g + 4, :], "xl0", w=[("ktg",)])
                for h in range(4):
                    hh = h % 2
                    vsrc = bass.AP(tensor=V_d.tensor, offset=(4 * g + h) * 64,
                                   ap=[[dil * 768, 128], [768, dil], [128 * dil * 768, nb], [1, 64]])
                    P.dma("sp" if h % 2 == 0 else "act", VZ[:, 0:dil * nb, h, hh * 64:(hh + 1) * 64].rearrange("p (c n) d -> p c n d", c=dil),
                          vsrc, "vl%d" % h, w=[("vzh", h)], r=[("vz", 0)])
                vzk = [("vzh", h) for h in range(4)]
                nxt = xload(H * NTH)
                for tt in range(NTH):
                    t = H * NTH + tt
                    n = TR
                    b = nxt
                    if tt + 1 < NTH:
                        nxt = xload(t + 1)
                    XT = XTL[b]
                    rb, rbk = banks.next()
                    rms_tile(P, c, XT, ("xt", b), XN, ("xn",), V["mix_norm%d" % layer], n, rb, rbk, SQ, RSTD, "at")
                    for h in range(4):
                        hq = 4 * g + h
                        pb = h % 2
                        bank, bk = banks.next()
                        for k in range(KD):
                            P.mm(bank[0:64, :n], WQ[:, k, hq * 64:(hq + 1) * 64], XN[:, k, :], start=(k == 0), stop=(k == KD - 1),
                                 r=[("wq", k), ("xn", k)], w=[bk])
                        flat_head_rms(P, c, bank, bk, n, KFt, SQt, RS, banks, "ath", pb)
                        dq = QG[0:64, h, :].rearrange("p (c i) -> p c i", c=dil)[:, :, tt * (TR // dil):(tt + 1) * (TR // dil)]
                        P.stt("dve", dq, KFt[pb][0:64, :n].rearrange("p (i c) -> p c i", c=dil), c.vecs[0:64, gq:gq + 1],
                              RS[pb][0:64, :n].rearrange("p (i c) -> p c i", c=dil), ALU.mult, ALU.mult,
                              r=[("ath", "kf", pb), ("ath", "rs", pb)], w=[("qg", h, tt)])
                qgk = [("qg", h, tt) for h in range(4) for tt in range(NTH)]
                for cidx in range(dil):
                    for nl in range(nbh):
                        nblk = H * nbh + nl
                        qcol = cidx * HL + nl * 128
                        for hp in range(2):
                            pts = []
                            for hh in range(2):
                                h = 2 * hp + hh
                                bank, bk = banks.next()
                                kcol = cidx * SL + nblk * 128
                                P.mm(bank[:, 0:128], KTG[0:64, h, kcol:kcol + 128], QG[0:64, h, qcol:qcol + 128], r=[("ktg",)] + qgk, w=[bk])
                                wcols = 128
                                if nblk > 0:
                                    P.mm(bank[:, 128:256], KTG[0:64, h, kcol - 128:kcol], QG[0:64, h, qcol:qcol + 128], r=[("ktg",)] + qgk, w=[bk])
                                    wcols = 256
                                li = (hp * 2 + hh) % 2
                                P.stt("dve", LG[li][:, 0:wcols], bank[:, 0:wcols], 0.125, BM[:, 4 * g + h, 0:wcols], ALU.mult, ALU.add,
                                      r=[bk] + [("bm", g, h, r_) for r_ in range(2)], w=[("lg", li)])
                                pi = hp * 2 + hh
                                P.act(PTb[pi][:, 0:wcols], LG[li][:, 0:wcols], AF.Exp, r=[("lg", li)], w=[("ptb", pi)])
                                pts.append((pi, h, hh, wcols))
                            bn, kn = banks.next()
                            bd, kd = banks.next()
                            mms = []
                            for (pi, h, hh, wcols) in pts:
                                mms.append((VZ[:, cidx * nb + nblk, h, :], c.onesh[hh], PTb[pi][:, 0:128], pi, h))
                                if wcols == 256:
                                    mms.append((VZ[:, cidx * nb + nblk - 1, h, :], c.onesh[hh], PTb[pi][:, 128:256], pi, h))
                            for i, (vz, oh, rhs, pi, h) in enumerate(mms):
                                P.mm(bn[:, 0:128], vz, rhs, start=(i == 0), stop=(i == len(mms) - 1), r=[("ptb", pi), ("vzh", h)], w=[kn])
                            for i, (vz, oh, rhs, pi, h) in enumerate(mms):
                                P.mm(bd[:, 0:128], oh, rhs, start=(i == 0), stop=(i == len(mms) - 1), r=[("ptb", pi)], w=[kd])
                            an = ACN[:, hp, :].rearrange("p (i c) -> p c i", c=dil)[:, cidx, nl * 128:(nl + 1) * 128]
                            ad = ACD[:, hp, :].rearrange("p (i c) -> p c i", c=dil)[:, cidx, nl * 128:(nl + 1) * 128]
                            P.tt("dve", an, an, bn[:, 0:128], ALU.add, r=[kn, ("acn",)], w=[("acn",)])
                            P.tt("act" if False else "dve", ad, ad, bd[:, 0:128], ALU.add, r=[kd, ("acd",)], w=[("acd",)])
            nxt = xload(H * NTH)
            for tt in range(NTH):
                t = H * NTH + tt
                n = TR
                b = nxt
                if tt + 1 < NTH:
                    nxt = xload(t + 1)
                XT = XTL[b]
                xk = ("xt", b)
                tsl = slice(tt * TR, (tt + 1) * TR)
                P.act(ACD[:, :, tsl], ACD[:, :, tsl], AF.Ln, r=[("acd",)], w=[("acd",)])
                P.act(ACD[:, :, tsl], ACD[:, :, tsl], AF.Exp, r=[("acd",)], w=[("acd",)], scale=-1.0)
                P.tt("dve", OB[:], ACN[:, :, tsl], ACD[:, :, tsl], ALU.mult, r=[("acn",), ("acd",)], w=[("ob",)])
                rb, rbk = banks.next()
                rms_tile(P, c, XT, xk, XN, ("xn",), V["mix_norm%d" % layer], n, rb, rbk, SQ, RSTD, "at")
                for q in range(2):
                    bq, kq = banks.next()
                    for k in range(KD):
                        P.mm(bq[:, :n], WQ[:, k, 768 + q * 128:768 + (q + 1) * 128], XN[:, k, :], start=(k == 0), stop=(k == KD - 1),
                             r=[("wq", k), ("xn", k)], w=[kq])
                    P.copy("act", QM[:, q, :], bq[:, :n], r=[kq], w=[("qm", q)])
                mem_attn_tile(P, c, QM, [("qm", 0), ("qm", 1)], YM, [("ym", 0), ("ym", 1)], n, layer, V, banks, MSQ, MTMP, QN, PT, RD, "ama")
                for m in range(KD):
                    bank, bk = banks.next()
                    for k in range(4):
                        rhs = OB[:, k, :] if k < 2 else YM[:, k - 2, :]
                        rk = ("ob",) if k < 2 else ("ym", k - 2)
                        P.mm(bank[:, :n], WO[:, k, m * 128:(m + 1) * 128], rhs, start=(k == 0), stop=(k == 3), r=[("wo",), rk], w=[bk])
                    P.tt("dve", XT[:, m, :], XT[:, m, :], bank[:, :n], ALU.add, r=[bk, xk], w=[xk])
                P.dma("sp", dst[:, t * TR:(t + 1) * TR].rearrange("(k p) n -> p k n", p=128), XT[:], "xs%d" % b,
                      r=[xk], w=[("X", dst.tensor.name, t)])
        P.barrier()


def vec_layout():
    V = {}
    off = 0

    def put(name, n):
        nonlocal off
        V[name] = off
        off += n
    for i in range(8):
        put("ffn%d" % i, KD)
    for l in range(4):
        put("mix_norm%d" % l, KD)
        put("mem_norm%d" % l, KD)
        put("mem_q_norm%d" % l, 1)
        put("mem_k_norm%d" % l, 1)
    for i in range(2):
        put("a_mu%d" % i, 20)
        for nm in ("a_w0", "a_a0", "a_kk_scale", "a_k_a", "a_r_k", "a_lnx_g", "a_lnx_b"):
            put(nm + "%d" % i, 6)
    for j in range(2):
        put("b_q_norm%d" % j, 1)
    put("kv_norm", KD)
    put("kv_k_norm", 1)
    return V, off


NCONST = 128 * 8 + 256
NKF = NCONST - 384


def build(n_stages=99, dbg_on=False):
    nc = bass.Bass("TRN2", target_bir_lowering=False)
    es = ExitStack()
    c = Ctx()
    V, NV = vec_layout()
    xT = nc.dram_tensor("xT", [D, S], F32, kind="ExternalInput").ap()
    memT = nc.dram_tensor("memT", [D, 256], F32, kind="ExternalInput").ap()
    outT = nc.dram_tensor("outT", [D, S], F32, kind="ExternalOutput").ap()
    dbg = nc.dram_tensor("dbg", [D, S], F32, kind="ExternalOutput").ap() if dbg_on else None
    XS = nc.dram_tensor("xs_scratch", [D, S], F32, kind="Internal").ap()
    vecs_d = nc.dram_tensor("vecs", [128, NV], F32, kind="ExternalInput").ap()
    consts_d = nc.dram_tensor("consts", [128, NCONST + 256], F32, kind="ExternalInput").ap()
    w_in_d = [nc.dram_tensor("w_in%d" % i, [2 * KF, 128, KD, 128], F32, kind="ExternalInput").ap() for i in range(8)]
    w_out_d = [nc.dram_tensor("w_out%d" % i, [FF, D], F32, kind="ExternalInput").ap() for i in range(8)]
    a_w_in_d = [nc.dram_tensor("a_w_in%d" % i, [22, 128, KD, 128], F32, kind="ExternalInput").ap() for i in range(2)]
    a_w_out_d = [nc.dram_tensor("a_w_out%d" % i, [D, D], F32, kind="ExternalInput").ap() for i in range(2)]
    a_wup_d = [nc.dram_tensor("a_w_up%d" % i, [64, 768], F32, kind="ExternalInput").ap() for i in range(2)]
    a_aup_d = [nc.dram_tensor("a_a_up%d" % i, [64, 768], F32, kind="ExternalInput").ap() for i in range(2)]
    a_gup_d = [nc.dram_tensor("a_g_up%d" % i, [128, 768], F32, kind="ExternalInput").ap() for i in range(2)]
    wkv_d = [nc.dram_tensor("mem_w_kv%d" % l, [D, 512], F32, kind="ExternalInput").ap() for l in range(4)]
    b_wq_d = [nc.dram_tensor("b_w_q%d" % i, [D, D], F32, kind="ExternalInput").ap() for i in range(2)]
    b_wo_d = [nc.dram_tensor("b_w_out%d" % i, [512, D], F32, kind="ExternalInput").ap() for i in range(2)]
    kvw_d = nc.dram_tensor("kv_w", [D, 1536], F32, kind="ExternalInput").ap()
    relb_d = nc.dram_tensor("rel_bias", [32, 12], F32, kind="ExternalInput").ap()
    sel_d = nc.dram_tensor("sel", [3, 33, 510], F32, kind="ExternalInput").ap()
    KT_d = nc.dram_tensor("kt_scratch", [64, 12, S], BF16, kind="Internal").ap()
    V_d = nc.dram_tensor("v_scratch", [S, 768], BF16, kind="Internal").ap()
    E_d = nc.dram_tensor("e_scratch", [3, 12, 510], F32, kind="Internal").ap()

    P = Prog(nc, es)

    def sbt(name, shape, dt):
        return es.enter_context(nc.sbuf_tensor(_u(name), shape, dt))
    c.vecs = sbt("c_vecs", [128, NV], F32)
    c.KF = sbt("c_kf", [128, NKF], F32)
    c.KB = sbt("c_kb", [128, 5 * 128], BF16)
    c.eps = sbt("c_eps", [128, 4], F32)
    c.MK = sbt("c_mk", [128, 2, 256], BF16)
    c.MVZ = sbt("c_mvz", [128, 2, 4, 128], BF16)
    P.dma("sp", c.vecs[:], vecs_d[:, :], "c0", w=[("vecs",)])
    P.dma("sp", c.KF[:], consts_d[:, 384:NCONST], "c1", w=[("kf",)])
    P.dma("pool", c.KB[:, 0:384], consts_d[:, 0:384], "c2", w=[("kb",)])
    P.dma("pool", c.KB[:, 384:640], consts_d[:, NCONST:NCONST + 256], "c3", w=[("kb2",)])
    P.add("dve", lambda e: e.memset(c.eps[:, 0:1], NORM_EPS), w=[("eps", 0)])
    P.add("dve", lambda e: e.memset(c.eps[:, 1:2], LNX_EPS), w=[("eps", 1)])
    P.add("dve", lambda e: e.memset(c.eps[:, 2:3], 1e-30), w=[("eps", 2)])
    P.add("dve", lambda e: e.memset(c.MVZ[:], 0.0), w=[("MVZ",)])
    c.bda_f = c.KF[:, 0:128]
    c.msu = c.KF[:, 128:256]
    c.msl = c.KF[:, 256:384]
    c.mui = c.KF[:, 384:512]
    c.cmask = c.KF[:, 512:768]
    c.jf = c.KF[:, 768:896]
    c.ident_b = c.KB[:, 0:128]
    c.ones_b = c.KB[:, 128:256]
    c.bd_b = c.KB[:, 256:384]
    c.ones_bf = c.ones_b
    c.onesh = [c.KB[:, 384:512], c.KB[:, 512:640]]
    P.barrier()

    stages = []
    for layer in range(4):
        stages.append(("ffn", 2 * layer))
        stages.append(("mix", layer))
        stages.append(("ffn", 2 * layer + 1))
        if layer == 1:
            stages.append(("kv", 0))
    stages = stages[:n_stages]
    cur = xT
    for si, (kind, i) in enumerate(stages):
        last = si == len(stages) - 1
        dst = outT if last else XS
        if kind == "ffn":
            ffn_phase(P, nc, c, cur, dst, w_in_d[i], w_out_d[i], gcol=V["ffn%d" % i])
        elif kind == "kv":
            kv_phase(P, nc, c, cur, kvw_d, KT_d, V_d, V)
            continue
        else:
            layer = i
            mem_prep(P, nc, c, memT, wkv_d[layer], layer, V)
            if layer < 2:
                rwkv_phase(P, nc, c, cur, dst, layer, layer, a_w_in_d[layer], a_w_out_d[layer], a_wup_d[layer], a_aup_d[layer],
                           a_gup_d[layer], V, dbg=dbg if last else None)
            else:
                attn_phase(P, nc, c, cur, dst, layer - 2, layer, b_wq_d[layer - 2], b_wo_d[layer - 2], KT_d, V_d, relb_d, sel_d, E_d, V)
        cur = dst
    P.barrier()
    es.close()
    print("[kernel] ops=%d instr=%d" % (P.nops, P.ninstr))
    return nc


def _rep2(v):
    return np.ascontiguousarray(np.concatenate([v, v]).reshape(128, 1))


def _t5_bucket(dist):
    dist = np.asarray(dist, np.int64)
    d_f = np.maximum(dist, 1).astype(np.float32)
    large = 16 + (np.log(d_f / np.float32(16)) / np.float32(np.log(2048 / 16)) * np.float32(16)).astype(np.int32)
    large = np.minimum(large, 31)
    return np.where(dist < 16, dist, large)


def make_sel():
    sel = np.zeros((3, 33, 510), np.float32)
    n = np.arange(255)
    for g, dil in enumerate((1, 4, 16)):
        own_valid = n >= 127
        bo = np.where(own_valid, _t5_bucket(np.maximum(n - 127, 0) * dil), 32)
        prev_valid = n <= 127
        bp = np.where(prev_valid, _t5_bucket((n + 1) * dil), 32)
        sel[g, bo, n] = 1.0
        sel[g, bp, 255 + n] = 1.0
    return sel


def make_consts():
    K = np.zeros((128, NCONST + 256), np.float32)
    K[:, NCONST:NCONST + 64] = 1.0
    K[:, NCONST + 192:NCONST + 256] = 1.0
    K[:, 0:128] = np.eye(128)
    K[:, 128:256] = 1.0
    bd = np.zeros((128, 128), np.float32)
    bd[:64, :64] = 1.0
    bd[64:, 64:] = 1.0
    K[:, 256:384] = bd
    K[:, 384:512] = bd / 64.0
    i = np.arange(128)
    K[:, 512:640] = (i[:, None] < i[None, :])
    K[:, 640:768] = (i[:, None] > i[None, :])
    K[:, 768:896] = (i[:, None] <= i[None, :])
    cm = np.ones(256, np.float32)
    cm[::128] = 0.0
    K[:, 896:896 + 256] = cm[None, :]
    K[:, 1152:1280] = np.eye(128)[::-1]
    return K


def kernel(**inputs):
    n_stages = int(inputs.pop("_n_stages", 99))
    cores = inputs.pop("_cores", list(range(8)))
    trace = inputs.pop("_trace", False)
    dbg_on = inputs.pop("_dbg", False)
    f = lambda a: np.asarray(a, dtype=np.float32)
    x = f(inputs["x"])
    mem = f(inputs["mem"])
    V, NV = vec_layout()
    vecs = np.zeros((128, NV), np.float32)

    def put(name, arr):
        arr = np.asarray(arr, np.float32)
        vecs[:, V[name]:V[name] + arr.shape[1]] = arr
    shared = {}
    for l in range(4):
        for nm in ("ffn_pre", "ffn_post"):
            i = 2 * l + (0 if nm == "ffn_pre" else 1)
            shared["w_in%d" % i] = _slots_in(f(inputs[nm + "_w_in"][l]))
            shared["w_out%d" % i] = np.ascontiguousarray(f(inputs[nm + "_w_out"][l]))
            put("ffn%d" % i, _vec_pk(f(inputs[nm + "_norm"][l])))
        put("mix_norm%d" % l, _vec_pk(f(inputs["mix_norm"][l])))
        put("mem_norm%d" % l, _vec_pk(f(inputs["mem_norm"][l])))
        put("mem_q_norm%d" % l, _rep2(f(inputs["mem_q_norm"][l])))
        put("mem_k_norm%d" % l, _rep2(f(inputs["mem_k_norm"][l])))
        shared["mem_w_kv%d" % l] = np.ascontiguousarray(f(inputs["mem_w_kv"][l]))
    for i in range(2):
        shared["a_w_in%d" % i] = _slots_in(f(inputs["a_w_in"][i]))
        shared["a_w_out%d" % i] = np.ascontiguousarray(f(inputs["a_w_out"][i]))
        shared["a_w_up%d" % i] = np.ascontiguousarray(f(inputs["a_w_up"][i]))
        shared["a_a_up%d" % i] = np.ascontiguousarray(f(inputs["a_a_up"][i]))
        shared["a_g_up%d" % i] = np.ascontiguousarray(f(inputs["a_g_up"][i]))
        put("a_mu%d" % i, _vec_pk(f(inputs["a_shift_mu"][i])))
        put("a_w0%d" % i, _vec_pk(f(inputs["a_w0"][i])))
        put("a_a0%d" % i, _vec_pk(f(inputs["a_a0"][i])))
        put("a_kk_scale%d" % i, _vec_pk(f(inputs["a_kk_scale"][i])))
        put("a_k_a%d" % i, _vec_pk(f(inputs["a_k_a"][i])))
        put("a_r_k%d" % i, _vec_pk(f(inputs["a_r_k"][i]).reshape(-1)))
        put("a_lnx_g%d" % i, _vec_pk(f(inputs["a_lnx_g"][i])))
        put("a_lnx_b%d" % i, _vec_pk(f(inputs["a_lnx_b"][i])))
    for j in range(2):
        put("b_q_norm%d" % j, _rep2(f(inputs["b_q_norm"][j])))
    put("kv_norm", _vec_pk(f(inputs["kv_norm"])))
    put("kv_k_norm", _rep2(f(inputs["kv_k_norm"])))
    for jj in range(2):
        shared["b_w_q%d" % jj] = np.ascontiguousarray(f(inputs["b_w_q"][jj]))
        shared["b_w_out%d" % jj] = np.ascontiguousarray(f(inputs["b_w_out"][jj]))
    shared["kv_w"] = np.ascontiguousarray(f(inputs["kv_w"]))
    shared["rel_bias"] = np.ascontiguousarray(f(inputs["rel_bias"]))
    shared["sel"] = make_sel()
    shared["vecs"] = vecs
    shared["consts"] = make_consts()
    nc = build(n_stages, dbg_on)
    in_maps = []
    for b in cores:
        m = dict(shared)
        m["xT"] = np.ascontiguousarray(x[b].T)
        m["memT"] = np.ascontiguousarray(mem[b].T)
        in_maps.append(m)
    if trace:
        res = run_bass_kernel_spmd(nc, in_maps, core_ids=list(range(len(cores))), trace=True)
        print("[kernel] exec_time_ns", res.exec_time_ns)
    else:
        res = run_bass_kernel_spmd(nc, in_maps, core_ids=list(range(len(cores))))
    if dbg_on:
        kernel.dbg = [np.ascontiguousarray(r["dbg"].T) for r in res.results]
    out = np.stack([np.ascontiguousarray(r["outT"].T) for r in res.results], axis=0)
    return out.astype(np.float32)
```
